# Optimizing a Trainium2 kernel written in Bass

```python
import jax, jax.numpy as jnp
from jax import lax
import numpy as np

D_MODEL = 1024
BATCH = 8
SEQ = 8192
DEPTH = 2

CTX_LEN = 256
GRID_W = 64
N_MIXERS = 4
GROUP_W = D_MODEL // N_MIXERS
HEADS = 4
HEAD_DIM = GROUP_W // HEADS
N_HEADS_TOTAL = N_MIXERS * HEADS
GLA_DK = HEAD_DIM // 2
GLA_RANK = 16
GLA_TAU = 16.0
D_FF = 4 * D_MODEL
CHUNK = 64
ROPE_BASE = 10000.0
RMS_EPS = 1e-6

IN_LAYOUT = (
    ('hg_q', GROUP_W), ('hg_f_fwd', GROUP_W), ('hg_f_bwd', GROUP_W), ('hg_i', GROUP_W), ('hg_g', GROUP_W),
    ('ml_q', GROUP_W), ('ml_k', GROUP_W), ('ml_v', GROUP_W), ('ml_if', 2 * 2 * HEADS), ('ml_o', GROUP_W),
    ('rt_q', GROUP_W), ('rt_k', GROUP_W), ('rt_v', GROUP_W), ('rt_g', GROUP_W),
    ('gl_q', HEADS * GLA_DK), ('gl_k', HEADS * GLA_DK), ('gl_v', GROUP_W),
    ('gl_a_fwd', GLA_RANK), ('gl_a_bwd', GLA_RANK), ('gl_g', GROUP_W),
)
IN_DIM = sum(s for _, s in IN_LAYOUT)

kernel_name = 'hybrid_bidir_recurrent_dit_block'


def rmsnorm(x):
    xf = x.astype(jnp.float32)
    return (xf * lax.rsqrt(jnp.mean(xf * xf, axis=-1, keepdims=True) + RMS_EPS)).astype(x.dtype)


def modulate(x, shift, scale):
    return rmsnorm(x) * (1.0 + scale) + shift


def split_proj(p):
    idx = np.cumsum([s for _, s in IN_LAYOUT])[:-1].tolist()
    parts = jnp.split(p, idx, axis=-1)
    return {name: t for (name, _), t in zip(IN_LAYOUT, parts)}


def to_heads(t):
    B, L, _ = t.shape
    return t.reshape(B, L, HEADS, -1).transpose(0, 2, 1, 3).astype(jnp.float32)


def from_heads(t):
    B, H, L, d = t.shape
    return t.transpose(0, 2, 1, 3).reshape(B, L, H * d)


def to_chunks(t):
    L = t.shape[2]
    t = t.reshape(t.shape[:2] + (L // CHUNK, CHUNK) + t.shape[3:])
    return jnp.moveaxis(t, 2, 0)


def from_chunks(t):
    t = jnp.moveaxis(t, 0, 2)
    return t.reshape(t.shape[:2] + (t.shape[2] * t.shape[3],) + t.shape[4:])


def chunk_linear(q, k, v, log_a, S0):
    causal = jnp.tril(jnp.ones((CHUNK, CHUNK), dtype=bool))

    def step(S, inp):
        qc, kc, vc, ac = inp
        b = jnp.cumsum(ac, axis=-2)
        if ac.shape[-1] == 1:
            diff = b[..., :, None, 0] - b[..., None, :, 0]
            decay = jnp.exp(jnp.where(causal, diff, -jnp.inf))
            attn = jnp.einsum('bhtk,bhsk->bhts', qc, kc) * decay
        else:
            diff = b[..., :, None, :] - b[..., None, :, :]
            decay = jnp.exp(jnp.where(causal[..., None], diff, -jnp.inf))
            attn = jnp.einsum('bhtk,bhsk,bhtsk->bhts', qc, kc, decay)
        b_last = b[..., -1:, :]
        o = (jnp.einsum('bhts,bhsv->bhtv', attn, vc)
             + jnp.einsum('bhtk,bhkv->bhtv', qc * jnp.exp(b), S))
        S_new = (jnp.exp(b_last[..., 0, :])[..., None] * S
                 + jnp.einsum('bhsk,bhsv->bhkv', kc * jnp.exp(b_last - b), vc))
        return S_new, o

    S, o = lax.scan(step, S0, (to_chunks(q), to_chunks(k), to_chunks(v), to_chunks(log_a)))
    return from_chunks(o), S


def chunk_mlstm(q, k, v, log_i, log_f, state):
    causal = jnp.tril(jnp.ones((CHUNK, CHUNK), dtype=bool))

    def step(carry, inp):
        Cm, n, m = carry
        qc, kc, vc, ic, fc = inp
        b = jnp.cumsum(fc, axis=-1)
        w = jnp.where(causal, b[..., :, None] - b[..., None, :] + ic[..., None, :], -jnp.inf)
        inter = b + m[..., None]
        m_t = jnp.maximum(jnp.max(w, axis=-1), inter)
        P = jnp.exp(w - m_t[..., None])
        g = jnp.exp(inter - m_t)
        s = jnp.einsum('bhtk,bhsk->bhts', qc, kc) * P
        num = (jnp.einsum('bhts,bhsv->bhtv', s, vc)
               + g[..., None] * jnp.einsum('bhtk,bhkv->bhtv', qc, Cm))
        den = jnp.sum(s, axis=-1) + g * jnp.einsum('bhtk,bhk->bht', qc, n)
        h = num / jnp.maximum(jnp.abs(den), jnp.exp(-m_t))[..., None]
        m_new = m_t[..., -1]
        a_state = jnp.exp(b[..., -1] + m - m_new)
        wk = jnp.exp(b[..., -1:] - b + ic - m_new[..., None])
        C_new = a_state[..., None, None] * Cm + jnp.einsum('bhsk,bhsv->bhkv', kc * wk[..., None], vc)
        n_new = a_state[..., None] * n + jnp.einsum('bhs,bhsk->bhk', wk, kc)
        return (C_new, n_new, m_new), h

    st, h = lax.scan(step, state, (to_chunks(q), to_chunks(k), to_chunks(v),
                                   to_chunks(log_i), to_chunks(log_f)))
    return from_chunks(h), st


def flip_seq(ts):
    return tuple(jnp.flip(t, axis=2) for t in ts)


def bidir(scan_fn, init, c_f, c_b, l_f, l_b):
    yc_f, s_f = scan_fn(*c_f, init)
    yc_b, s_b = scan_fn(*flip_seq(c_b), init)
    yl_f, _ = scan_fn(*l_f, s_f)
    yl_b, _ = scan_fn(*flip_seq(l_b), s_b)
    return yc_f + jnp.flip(yc_b, axis=2), yl_f + jnp.flip(yl_b, axis=2)


def hgrn2_inputs(p, lb_f, lb_b):
    q = jax.nn.silu(to_heads(p['hg_q']))
    v = to_heads(p['hg_i'])

    def direction(logits, lb):
        lb = lb.reshape(1, HEADS, 1, HEAD_DIM)
        z = to_heads(logits)
        log_f = jnp.logaddexp(jnp.log(lb), jnp.log1p(-lb) + jax.nn.log_sigmoid(z))
        k = (1.0 - lb) * jax.nn.sigmoid(-z)
        return (q, k, v, log_f)

    return direction(p['hg_f_fwd'], lb_f), direction(p['hg_f_bwd'], lb_b)


def hgrn2_mixer(pc, pl, lb):
    B = pl['hg_q'].shape[0]
    init = jnp.zeros((B, HEADS, HEAD_DIM, HEAD_DIM), jnp.float32)
    c_f, c_b = hgrn2_inputs(pc, lb[0], lb[1])
    l_f, l_b = hgrn2_inputs(pl, lb[0], lb[1])
    return bidir(chunk_linear, init, c_f, c_b, l_f, l_b)


def mlstm_inputs(p, gate_bias):
    q = to_heads(p['ml_q'])
    k = to_heads(p['ml_k']) * HEAD_DIM ** -0.5
    v = to_heads(p['ml_v'])
    B, L, _ = p['ml_if'].shape
    pre = p['ml_if'].astype(jnp.float32).reshape(B, L, 2, 2, HEADS) + gate_bias
    pre = jnp.transpose(pre, (2, 3, 0, 4, 1))
    fwd = (q, k, v, pre[0, 0], jax.nn.log_sigmoid(pre[0, 1]))
    bwd = (q, k, v, pre[1, 0], jax.nn.log_sigmoid(pre[1, 1]))
    return fwd, bwd


def mlstm_mixer(pc, pl, gate_bias):
    B = pl['ml_q'].shape[0]
    init = (jnp.zeros((B, HEADS, HEAD_DIM, HEAD_DIM), jnp.float32),
            jnp.zeros((B, HEADS, HEAD_DIM), jnp.float32),
            jnp.zeros((B, HEADS), jnp.float32))
    c_f, c_b = mlstm_inputs(pc, gate_bias.astype(jnp.float32))
    l_f, l_b = mlstm_inputs(pl, gate_bias.astype(jnp.float32))
    return bidir(chunk_mlstm, init, c_f, c_b, l_f, l_b)


def grid_rotary(rows):
    r = jnp.repeat(jnp.arange(rows), GRID_W).astype(jnp.float32)
    col = jnp.tile(jnp.arange(GRID_W), rows).astype(jnp.float32)
    n_freq = HEAD_DIM // 4
    inv = ROPE_BASE ** (-jnp.arange(n_freq, dtype=jnp.float32) / n_freq)
    ang_r = r[:, None] * inv[None, :]
    ang_c = col[:, None] * inv[None, :]
    return (jnp.cos(ang_r), jnp.sin(ang_r), jnp.cos(ang_c), jnp.sin(ang_c))


def rotate(x, cos, sin):
    x1, x2 = jnp.split(x, 2, axis=-1)
    return jnp.concatenate([x1 * cos - x2 * sin, x1 * sin + x2 * cos], axis=-1)


def rope2d(x, rot):
    cr, sr, cc, sc = rot
    xa, xb = jnp.split(x, 2, axis=-1)
    return jnp.concatenate([rotate(xa, cr, sr), rotate(xb, cc, sc)], axis=-1)


def retention_inputs(p, decay_logit, rot):
    q = to_heads(p['rt_q'])
    k = to_heads(p['rt_k']) * HEAD_DIM ** -0.5
    v = to_heads(p['rt_v'])
    if rot is not None:
        q = rope2d(q, rot)
        k = rope2d(k, rot)
    B, _, L, _ = q.shape
    log_g = jax.nn.log_sigmoid(decay_logit.astype(jnp.float32))
    la_f = jnp.broadcast_to(log_g[0][None, :, None, None], (B, HEADS, L, 1))
    la_b = jnp.broadcast_to(log_g[1][None, :, None, None], (B, HEADS, L, 1))
    return (q, k, v, la_f), (q, k, v, la_b)


def retention_mixer(pc, pl, decay_logit, rot):
    B = pl['rt_q'].shape[0]
    init = jnp.zeros((B, HEADS, HEAD_DIM, HEAD_DIM), jnp.float32)
    c_f, c_b = retention_inputs(pc, decay_logit, None)
    l_f, l_b = retention_inputs(pl, decay_logit, rot)
    return bidir(chunk_linear, init, c_f, c_b, l_f, l_b)


def gla_inputs(p, w_a, b_a):
    q = to_heads(p['gl_q'])
    k = to_heads(p['gl_k']) * GLA_DK ** -0.5
    v = to_heads(p['gl_v'])

    def log_alpha(z, d):
        za = z.astype(jnp.float32) @ w_a[d].astype(jnp.float32) + b_a[d].astype(jnp.float32)
        return to_heads(jax.nn.log_sigmoid(za) / GLA_TAU)

    return (q, k, v, log_alpha(p['gl_a_fwd'], 0)), (q, k, v, log_alpha(p['gl_a_bwd'], 1))


def gla_mixer(pc, pl, w_a, b_a):
    B = pl['gl_q'].shape[0]
    init = jnp.zeros((B, HEADS, GLA_DK, HEAD_DIM), jnp.float32)
    c_f, c_b = gla_inputs(pc, w_a, b_a)
    l_f, l_b = gla_inputs(pl, w_a, b_a)
    return bidir(chunk_linear, init, c_f, c_b, l_f, l_b)


def out_gates(p):
    return jnp.concatenate([jax.nn.sigmoid(p['hg_g']), jax.nn.sigmoid(p['ml_o']),
                            jax.nn.silu(p['rt_g']), jax.nn.silu(p['gl_g'])], axis=-1)


def head_norm(y, g):
    B, L, _ = y.shape
    yh = y.astype(jnp.float32).reshape(B, L, N_HEADS_TOTAL, -1)
    yh = yh * lax.rsqrt(jnp.mean(yh * yh, axis=-1, keepdims=True) + RMS_EPS)
    return yh.reshape(B, L, -1) * g


def mixer_out(raw, p, g, w_o):
    return (head_norm(raw, g) * out_gates(p)).astype(p['hg_g'].dtype) @ w_o


def sq_relu_mlp(h, w1, w2):
    return jnp.square(jax.nn.relu(h @ w1)) @ w2


def setup_inputs(seed: int = 0) -> dict:
    key = jax.random.key(seed)
    ks = jax.random.split(key, 20)
    f32 = jnp.float32
    x = jax.random.normal(ks[0], (BATCH, SEQ, D_MODEL), f32)
    c = jax.random.normal(ks[1], (BATCH, D_MODEL), f32)
    ctx = jax.random.normal(ks[2], (BATCH, CTX_LEN, D_MODEL), f32)
    c_ctx = jax.random.normal(ks[3], (D_MODEL,), f32)
    w_ada = jax.random.normal(ks[4], (DEPTH, D_MODEL, 6 * D_MODEL), f32) * (0.5 * D_MODEL ** -0.5)
    b_ada = 0.01 * jax.random.normal(ks[5], (DEPTH, 6 * D_MODEL), f32)
    w_in = jax.random.normal(ks[6], (DEPTH, D_MODEL, IN_DIM), f32) * D_MODEL ** -0.5
    g_heads = 1.0 + 0.02 * jax.random.normal(ks[7], (DEPTH, D_MODEL), f32)
    hgrn_lb_logits = 0.5 * jax.random.normal(ks[8], (DEPTH, 2, GROUP_W), f32)
    ig_bias = 0.1 * jax.random.normal(ks[9], (DEPTH, 2, 1, HEADS), f32)
    fg_bias = jnp.linspace(3.0, 6.0, HEADS, dtype=f32)[None, None, None, :] + 0.1 * jax.random.normal(ks[10], (DEPTH, 2, 1, HEADS), f32)
    ml_gate_bias = jnp.concatenate([ig_bias, fg_bias], axis=2)
    rt_base = jnp.log(2.0 ** (5.0 + jnp.arange(HEADS, dtype=f32)) - 1.0)
    rt_decay_logit = rt_base[None, None, :] + 0.1 * jax.random.normal(ks[11], (DEPTH, 2, HEADS), f32)
    gla_w_a = jax.random.normal(ks[12], (DEPTH, 2, GLA_RANK, HEADS * GLA_DK), f32) * GLA_RANK ** -0.5
    gla_b_a = 0.1 * jax.random.normal(ks[13], (DEPTH, 2, HEADS * GLA_DK), f32)
    w_out = jax.random.normal(ks[14], (DEPTH, D_MODEL, D_MODEL), f32) * D_MODEL ** -0.5
    w_ff1 = jax.random.normal(ks[15], (DEPTH, D_MODEL, D_FF), f32) * D_MODEL ** -0.5
    w_ff2 = jax.random.normal(ks[16], (DEPTH, D_FF, D_MODEL), f32) * D_FF ** -0.5
    g_final = 1.0 + 0.02 * jax.random.normal(ks[17], (D_MODEL,), f32)
    return {'x': x, 'c': c, 'ctx': ctx, 'c_ctx': c_ctx, 'w_ada': w_ada, 'b_ada': b_ada,
            'w_in': w_in, 'g_heads': g_heads, 'hgrn_lb_logits': hgrn_lb_logits,
            'ml_gate_bias': ml_gate_bias, 'rt_decay_logit': rt_decay_logit,
            'gla_w_a': gla_w_a, 'gla_b_a': gla_b_a, 'w_out': w_out,
            'w_ff1': w_ff1, 'w_ff2': w_ff2, 'g_final': g_final}


def reference(x, c, ctx, c_ctx, w_ada, b_ada, w_in, g_heads, hgrn_lb_logits, ml_gate_bias,
              rt_decay_logit, gla_w_a, gla_b_a, w_out, w_ff1, w_ff2, g_final):
    ROWS = x.shape[1] // GRID_W
    rot = grid_rotary(ROWS)
    sm = jax.nn.softmax(hgrn_lb_logits.astype(jnp.float32), axis=0)
    lb_all = jnp.maximum(jnp.cumsum(sm, axis=0) - sm[:1], 0.0)
    xl, xc = x, ctx
    for layer in range(DEPTH):
        last = layer == DEPTH - 1
        mod_l = (jax.nn.silu(c) @ w_ada[layer] + b_ada[layer])[:, None, :]
        mod_c = (jax.nn.silu(c_ctx) @ w_ada[layer] + b_ada[layer])[None, None, :]
        sh1_l, sc1_l, g1_l, sh2_l, sc2_l, g2_l = jnp.split(mod_l, 6, axis=-1)
        sh1_c, sc1_c, g1_c, sh2_c, sc2_c, g2_c = jnp.split(mod_c, 6, axis=-1)
        pl = split_proj(modulate(xl, sh1_l, sc1_l) @ w_in[layer])
        pc = split_proj(modulate(xc, sh1_c, sc1_c) @ w_in[layer])
        mixed = (hgrn2_mixer(pc, pl, lb_all[layer]),
                 mlstm_mixer(pc, pl, ml_gate_bias[layer]),
                 retention_mixer(pc, pl, rt_decay_logit[layer], rot),
                 gla_mixer(pc, pl, gla_w_a[layer], gla_b_a[layer]))
        raw_l = jnp.concatenate([from_heads(m[1]) for m in mixed], axis=-1)
        xl = xl + g1_l * mixer_out(raw_l, pl, g_heads[layer], w_out[layer])
        if not last:
            raw_c = jnp.concatenate([from_heads(m[0]) for m in mixed], axis=-1)
            xc = xc + g1_c * mixer_out(raw_c, pc, g_heads[layer], w_out[layer])
        xl = xl + g2_l * sq_relu_mlp(modulate(xl, sh2_l, sc2_l), w_ff1[layer], w_ff2[layer])
        if not last:
            xc = xc + g2_c * sq_relu_mlp(modulate(xc, sh2_c, sc2_c), w_ff1[layer], w_ff2[layer])
    return rmsnorm(xl) * g_final
```

```python
import numpy as np
from contextlib import ExitStack
import concourse.bass as bass
import concourse.mybir as mybir
from concourse.bass_utils import run_bass_kernel_spmd

F32 = mybir.dt.float32
BF16 = mybir.dt.bfloat16
AF = mybir.ActivationFunctionType
ALU = mybir.AluOpType
AX = mybir.AxisListType

D = 1024
CTX = 256
DEPTH = 2
DFF = 4096
VD = 66
NFM = 22 * 128 + 32
NTM = 2064
NC1 = NFM + NTM
EPS = 1e-6
ENGS = ("pe", "act", "dve", "pool", "sp")
HG, ML, RT, GL = 0, 1, 2, 3
HG_SHIFT = 20.0


class Buf:
    __slots__ = ("name", "last_write", "reads", "dsem", "dcount")

    def __init__(self, name):
        self.name = name
        self.last_write = None
        self.reads = []
        self.dsem = None
        self.dcount = 0


class Tl:
    def __init__(self, t, name, b=None):
        self.t = t
        self.b = b if b is not None else Buf(name)

    def __getitem__(self, i):
        return self.t[i]


def _b(x):
    return x.b if isinstance(x, Tl) else x


class Op:
    __slots__ = ("eng", "fn", "deps", "is_dma", "slot", "ticket", "signal", "retired", "seq")

    def __init__(self, eng, fn, is_dma, slot):
        self.eng = eng
        self.fn = fn
        self.deps = []
        self.is_dma = is_dma
        self.slot = slot
        self.ticket = None
        self.signal = False
        self.retired = False


class Prog:
    def __init__(self, nc, stack):
        self.nc = nc
        self.stack = stack
        self.ops = []
        self.esem = {e: stack.enter_context(nc.semaphore("sem_" + e)) for e in ENGS}
        self.ecount = {e: 0 for e in ENGS}
        self.waited = {e: {} for e in ENGS}
        self.pending_bar = {e: [] for e in ENGS}
        self.sempool = []
        self.nsem = 0
        self.engobj = {"pe": nc.tensor, "act": nc.scalar, "dve": nc.vector, "pool": nc.gpsimd, "sp": nc.sync}
        self.ninst = 0
        self.deferred = []
        self.seq = 0

    def defer_dma(self, fn, slot, reads=(), writes=(), eng="sp"):
        self.deferred.append((fn, slot, reads, writes, eng))

    def release(self):
        dd = self.deferred
        self.deferred = []
        for fn, slot, reads, writes, eng in dd:
            self.dma(fn, slot, reads, writes, eng)

    def add(self, eng, fn, reads=(), writes=(), is_dma=False, slot=None):
        op = Op(eng, fn, is_dma, slot)
        deps = set()
        for x in reads:
            b = _b(x)
            if b.last_write is not None:
                deps.add(b.last_write)
        for x in writes:
            b = _b(x)
            if b.last_write is not None:
                deps.add(b.last_write)
            for r in b.reads:
                deps.add(r)
        dl = [d for d in deps if (not d.retired) and not (eng == "pe" and d.eng == "pe" and not d.is_dma)]
        best = {}
        keep = []
        for d in dl:
            if d.is_dma:
                keep.append(d)
            elif d.eng not in best or best[d.eng].seq < d.seq:
                best[d.eng] = d
        op.deps = keep + list(best.values())
        self.seq += 1
        op.seq = self.seq
        for x in reads:
            _b(x).reads.append(op)
        for x in writes:
            b = _b(x)
            b.last_write = op
            b.reads = []
        self.ops.append(op)
        return op

    def pe(self, fn, reads=(), writes=()):
        return self.add("pe", fn, reads, writes)

    def act(self, fn, reads=(), writes=()):
        return self.add("act", fn, reads, writes)

    def dve(self, fn, reads=(), writes=()):
        return self.add("dve", fn, reads, writes)

    def pool(self, fn, reads=(), writes=()):
        return self.add("pool", fn, reads, writes)

    def dma(self, fn, slot, reads=(), writes=(), eng="sp"):
        return self.add(eng, fn, reads, writes, is_dma=True, slot=_b(slot))

    def flush(self, final=False):
        nc = self.nc
        self.release()
        ops = self.ops
        self.ops = []
        for op in ops:
            for d in op.deps:
                d.signal = True
        last = {}
        for op in ops:
            if not op.is_dma:
                last[op.eng] = op
        for op in last.values():
            op.signal = True
        slots = []
        swbar = []
        for op in ops:
            if op.is_dma and op.eng == "pool":
                sem = self.stack.enter_context(nc.semaphore("sw%d" % self.nsem))
                self.nsem += 1
                op.ticket = (sem, 16)
                swbar.append(op.ticket)
            elif op.is_dma:
                s = op.slot
                if s.dsem is None:
                    if self.sempool:
                        s.dsem, s.dcount = self.sempool.pop()
                    else:
                        s.dsem = self.stack.enter_context(nc.semaphore("ds%d" % self.nsem))
                        self.nsem += 1
                        s.dcount = 0
                    slots.append(s)
                s.dcount += 16
                op.ticket = (s.dsem, s.dcount)
            elif op.signal:
                self.ecount[op.eng] += 1
                op.ticket = (self.esem[op.eng], self.ecount[op.eng])
        per = {e: [] for e in ENGS}
        for op in ops:
            per[op.eng].append(op)
        bar = [last[e].ticket for e in last] + [(s.dsem, s.dcount) for s in slots] + swbar

        def run(ename, eng):
            waited = self.waited[ename]

            def w(sem, val):
                k = id(sem)
                if waited.get(k, 0) < val:
                    eng.wait_ge(sem, val)
                    waited[k] = val

            if per[ename] or final:
                for sem, val in self.pending_bar[ename]:
                    w(sem, val)
                self.pending_bar[ename] = []
            for op in per[ename]:
                need = {}
                for d in op.deps:
                    sem, val = d.ticket
                    k = id(sem)
                    if k not in need or need[k][1] < val:
                        need[k] = (sem, val)
                for sem, val in need.values():
                    w(sem, val)
                ins = op.fn(eng)
                self.ninst += 1
                if op.is_dma:
                    ins.then_inc(op.ticket[0], 16)
                elif op.signal:
                    ins.then_inc(op.ticket[0], 1)
            if final and ename == "sp":
                for sem, val in bar:
                    w(sem, val)

        with nc.Block() as block:
            @block.tensor
            def _(eng):
                run("pe", eng)

            @block.scalar
            def _(eng):
                run("act", eng)

            @block.vector
            def _(eng):
                run("dve", eng)

            @block.gpsimd
            def _(eng):
                run("pool", eng)

            @block.sync
            def _(eng):
                run("sp", eng)

        for e in ENGS:
            self.pending_bar[e].extend(bar)
        for op in ops:
            op.retired = True
        for s in slots:
            self.sempool.append((s.dsem, s.dcount))
            s.dsem = None


class Rot:
    def __init__(self, items):
        self.items = items
        self.i = 0

    def nxt(self):
        r = self.items[self.i % len(self.items)]
        self.i += 1
        return r


class Em:
    def __init__(self, P):
        self.P = P

    def act(self, out, in_, func, R, W, **kw):
        self.P.act(lambda e: e.activation(out=out, in_=in_, func=func, **kw), R, W)

    def tt(self, eng, out, in0, in1, op, R, W):
        self.P.add(eng, lambda e: e.tensor_tensor(out=out, in0=in0, in1=in1, op=op), R, W)

    def ts(self, eng, out, in0, s1, s2, op0, op1, R, W):
        if op1 is None:
            self.P.add(eng, lambda e: e.tensor_scalar(out=out, in0=in0, scalar1=s1, scalar2=None, op0=op0), R, W)
        else:
            self.P.add(eng, lambda e: e.tensor_scalar(out=out, in0=in0, scalar1=s1, scalar2=s2, op0=op0, op1=op1), R, W)

    def stt(self, out, in0, scalar, in1, op0, op1, R, W):
        self.P.dve(lambda e: e.scalar_tensor_tensor(out=out, in0=in0, scalar=scalar, in1=in1, op0=op0, op1=op1), R, W)

    def copy(self, eng, out, in_, R, W):
        if eng == "act":
            self.P.act(lambda e: e.copy(out=out, in_=in_), R, W)
        else:
            self.P.add(eng, lambda e: e.tensor_copy(out=out, in_=in_), R, W)

    def memset(self, ap, val, W, R=()):
        self.P.pool(lambda e: e.memset(ap, val), R, W)

    def load(self, out, in_, slot, W=None):
        self.P.dma(lambda e: e.dma_start(out=out, in_=in_), slot, writes=[slot] if W is None else W)

    def loadc(self, out, in_, slot):
        self.P.dma(lambda e: e.dma_start(out=out, in_=in_), slot, writes=[slot], eng="pool")

    def store(self, out, in_, slot):
        self.P.defer_dma(lambda e: e.dma_start(out=out, in_=in_), slot, reads=[slot])

    def mm(self, items, R, W):
        def f(e):
            r = None
            for (o, l, rh, st, sp) in items:
                r = e.matmul(o, lhsT=l, rhs=rh, start=st, stop=sp)
            return r
        self.P.pe(f, R, W)

    def tr(self, items, ident, R, W):
        def f(e):
            r = None
            for (o, i) in items:
                r = e.transpose(o, i, ident)
            return r
        self.P.pe(f, R, W)

    def scan(self, out, d0, d1, R, W):
        self.P.dve(lambda e: e.tensor_tensor_scan(out=out, data0=d0, data1=d1, initial=0.0, op0=ALU.mult, op1=ALU.add), R, W)

    def recip(self, out, in_, R, W):
        self.P.dve(lambda e: e.reciprocal(out=out, in_=in_), R, W)

    def reduce(self, out, in_, R, W):
        self.P.dve(lambda e: e.tensor_reduce(out=out, in_=in_, axis=AX.X, op=ALU.add), R, W)

    def sss(self, out, in_, scalar, op, R, W):
        self.P.dve(lambda e: e.tensor_single_scalar(out=out, in_=in_, scalar=scalar, op=op), R, W)


class _Stop(Exception):
    pass


def build(LAT, depth=DEPTH, dbg=False, stop=None):
    T = CTX + LAT
    NT = T // 128
    NCH = T // 64
    NG = T // 256
    nc = bass.Bass("TRN2", target_bir_lowering=False)

    def din(name, shape, dt=F32):
        return nc.dram_tensor(name, list(shape), dt, kind="ExternalInput").ap()

    def dscr(name, shape, dt):
        return nc.dram_tensor(name, list(shape), dt, kind="ExternalOutput" if dbg else "Internal").ap()

    XIN = din("xin", [T, D])
    CC_IN = din("cc", [128, 8, 2])
    W_ADA = din("w_ada", [DEPTH, D, 6 * D])
    BADA_COL = din("bada_col", [DEPTH, 128, 4, 8])
    BADA_G = din("bada_g", [DEPTH, 2, 128, D])
    W1 = din("w1", [DEPTH, D, NC1])
    GHR = din("ghr", [DEPTH, 128, D])
    HGL = din("hgl", [128, DEPTH, 2, 2])
    MLB = din("mlb", [DEPTH, 128, 16])
    RTL = din("rtl", [128, DEPTH, 2, 2])
    WA = din("wa", [DEPTH, 2, 16, 256])
    BA = din("ba", [128, DEPTH, 2, 2])
    W_OUT = din("w_out", [DEPTH, D, D])
    W_FF1 = din("w_ff1", [DEPTH, D, DFF])
    W_FF2 = din("w_ff2", [DEPTH, DFF, D])
    GFIN = din("gfin", [128, D])
    ROPEC = din("ropec", [128, T])
    ROPES = din("ropes", [128, T])
    CONSTS = din("consts", [128, 8, 128])
    OUT = nc.dram_tensor("out", [LAT, D], F32, kind="ExternalOutput").ap()

    XS = dscr("xs", [T, D], F32)
    QT = dscr("qt", [2, 8, 128, T], BF16)
    KT = dscr("kt", [2, 8, 128, T], BF16)
    KK = dscr("kk", [2, T, 1024], BF16)
    VV = dscr("vv", [2, T, 16, VD], BF16)
    GG = dscr("gg", [T, D], BF16)
    OO = dscr("oo", [2, T, D], F32)

    with ExitStack() as gs:
        P = Prog(nc, gs)
        A = Em(P)
        cnt = [0]

        def sbt(st, shape, dt=F32, name=None):
            cnt[0] += 1
            nm = "%s_%d" % (name or "t", cnt[0])
            return Tl(st.enter_context(nc.sbuf_tensor(nm, list(shape), dt)), nm)

        def pst(st, shape, dt=F32, name=None):
            cnt[0] += 1
            nm = "%s_%d" % (name or "p", cnt[0])
            return Tl(st.enter_context(nc.psum_tensor(nm, list(shape), dt)), nm)

        def rot(st, n, shape, dt=F32, name=None, psum=False):
            return Rot([(pst if psum else sbt)(st, shape, dt, name) for _ in range(n)])

        def sub(tl, ap, name):
            return Tl(ap, name)

        cst = sbt(gs, [128, 8, 128], F32, "cst")
        A.load(cst[:], CONSTS[:, :, :], cst)
        identb = sbt(gs, [128, 128], BF16, "identb")
        A.copy("dve", identb[:], cst[:, 0, :], [cst], [identb])
        maskf = sbt(gs, [64, 2, 64], F32, "maskf")
        A.copy("dve", maskf[:], cst[0:64, 5:7, 0:64], [cst], [maskf])
        masku = maskf[:].bitcast(mybir.dt.uint32)
        rmask = sbt(gs, [128, 512], F32, "rmask")
        A.memset(rmask[:], 1.0, [rmask])
        A.memset(rmask[:].rearrange("p (c t) -> p c t", t=64)[:, :, 0:1], 0.0, [rmask], [rmask])
        onescol = sbt(gs, [128, 1], F32, "onescol")
        A.memset(onescol[:], 1.0, [onescol])
        zcol = sbt(gs, [128, 1], F32, "zcol")
        A.memset(zcol[:], 0.0, [zcol])
        shcol = sbt(gs, [128, 2], F32, "shcol")
        A.memset(shcol[:, 0:1], -HG_SHIFT, [shcol])
        A.memset(shcol[:, 1:2], HG_SHIFT, [shcol], [shcol])
        epscol = sbt(gs, [128, 1], F32, "epscol")
        A.memset(epscol[:], EPS, [epscol])
        ccs = sbt(gs, [128, 8, 2], F32, "ccs")
        A.load(ccs[:], CC_IN[:, :, :], ccs)
        scs = sbt(gs, [128, 8, 2], F32, "scs")
        A.act(scs[:], ccs[:], AF.Silu, [ccs], [scs])
        P.flush()

        def norm_T(pools, src, r0, hT, col0, sc, sh):
            xp, jp, sp_, xnp, ptp = pools
            xt = xp.nxt()
            A.load(xt[:], src[r0:r0 + 128, :], xt)
            jk = jp.nxt()
            ss = sp_.nxt()
            A.act(jk[:], xt[:], AF.Square, [xt], [jk, ss], accum_out=ss[:, 0:1])
            A.act(ss[:, 1:2], ss[:, 0:1], AF.Sqrt, [ss, epscol], [ss], scale=1.0 / D, bias=epscol[:, 0:1])
            A.recip(ss[:, 2:3], ss[:, 1:2], [ss], [ss])
            xn = xnp.nxt()
            A.act(xn[:], xt[:], AF.Identity, [xt, ss], [xn], scale=ss[:, 2:3])
            pt = ptp.nxt()
            A.tr([(pt[:, kt, :], xn[:, kt * 128:(kt + 1) * 128]) for kt in range(8)], identb[:], [xn, identb], [pt])
            if not hasattr(hT, "kb"):
                hT.kb = [Buf("hTk") for _ in range(8)]
            ncount[0] += 1
            for kt in range(8):
                if ncount[0] % 2 == 0:
                    A.act(hT[:, kt, col0:col0 + 128], pt[:, kt, :], AF.Identity, [pt, modcol_ref[0]], [hT.kb[kt]], scale=sc[kt], bias=sh[kt])
                else:
                    A.ts("dve", hT[:, kt, col0:col0 + 128], pt[:, kt, :], sc[kt], sh[kt], ALU.mult, ALU.add, [pt, modcol_ref[0]], [hT.kb[kt]])
            return xt

        modcol_ref = [None]
        ncount = [0]

        def load_w_cast(dst, src2, nk):
            for k0 in range(0, nk, 8):
                A.loadc(dst[:, k0:k0 + 8, :], src2[k0 * 128:(k0 + 8) * 128, :].rearrange("(kt p) n -> p kt n", p=128), dst)

        def decay_core(lf, n, d, escale, bT, Dt, E1, E2, cc, cctl, tmpc, sh=0.0):
            if d == 0:
                A.scan(bT[:, 0:n], rmask[:, 0:n], lf[:, 0:n], [lf, rmask], [bT])
            else:
                A.scan(bT[:, 0:n][:, ::-1], rmask[:, 0:n], lf[:, 0:n][:, ::-1], [lf, rmask], [bT])
            nb = n // 64
            mid = 31 if d == 0 else 32
            last = 63 if d == 0 else 0
            b3 = bT[:, 0:n].rearrange("p (c t) -> p c t", t=64)
            D3 = Dt[:, 0:n].rearrange("p (c t) -> p c t", t=64)
            A.tt("pool", D3, b3, b3[:, :, mid:mid + 1].to_broadcast([128, nb, 64]), ALU.subtract, [bT], [Dt])
            bneg = shcol[:, 0:1] if sh else zcol[:, 0:1]
            bpos = shcol[:, 1:2] if sh else zcol[:, 0:1]
            A.act(E1[:, 0:n], Dt[:, 0:n], AF.Exp, [Dt], [E1], scale=escale, bias=bneg)
            A.act(E2[:, 0:n], Dt[:, 0:n], AF.Exp, [Dt], [E2], scale=-escale, bias=bneg)
            ref2 = bT[:, mid:n:64]
            bl2 = bT[:, last:n:64]
            A.act(cc[0], ref2, AF.Exp, [bT], [cctl], scale=escale, bias=bneg)
            A.act(cc[1], bl2, AF.Exp, [bT], [cctl], scale=escale)
            A.tt("pool", tmpc[:, 0:nb], bl2, ref2, ALU.subtract, [bT], [tmpc])
            A.act(cc[2], tmpc[:, 0:nb], AF.Exp, [tmpc], [cctl], scale=escale, bias=bpos)

        blocks = [(0, CTX, 0)] + [(CTX + 512 * i, 512, 1) for i in range(LAT // 512)]

        try:
          for l in range(depth):
            xsrc = XIN if l == 0 else XS
            with ExitStack() as ls:
                modcol = sbt(ls, [128, 4, 8, 2], F32, "modcol")
                modcol_ref[0] = modcol

                def mcol(which, cl):
                    return [modcol[:, which, kt, cl:cl + 1] for kt in range(8)]

                with ExitStack() as ls2:
                    CCt = sbt(ls2, [128, 4, 2, 2, 3, NCH], F32, "CCt")
                    ALt = sbt(ls2, [128, NCH, 8], F32, "ALt")
                    FLt = sbt(ls2, [64, NCH, 8], F32, "FLt")
                    ALP = sbt(ls2, [128, NCH, 2, 2], F32, "ALP")
                    Ec = [[[sbt(ls2, [128, 512], F32, "Ec") for _ in range(2)] for _ in range(2)] for _ in range(2)]
                    lbt = sbt(ls2, [128, 2, 2, 2], F32, "lbt")
                    lgam = sbt(ls2, [128, 2, 2], F32, "lgam")
                    bat = sbt(ls2, [128, 2, 2], F32, "bat")
                    wat = sbt(ls2, [16, 2, 256], F32, "wat")
                    mlbt = sbt(ls2, [128, 16], F32, "mlbt")
                    ghr = sbt(ls2, [128, D], F32, "ghr")

                    with ExitStack() as ph:
                        wad = rot(ph, 2, [128, 8, 512], F32, "wad")
                        pm = pst(ph, [128, 64], F32, "pm")
                        bcol = sbt(ph, [128, 4, 8], F32, "bcol")
                        A.load(bcol[:], BADA_COL[l, :, :, :], bcol)
                        for which, vec in enumerate((0, 1, 3, 4)):
                            for half in range(2):
                                blk = vec * 2 + half
                                w = wad.nxt()
                                A.load(w[:], W_ADA[l, :, blk * 512:(blk + 1) * 512].rearrange("(kt p) n -> p kt n", p=128), w)
                                items = []
                                for f4 in range(4):
                                    c0 = (which * 8 + half * 4 + f4) * 2
                                    for kt in range(8):
                                        items.append((pm[:, c0:c0 + 2], w[:, kt, f4 * 128:(f4 + 1) * 128], scs[:, kt, :], kt == 0, kt == 7))
                                A.mm(items, [w, scs], [pm])
                        A.tt("dve", modcol[:].rearrange("p a b c -> p (a b) c"), pm[:].rearrange("p (a c) -> p a c", c=2),
                             bcol[:].rearrange("p a b -> p (a b)").unsqueeze(2).to_broadcast([128, 32, 2]), ALU.add, [pm, bcol], [modcol])
                        for which in (1, 3):
                            A.ts("dve", modcol[:, which, :, :], modcol[:, which, :, :], 1.0, None, ALU.add, None, [modcol], [modcol])
                        hgl = sbt(ph, [128, DEPTH, 2, 2], F32, "hgl")
                        A.load(hgl[:], HGL[:, :, :, :], hgl)
                        if l == 0:
                            A.memset(lbt[:, 0, :, :], 0.0, [lbt])
                        else:
                            dl = sbt(ph, [128, 2, 2], F32, "dl")
                            A.tt("dve", dl[:], hgl[:, 1, :, :], hgl[:, 0, :, :], ALU.subtract, [hgl], [dl])
                            A.act(lbt[:, 0, :, :], dl[:], AF.Sigmoid, [dl], [lbt])
                        A.ts("dve", lbt[:, 1, :, :], lbt[:, 0, :, :], -1.0, 1.0, ALU.mult, ALU.add, [lbt], [lbt])
                        rtl = sbt(ph, [128, DEPTH, 2, 2], F32, "rtl")
                        A.load(rtl[:], RTL[:, :, :, :], rtl)
                        sgr = sbt(ph, [128, 2, 2], F32, "sgr")
                        A.act(sgr[:], rtl[:, l, :, :], AF.Sigmoid, [rtl], [sgr])
                        A.act(lgam[:], sgr[:], AF.Ln, [sgr], [lgam])
                        bap = sbt(ph, [128, DEPTH, 2, 2], F32, "bap")
                        A.load(bap[:], BA[:, :, :, :], bap)
                        A.copy("dve", bat[:], bap[:, l, :, :], [bap], [bat])
                        A.load(wat[:], WA[l, :, :, :].rearrange("d r c -> r d c"), wat)
                        A.load(mlbt[:], MLB[l, :, :], mlbt)
                        A.load(ghr[:], GHR[l, :, :], ghr)
                        lfc = sbt(ph, [128, 512], F32, "lfc")
                        bTc = sbt(ph, [128, 512], F32, "bTc")
                        Dtc = sbt(ph, [128, 512], F32, "Dtc")
                        ctmp = sbt(ph, [128, 3, 8], F32, "ctmp")
                        tmpc = sbt(ph, [128, 8], F32, "tmpc")
                        for d in range(2):
                            for j in range(2):
                                A.copy("dve", lfc[:], lgam[:, d, j:j + 1].to_broadcast([128, 512]), [lgam], [lfc])
                                decay_core(lfc, 512, d, 1.0, bTc, Dtc, Ec[d][j][0], Ec[d][j][1],
                                           [ctmp[:, k, :] for k in range(3)], ctmp, tmpc)
                                for kind in range(3):
                                    A.copy("dve", CCt[:, RT, d, j, kind, :], ctmp[:, kind, 0:1].to_broadcast([128, NCH]), [ctmp], [CCt])
                        P.flush()
                        if stop == 'PA':
                            raise _Stop()

                    with ExitStack() as ph:
                        w1 = sbt(ph, [128, 8, NFM], BF16, "w1fm")
                        load_w_cast(w1, W1[l, :, 0:NFM], 8)
                        hTp = rot(ph, 2, [128, 8, 512], BF16, "hT")
                        npools = (rot(ph, 2, [128, D], F32, "xt"), rot(ph, 1, [128, D], BF16, "jk"), rot(ph, 2, [128, 4], F32, "ss"),
                                  rot(ph, 2, [128, D], BF16, "xn"), rot(ph, 2, [128, 8, 128], BF16, "ptr", psum=True))
                        pf = rot(ph, 4, [128, 512], F32, "pf", psum=True)
                        ptk = rot(ph, 2, [128, 4, 128], BF16, "ptk", psum=True)
                        wk = rot(ph, 10, [128, 512], F32, "wk")
                        fq = rot(ph, 4, [128, 512], F32, "fq")
                        bfp = rot(ph, 6, [128, 512], BF16, "bfp")
                        kst = [sbt(ph, [128, 4, 1024], BF16, "kst") for _ in range(2)]
                        rope = [sbt(ph, [128, 512], F32, "rope") for _ in range(2)]
                        ga = [sbt(ph, [16, 512], F32, "ga") for _ in range(2)]
                        tmpcp = rot(ph, 2, [128, 8], F32, "tmpc")

                        def do_block(t0, n, cl):
                            ntile = n // 128
                            ch0 = t0 // 64
                            nb = n // 64
                            hT = hTp.nxt()
                            for i in range(ntile):
                                norm_T(npools, xsrc, t0 + 128 * i, hT, 128 * i, mcol(1, cl), mcol(0, cl))
                            A.load(rope[0][:, 0:n], ROPEC[:, t0:t0 + n], rope[0])
                            A.load(rope[1][:, 0:n], ROPES[:, t0:t0 + n], rope[1])
                            P.release()

                            def fm(col0, M=128):
                                ps = pf.nxt()
                                A.mm([(ps[0:M, 0:n], w1[:, kt, col0:col0 + M], hT[:, kt, 0:n], kt == 0, kt == 7) for kt in range(8)],
                                     [w1] + hT.kb, [ps])
                                return ps

                            def finish(m, d, j, q, k, lf, escale, kscale, E=None, dirs=None, sh=0.0):
                                P.release()
                                if E is None:
                                    bT, Dt, E1, E2 = wk.nxt(), wk.nxt(), wk.nxt(), wk.nxt()
                                    decay_core(lf, n, d, escale, bT, Dt, E1, E2,
                                               [CCt[:, m, d, j, kind, ch0:ch0 + nb] for kind in range(3)], CCt, tmpcp.nxt(), sh=sh)
                                elif E == "none":
                                    E1 = E2 = None
                                else:
                                    E1, E2 = E
                                qh = bfp.nxt()
                                kh = bfp.nxt()
                                if E1 is None:
                                    A.copy("act", qh[:, 0:n], q[:, 0:n], [q], [qh])
                                    P.act(lambda e: e.mul(out=kh[:, 0:n], in_=k[:, 0:n], mul=kscale), [k], [kh])
                                else:
                                    A.tt("dve", qh[:, 0:n], q[:, 0:n], E1[:, 0:n], ALU.mult, [q, E1], [qh])
                                    A.stt(kh[:, 0:n], k[:, 0:n], kscale, E2[:, 0:n], ALU.mult, ALU.mult, [k, E2], [kh])
                                for dd in (dirs or [d]):
                                    A.store(QT[dd, m * 2 + j, :, t0:t0 + n], qh[:, 0:n], qh)
                                    A.store(KT[dd, m * 2 + j, :, t0:t0 + n], kh[:, 0:n], kh)
                                pk = ptk.nxt()
                                A.tr([(pk[:, i, :], kh[:, 128 * i:128 * (i + 1)]) for i in range(ntile)], identb[:], [kh, identb], [pk])
                                for dd in (dirs or [d]):
                                    A.copy("act", kst[dd][:, 0:ntile, m * 256 + j * 128:m * 256 + (j + 1) * 128], pk[:, 0:ntile, :],
                                           [pk], [kst[dd]])

                            for j in range(2):
                                psq = fm((0 + j) * 128)
                                qs = fq.nxt()
                                A.act(qs[:, 0:n], psq[:, 0:n], AF.Silu, [psq], [qs])
                                for d in range(2):
                                    psz = fm((2 + 2 * d + j) * 128)
                                    sg, f, lf, kk = wk.nxt(), wk.nxt(), wk.nxt(), wk.nxt()
                                    A.act(sg[:, 0:n], psz[:, 0:n], AF.Sigmoid, [psz], [sg])
                                    A.ts("dve", f[:, 0:n], sg[:, 0:n], lbt[:, 1, d, j:j + 1], lbt[:, 0, d, j:j + 1], ALU.mult, ALU.add,
                                         [sg, lbt], [f])
                                    A.act(lf[:, 0:n], f[:, 0:n], AF.Ln, [f], [lf])
                                    A.act(kk[:, 0:n], f[:, 0:n], AF.Identity, [f, onescol], [kk], scale=-1.0, bias=onescol[:, 0:1])
                                    finish(HG, d, j, qs, kk, lf, 1.0, 1.0, sh=HG_SHIFT)
                            for j in range(2):
                                psq = fm((6 + j) * 128)
                                psk = fm((8 + j) * 128)
                                finish(ML, 0, j, psq, psk, None, 1.0, 0.125, E="none", dirs=[0, 1])
                            for j in range(2):
                                rr = []
                                for base in (10, 14):
                                    ps0 = fm((base + j) * 128)
                                    ps1 = fm((base + 2 + j) * 128)
                                    t1, t2 = wk.nxt(), wk.nxt()
                                    r = fq.nxt()
                                    A.tt("dve", t1[:, 0:n], ps0[:, 0:n], rope[0][:, 0:n], ALU.mult, [ps0, rope[0]], [t1])
                                    A.tt("dve", t2[:, 0:n], ps1[:, 0:n], rope[1][:, 0:n], ALU.mult, [ps1, rope[1]], [t2])
                                    A.tt("pool", r[:, 0:n], t1[:, 0:n], t2[:, 0:n], ALU.add, [t1, t2], [r])
                                    rr.append(r)
                                for d in range(2):
                                    finish(RT, d, j, rr[0], rr[1], None, 1.0, 0.125, E=(Ec[d][j][0], Ec[d][j][1]))
                            for d in range(2):
                                psa = fm(22 * 128 + 16 * d, M=16)
                                A.copy("act", ga[d][:, 0:n], psa[0:16, 0:n], [psa], [ga[d]])
                            for j in range(2):
                                psq = fm((18 + j) * 128)
                                psk = fm((20 + j) * 128)
                                qr, kr = fq.nxt(), fq.nxt()
                                A.copy("act", qr[:, 0:n], psq[:, 0:n], [psq], [qr])
                                A.copy("dve", kr[:, 0:n], psk[:, 0:n], [psk], [kr])
                                for d in range(2):
                                    psz = pf.nxt()
                                    A.mm([(psz[:, 0:n], wat[0:16, d, j * 128:(j + 1) * 128], ga[d][0:16, 0:n], True, True)], [wat, ga[d]], [psz])
                                    sg, lf = wk.nxt(), wk.nxt()
                                    A.act(sg[:, 0:n], psz[:, 0:n], AF.Sigmoid, [psz, bat], [sg], bias=bat[:, d, j:j + 1])
                                    A.act(lf[:, 0:n], sg[:, 0:n], AF.Ln, [sg], [lf])
                                    finish(GL, d, j, qr, kr, lf, 1.0 / 16.0, 32.0 ** -0.5)
                            for dd in range(2):
                                A.store(KK[dd, t0:t0 + n, :].rearrange("(i p) x -> p i x", p=128), kst[dd][:, 0:ntile, :], kst[dd])

                        for (t0, n, cl) in blocks:
                            do_block(t0, n, cl)
                        P.flush()
                        if stop == 'P1a':
                            raise _Stop()

                    with ExitStack() as ph:
                        w1 = sbt(ph, [128, 8, NTM], BF16, "w1tm")
                        load_w_cast(w1, W1[l, :, NFM:NC1], 8)
                        hTp = rot(ph, 4, [128, 8, 128], BF16, "hT")
                        npools = (rot(ph, 4, [128, D], F32, "xt"), rot(ph, 2, [128, D], BF16, "jk"), rot(ph, 4, [128, 4], F32, "ss"),
                                  rot(ph, 4, [128, D], BF16, "xn"), rot(ph, 2, [128, 8, 128], BF16, "ptr", psum=True))
                        pt = rot(ph, 4, [128, 512], F32, "pt", psum=True)
                        psm = rot(ph, 2, [128, 64], F32, "psm", psum=True)
                        vstp = rot(ph, 4, [128, 16, VD], BF16, "vst")
                        vmlp = rot(ph, 4, [128, 4, VD], BF16, "vml")
                        vs1p = rot(ph, 4, [128, 4, VD], BF16, "vs1")
                        for tl_ in vstp.items + vmlp.items:
                            A.memset(tl_[:], 0.0, [tl_])
                            A.memset(tl_[:, :, 64:65], 1.0, [tl_], [tl_])
                        sgp = rot(ph, 6, [128, 512], F32, "sgp")
                        ggp = rot(ph, 4, [128, D], BF16, "ggp")
                        smp = rot(ph, 4, [128, 64], F32, "smp")

                        def do_tile(ti):
                            r0 = ti * 128
                            cl = 0 if r0 < CTX else 1
                            hT = hTp.nxt()
                            norm_T(npools, xsrc, r0, hT, 0, mcol(1, cl), mcol(0, cl))
                            P.release()

                            def tm(col0, N):
                                ps = pt.nxt()
                                import os
                                if os.environ.get("EXPA"):
                                    A.mm([(ps[:, 0:128], w1[:, kt, col0:col0 + 128], hT[:, kt, :], kt == 0, kt == 7) for kt in range(8)], [w1] + hT.kb, [ps])
                                elif os.environ.get("EXPB"):
                                    A.mm([(ps[:, 0:256], hT[:, kt, :], w1[:, kt, col0:col0 + 256], kt == 0, kt == 7) for kt in range(8)], [w1] + hT.kb, [ps])
                                else:
                                    items = []
                                    for n0 in range(0, N, 256):
                                        n1 = min(N, n0 + 256)
                                        items += [(ps[:, n0:n1], hT[:, kt, :], w1[:, kt, col0 + n0:col0 + n1], kt == 0, kt == 7) for kt in range(8)]
                                    A.mm(items, [w1] + hT.kb, [ps])
                                return ps
                            vst, vml, vs1 = vstp.nxt(), vmlp.nxt(), vs1p.nxt()
                            import os
                            SKIP = os.environ.get("SKIP", "")
                            if "v" in SKIP:
                                return
                            psA = tm(0, 512)
                            if "1" in SKIP:
                                return
                            if "a" not in SKIP:
                                A.copy("act", vst[:, 0:4, 0:64], psA[:, 0:256].rearrange("p (h v) -> p h v", v=64), [psA], [vst])
                            if "b" not in SKIP:
                                A.copy("act", vml[:, :, 0:64], psA[:, 256:512].rearrange("p (h v) -> p h v", v=64), [psA], [vml])
                            if "2" in SKIP:
                                return
                            psB = tm(512, 512)
                            A.copy("dve", vst[:, 8:16, 0:64], psB[:, 0:512].rearrange("p (h v) -> p h v", v=64), [psB], [vst])
                            if "g" in SKIP:
                                return
                            gg = ggp.nxt()
                            psC = tm(1024, 512)
                            s1 = sgp.nxt()
                            A.act(s1[:], psC[:], AF.Sigmoid, [psC], [s1])
                            A.tt("dve", gg[:, 0:512], s1[:], ghr[:, 0:512], ALU.mult, [s1, ghr], [gg])
                            psD = tm(1536, 512)
                            s2 = sgp.nxt()
                            A.act(s2[:], psD[:], AF.Silu, [psD], [s2])
                            A.tt("pool", gg[:, 512:1024], s2[:], ghr[:, 512:1024], ALU.mult, [s2, ghr], [gg])
                            A.store(GG[r0:r0 + 128, :], gg[:], gg)
                            if "m" in SKIP:
                                return
                            psG = tm(2048, 16)
                            sm = smp.nxt()
                            A.tt("dve", sm[:, 0:16], psG[:, 0:16], mlbt[:], ALU.add, [psG, mlbt], [sm])
                            gv = sm[:, 0:16].rearrange("p (d g h) -> p d g h", d=2, g=2)
                            A.act(sm[:, 16:24].rearrange("p (d h) -> p d h", d=2), gv[:, :, 1, :], AF.Sigmoid, [sm], [sm])
                            A.act(sm[:, 24:32], sm[:, 16:24], AF.Ln, [sm], [sm])
                            pb = psm.nxt()
                            A.mm([(pb[:, 0:4], cst[:, 1, :], sm[:, 24:28], True, True),
                                  (pb[:, 4:8], cst[:, 2, :], sm[:, 28:32], True, True),
                                  (pb[:, 8:16], cst[:, 3, :], sm[:, 24:32], True, True),
                                  (pb[:, 16:24], cst[:, 4, :], sm[:, 24:32], True, True)], [sm, cst], [pb])
                            A.act(ALt[:, 2 * ti:2 * ti + 2, :], pb[:, 8:24].rearrange("p (c g) -> p c g", g=8), AF.Exp, [pb], [ALt, pb])
                            for hh_ in range(2):
                                A.copy("pool", ALP[64 * hh_:64 * hh_ + 64, 2 * ti:2 * ti + 2, :, :],
                                       ALt[64 * hh_:64 * hh_ + 64, 2 * ti:2 * ti + 2, :].rearrange("p c (d a b) -> p c d a b", d=2, a=2)[:, :, :, :, hh_],
                                       [ALt], [ALP])
                            A.tt("dve", sm[:, 32:40].rearrange("p (d h) -> p d h", d=2), gv[:, :, 0, :],
                                 pb[:, 0:8].rearrange("p (d h) -> p d h", d=2), ALU.subtract, [pb, sm], [sm, pb])
                            A.act(sm[:, 40:48], sm[:, 32:40], AF.Exp, [sm], [sm])
                            if "f" in SKIP:
                                return
                            A.act(sm[:, 48:56], pb[:, 0:8], AF.Exp, [pb], [sm, pb], scale=-1.0)
                            A.copy("dve", FLt[0:64, 2 * ti, :], sm[0:64, 48:56], [sm], [FLt])
                            A.copy("dve", FLt[0:64, 2 * ti + 1, :], sm[64:128, 48:56], [sm], [FLt])
                            if "w" in SKIP:
                                return
                            A.tt("dve", vst[:, 4:8, :], vml[:], sm[:, 40:44].unsqueeze(2).to_broadcast([128, 4, VD]), ALU.mult, [sm, vml], [vst])
                            A.tt("dve", vs1[:], vml[:], sm[:, 44:48].unsqueeze(2).to_broadcast([128, 4, VD]), ALU.mult, [sm, vml], [vs1])
                            A.store(VV[0, r0:r0 + 128, :, :], vst[:], vst)
                            A.store(VV[1, r0:r0 + 128, 4:8, :], vs1[:], vs1)

                        for ti in range(NT):
                            do_tile(ti)
                        P.flush()
                        if stop == 'P1b':
                            raise _Stop()

                    with ExitStack() as ph:
                        qTp = [rot(ph, 2, [128, 8, 2, 256], BF16, "qbd") for _ in range(2)]
                        for d_ in range(2):
                            for t_ in qTp[d_].items:
                                A.memset(t_[:], 0.0, [t_])
                        kTp = [rot(ph, 2, [128, 8, 256], BF16, "kT") for _ in range(2)]
                        ktp = [rot(ph, 2, [64, 4, 1024], BF16, "ktok") for _ in range(2)]
                        vvp = [rot(ph, 2, [64, 4, 16 * VD], BF16, "vv") for _ in range(2)]
                        vmp = rot(ph, 2, [64, 4, 4 * VD], BF16, "vm")
                        S = [[[sbt(ph, [128, VD], F32, "S") for _ in range(2)] for _ in range(4)] for _ in range(2)]
                        Sb = [[[sbt(ph, [128, VD], BF16, "Sb") for _ in range(2)] for _ in range(4)] for _ in range(2)]
                        for d in range(2):
                            for m in range(4):
                                for hp in range(2):
                                    A.memset(S[d][m][hp][:], 0.0, [S[d][m][hp]])
                        psA = Rot([Tl(t_[:, 0:256].rearrange("p (h v) -> p h v", v=64), "psAv") for t_ in
                                   [pst(ph, [64, 512], F32, "psA") for _ in range(2)]])
                        pso = Rot([Tl(t_[:, 0:4 * VD].rearrange("p (h v) -> p h v", v=VD), "psov") for t_ in
                                   [pst(ph, [64, 512], F32, "pso") for _ in range(2)]])
                        psS = Rot([Tl(t_[:, 0:4 * VD].rearrange("p (a v) -> p a v", v=2 * VD), "psSv") for t_ in
                                   [pst(ph, [128, 512], F32, "psS") for _ in range(2)]])
                        mask4 = sbt(ph, [64, 2, 4, 64], F32, "mask4")
                        for d_ in range(2):
                            A.copy("dve", mask4[:, d_, :, :], maskf[:, d_, :].unsqueeze(1).to_broadcast([64, 4, 64]), [maskf], [mask4])
                        mask4u = mask4[:].bitcast(mybir.dt.uint32)
                        Asbd = [rot(ph, 3, [64, 4, 64], BF16, "Asb") for _ in range(2)]
                        for dd_ in range(2):
                            for t_ in Asbd[dd_].items:
                                A.memset(t_[:], 0.0, [t_])
                        tmpS = rot(ph, 6, [128, VD], F32, "tmpS")
                        osb = rot(ph, 2, [64, D], F32, "osb")
                        nrm = rot(ph, 2, [64, 16], F32, "nrm")

                        def col(m, d, hp, kind, c):
                            if m == ML:
                                if kind == 0:
                                    return onescol[:, 0:1], onescol
                                return ALP[:, c, d, hp:hp + 1], ALP
                            return CCt[:, m, d, hp, kind, c:c + 1], CCt

                        def do_chunk(d, c, cc, qT, kT, ktok, vv, vm):
                            import os
                            STG = int(os.environ.get("P2STAGE", "9"))
                            ts = slice(64 * cc, 64 * cc + 64)
                            ob = osb.nxt()
                            if STG < 5:
                                A.memset(ob[:], 0.0, [ob])
                            for m in range(4):
                                if m == ML and d == 1:
                                    vsrc, vbase = vm, 0
                                else:
                                    vsrc, vbase = vv, 4 * m * VD
                                for hp in range(2 if STG >= 1 else 0):
                                    St, Sbt = S[d][m][hp], Sb[d][m][hp]
                                    c1, c1t = col(m, d, hp, 0, c)
                                    A.act(Sbt[:, :], St[:, :], AF.Identity, [St, c1t], [Sbt], scale=c1)
                                if STG < 2:
                                    continue
                                pa = psA.nxt()
                                A.mm([(pa[:, h, :], kT[:, 2 * m + h // 2, ts], qT[:, 2 * m + h // 2, h % 2, ts], True, True) for h in range(4)],
                                     [kT, qT], [pa])
                                if os.environ.get("NOMASK"):
                                    continue
                                a4 = Asbd[d].nxt()
                                A.tt("dve", a4[:], pa[:], mask4[:, d, :, :], ALU.mult, [pa, mask4], [a4])
                                if STG < 3:
                                    continue
                                po = pso.nxt()
                                items = []
                                for h in range(4):
                                    p0 = 64 * (h % 2)
                                    items.append((po[:, h, :], a4[:, h, :], vsrc[0:64, cc, vbase + h * VD:vbase + (h + 1) * VD], True, False))
                                    items.append((po[:, h, :], qT[:, 2 * m + h // 2, h % 2, ts], Sb[d][m][h // 2][:, :], False, True))
                                A.mm(items, [a4, vsrc, qT, Sb[d][m][0], Sb[d][m][1]], [po])
                                if STG < 4:
                                    continue
                                pS = psS.nxt()
                                A.mm([(pS[:, hp, :], ktok[0:64, cc, m * 256 + hp * 128:m * 256 + (hp + 1) * 128],
                                       vsrc[0:64, cc, vbase + 2 * hp * VD:vbase + (2 * hp + 2) * VD], True, True) for hp in range(2)],
                                     [ktok, vsrc], [pS])
                                for hp in range(2):
                                    St = S[d][m][hp]
                                    tS = tmpS.nxt()
                                    c2, c2t = col(m, d, hp, 1, c)
                                    c3, c3t = col(m, d, hp, 2, c)
                                    A.act(tS[:, :], St[:, :], AF.Identity, [St, c2t], [tS], scale=c2)
                                    for hh in range(2):
                                        p0 = 64 * hh
                                        A.stt(St[p0:p0 + 64, :], pS[p0:p0 + 64, hp, hh * VD:(hh + 1) * VD], c3[p0:p0 + 64, :], tS[p0:p0 + 64, :],
                                              ALU.mult, ALU.add, [pS, tS, c3t], [St])
                                if STG < 5:
                                    continue
                                ov = ob[:, m * 256:(m + 1) * 256].rearrange("p (h v) -> p h v", v=64)
                                if m == HG:
                                    P.act(lambda e, ov=ov, pin=po[:, :, 0:64]: e.mul(out=ov, in_=pin, mul=float(np.exp(2.0 * HG_SHIFT))), [po], [ob])
                                elif m != ML:
                                    A.copy("act", ov, po[:, :, 0:64], [po], [ob])
                                else:
                                    nr = nrm.nxt()
                                    A.act(nr[:, 0:4], po[:, :, 64], AF.Abs, [po], [nr, po])
                                    A.tt("dve", nr[:, 4:8], nr[:, 0:4], FLt[0:64, c, d * 4:d * 4 + 4], ALU.max, [nr, FLt], [nr])
                                    A.recip(nr[:, 8:12], nr[:, 4:8], [nr], [nr])
                                    A.tt("dve", ov, po[:, :, 0:64], nr[:, 8:12].unsqueeze(2).to_broadcast([64, 4, 64]), ALU.mult, [nr, po], [ob, po])
                            A.store(OO[d, 64 * c:64 * c + 64, :], ob[:], ob)

                        grp_order = [list(range(NG)), [0] + list(range(NG - 1, 0, -1))]
                        cc_order = [[0, 1, 2, 3], [3, 2, 1, 0]]
                        for gi in range(NG):
                            for d in range(2):
                                g = grp_order[d][gi]
                                tg0 = 256 * g
                                qT, kT, ktok, vv = qTp[d].nxt(), kTp[d].nxt(), ktp[d].nxt(), vvp[d].nxt()
                                A.load(qT[0:64, :, 0, :], QT[d, :, 0:64, tg0:tg0 + 256].rearrange("m p t -> p m t"), qT)
                                A.load(qT[64:128, :, 1, :], QT[d, :, 64:128, tg0:tg0 + 256].rearrange("m p t -> p m t"), qT)
                                A.load(kT[:], KT[d, :, :, tg0:tg0 + 256].rearrange("m p t -> p m t"), kT)
                                A.load(ktok[:], KK[d, tg0:tg0 + 256, :].rearrange("(c p) x -> p c x", p=64), ktok)
                                A.load(vv[:], VV[0, tg0:tg0 + 256, :, :].rearrange("(c p) h v -> p c (h v)", p=64), vv)
                                vm = None
                                if d == 1:
                                    vm = vmp.nxt()
                                    A.load(vm[:], VV[1, tg0:tg0 + 256, 4:8, :].rearrange("(c p) h v -> p c (h v)", p=64), vm)
                                for cc in cc_order[d]:
                                    P.release()
                                    do_chunk(d, 4 * g + cc, cc, qT, kT, ktok, vv, vm)
                        P.flush()
                        if stop == 'P2':
                            raise _Stop()

                with ExitStack() as ph:
                    g1 = [sbt(ph, [128, D], F32, "g1") for _ in range(2)]
                    g2 = [sbt(ph, [128, D], F32, "g2") for _ in range(2)]

                    def gtiles(gi, dst):
                        with ExitStack() as ph2:
                            wad = rot(ph2, 2, [128, 8, 512], F32, "wad")
                            bg = sbt(ph2, [128, D], F32, "bg")
                            pg = rot(ph2, 2, [128, 512], F32, "pg", psum=True)
                            crep = [sbt(ph2, [128, 8, 128], F32, "crep") for _ in range(2)]
                            for cl in range(2):
                                A.copy("dve", crep[cl][:], scs[:, :, cl:cl + 1].to_broadcast([128, 8, 128]), [scs], [crep[cl]])
                            A.load(bg[:], BADA_G[l, gi, :, :], bg)
                            vec = (2, 5)[gi]
                            for half in range(2):
                                blk = vec * 2 + half
                                w = wad.nxt()
                                A.load(w[:], W_ADA[l, :, blk * 512:(blk + 1) * 512].rearrange("(kt p) n -> p kt n", p=128), w)
                                for cl in range(2):
                                    ps = pg.nxt()
                                    A.mm([(ps[:], crep[cl][:, kt, :], w[:, kt, :], kt == 0, kt == 7) for kt in range(8)], [w, crep[cl]], [ps])
                                    A.tt("dve", dst[cl][:, half * 512:(half + 1) * 512], ps[:], bg[:, half * 512:(half + 1) * 512], ALU.add,
                                         [ps, bg], [dst[cl]])
                            P.flush()
                            if stop == 'P3a':
                                raise _Stop()
                    gtiles(0, g1)
                    with ExitStack() as ph2:
                        wo = sbt(ph2, [128, 8, D], BF16, "wo")
                        load_w_cast(wo, W_OUT[l, :, :], 8)
                        o0p = rot(ph2, 2, [128, D], F32, "o0")
                        o1p = rot(ph2, 2, [128, D], F32, "o1")
                        gglp = rot(ph2, 2, [128, D], BF16, "ggl")
                        xp = rot(ph2, 3, [128, D], F32, "x3")
                        sqp = rot(ph2, 1, [128, D], F32, "sq")
                        ssp = rot(ph2, 2, [128, 48], F32, "ss3")
                        yp = rot(ph2, 2, [128, D], BF16, "y")
                        yTp = rot(ph2, 2, [128, 8, 128], BF16, "yT")
                        ptr = rot(ph2, 2, [128, 8, 128], BF16, "ptr3", psum=True)
                        pop = rot(ph2, 4, [128, 512], F32, "pop", psum=True)
                        tp = rot(ph2, 2, [128, 512], F32, "t3")

                        def do_tile3(ti):
                            r0 = ti * 128
                            cl = 0 if r0 < CTX else 1
                            o0, o1, ggl, xt = o0p.nxt(), o1p.nxt(), gglp.nxt(), xp.nxt()
                            A.load(o0[:], OO[0, r0:r0 + 128, :], o0)
                            A.load(o1[:], OO[1, r0:r0 + 128, :], o1)
                            A.load(ggl[:], GG[r0:r0 + 128, :], ggl)
                            A.load(xt[:], xsrc[r0:r0 + 128, :], xt)
                            P.release()
                            A.tt("pool", o0[:], o0[:], o1[:], ALU.add, [o0, o1], [o0])
                            sq, ss = sqp.nxt(), ssp.nxt()
                            A.act(sq[:], o0[:], AF.Square, [o0], [sq])
                            A.reduce(ss[:, 0:16], sq[:].rearrange("p (h v) -> p h v", v=64), [sq], [ss])
                            A.act(ss[:, 16:32], ss[:, 0:16], AF.Sqrt, [ss, epscol], [ss], scale=1.0 / 64, bias=epscol[:, 0:1])
                            A.recip(ss[:, 32:48], ss[:, 16:32], [ss], [ss])
                            o3 = o0[:].rearrange("p (h v) -> p h v", v=64)
                            A.tt("dve", o3, o3, ss[:, 32:48].unsqueeze(2).to_broadcast([128, 16, 64]), ALU.mult, [o0, ss], [o0])
                            y = yp.nxt()
                            A.tt("pool", y[:], o0[:], ggl[:], ALU.mult, [o0, ggl], [y])
                            pt = ptr.nxt()
                            A.tr([(pt[:, kt, :], y[:, kt * 128:(kt + 1) * 128]) for kt in range(8)], identb[:], [y, identb], [pt])
                            yT = yTp.nxt()
                            A.copy("act", yT[:, 0:4, :], pt[:, 0:4, :], [pt], [yT])
                            A.copy("dve", yT[:, 4:8, :], pt[:, 4:8, :], [pt], [yT])
                            for nb in range(2):
                                ps = pop.nxt()
                                A.mm([(ps[:], yT[:, kt, :], wo[:, kt, nb * 512:(nb + 1) * 512], kt == 0, kt == 7) for kt in range(8)], [yT, wo], [ps])
                                t = tp.nxt()
                                A.tt("dve", t[:], ps[:], g1[cl][:, nb * 512:(nb + 1) * 512], ALU.mult, [ps, g1[cl]], [t])
                                A.tt("pool", xt[:, nb * 512:(nb + 1) * 512], xt[:, nb * 512:(nb + 1) * 512], t[:], ALU.add, [t, xt], [xt])
                            A.store(XS[r0:r0 + 128, :], xt[:], xt)

                        for ti in range(NT):
                            do_tile3(ti)
                        P.flush()
                        if stop == 'P3a':
                            raise _Stop()

                    gtiles(1, g2)
                    with ExitStack() as ph2:
                        wf1 = sbt(ph2, [128, 8, DFF], BF16, "wf1")
                        wf2 = sbt(ph2, [128, 32, D], BF16, "wf2")
                        load_w_cast(wf1, W_FF1[l, :, :], 8)
                        load_w_cast(wf2, W_FF2[l, :, :], 32)
                        hTp = rot(ph2, 2, [128, 8, 256], BF16, "h2T")
                        npools = (rot(ph2, 3, [128, D], F32, "xt"), rot(ph2, 1, [128, D], BF16, "jk"), rot(ph2, 2, [128, 4], F32, "ss"),
                                  rot(ph2, 2, [128, D], BF16, "xn"), rot(ph2, 2, [128, 8, 128], BF16, "ptr", psum=True))
                        uTp = rot(ph2, 1, [128, 32, 256], BF16, "uT")
                        pu = Rot([Tl(t_[:, 0:256], "pus") for t_ in [pst(ph2, [128, 512], F32, "pu") for _ in range(3)]])
                        po2 = rot(ph2, 2, [128, 512], F32, "po2", psum=True)
                        sqp = rot(ph2, 3, [128, 256], F32, "sq2")
                        tp = rot(ph2, 1, [128, 512], F32, "t4")

                        def do_blk(bi):
                            cl = 0 if bi * 256 < CTX else 1
                            hT = hTp.nxt()
                            xts = []
                            for i in range(2):
                                xts.append(norm_T(npools, XS, bi * 256 + 128 * i, hT, 128 * i, mcol(3, cl), mcol(2, cl)))
                                if i == 0:
                                    P.release()
                            uT = uTp.nxt()
                            for fb in range(32):
                                ps = pu.nxt()
                                A.mm([(ps[:], wf1[:, kt, fb * 128:(fb + 1) * 128], hT[:, kt, :], kt == 0, kt == 7) for kt in range(8)], [wf1] + hT.kb, [ps])
                                sq = sqp.nxt()
                                A.act(sq[:], ps[:], AF.Square, [ps], [sq])
                                A.stt(uT[:, fb, :], ps[:], 0.0, sq[:], ALU.is_gt, ALU.mult, [ps, sq], [uT])
                            for i in range(2):
                                xt = xts[i]
                                for nb in range(2):
                                    ps = po2.nxt()
                                    A.mm([(ps[:], uT[:, fb, 128 * i:128 * (i + 1)], wf2[:, fb, nb * 512:(nb + 1) * 512], fb == 0, fb == 31)
                                          for fb in range(32)], [uT, wf2], [ps])
                                    t = tp.nxt()
                                    A.tt("dve", t[:], ps[:], g2[cl][:, nb * 512:(nb + 1) * 512], ALU.mult, [ps, g2[cl]], [t])
                                    A.tt("pool", xt[:, nb * 512:(nb + 1) * 512], xt[:, nb * 512:(nb + 1) * 512], t[:], ALU.add, [t, xt], [xt])
                                r0 = bi * 256 + 128 * i
                                A.store(XS[r0:r0 + 128, :], xt[:], xt)

                        for bi in range(T // 256):
                            do_blk(bi)
                        P.flush()
                        if stop == 'P3b':
                            raise _Stop()

        except _Stop:
            P.flush(final=True)
            build.ninst = P.ninst
            gs.pop_all()
            return nc
        with ExitStack() as ph:
            gf = sbt(ph, [128, D], F32, "gf")
            A.load(gf[:], GFIN[:, :], gf)
            xp = rot(ph, 3, [128, D], F32, "xf")
            jp = rot(ph, 1, [128, D], BF16, "jkf")
            sp_ = rot(ph, 2, [128, 4], F32, "ssf")
            for ti in range(LAT // 128):
                r0 = CTX + ti * 128
                xt, jk, ss = xp.nxt(), jp.nxt(), sp_.nxt()
                A.load(xt[:], XS[r0:r0 + 128, :], xt)
                P.release()
                A.act(jk[:], xt[:], AF.Square, [xt], [jk, ss], accum_out=ss[:, 0:1])
                A.act(ss[:, 1:2], ss[:, 0:1], AF.Sqrt, [ss, epscol], [ss], scale=1.0 / D, bias=epscol[:, 0:1])
                A.recip(ss[:, 2:3], ss[:, 1:2], [ss], [ss])
                A.stt(xt[:], xt[:], ss[:, 2:3], gf[:], ALU.mult, ALU.mult, [xt, ss, gf], [xt])
                A.store(OUT[ti * 128:(ti + 1) * 128, :], xt[:], xt)
            P.flush(final=True)
        build.ninst = P.ninst
    return nc


_IN_LAYOUT = (
    ('hg_q', 256), ('hg_f_fwd', 256), ('hg_f_bwd', 256), ('hg_i', 256), ('hg_g', 256),
    ('ml_q', 256), ('ml_k', 256), ('ml_v', 256), ('ml_if', 16), ('ml_o', 256),
    ('rt_q', 256), ('rt_k', 256), ('rt_v', 256), ('rt_g', 256),
    ('gl_q', 128), ('gl_k', 128), ('gl_v', 256),
    ('gl_a_fwd', 16), ('gl_a_bwd', 16), ('gl_g', 256),
)


def _col_ranges():
    off = {}
    o = 0
    for nme, s in _IN_LAYOUT:
        off[nme] = (o, s)
        o += s
    return off


def _w1_layout(w_in):
    off = _col_ranges()
    dep = w_in.shape[0]
    out = np.zeros((dep, D, NC1), np.float32)

    def cols(nme):
        o, s = off[nme]
        return w_in[:, :, o:o + s]
    perm = np.zeros(256, np.int64)
    for h in range(4):
        for dd in range(64):
            r = dd % 32
            partner = dd + 16 if r < 16 else dd - 16
            perm[h * 64 + dd] = h * 64 + partner

    def pad_gla(a):
        p = np.zeros((dep, D, 256), np.float32)
        for h in range(4):
            p[:, :, h * 64:h * 64 + 32] = a[:, :, h * 32:(h + 1) * 32]
        return p
    fmc = [cols('hg_q'), cols('hg_f_fwd'), cols('hg_f_bwd'), cols('ml_q'), cols('ml_k'),
           cols('rt_q'), cols('rt_q')[:, :, perm], cols('rt_k'), cols('rt_k')[:, :, perm],
           pad_gla(cols('gl_q')), pad_gla(cols('gl_k')), cols('gl_a_fwd'), cols('gl_a_bwd')]
    tmc = [cols('hg_i'), cols('ml_v'), cols('rt_v'), cols('gl_v'), cols('hg_g'), cols('ml_o'), cols('rt_g'), cols('gl_g'), cols('ml_if')]
    o = 0
    for a in fmc + tmc:
        out[:, :, o:o + a.shape[2]] = a
        o += a.shape[2]
    assert o == NC1
    return out


def _rope_tables(T):
    LATn = T - CTX
    tl = np.arange(LATn)
    row = (tl // 64).astype(np.float32)
    colp = (tl % 64).astype(np.float32)
    inv = (np.float32(10000.0) ** (-np.arange(16, dtype=np.float32) / np.float32(16))).astype(np.float32)
    cosT = np.ones((128, T), np.float32)
    sinT = np.zeros((128, T), np.float32)
    for p in range(128):
        dd = p % 64
        pos = row if dd < 32 else colp
        ang = (pos * inv[dd % 16]).astype(np.float32)
        sgn = -1.0 if (dd % 32) < 16 else 1.0
        cosT[p, CTX:] = np.cos(ang).astype(np.float32)
        sinT[p, CTX:] = (sgn * np.sin(ang)).astype(np.float32)
    return cosT, sinT


def _consts():
    c = np.zeros((128, 8, 128), np.float32)
    s = np.arange(128)[:, None]
    t = np.arange(128)[None, :]
    same = (s // 64) == (t // 64)
    c[:, 0, :] = (s == t)
    c[:, 1, :] = same & (s <= t)
    c[:, 2, :] = same & (s >= t)
    c[:, 3, :] = (s < 64) & (t >= 0)
    c[:, 4, :] = (s >= 64) & (t >= 0)
    c[:64, 5, :64] = (s[:64] <= t[:, :64])
    c[:64, 6, :64] = (s[:64] >= t[:, :64])
    return c


def make_shared(inp, T):
    dep = inp['w_ada'].shape[0]
    f32 = np.float32
    sh = {}
    sh['w_ada'] = np.ascontiguousarray(inp['w_ada'], f32)
    b_ada = np.asarray(inp['b_ada'], f32)
    bc = np.zeros((dep, 128, 4, 8), f32)
    for which, vec in enumerate((0, 1, 3, 4)):
        bc[:, :, which, :] = b_ada[:, vec * D:(vec + 1) * D].reshape(dep, 8, 128).transpose(0, 2, 1)
    sh['bada_col'] = bc
    bg = np.zeros((dep, 2, 128, D), f32)
    for gi, vec in enumerate((2, 5)):
        bg[:, gi, :, :] = b_ada[:, None, vec * D:(vec + 1) * D]
    sh['bada_g'] = bg
    sh['w1'] = _w1_layout(np.asarray(inp['w_in'], f32))
    sh['ghr'] = np.ascontiguousarray(np.broadcast_to(np.asarray(inp['g_heads'], f32)[:, None, :], (dep, 128, D)))
    hl = np.asarray(inp['hgrn_lb_logits'], f32)
    sh['hgl'] = np.ascontiguousarray(hl.reshape(dep, 2, 2, 128).transpose(3, 0, 1, 2))
    mb = np.asarray(inp['ml_gate_bias'], f32).reshape(dep, 16)
    sh['mlb'] = np.ascontiguousarray(np.broadcast_to(mb[:, None, :], (dep, 128, 16)))
    rl = np.asarray(inp['rt_decay_logit'], f32)
    rt = np.zeros((128, dep, 2, 2), f32)
    for j in range(2):
        for hh in range(2):
            rt[hh * 64:(hh + 1) * 64, :, :, j] = rl[None, :, :, 2 * j + hh]
    sh['rtl'] = rt
    wa = np.asarray(inp['gla_w_a'], f32)
    wap = np.zeros((dep, 2, 16, 256), f32)
    ba = np.asarray(inp['gla_b_a'], f32)
    bap = np.zeros((dep, 2, 256), f32)
    for h in range(4):
        wap[:, :, :, h * 64:h * 64 + 32] = wa[:, :, :, h * 32:(h + 1) * 32]
        bap[:, :, h * 64:h * 64 + 32] = ba[:, :, h * 32:(h + 1) * 32]
    sh['wa'] = wap
    sh['ba'] = np.ascontiguousarray(bap.reshape(dep, 2, 2, 128).transpose(3, 0, 1, 2))
    sh['w_out'] = np.ascontiguousarray(inp['w_out'], f32)
    sh['w_ff1'] = np.ascontiguousarray(inp['w_ff1'], f32)
    sh['w_ff2'] = np.ascontiguousarray(inp['w_ff2'], f32)
    sh['gfin'] = np.ascontiguousarray(np.broadcast_to(np.asarray(inp['g_final'], f32)[None, :], (128, D)))
    c, s = _rope_tables(T)
    sh['ropec'] = c
    sh['ropes'] = s
    sh['consts'] = _consts()
    return sh


def make_core(inp, b):
    f32 = np.float32
    m = {}
    m['xin'] = np.ascontiguousarray(np.concatenate([np.asarray(inp['ctx'][b], f32), np.asarray(inp['x'][b], f32)], axis=0))
    cc = np.zeros((128, 8, 2), f32)
    cc[:, :, 0] = np.asarray(inp['c_ctx'], f32).reshape(8, 128).T
    cc[:, :, 1] = np.asarray(inp['c'][b], f32).reshape(8, 128).T
    m['cc'] = cc
    return m


_CACHE = {}


def kernel(**inputs):
    x = inputs['x']
    B, LAT, _ = x.shape
    T = CTX + LAT
    if LAT not in _CACHE:
        _CACHE[LAT] = build(LAT)
    nc = _CACHE[LAT]
    sh = make_shared(inputs, T)
    in_maps = []
    for b in range(B):
        m = dict(sh)
        m.update(make_core(inputs, b))
        in_maps.append(m)
    res = run_bass_kernel_spmd(nc, in_maps, core_ids=list(range(B)))
    return np.stack([np.asarray(r["out"], np.float32) for r in res.results], axis=0)
```

```python
import numpy as np
from contextlib import ExitStack
import concourse.bass as bass
import concourse.mybir as mybir
from concourse.bass_utils import run_bass_kernel_spmd

F32 = mybir.dt.float32
BF16 = mybir.dt.bfloat16
AF = mybir.ActivationFunctionType
ALU = mybir.AluOpType
AX = mybir.AxisListType

D = 1024
CTX = 256
DEPTH = 2
DFF = 4096
VD = 66
NFM = 22 * 128 + 32
NTM = 2064
NC1 = NFM + NTM
EPS = 1e-6
ENGS = ("pe", "act", "dve", "pool", "sp")
HG, ML, RT, GL = 0, 1, 2, 3
HG_SHIFT = 20.0


class Buf:
    __slots__ = ("name", "last_write", "reads", "dsem", "dcount")

    def __init__(self, name):
        self.name = name
        self.last_write = None
        self.reads = []
        self.dsem = None
        self.dcount = 0


class Tl:
    def __init__(self, t, name, b=None):
        self.t = t
        self.b = b if b is not None else Buf(name)

    def __getitem__(self, i):
        return self.t[i]


def _b(x):
    return x.b if isinstance(x, Tl) else x


class Op:
    __slots__ = ("eng", "fn", "deps", "is_dma", "slot", "ticket", "signal", "retired", "seq")

    def __init__(self, eng, fn, is_dma, slot):
        self.eng = eng
        self.fn = fn
        self.deps = []
        self.is_dma = is_dma
        self.slot = slot
        self.ticket = None
        self.signal = False
        self.retired = False


class Prog:
    def __init__(self, nc, stack):
        self.nc = nc
        self.stack = stack
        self.ops = []
        self.esem = {e: stack.enter_context(nc.semaphore("sem_" + e)) for e in ENGS}
        self.ecount = {e: 0 for e in ENGS}
        self.waited = {e: {} for e in ENGS}
        self.pending_bar = {e: [] for e in ENGS}
        self.sempool = []
        self.nsem = 0
        self.engobj = {"pe": nc.tensor, "act": nc.scalar, "dve": nc.vector, "pool": nc.gpsimd, "sp": nc.sync}
        self.ninst = 0
        self.deferred = []
        self.seq = 0

    def defer_dma(self, fn, slot, reads=(), writes=(), eng="sp"):
        self.deferred.append((fn, slot, reads, writes, eng))

    def release(self):
        dd = self.deferred
        self.deferred = []
        for fn, slot, reads, writes, eng in dd:
            self.dma(fn, slot, reads, writes, eng)

    def add(self, eng, fn, reads=(), writes=(), is_dma=False, slot=None):
        op = Op(eng, fn, is_dma, slot)
        deps = set()
        for x in reads:
            b = _b(x)
            if b.last_write is not None:
                deps.add(b.last_write)
        for x in writes:
            b = _b(x)
            if b.last_write is not None:
                deps.add(b.last_write)
            for r in b.reads:
                deps.add(r)
        dl = [d for d in deps if (not d.retired) and not (eng == "pe" and d.eng == "pe" and not d.is_dma)]
        best = {}
        keep = []
        for d in dl:
            if d.is_dma:
                keep.append(d)
            elif d.eng not in best or best[d.eng].seq < d.seq:
                best[d.eng] = d
        op.deps = keep + list(best.values())
        self.seq += 1
        op.seq = self.seq
        for x in reads:
            _b(x).reads.append(op)
        for x in writes:
            b = _b(x)
            b.last_write = op
            b.reads = []
        self.ops.append(op)
        return op

    def pe(self, fn, reads=(), writes=()):
        return self.add("pe", fn, reads, writes)

    def act(self, fn, reads=(), writes=()):
        return self.add("act", fn, reads, writes)

    def dve(self, fn, reads=(), writes=()):
        return self.add("dve", fn, reads, writes)

    def pool(self, fn, reads=(), writes=()):
        return self.add("pool", fn, reads, writes)

    def dma(self, fn, slot, reads=(), writes=(), eng="sp"):
        return self.add(eng, fn, reads, writes, is_dma=True, slot=_b(slot))

    def flush(self, final=False):
        nc = self.nc
        self.release()
        ops = self.ops
        self.ops = []
        for op in ops:
            for d in op.deps:
                d.signal = True
        last = {}
        for op in ops:
            if not op.is_dma:
                last[op.eng] = op
        for op in last.values():
            op.signal = True
        slots = []
        swbar = []
        for op in ops:
            if op.is_dma and op.eng == "pool":
                sem = self.stack.enter_context(nc.semaphore("sw%d" % self.nsem))
                self.nsem += 1
                op.ticket = (sem, 16)
                swbar.append(op.ticket)
            elif op.is_dma:
                s = op.slot
                if s.dsem is None:
                    if self.sempool:
                        s.dsem, s.dcount = self.sempool.pop()
                    else:
                        s.dsem = self.stack.enter_context(nc.semaphore("ds%d" % self.nsem))
                        self.nsem += 1
                        s.dcount = 0
                    slots.append(s)
                s.dcount += 16
                op.ticket = (s.dsem, s.dcount)
            elif op.signal:
                self.ecount[op.eng] += 1
                op.ticket = (self.esem[op.eng], self.ecount[op.eng])
        per = {e: [] for e in ENGS}
        for op in ops:
            per[op.eng].append(op)
        bar = [last[e].ticket for e in last] + [(s.dsem, s.dcount) for s in slots] + swbar

        def run(ename, eng):
            waited = self.waited[ename]

            def w(sem, val):
                k = id(sem)
                if waited.get(k, 0) < val:
                    eng.wait_ge(sem, val)
                    waited[k] = val

            if per[ename] or final:
                for sem, val in self.pending_bar[ename]:
                    w(sem, val)
                self.pending_bar[ename] = []
            for op in per[ename]:
                need = {}
                for d in op.deps:
                    sem, val = d.ticket
                    k = id(sem)
                    if k not in need or need[k][1] < val:
                        need[k] = (sem, val)
                for sem, val in need.values():
                    w(sem, val)
                ins = op.fn(eng)
                self.ninst += 1
                if op.is_dma:
                    ins.then_inc(op.ticket[0], 16)
                elif op.signal:
                    ins.then_inc(op.ticket[0], 1)
            if final and ename == "sp":
                for sem, val in bar:
                    w(sem, val)

        with nc.Block() as block:
            @block.tensor
            def _(eng):
                run("pe", eng)

            @block.scalar
            def _(eng):
                run("act", eng)

            @block.vector
            def _(eng):
                run("dve", eng)

            @block.gpsimd
            def _(eng):
                run("pool", eng)

            @block.sync
            def _(eng):
                run("sp", eng)

        for e in ENGS:
            self.pending_bar[e].extend(bar)
        for op in ops:
            op.retired = True
        for s in slots:
            self.sempool.append((s.dsem, s.dcount))
            s.dsem = None


class Rot:
    def __init__(self, items):
        self.items = items
        self.i = 0

    def nxt(self):
        r = self.items[self.i % len(self.items)]
        self.i += 1
        return r


class Em:
    def __init__(self, P):
        self.P = P

    def act(self, out, in_, func, R, W, **kw):
        self.P.act(lambda e: e.activation(out=out, in_=in_, func=func, **kw), R, W)

    def tt(self, eng, out, in0, in1, op, R, W):
        self.P.add(eng, lambda e: e.tensor_tensor(out=out, in0=in0, in1=in1, op=op), R, W)

    def ts(self, eng, out, in0, s1, s2, op0, op1, R, W):
        if op1 is None:
            self.P.add(eng, lambda e: e.tensor_scalar(out=out, in0=in0, scalar1=s1, scalar2=None, op0=op0), R, W)
        else:
            self.P.add(eng, lambda e: e.tensor_scalar(out=out, in0=in0, scalar1=s1, scalar2=s2, op0=op0, op1=op1), R, W)

    def stt(self, out, in0, scalar, in1, op0, op1, R, W):
        self.P.dve(lambda e: e.scalar_tensor_tensor(out=out, in0=in0, scalar=scalar, in1=in1, op0=op0, op1=op1), R, W)

    def copy(self, eng, out, in_, R, W):
        if eng == "act":
            self.P.act(lambda e: e.copy(out=out, in_=in_), R, W)
        else:
            self.P.add(eng, lambda e: e.tensor_copy(out=out, in_=in_), R, W)

    def memset(self, ap, val, W, R=()):
        self.P.pool(lambda e: e.memset(ap, val), R, W)

    def load(self, out, in_, slot, W=None):
        self.P.dma(lambda e: e.dma_start(out=out, in_=in_), slot, writes=[slot] if W is None else W)

    def loadc(self, out, in_, slot):
        self.P.dma(lambda e: e.dma_start(out=out, in_=in_), slot, writes=[slot], eng="pool")

    def store(self, out, in_, slot):
        self.P.defer_dma(lambda e: e.dma_start(out=out, in_=in_), slot, reads=[slot])

    def mm(self, items, R, W):
        def f(e):
            r = None
            for (o, l, rh, st, sp) in items:
                r = e.matmul(o, lhsT=l, rhs=rh, start=st, stop=sp)
            return r
        self.P.pe(f, R, W)

    def tr(self, items, ident, R, W):
        def f(e):
            r = None
            for (o, i) in items:
                r = e.transpose(o, i, ident)
            return r
        self.P.pe(f, R, W)

    def scan(self, out, d0, d1, R, W):
        self.P.dve(lambda e: e.tensor_tensor_scan(out=out, data0=d0, data1=d1, initial=0.0, op0=ALU.mult, op1=ALU.add), R, W)

    def recip(self, out, in_, R, W):
        self.P.dve(lambda e: e.reciprocal(out=out, in_=in_), R, W)

    def reduce(self, out, in_, R, W):
        self.P.dve(lambda e: e.tensor_reduce(out=out, in_=in_, axis=AX.X, op=ALU.add), R, W)

    def sss(self, out, in_, scalar, op, R, W):
        self.P.dve(lambda e: e.tensor_single_scalar(out=out, in_=in_, scalar=scalar, op=op), R, W)


class _Stop(Exception):
    pass


def build(LAT, depth=DEPTH, dbg=False, stop=None):
    T = CTX + LAT
    NT = T // 128
    NCH = T // 64
    NG = T // 256
    nc = bass.Bass("TRN2", target_bir_lowering=False)

    def din(name, shape, dt=F32):
        return nc.dram_tensor(name, list(shape), dt, kind="ExternalInput").ap()

    def dscr(name, shape, dt):
        return nc.dram_tensor(name, list(shape), dt, kind="ExternalOutput" if dbg else "Internal").ap()

    XIN = din("xin", [T, D])
    CC_IN = din("cc", [128, 8, 2])
    W_ADA = din("w_ada", [DEPTH, D, 6 * D])
    BADA_COL = din("bada_col", [DEPTH, 128, 4, 8])
    BADA_G = din("bada_g", [DEPTH, 2, 128, D])
    W1 = din("w1", [DEPTH, D, NC1])
    GHR = din("ghr", [DEPTH, 128, D])
    HGL = din("hgl", [128, DEPTH, 2, 2])
    MLB = din("mlb", [DEPTH, 128, 16])
    RTL = din("rtl", [128, DEPTH, 2, 2])
    WA = din("wa", [DEPTH, 2, 16, 256])
    BA = din("ba", [128, DEPTH, 2, 2])
    W_OUT = din("w_out", [DEPTH, D, D])
    W_FF1 = din("w_ff1", [DEPTH, D, DFF])
    W_FF2 = din("w_ff2", [DEPTH, DFF, D])
    GFIN = din("gfin", [128, D])
    ROPEC = din("ropec", [128, T])
    ROPES = din("ropes", [128, T])
    CONSTS = din("consts", [128, 8, 128])
    OUT = nc.dram_tensor("out", [LAT, D], F32, kind="ExternalOutput").ap()

    XS = dscr("xs", [T, D], F32)
    QT = dscr("qt", [2, 8, 128, T], BF16)
    KT = dscr("kt", [2, 8, 128, T], BF16)
    KK = dscr("kk", [2, T, 1024], BF16)
    VV = dscr("vv", [2, T, 16, VD], BF16)
    GG = dscr("gg", [T, D], BF16)
    OO = dscr("oo", [2, T, D], F32)

    with ExitStack() as gs:
        P = Prog(nc, gs)
        A = Em(P)
        cnt = [0]

        def sbt(st, shape, dt=F32, name=None):
            cnt[0] += 1
            nm = "%s_%d" % (name or "t", cnt[0])
            return Tl(st.enter_context(nc.sbuf_tensor(nm, list(shape), dt)), nm)

        def pst(st, shape, dt=F32, name=None):
            cnt[0] += 1
            nm = "%s_%d" % (name or "p", cnt[0])
            return Tl(st.enter_context(nc.psum_tensor(nm, list(shape), dt)), nm)

        def rot(st, n, shape, dt=F32, name=None, psum=False):
            return Rot([(pst if psum else sbt)(st, shape, dt, name) for _ in range(n)])

        def sub(tl, ap, name):
            return Tl(ap, name)

        cst = sbt(gs, [128, 8, 128], F32, "cst")
        A.load(cst[:], CONSTS[:, :, :], cst)
        identb = sbt(gs, [128, 128], BF16, "identb")
        A.copy("dve", identb[:], cst[:, 0, :], [cst], [identb])
        maskf = sbt(gs, [64, 2, 64], F32, "maskf")
        A.copy("dve", maskf[:], cst[0:64, 5:7, 0:64], [cst], [maskf])
        masku = maskf[:].bitcast(mybir.dt.uint32)
        rmask = sbt(gs, [128, 512], F32, "rmask")
        A.memset(rmask[:], 1.0, [rmask])
        A.memset(rmask[:].rearrange("p (c t) -> p c t", t=64)[:, :, 0:1], 0.0, [rmask], [rmask])
        onescol = sbt(gs, [128, 1], F32, "onescol")
        A.memset(onescol[:], 1.0, [onescol])
        zcol = sbt(gs, [128, 1], F32, "zcol")
        A.memset(zcol[:], 0.0, [zcol])
        shcol = sbt(gs, [128, 2], F32, "shcol")
        A.memset(shcol[:, 0:1], -HG_SHIFT, [shcol])
        A.memset(shcol[:, 1:2], HG_SHIFT, [shcol], [shcol])
        epscol = sbt(gs, [128, 1], F32, "epscol")
        A.memset(epscol[:], EPS, [epscol])
        ccs = sbt(gs, [128, 8, 2], F32, "ccs")
        A.load(ccs[:], CC_IN[:, :, :], ccs)
        scs = sbt(gs, [128, 8, 2], F32, "scs")
        A.act(scs[:], ccs[:], AF.Silu, [ccs], [scs])
        P.flush()

        def norm_T(pools, src, r0, hT, col0, sc, sh):
            xp, jp, sp_, xnp, ptp = pools
            xt = xp.nxt()
            A.load(xt[:], src[r0:r0 + 128, :], xt)
            jk = jp.nxt()
            ss = sp_.nxt()
            A.act(jk[:], xt[:], AF.Square, [xt], [jk, ss], accum_out=ss[:, 0:1])
            A.act(ss[:, 1:2], ss[:, 0:1], AF.Sqrt, [ss, epscol], [ss], scale=1.0 / D, bias=epscol[:, 0:1])
            A.recip(ss[:, 2:3], ss[:, 1:2], [ss], [ss])
            xn = xnp.nxt()
            A.act(xn[:], xt[:], AF.Identity, [xt, ss], [xn], scale=ss[:, 2:3])
            pt = ptp.nxt()
            A.tr([(pt[:, kt, :], xn[:, kt * 128:(kt + 1) * 128]) for kt in range(8)], identb[:], [xn, identb], [pt])
            if not hasattr(hT, "kb"):
                hT.kb = [Buf("hTk") for _ in range(8)]
            ncount[0] += 1
            for kt in range(8):
                if ncount[0] % 2 == 0:
                    A.act(hT[:, kt, col0:col0 + 128], pt[:, kt, :], AF.Identity, [pt, modcol_ref[0]], [hT.kb[kt]], scale=sc[kt], bias=sh[kt])
                else:
                    A.ts("dve", hT[:, kt, col0:col0 + 128], pt[:, kt, :], sc[kt], sh[kt], ALU.mult, ALU.add, [pt, modcol_ref[0]], [hT.kb[kt]])
            return xt

        modcol_ref = [None]
        ncount = [0]

        def load_w_cast(dst, src2, nk):
            for k0 in range(0, nk, 8):
                A.loadc(dst[:, k0:k0 + 8, :], src2[k0 * 128:(k0 + 8) * 128, :].rearrange("(kt p) n -> p kt n", p=128), dst)

        def decay_core(lf, n, d, escale, bT, Dt, E1, E2, cc, cctl, tmpc, sh=0.0):
            if d == 0:
                A.scan(bT[:, 0:n], rmask[:, 0:n], lf[:, 0:n], [lf, rmask], [bT])
            else:
                A.scan(bT[:, 0:n][:, ::-1], rmask[:, 0:n], lf[:, 0:n][:, ::-1], [lf, rmask], [bT])
            nb = n // 64
            mid = 31 if d == 0 else 32
            last = 63 if d == 0 else 0
            b3 = bT[:, 0:n].rearrange("p (c t) -> p c t", t=64)
            D3 = Dt[:, 0:n].rearrange("p (c t) -> p c t", t=64)
            A.tt("pool", D3, b3, b3[:, :, mid:mid + 1].to_broadcast([128, nb, 64]), ALU.subtract, [bT], [Dt])
            bneg = shcol[:, 0:1] if sh else zcol[:, 0:1]
            bpos = shcol[:, 1:2] if sh else zcol[:, 0:1]
            A.act(E1[:, 0:n], Dt[:, 0:n], AF.Exp, [Dt], [E1], scale=escale, bias=bneg)
            A.act(E2[:, 0:n], Dt[:, 0:n], AF.Exp, [Dt], [E2], scale=-escale, bias=bneg)
            ref2 = bT[:, mid:n:64]
            bl2 = bT[:, last:n:64]
            A.act(cc[0], ref2, AF.Exp, [bT], [cctl], scale=escale, bias=bneg)
            A.act(cc[1], bl2, AF.Exp, [bT], [cctl], scale=escale)
            A.tt("pool", tmpc[:, 0:nb], bl2, ref2, ALU.subtract, [bT], [tmpc])
            A.act(cc[2], tmpc[:, 0:nb], AF.Exp, [tmpc], [cctl], scale=escale, bias=bpos)

        blocks = [(0, CTX, 0)] + [(CTX + 512 * i, 512, 1) for i in range(LAT // 512)]

        try:
          for l in range(depth):
            xsrc = XIN if l == 0 else XS
            with ExitStack() as ls:
                modcol = sbt(ls, [128, 4, 8, 2], F32, "modcol")
                modcol_ref[0] = modcol

                def mcol(which, cl):
                    return [modcol[:, which, kt, cl:cl + 1] for kt in range(8)]

                with ExitStack() as ls2:
                    CCt = sbt(ls2, [128, 4, 2, 2, 3, NCH], F32, "CCt")
                    ALt = sbt(ls2, [128, NCH, 8], F32, "ALt")
                    FLt = sbt(ls2, [64, NCH, 8], F32, "FLt")
                    ALP = sbt(ls2, [128, NCH, 2, 2], F32, "ALP")
                    Ec = [[[sbt(ls2, [128, 512], F32, "Ec") for _ in range(2)] for _ in range(2)] for _ in range(2)]
                    lbt = sbt(ls2, [128, 2, 2, 2], F32, "lbt")
                    lgam = sbt(ls2, [128, 2, 2], F32, "lgam")
                    bat = sbt(ls2, [128, 2, 2], F32, "bat")
                    wat = sbt(ls2, [16, 2, 256], F32, "wat")
                    mlbt = sbt(ls2, [128, 16], F32, "mlbt")
                    ghr = sbt(ls2, [128, D], F32, "ghr")

                    with ExitStack() as ph:
                        wad = rot(ph, 2, [128, 8, 512], F32, "wad")
                        pm = pst(ph, [128, 64], F32, "pm")
                        bcol = sbt(ph, [128, 4, 8], F32, "bcol")
                        A.load(bcol[:], BADA_COL[l, :, :, :], bcol)
                        for which, vec in enumerate((0, 1, 3, 4)):
                            for half in range(2):
                                blk = vec * 2 + half
                                w = wad.nxt()
                                A.load(w[:], W_ADA[l, :, blk * 512:(blk + 1) * 512].rearrange("(kt p) n -> p kt n", p=128), w)
                                items = []
                                for f4 in range(4):
                                    c0 = (which * 8 + half * 4 + f4) * 2
                                    for kt in range(8):
                                        items.append((pm[:, c0:c0 + 2], w[:, kt, f4 * 128:(f4 + 1) * 128], scs[:, kt, :], kt == 0, kt == 7))
                                A.mm(items, [w, scs], [pm])
                        A.tt("dve", modcol[:].rearrange("p a b c -> p (a b) c"), pm[:].rearrange("p (a c) -> p a c", c=2),
                             bcol[:].rearrange("p a b -> p (a b)").unsqueeze(2).to_broadcast([128, 32, 2]), ALU.add, [pm, bcol], [modcol])
                        for which in (1, 3):
                            A.ts("dve", modcol[:, which, :, :], modcol[:, which, :, :], 1.0, None, ALU.add, None, [modcol], [modcol])
                        hgl = sbt(ph, [128, DEPTH, 2, 2], F32, "hgl")
                        A.load(hgl[:], HGL[:, :, :, :], hgl)
                        if l == 0:
                            A.memset(lbt[:, 0, :, :], 0.0, [lbt])
                        else:
                            dl = sbt(ph, [128, 2, 2], F32, "dl")
                            A.tt("dve", dl[:], hgl[:, 1, :, :], hgl[:, 0, :, :], ALU.subtract, [hgl], [dl])
                            A.act(lbt[:, 0, :, :], dl[:], AF.Sigmoid, [dl], [lbt])
                        A.ts("dve", lbt[:, 1, :, :], lbt[:, 0, :, :], -1.0, 1.0, ALU.mult, ALU.add, [lbt], [lbt])
                        rtl = sbt(ph, [128, DEPTH, 2, 2], F32, "rtl")
                        A.load(rtl[:], RTL[:, :, :, :], rtl)
                        sgr = sbt(ph, [128, 2, 2], F32, "sgr")
                        A.act(sgr[:], rtl[:, l, :, :], AF.Sigmoid, [rtl], [sgr])
                        A.act(lgam[:], sgr[:], AF.Ln, [sgr], [lgam])
                        bap = sbt(ph, [128, DEPTH, 2, 2], F32, "bap")
                        A.load(bap[:], BA[:, :, :, :], bap)
                        A.copy("dve", bat[:], bap[:, l, :, :], [bap], [bat])
                        A.load(wat[:], WA[l, :, :, :].rearrange("d r c -> r d c"), wat)
                        A.load(mlbt[:], MLB[l, :, :], mlbt)
                        A.load(ghr[:], GHR[l, :, :], ghr)
                        lfc = sbt(ph, [128, 512], F32, "lfc")
                        bTc = sbt(ph, [128, 512], F32, "bTc")
                        Dtc = sbt(ph, [128, 512], F32, "Dtc")
                        ctmp = sbt(ph, [128, 3, 8], F32, "ctmp")
                        tmpc = sbt(ph, [128, 8], F32, "tmpc")
                        for d in range(2):
                            for j in range(2):
                                A.copy("dve", lfc[:], lgam[:, d, j:j + 1].to_broadcast([128, 512]), [lgam], [lfc])
                                decay_core(lfc, 512, d, 1.0, bTc, Dtc, Ec[d][j][0], Ec[d][j][1],
                                           [ctmp[:, k, :] for k in range(3)], ctmp, tmpc)
                                for kind in range(3):
                                    A.copy("dve", CCt[:, RT, d, j, kind, :], ctmp[:, kind, 0:1].to_broadcast([128, NCH]), [ctmp], [CCt])
                        P.flush()
                        if stop == 'PA':
                            raise _Stop()

                    with ExitStack() as ph:
                        w1 = sbt(ph, [128, 8, NFM], BF16, "w1fm")
                        load_w_cast(w1, W1[l, :, 0:NFM], 8)
                        hTp = rot(ph, 2, [128, 8, 512], BF16, "hT")
                        npools = (rot(ph, 2, [128, D], F32, "xt"), rot(ph, 1, [128, D], BF16, "jk"), rot(ph, 2, [128, 4], F32, "ss"),
                                  rot(ph, 2, [128, D], BF16, "xn"), rot(ph, 2, [128, 8, 128], BF16, "ptr", psum=True))
                        pf = rot(ph, 4, [128, 512], F32, "pf", psum=True)
                        ptk = rot(ph, 2, [128, 4, 128], BF16, "ptk", psum=True)
                        wk = rot(ph, 10, [128, 512], F32, "wk")
                        fq = rot(ph, 4, [128, 512], F32, "fq")
                        bfp = rot(ph, 6, [128, 512], BF16, "bfp")
                        kst = [sbt(ph, [128, 4, 1024], BF16, "kst") for _ in range(2)]
                        rope = [sbt(ph, [128, 512], F32, "rope") for _ in range(2)]
                        ga = [sbt(ph, [16, 512], F32, "ga") for _ in range(2)]
                        tmpcp = rot(ph, 2, [128, 8], F32, "tmpc")

                        def do_block(t0, n, cl):
                            ntile = n // 128
                            ch0 = t0 // 64
                            nb = n // 64
                            hT = hTp.nxt()
                            for i in range(ntile):
                                norm_T(npools, xsrc, t0 + 128 * i, hT, 128 * i, mcol(1, cl), mcol(0, cl))
                            A.load(rope[0][:, 0:n], ROPEC[:, t0:t0 + n], rope[0])
                            A.load(rope[1][:, 0:n], ROPES[:, t0:t0 + n], rope[1])
                            P.release()

                            def fm(col0, M=128):
                                ps = pf.nxt()
                                A.mm([(ps[0:M, 0:n], w1[:, kt, col0:col0 + M], hT[:, kt, 0:n], kt == 0, kt == 7) for kt in range(8)],
                                     [w1] + hT.kb, [ps])
                                return ps

                            def finish(m, d, j, q, k, lf, escale, kscale, E=None, dirs=None, sh=0.0):
                                P.release()
                                if E is None:
                                    bT, Dt, E1, E2 = wk.nxt(), wk.nxt(), wk.nxt(), wk.nxt()
                                    decay_core(lf, n, d, escale, bT, Dt, E1, E2,
                                               [CCt[:, m, d, j, kind, ch0:ch0 + nb] for kind in range(3)], CCt, tmpcp.nxt(), sh=sh)
                                elif E == "none":
                                    E1 = E2 = None
                                else:
                                    E1, E2 = E
                                qh = bfp.nxt()
                                kh = bfp.nxt()
                                if E1 is None:
                                    A.copy("act", qh[:, 0:n], q[:, 0:n], [q], [qh])
                                    P.act(lambda e: e.mul(out=kh[:, 0:n], in_=k[:, 0:n], mul=kscale), [k], [kh])
                                else:
                                    A.tt("dve", qh[:, 0:n], q[:, 0:n], E1[:, 0:n], ALU.mult, [q, E1], [qh])
                                    A.stt(kh[:, 0:n], k[:, 0:n], kscale, E2[:, 0:n], ALU.mult, ALU.mult, [k, E2], [kh])
                                for dd in (dirs or [d]):
                                    A.store(QT[dd, m * 2 + j, :, t0:t0 + n], qh[:, 0:n], qh)
                                    A.store(KT[dd, m * 2 + j, :, t0:t0 + n], kh[:, 0:n], kh)
                                pk = ptk.nxt()
                                A.tr([(pk[:, i, :], kh[:, 128 * i:128 * (i + 1)]) for i in range(ntile)], identb[:], [kh, identb], [pk])
                                for dd in (dirs or [d]):
                                    A.copy("act", kst[dd][:, 0:ntile, m * 256 + j * 128:m * 256 + (j + 1) * 128], pk[:, 0:ntile, :],
                                           [pk], [kst[dd]])

                            for j in range(2):
                                psq = fm((0 + j) * 128)
                                qs = fq.nxt()
                                A.act(qs[:, 0:n], psq[:, 0:n], AF.Silu, [psq], [qs])
                                for d in range(2):
                                    psz = fm((2 + 2 * d + j) * 128)
                                    sg, f, lf, kk = wk.nxt(), wk.nxt(), wk.nxt(), wk.nxt()
                                    A.act(sg[:, 0:n], psz[:, 0:n], AF.Sigmoid, [psz], [sg])
                                    A.ts("dve", f[:, 0:n], sg[:, 0:n], lbt[:, 1, d, j:j + 1], lbt[:, 0, d, j:j + 1], ALU.mult, ALU.add,
                                         [sg, lbt], [f])
                                    A.act(lf[:, 0:n], f[:, 0:n], AF.Ln, [f], [lf])
                                    A.act(kk[:, 0:n], f[:, 0:n], AF.Identity, [f, onescol], [kk], scale=-1.0, bias=onescol[:, 0:1])
                                    finish(HG, d, j, qs, kk, lf, 1.0, 1.0, sh=HG_SHIFT)
                            for j in range(2):
                                psq = fm((6 + j) * 128)
                                psk = fm((8 + j) * 128)
                                finish(ML, 0, j, psq, psk, None, 1.0, 0.125, E="none", dirs=[0, 1])
                            for j in range(2):
                                rr = []
                                for base in (10, 14):
                                    ps0 = fm((base + j) * 128)
                                    ps1 = fm((base + 2 + j) * 128)
                                    t1, t2 = wk.nxt(), wk.nxt()
                                    r = fq.nxt()
                                    A.tt("dve", t1[:, 0:n], ps0[:, 0:n], rope[0][:, 0:n], ALU.mult, [ps0, rope[0]], [t1])
                                    A.tt("dve", t2[:, 0:n], ps1[:, 0:n], rope[1][:, 0:n], ALU.mult, [ps1, rope[1]], [t2])
                                    A.tt("pool", r[:, 0:n], t1[:, 0:n], t2[:, 0:n], ALU.add, [t1, t2], [r])
                                    rr.append(r)
                                for d in range(2):
                                    finish(RT, d, j, rr[0], rr[1], None, 1.0, 0.125, E=(Ec[d][j][0], Ec[d][j][1]))
                            for d in range(2):
                                psa = fm(22 * 128 + 16 * d, M=16)
                                A.copy("act", ga[d][:, 0:n], psa[0:16, 0:n], [psa], [ga[d]])
                            for j in range(2):
                                psq = fm((18 + j) * 128)
                                psk = fm((20 + j) * 128)
                                qr, kr = fq.nxt(), fq.nxt()
                                A.copy("act", qr[:, 0:n], psq[:, 0:n], [psq], [qr])
                                A.copy("dve", kr[:, 0:n], psk[:, 0:n], [psk], [kr])
                                for d in range(2):
                                    psz = pf.nxt()
                                    A.mm([(psz[:, 0:n], wat[0:16, d, j * 128:(j + 1) * 128], ga[d][0:16, 0:n], True, True)], [wat, ga[d]], [psz])
                                    sg, lf = wk.nxt(), wk.nxt()
                                    A.act(sg[:, 0:n], psz[:, 0:n], AF.Sigmoid, [psz, bat], [sg], bias=bat[:, d, j:j + 1])
                                    A.act(lf[:, 0:n], sg[:, 0:n], AF.Ln, [sg], [lf])
                                    finish(GL, d, j, qr, kr, lf, 1.0 / 16.0, 32.0 ** -0.5)
                            for dd in range(2):
                                A.store(KK[dd, t0:t0 + n, :].rearrange("(i p) x -> p i x", p=128), kst[dd][:, 0:ntile, :], kst[dd])

                        for (t0, n, cl) in blocks:
                            do_block(t0, n, cl)
                        P.flush()
                        if stop == 'P1a':
                            raise _Stop()

                    with ExitStack() as ph:
                        w1 = sbt(ph, [128, 8, NTM], BF16, "w1tm")
                        load_w_cast(w1, W1[l, :, NFM:NC1], 8)
                        hTp = rot(ph, 4, [128, 8, 128], BF16, "hT")
                        npools = (rot(ph, 4, [128, D], F32, "xt"), rot(ph, 2, [128, D], BF16, "jk"), rot(ph, 4, [128, 4], F32, "ss"),
                                  rot(ph, 4, [128, D], BF16, "xn"), rot(ph, 2, [128, 8, 128], BF16, "ptr", psum=True))
                        pt = rot(ph, 4, [128, 512], F32, "pt", psum=True)
                        psm = rot(ph, 2, [128, 64], F32, "psm", psum=True)
                        vstp = rot(ph, 4, [128, 16, VD], BF16, "vst")
                        vmlp = rot(ph, 4, [128, 4, VD], BF16, "vml")
                        vs1p = rot(ph, 4, [128, 4, VD], BF16, "vs1")
                        for tl_ in vstp.items + vmlp.items:
                            A.memset(tl_[:], 0.0, [tl_])
                            A.memset(tl_[:, :, 64:65], 1.0, [tl_], [tl_])
                        sgp = rot(ph, 6, [128, 512], F32, "sgp")
                        ggp = rot(ph, 4, [128, D], BF16, "ggp")
                        smp = rot(ph, 4, [128, 64], F32, "smp")

                        def do_tile(ti):
                            r0 = ti * 128
                            cl = 0 if r0 < CTX else 1
                            hT = hTp.nxt()
                            norm_T(npools, xsrc, r0, hT, 0, mcol(1, cl), mcol(0, cl))
                            P.release()

                            def tm(col0, N):
                                ps = pt.nxt()
                                import os
                                if os.environ.get("EXPA"):
                                    A.mm([(ps[:, 0:128], w1[:, kt, col0:col0 + 128], hT[:, kt, :], kt == 0, kt == 7) for kt in range(8)], [w1] + hT.kb, [ps])
                                elif os.environ.get("EXPB"):
                                    A.mm([(ps[:, 0:256], hT[:, kt, :], w1[:, kt, col0:col0 + 256], kt == 0, kt == 7) for kt in range(8)], [w1] + hT.kb, [ps])
                                else:
                                    items = []
                                    for n0 in range(0, N, 256):
                                        n1 = min(N, n0 + 256)
                                        items += [(ps[:, n0:n1], hT[:, kt, :], w1[:, kt, col0 + n0:col0 + n1], kt == 0, kt == 7) for kt in range(8)]
                                    A.mm(items, [w1] + hT.kb, [ps])
                                return ps
                            vst, vml, vs1 = vstp.nxt(), vmlp.nxt(), vs1p.nxt()
                            import os
                            SKIP = os.environ.get("SKIP", "")
                            if "v" in SKIP:
                                return
                            psA = tm(0, 512)
                            if "1" in SKIP:
                                return
                            if "a" not in SKIP:
                                A.copy("act", vst[:, 0:4, 0:64], psA[:, 0:256].rearrange("p (h v) -> p h v", v=64), [psA], [vst])
                            if "b" not in SKIP:
                                A.copy("act", vml[:, :, 0:64], psA[:, 256:512].rearrange("p (h v) -> p h v", v=64), [psA], [vml])
                            if "2" in SKIP:
                                return
                            psB = tm(512, 512)
                            A.copy("dve", vst[:, 8:16, 0:64], psB[:, 0:512].rearrange("p (h v) -> p h v", v=64), [psB], [vst])
                            if "g" in SKIP:
                                return
                            gg = ggp.nxt()
                            psC = tm(1024, 512)
                            s1 = sgp.nxt()
                            A.act(s1[:], psC[:], AF.Sigmoid, [psC], [s1])
                            A.tt("dve", gg[:, 0:512], s1[:], ghr[:, 0:512], ALU.mult, [s1, ghr], [gg])
                            psD = tm(1536, 512)
                            s2 = sgp.nxt()
                            A.act(s2[:], psD[:], AF.Silu, [psD], [s2])
                            A.tt("pool", gg[:, 512:1024], s2[:], ghr[:, 512:1024], ALU.mult, [s2, ghr], [gg])
                            A.store(GG[r0:r0 + 128, :], gg[:], gg)
                            if "m" in SKIP:
                                return
                            psG = tm(2048, 16)
                            sm = smp.nxt()
                            A.tt("dve", sm[:, 0:16], psG[:, 0:16], mlbt[:], ALU.add, [psG, mlbt], [sm])
                            gv = sm[:, 0:16].rearrange("p (d g h) -> p d g h", d=2, g=2)
                            A.act(sm[:, 16:24].rearrange("p (d h) -> p d h", d=2), gv[:, :, 1, :], AF.Sigmoid, [sm], [sm])
                            A.act(sm[:, 24:32], sm[:, 16:24], AF.Ln, [sm], [sm])
                            pb = psm.nxt()
                            A.mm([(pb[:, 0:4], cst[:, 1, :], sm[:, 24:28], True, True),
                                  (pb[:, 4:8], cst[:, 2, :], sm[:, 28:32], True, True),
                                  (pb[:, 8:16], cst[:, 3, :], sm[:, 24:32], True, True),
                                  (pb[:, 16:24], cst[:, 4, :], sm[:, 24:32], True, True)], [sm, cst], [pb])
                            A.act(ALt[:, 2 * ti:2 * ti + 2, :], pb[:, 8:24].rearrange("p (c g) -> p c g", g=8), AF.Exp, [pb], [ALt, pb])
                            for hh_ in range(2):
                                A.copy("pool", ALP[64 * hh_:64 * hh_ + 64, 2 * ti:2 * ti + 2, :, :],
                                       ALt[64 * hh_:64 * hh_ + 64, 2 * ti:2 * ti + 2, :].rearrange("p c (d a b) -> p c d a b", d=2, a=2)[:, :, :, :, hh_],
                                       [ALt], [ALP])
                            A.tt("dve", sm[:, 32:40].rearrange("p (d h) -> p d h", d=2), gv[:, :, 0, :],
                                 pb[:, 0:8].rearrange("p (d h) -> p d h", d=2), ALU.subtract, [pb, sm], [sm, pb])
                            A.act(sm[:, 40:48], sm[:, 32:40], AF.Exp, [sm], [sm])
                            if "f" in SKIP:
                                return
                            A.act(sm[:, 48:56], pb[:, 0:8], AF.Exp, [pb], [sm, pb], scale=-1.0)
                            A.copy("dve", FLt[0:64, 2 * ti, :], sm[0:64, 48:56], [sm], [FLt])
                            A.copy("dve", FLt[0:64, 2 * ti + 1, :], sm[64:128, 48:56], [sm], [FLt])
                            if "w" in SKIP:
                                return
                            A.tt("dve", vst[:, 4:8, :], vml[:], sm[:, 40:44].unsqueeze(2).to_broadcast([128, 4, VD]), ALU.mult, [sm, vml], [vst])
                            A.tt("dve", vs1[:], vml[:], sm[:, 44:48].unsqueeze(2).to_broadcast([128, 4, VD]), ALU.mult, [sm, vml], [vs1])
                            A.store(VV[0, r0:r0 + 128, :, :], vst[:], vst)
                            A.store(VV[1, r0:r0 + 128, 4:8, :], vs1[:], vs1)

                        for ti in range(NT):
                            do_tile(ti)
                        P.flush()
                        if stop == 'P1b':
                            raise _Stop()

                    with ExitStack() as ph:
                        qTp = [rot(ph, 2, [128, 8, 2, 256], BF16, "qbd") for _ in range(2)]
                        for d_ in range(2):
                            for t_ in qTp[d_].items:
                                A.memset(t_[:], 0.0, [t_])
                        kTp = [rot(ph, 2, [128, 8, 256], BF16, "kT") for _ in range(2)]
                        ktp = [rot(ph, 2, [64, 4, 1024], BF16, "ktok") for _ in range(2)]
                        vvp = [rot(ph, 2, [64, 4, 16 * VD], BF16, "vv") for _ in range(2)]
                        vmp = rot(ph, 2, [64, 4, 4 * VD], BF16, "vm")
                        S = [[[sbt(ph, [128, VD], F32, "S") for _ in range(2)] for _ in range(4)] for _ in range(2)]
                        Sb = [[[sbt(ph, [128, VD], BF16, "Sb") for _ in range(2)] for _ in range(4)] for _ in range(2)]
                        for d in range(2):
                            for m in range(4):
                                for hp in range(2):
                                    A.memset(S[d][m][hp][:], 0.0, [S[d][m][hp]])
                        psA = Rot([Tl(t_[:, 0:256].rearrange("p (h v) -> p h v", v=64), "psAv") for t_ in
                                   [pst(ph, [64, 512], F32, "psA") for _ in range(4)]])
                        pso = Rot([Tl(t_[:, 0:4 * VD].rearrange("p (h v) -> p h v", v=VD), "psov") for t_ in
                                   [pst(ph, [64, 512], F32, "pso") for _ in range(2)]])
                        psS = Rot([Tl(t_[:, 0:4 * VD].rearrange("p (a v) -> p a v", v=2 * VD), "psSv") for t_ in
                                   [pst(ph, [128, 512], F32, "psS") for _ in range(2)]])
                        mask4 = sbt(ph, [64, 2, 4, 64], F32, "mask4")
                        for d_ in range(2):
                            A.copy("dve", mask4[:, d_, :, :], maskf[:, d_, :].unsqueeze(1).to_broadcast([64, 4, 64]), [maskf], [mask4])
                        mask4u = mask4[:].bitcast(mybir.dt.uint32)
                        Asbd = [rot(ph, 5, [64, 4, 64], BF16, "Asb") for _ in range(2)]
                        for dd_ in range(2):
                            for t_ in Asbd[dd_].items:
                                A.memset(t_[:], 0.0, [t_])
                        tmpS = rot(ph, 6, [128, VD], F32, "tmpS")
                        osb = rot(ph, 2, [64, D], F32, "osb")
                        nrm = rot(ph, 2, [64, 16], F32, "nrm")

                        def col(m, d, hp, kind, c):
                            if m == ML:
                                if kind == 0:
                                    return onescol[:, 0:1], onescol
                                return ALP[:, c, d, hp:hp + 1], ALP
                            return CCt[:, m, d, hp, kind, c:c + 1], CCt

                        def do_chunk(d, c, cc, qT, kT, ktok, vv, vm):
                            ts = slice(64 * cc, 64 * cc + 64)
                            ob = osb.nxt()
                            vs = []
                            for m in range(4):
                                if m == ML and d == 1:
                                    vs.append((vm, 0))
                                else:
                                    vs.append((vv, 4 * m * VD))
                            a4s = []
                            for m in range(4):
                                for hp in range(2):
                                    St, Sbt = S[d][m][hp], Sb[d][m][hp]
                                    c1, c1t = col(m, d, hp, 0, c)
                                    A.act(Sbt[:, :], St[:, :], AF.Identity, [St, c1t], [Sbt], scale=c1)
                                pa = psA.nxt()
                                A.mm([(pa[:, h, :], kT[:, 2 * m + h // 2, ts], qT[:, 2 * m + h // 2, h % 2, ts], True, True) for h in range(4)],
                                     [kT, qT], [pa])
                                a4 = Asbd[d].nxt()
                                A.tt("dve", a4[:], pa[:], mask4[:, d, :, :], ALU.mult, [pa, mask4], [a4])
                                a4s.append(a4)
                            pos, pSs = [], []
                            for m in range(4):
                                vsrc, vbase = vs[m]
                                a4 = a4s[m]
                                po = pso.nxt()
                                items = []
                                for h in range(4):
                                    items.append((po[:, h, :], a4[:, h, :], vsrc[0:64, cc, vbase + h * VD:vbase + (h + 1) * VD], True, False))
                                    items.append((po[:, h, :], qT[:, 2 * m + h // 2, h % 2, ts], Sb[d][m][h // 2][:, :], False, True))
                                A.mm(items, [a4, vsrc, qT, Sb[d][m][0], Sb[d][m][1]], [po])
                                pS = psS.nxt()
                                A.mm([(pS[:, hp, :], ktok[0:64, cc, m * 256 + hp * 128:m * 256 + (hp + 1) * 128],
                                       vsrc[0:64, cc, vbase + 2 * hp * VD:vbase + (2 * hp + 2) * VD], True, True) for hp in range(2)],
                                     [ktok, vsrc], [pS])
                                for hp in range(2):
                                    St = S[d][m][hp]
                                    tS = tmpS.nxt()
                                    c2, c2t = col(m, d, hp, 1, c)
                                    c3, c3t = col(m, d, hp, 2, c)
                                    A.act(tS[:, :], St[:, :], AF.Identity, [St, c2t], [tS], scale=c2)
                                    for hh in range(2):
                                        p0 = 64 * hh
                                        A.stt(St[p0:p0 + 64, :], pS[p0:p0 + 64, hp, hh * VD:(hh + 1) * VD], c3[p0:p0 + 64, :], tS[p0:p0 + 64, :],
                                              ALU.mult, ALU.add, [pS, tS, c3t], [St])
                                ov = ob[:, m * 256:(m + 1) * 256].rearrange("p (h v) -> p h v", v=64)
                                if m == HG:
                                    P.act(lambda e, ov=ov, pin=po[:, :, 0:64]: e.mul(out=ov, in_=pin, mul=float(np.exp(2.0 * HG_SHIFT))), [po], [ob])
                                elif m != ML:
                                    A.copy("act", ov, po[:, :, 0:64], [po], [ob])
                                else:
                                    nr = nrm.nxt()
                                    A.act(nr[:, 0:4], po[:, :, 64], AF.Abs, [po], [nr, po])
                                    A.tt("dve", nr[:, 4:8], nr[:, 0:4], FLt[0:64, c, d * 4:d * 4 + 4], ALU.max, [nr, FLt], [nr])
                                    A.recip(nr[:, 8:12], nr[:, 4:8], [nr], [nr])
                                    A.tt("dve", ov, po[:, :, 0:64], nr[:, 8:12].unsqueeze(2).to_broadcast([64, 4, 64]), ALU.mult, [nr, po], [ob, po])
                            A.store(OO[d, 64 * c:64 * c + 64, :], ob[:], ob)

                        grp_order = [list(range(NG)), [0] + list(range(NG - 1, 0, -1))]
                        cc_order = [[0, 1, 2, 3], [3, 2, 1, 0]]
                        for gi in range(NG):
                            grp = []
                            for d in range(2):
                                g = grp_order[d][gi]
                                tg0 = 256 * g
                                qT, kT, ktok, vv = qTp[d].nxt(), kTp[d].nxt(), ktp[d].nxt(), vvp[d].nxt()
                                A.load(qT[0:64, :, 0, :], QT[d, :, 0:64, tg0:tg0 + 256].rearrange("m p t -> p m t"), qT)
                                A.load(qT[64:128, :, 1, :], QT[d, :, 64:128, tg0:tg0 + 256].rearrange("m p t -> p m t"), qT)
                                A.load(kT[:], KT[d, :, :, tg0:tg0 + 256].rearrange("m p t -> p m t"), kT)
                                A.load(ktok[:], KK[d, tg0:tg0 + 256, :].rearrange("(c p) x -> p c x", p=64), ktok)
                                A.load(vv[:], VV[0, tg0:tg0 + 256, :, :].rearrange("(c p) h v -> p c (h v)", p=64), vv)
                                vm = None
                                if d == 1:
                                    vm = vmp.nxt()
                                    A.load(vm[:], VV[1, tg0:tg0 + 256, 4:8, :].rearrange("(c p) h v -> p c (h v)", p=64), vm)
                                grp.append((g, qT, kT, ktok, vv, vm))
                            for k_ in range(4):
                                for d in range(2):
                                    g, qT, kT, ktok, vv, vm = grp[d]
                                    cc = cc_order[d][k_]
                                    P.release()
                                    do_chunk(d, 4 * g + cc, cc, qT, kT, ktok, vv, vm)
                        P.flush()
                        if stop == 'P2':
                            raise _Stop()

                with ExitStack() as ph:
                    g1 = [sbt(ph, [128, D], F32, "g1") for _ in range(2)]
                    g2 = [sbt(ph, [128, D], F32, "g2") for _ in range(2)]

                    def gtiles(gi, dst):
                        with ExitStack() as ph2:
                            wad = rot(ph2, 2, [128, 8, 512], F32, "wad")
                            bg = sbt(ph2, [128, D], F32, "bg")
                            pg = rot(ph2, 2, [128, 512], F32, "pg", psum=True)
                            crep = [sbt(ph2, [128, 8, 128], F32, "crep") for _ in range(2)]
                            for cl in range(2):
                                A.copy("dve", crep[cl][:], scs[:, :, cl:cl + 1].to_broadcast([128, 8, 128]), [scs], [crep[cl]])
                            A.load(bg[:], BADA_G[l, gi, :, :], bg)
                            vec = (2, 5)[gi]
                            for half in range(2):
                                blk = vec * 2 + half
                                w = wad.nxt()
                                A.load(w[:], W_ADA[l, :, blk * 512:(blk + 1) * 512].rearrange("(kt p) n -> p kt n", p=128), w)
                                for cl in range(2):
                                    ps = pg.nxt()
                                    A.mm([(ps[:], crep[cl][:, kt, :], w[:, kt, :], kt == 0, kt == 7) for kt in range(8)], [w, crep[cl]], [ps])
                                    A.tt("dve", dst[cl][:, half * 512:(half + 1) * 512], ps[:], bg[:, half * 512:(half + 1) * 512], ALU.add,
                                         [ps, bg], [dst[cl]])
                            P.flush()
                            if stop == 'P3a':
                                raise _Stop()
                    gtiles(0, g1)
                    with ExitStack() as ph2:
                        wo = sbt(ph2, [128, 8, D], BF16, "wo")
                        load_w_cast(wo, W_OUT[l, :, :], 8)
                        o0p = rot(ph2, 2, [128, D], F32, "o0")
                        o1p = rot(ph2, 2, [128, D], F32, "o1")
                        gglp = rot(ph2, 2, [128, D], BF16, "ggl")
                        xp = rot(ph2, 3, [128, D], F32, "x3")
                        sqp = rot(ph2, 1, [128, D], F32, "sq")
                        ssp = rot(ph2, 2, [128, 48], F32, "ss3")
                        yp = rot(ph2, 2, [128, D], BF16, "y")
                        yTp = rot(ph2, 2, [128, 8, 128], BF16, "yT")
                        ptr = rot(ph2, 2, [128, 8, 128], BF16, "ptr3", psum=True)
                        pop = rot(ph2, 4, [128, 512], F32, "pop", psum=True)
                        tp = rot(ph2, 2, [128, 512], F32, "t3")

                        def do_tile3(ti):
                            r0 = ti * 128
                            cl = 0 if r0 < CTX else 1
                            o0, o1, ggl, xt = o0p.nxt(), o1p.nxt(), gglp.nxt(), xp.nxt()
                            A.load(o0[:], OO[0, r0:r0 + 128, :], o0)
                            A.load(o1[:], OO[1, r0:r0 + 128, :], o1)
                            A.load(ggl[:], GG[r0:r0 + 128, :], ggl)
                            A.load(xt[:], xsrc[r0:r0 + 128, :], xt)
                            P.release()
                            A.tt("pool", o0[:], o0[:], o1[:], ALU.add, [o0, o1], [o0])
                            sq, ss = sqp.nxt(), ssp.nxt()
                            A.act(sq[:], o0[:], AF.Square, [o0], [sq])
                            A.reduce(ss[:, 0:16], sq[:].rearrange("p (h v) -> p h v", v=64), [sq], [ss])
                            A.act(ss[:, 16:32], ss[:, 0:16], AF.Sqrt, [ss, epscol], [ss], scale=1.0 / 64, bias=epscol[:, 0:1])
                            A.recip(ss[:, 32:48], ss[:, 16:32], [ss], [ss])
                            o3 = o0[:].rearrange("p (h v) -> p h v", v=64)
                            A.tt("dve", o3, o3, ss[:, 32:48].unsqueeze(2).to_broadcast([128, 16, 64]), ALU.mult, [o0, ss], [o0])
                            y = yp.nxt()
                            A.tt("pool", y[:], o0[:], ggl[:], ALU.mult, [o0, ggl], [y])
                            pt = ptr.nxt()
                            A.tr([(pt[:, kt, :], y[:, kt * 128:(kt + 1) * 128]) for kt in range(8)], identb[:], [y, identb], [pt])
                            yT = yTp.nxt()
                            A.copy("act", yT[:, 0:4, :], pt[:, 0:4, :], [pt], [yT])
                            A.copy("dve", yT[:, 4:8, :], pt[:, 4:8, :], [pt], [yT])
                            for nb in range(2):
                                ps = pop.nxt()
                                A.mm([(ps[:], yT[:, kt, :], wo[:, kt, nb * 512:(nb + 1) * 512], kt == 0, kt == 7) for kt in range(8)], [yT, wo], [ps])
                                t = tp.nxt()
                                A.tt("dve", t[:], ps[:], g1[cl][:, nb * 512:(nb + 1) * 512], ALU.mult, [ps, g1[cl]], [t])
                                A.tt("pool", xt[:, nb * 512:(nb + 1) * 512], xt[:, nb * 512:(nb + 1) * 512], t[:], ALU.add, [t, xt], [xt])
                            A.store(XS[r0:r0 + 128, :], xt[:], xt)

                        for ti in range(NT):
                            do_tile3(ti)
                        P.flush()
                        if stop == 'P3a':
                            raise _Stop()

                    gtiles(1, g2)
                    with ExitStack() as ph2:
                        wf1 = sbt(ph2, [128, 8, DFF], BF16, "wf1")
                        wf2 = sbt(ph2, [128, 32, D], BF16, "wf2")
                        load_w_cast(wf1, W_FF1[l, :, :], 8)
                        load_w_cast(wf2, W_FF2[l, :, :], 32)
                        hTp = rot(ph2, 2, [128, 8, 256], BF16, "h2T")
                        npools = (rot(ph2, 3, [128, D], F32, "xt"), rot(ph2, 1, [128, D], BF16, "jk"), rot(ph2, 2, [128, 4], F32, "ss"),
                                  rot(ph2, 2, [128, D], BF16, "xn"), rot(ph2, 2, [128, 8, 128], BF16, "ptr", psum=True))
                        uTp = rot(ph2, 1, [128, 32, 256], BF16, "uT")
                        pu = Rot([Tl(t_[:, 0:256], "pus") for t_ in [pst(ph2, [128, 512], F32, "pu") for _ in range(3)]])
                        po2 = rot(ph2, 2, [128, 512], F32, "po2", psum=True)
                        sqp = rot(ph2, 3, [128, 256], F32, "sq2")
                        tp = rot(ph2, 1, [128, 512], F32, "t4")

                        def do_blk(bi):
                            cl = 0 if bi * 256 < CTX else 1
                            hT = hTp.nxt()
                            xts = []
                            for i in range(2):
                                xts.append(norm_T(npools, XS, bi * 256 + 128 * i, hT, 128 * i, mcol(3, cl), mcol(2, cl)))
                                if i == 0:
                                    P.release()
                            uT = uTp.nxt()
                            for fb in range(32):
                                ps = pu.nxt()
                                A.mm([(ps[:], wf1[:, kt, fb * 128:(fb + 1) * 128], hT[:, kt, :], kt == 0, kt == 7) for kt in range(8)], [wf1] + hT.kb, [ps])
                                sq = sqp.nxt()
                                A.act(sq[:], ps[:], AF.Square, [ps], [sq])
                                A.stt(uT[:, fb, :], ps[:], 0.0, sq[:], ALU.is_gt, ALU.mult, [ps, sq], [uT])
                            for i in range(2):
                                xt = xts[i]
                                for nb in range(2):
                                    ps = po2.nxt()
                                    A.mm([(ps[:], uT[:, fb, 128 * i:128 * (i + 1)], wf2[:, fb, nb * 512:(nb + 1) * 512], fb == 0, fb == 31)
                                          for fb in range(32)], [uT, wf2], [ps])
                                    t = tp.nxt()
                                    A.tt("dve", t[:], ps[:], g2[cl][:, nb * 512:(nb + 1) * 512], ALU.mult, [ps, g2[cl]], [t])
                                    A.tt("pool", xt[:, nb * 512:(nb + 1) * 512], xt[:, nb * 512:(nb + 1) * 512], t[:], ALU.add, [t, xt], [xt])
                                r0 = bi * 256 + 128 * i
                                A.store(XS[r0:r0 + 128, :], xt[:], xt)

                        for bi in range(T // 256):
                            do_blk(bi)
                        P.flush()
                        if stop == 'P3b':
                            raise _Stop()

        except _Stop:
            P.flush(final=True)
            build.ninst = P.ninst
            gs.pop_all()
            return nc
        with ExitStack() as ph:
            gf = sbt(ph, [128, D], F32, "gf")
            A.load(gf[:], GFIN[:, :], gf)
            xp = rot(ph, 3, [128, D], F32, "xf")
            jp = rot(ph, 1, [128, D], BF16, "jkf")
            sp_ = rot(ph, 2, [128, 4], F32, "ssf")
            for ti in range(LAT // 128):
                r0 = CTX + ti * 128
                xt, jk, ss = xp.nxt(), jp.nxt(), sp_.nxt()
                A.load(xt[:], XS[r0:r0 + 128, :], xt)
                P.release()
                A.act(jk[:], xt[:], AF.Square, [xt], [jk, ss], accum_out=ss[:, 0:1])
                A.act(ss[:, 1:2], ss[:, 0:1], AF.Sqrt, [ss, epscol], [ss], scale=1.0 / D, bias=epscol[:, 0:1])
                A.recip(ss[:, 2:3], ss[:, 1:2], [ss], [ss])
                A.stt(xt[:], xt[:], ss[:, 2:3], gf[:], ALU.mult, ALU.mult, [xt, ss, gf], [xt])
                A.store(OUT[ti * 128:(ti + 1) * 128, :], xt[:], xt)
            P.flush(final=True)
        build.ninst = P.ninst
    return nc


_IN_LAYOUT = (
    ('hg_q', 256), ('hg_f_fwd', 256), ('hg_f_bwd', 256), ('hg_i', 256), ('hg_g', 256),
    ('ml_q', 256), ('ml_k', 256), ('ml_v', 256), ('ml_if', 16), ('ml_o', 256),
    ('rt_q', 256), ('rt_k', 256), ('rt_v', 256), ('rt_g', 256),
    ('gl_q', 128), ('gl_k', 128), ('gl_v', 256),
    ('gl_a_fwd', 16), ('gl_a_bwd', 16), ('gl_g', 256),
)


def _col_ranges():
    off = {}
    o = 0
    for nme, s in _IN_LAYOUT:
        off[nme] = (o, s)
        o += s
    return off


def _w1_layout(w_in):
    off = _col_ranges()
    dep = w_in.shape[0]
    out = np.zeros((dep, D, NC1), np.float32)

    def cols(nme):
        o, s = off[nme]
        return w_in[:, :, o:o + s]
    perm = np.zeros(256, np.int64)
    for h in range(4):
        for dd in range(64):
            r = dd % 32
            partner = dd + 16 if r < 16 else dd - 16
            perm[h * 64 + dd] = h * 64 + partner

    def pad_gla(a):
        p = np.zeros((dep, D, 256), np.float32)
        for h in range(4):
            p[:, :, h * 64:h * 64 + 32] = a[:, :, h * 32:(h + 1) * 32]
        return p
    fmc = [cols('hg_q'), cols('hg_f_fwd'), cols('hg_f_bwd'), cols('ml_q'), cols('ml_k'),
           cols('rt_q'), cols('rt_q')[:, :, perm], cols('rt_k'), cols('rt_k')[:, :, perm],
           pad_gla(cols('gl_q')), pad_gla(cols('gl_k')), cols('gl_a_fwd'), cols('gl_a_bwd')]
    tmc = [cols('hg_i'), cols('ml_v'), cols('rt_v'), cols('gl_v'), cols('hg_g'), cols('ml_o'), cols('rt_g'), cols('gl_g'), cols('ml_if')]
    o = 0
    for a in fmc + tmc:
        out[:, :, o:o + a.shape[2]] = a
        o += a.shape[2]
    assert o == NC1
    return out


def _rope_tables(T):
    LATn = T - CTX
    tl = np.arange(LATn)
    row = (tl // 64).astype(np.float32)
    colp = (tl % 64).astype(np.float32)
    inv = (np.float32(10000.0) ** (-np.arange(16, dtype=np.float32) / np.float32(16))).astype(np.float32)
    cosT = np.ones((128, T), np.float32)
    sinT = np.zeros((128, T), np.float32)
    for p in range(128):
        dd = p % 64
        pos = row if dd < 32 else colp
        ang = (pos * inv[dd % 16]).astype(np.float32)
        sgn = -1.0 if (dd % 32) < 16 else 1.0
        cosT[p, CTX:] = np.cos(ang).astype(np.float32)
        sinT[p, CTX:] = (sgn * np.sin(ang)).astype(np.float32)
    return cosT, sinT


def _consts():
    c = np.zeros((128, 8, 128), np.float32)
    s = np.arange(128)[:, None]
    t = np.arange(128)[None, :]
    same = (s // 64) == (t // 64)
    c[:, 0, :] = (s == t)
    c[:, 1, :] = same & (s <= t)
    c[:, 2, :] = same & (s >= t)
    c[:, 3, :] = (s < 64) & (t >= 0)
    c[:, 4, :] = (s >= 64) & (t >= 0)
    c[:64, 5, :64] = (s[:64] <= t[:, :64])
    c[:64, 6, :64] = (s[:64] >= t[:, :64])
    return c


def make_shared(inp, T):
    dep = inp['w_ada'].shape[0]
    f32 = np.float32
    sh = {}
    sh['w_ada'] = np.ascontiguousarray(inp['w_ada'], f32)
    b_ada = np.asarray(inp['b_ada'], f32)
    bc = np.zeros((dep, 128, 4, 8), f32)
    for which, vec in enumerate((0, 1, 3, 4)):
        bc[:, :, which, :] = b_ada[:, vec * D:(vec + 1) * D].reshape(dep, 8, 128).transpose(0, 2, 1)
    sh['bada_col'] = bc
    bg = np.zeros((dep, 2, 128, D), f32)
    for gi, vec in enumerate((2, 5)):
        bg[:, gi, :, :] = b_ada[:, None, vec * D:(vec + 1) * D]
    sh['bada_g'] = bg
    sh['w1'] = _w1_layout(np.asarray(inp['w_in'], f32))
    sh['ghr'] = np.ascontiguousarray(np.broadcast_to(np.asarray(inp['g_heads'], f32)[:, None, :], (dep, 128, D)))
    hl = np.asarray(inp['hgrn_lb_logits'], f32)
    sh['hgl'] = np.ascontiguousarray(hl.reshape(dep, 2, 2, 128).transpose(3, 0, 1, 2))
    mb = np.asarray(inp['ml_gate_bias'], f32).reshape(dep, 16)
    sh['mlb'] = np.ascontiguousarray(np.broadcast_to(mb[:, None, :], (dep, 128, 16)))
    rl = np.asarray(inp['rt_decay_logit'], f32)
    rt = np.zeros((128, dep, 2, 2), f32)
    for j in range(2):
        for hh in range(2):
            rt[hh * 64:(hh + 1) * 64, :, :, j] = rl[None, :, :, 2 * j + hh]
    sh['rtl'] = rt
    wa = np.asarray(inp['gla_w_a'], f32)
    wap = np.zeros((dep, 2, 16, 256), f32)
    ba = np.asarray(inp['gla_b_a'], f32)
    bap = np.zeros((dep, 2, 256), f32)
    for h in range(4):
        wap[:, :, :, h * 64:h * 64 + 32] = wa[:, :, :, h * 32:(h + 1) * 32]
        bap[:, :, h * 64:h * 64 + 32] = ba[:, :, h * 32:(h + 1) * 32]
    sh['wa'] = wap
    sh['ba'] = np.ascontiguousarray(bap.reshape(dep, 2, 2, 128).transpose(3, 0, 1, 2))
    sh['w_out'] = np.ascontiguousarray(inp['w_out'], f32)
    sh['w_ff1'] = np.ascontiguousarray(inp['w_ff1'], f32)
    sh['w_ff2'] = np.ascontiguousarray(inp['w_ff2'], f32)
    sh['gfin'] = np.ascontiguousarray(np.broadcast_to(np.asarray(inp['g_final'], f32)[None, :], (128, D)))
    c, s = _rope_tables(T)
    sh['ropec'] = c
    sh['ropes'] = s
    sh['consts'] = _consts()
    return sh


def make_core(inp, b):
    f32 = np.float32
    m = {}
    m['xin'] = np.ascontiguousarray(np.concatenate([np.asarray(inp['ctx'][b], f32), np.asarray(inp['x'][b], f32)], axis=0))
    cc = np.zeros((128, 8, 2), f32)
    cc[:, :, 0] = np.asarray(inp['c_ctx'], f32).reshape(8, 128).T
    cc[:, :, 1] = np.asarray(inp['c'][b], f32).reshape(8, 128).T
    m['cc'] = cc
    return m


_CACHE = {}


def kernel(**inputs):
    x = inputs['x']
    B, LAT, _ = x.shape
    T = CTX + LAT
    if LAT not in _CACHE:
        _CACHE[LAT] = build(LAT)
    nc = _CACHE[LAT]
    sh = make_shared(inputs, T)
    in_maps = []
    for b in range(B):
        m = dict(sh)
        m.update(make_core(inputs, b))
        in_maps.append(m)
    res = run_bass_kernel_spmd(nc, in_maps, core_ids=list(range(B)))
    return np.stack([np.asarray(r["out"], np.float32) for r in res.results], axis=0)
```

```python
import numpy as np
from contextlib import ExitStack
import concourse.bass as bass
import concourse.mybir as mybir
from concourse.bass_utils import run_bass_kernel_spmd

F32 = mybir.dt.float32
BF16 = mybir.dt.bfloat16
AF = mybir.ActivationFunctionType
ALU = mybir.AluOpType
AX = mybir.AxisListType

D = 1024
CTX = 256
DEPTH = 2
DFF = 4096
VD = 66
NFM = 22 * 128 + 32
NTM = 2064
NC1 = NFM + NTM
EPS = 1e-6
ENGS = ("pe", "act", "dve", "pool", "sp")
HG, ML, RT, GL = 0, 1, 2, 3
HG_SHIFT = 20.0


class Buf:
    __slots__ = ("name", "last_write", "reads", "dsem", "dcount")

    def __init__(self, name):
        self.name = name
        self.last_write = None
        self.reads = []
        self.dsem = None
        self.dcount = 0


class Tl:
    def __init__(self, t, name, b=None):
        self.t = t
        self.b = b if b is not None else Buf(name)

    def __getitem__(self, i):
        return self.t[i]


def _b(x):
    return x.b if isinstance(x, Tl) else x


class Op:
    __slots__ = ("eng", "fn", "deps", "is_dma", "slot", "ticket", "signal", "retired", "seq")

    def __init__(self, eng, fn, is_dma, slot):
        self.eng = eng
        self.fn = fn
        self.deps = []
        self.is_dma = is_dma
        self.slot = slot
        self.ticket = None
        self.signal = False
        self.retired = False


class Prog:
    def __init__(self, nc, stack):
        self.nc = nc
        self.stack = stack
        self.ops = []
        self.esem = {e: stack.enter_context(nc.semaphore("sem_" + e)) for e in ENGS}
        self.ecount = {e: 0 for e in ENGS}
        self.waited = {e: {} for e in ENGS}
        self.pending_bar = {e: [] for e in ENGS}
        self.sempool = []
        self.nsem = 0
        self.engobj = {"pe": nc.tensor, "act": nc.scalar, "dve": nc.vector, "pool": nc.gpsimd, "sp": nc.sync}
        self.ninst = 0
        self.deferred = []
        self.seq = 0

    def defer_dma(self, fn, slot, reads=(), writes=(), eng="sp"):
        self.deferred.append((fn, slot, reads, writes, eng))

    def release(self):
        dd = self.deferred
        self.deferred = []
        for fn, slot, reads, writes, eng in dd:
            self.dma(fn, slot, reads, writes, eng)

    def add(self, eng, fn, reads=(), writes=(), is_dma=False, slot=None):
        op = Op(eng, fn, is_dma, slot)
        deps = set()
        for x in reads:
            b = _b(x)
            if b.last_write is not None:
                deps.add(b.last_write)
        for x in writes:
            b = _b(x)
            if b.last_write is not None:
                deps.add(b.last_write)
            for r in b.reads:
                deps.add(r)
        dl = [d for d in deps if (not d.retired) and not (eng == "pe" and d.eng == "pe" and not d.is_dma)]
        best = {}
        keep = []
        for d in dl:
            if d.is_dma:
                keep.append(d)
            elif d.eng not in best or best[d.eng].seq < d.seq:
                best[d.eng] = d
        op.deps = keep + list(best.values())
        self.seq += 1
        op.seq = self.seq
        for x in reads:
            _b(x).reads.append(op)
        for x in writes:
            b = _b(x)
            b.last_write = op
            b.reads = []
        self.ops.append(op)
        return op

    def pe(self, fn, reads=(), writes=()):
        return self.add("pe", fn, reads, writes)

    def act(self, fn, reads=(), writes=()):
        return self.add("act", fn, reads, writes)

    def dve(self, fn, reads=(), writes=()):
        return self.add("dve", fn, reads, writes)

    def pool(self, fn, reads=(), writes=()):
        return self.add("pool", fn, reads, writes)

    def dma(self, fn, slot, reads=(), writes=(), eng="sp"):
        return self.add(eng, fn, reads, writes, is_dma=True, slot=_b(slot))

    def flush(self, final=False):
        nc = self.nc
        self.release()
        ops = self.ops
        self.ops = []
        for op in ops:
            for d in op.deps:
                d.signal = True
        last = {}
        for op in ops:
            if not op.is_dma:
                last[op.eng] = op
        for op in last.values():
            op.signal = True
        slots = []
        swbar = []
        for op in ops:
            if op.is_dma and op.eng == "pool":
                sem = self.stack.enter_context(nc.semaphore("sw%d" % self.nsem))
                self.nsem += 1
                op.ticket = (sem, 16)
                swbar.append(op.ticket)
            elif op.is_dma:
                s = op.slot
                if s.dsem is None:
                    if self.sempool:
                        s.dsem, s.dcount = self.sempool.pop()
                    else:
                        s.dsem = self.stack.enter_context(nc.semaphore("ds%d" % self.nsem))
                        self.nsem += 1
                        s.dcount = 0
                    slots.append(s)
                s.dcount += 16
                op.ticket = (s.dsem, s.dcount)
            elif op.signal:
                self.ecount[op.eng] += 1
                op.ticket = (self.esem[op.eng], self.ecount[op.eng])
        per = {e: [] for e in ENGS}
        for op in ops:
            per[op.eng].append(op)
        bar = [last[e].ticket for e in last] + [(s.dsem, s.dcount) for s in slots] + swbar

        def run(ename, eng):
            waited = self.waited[ename]

            def w(sem, val):
                k = id(sem)
                if waited.get(k, 0) < val:
                    eng.wait_ge(sem, val)
                    waited[k] = val

            if per[ename] or final:
                for sem, val in self.pending_bar[ename]:
                    w(sem, val)
                self.pending_bar[ename] = []
            for op in per[ename]:
                need = {}
                for d in op.deps:
                    sem, val = d.ticket
                    k = id(sem)
                    if k not in need or need[k][1] < val:
                        need[k] = (sem, val)
                for sem, val in need.values():
                    w(sem, val)
                ins = op.fn(eng)
                self.ninst += 1
                if op.is_dma:
                    ins.then_inc(op.ticket[0], 16)
                elif op.signal:
                    ins.then_inc(op.ticket[0], 1)
            if final and ename == "sp":
                for sem, val in bar:
                    w(sem, val)

        with nc.Block() as block:
            @block.tensor
            def _(eng):
                run("pe", eng)

            @block.scalar
            def _(eng):
                run("act", eng)

            @block.vector
            def _(eng):
                run("dve", eng)

            @block.gpsimd
            def _(eng):
                run("pool", eng)

            @block.sync
            def _(eng):
                run("sp", eng)

        for e in ENGS:
            self.pending_bar[e].extend(bar)
        for op in ops:
            op.retired = True
        for s in slots:
            self.sempool.append((s.dsem, s.dcount))
            s.dsem = None


class Rot:
    def __init__(self, items):
        self.items = items
        self.i = 0

    def nxt(self):
        r = self.items[self.i % len(self.items)]
        self.i += 1
        return r


class Em:
    def __init__(self, P):
        self.P = P

    def act(self, out, in_, func, R, W, **kw):
        self.P.act(lambda e: e.activation(out=out, in_=in_, func=func, **kw), R, W)

    def tt(self, eng, out, in0, in1, op, R, W):
        self.P.add(eng, lambda e: e.tensor_tensor(out=out, in0=in0, in1=in1, op=op), R, W)

    def ts(self, eng, out, in0, s1, s2, op0, op1, R, W):
        if op1 is None:
            self.P.add(eng, lambda e: e.tensor_scalar(out=out, in0=in0, scalar1=s1, scalar2=None, op0=op0), R, W)
        else:
            self.P.add(eng, lambda e: e.tensor_scalar(out=out, in0=in0, scalar1=s1, scalar2=s2, op0=op0, op1=op1), R, W)

    def stt(self, out, in0, scalar, in1, op0, op1, R, W):
        self.P.dve(lambda e: e.scalar_tensor_tensor(out=out, in0=in0, scalar=scalar, in1=in1, op0=op0, op1=op1), R, W)

    def copy(self, eng, out, in_, R, W):
        if eng == "act":
            self.P.act(lambda e: e.copy(out=out, in_=in_), R, W)
        else:
            self.P.add(eng, lambda e: e.tensor_copy(out=out, in_=in_), R, W)

    def memset(self, ap, val, W, R=()):
        self.P.pool(lambda e: e.memset(ap, val), R, W)

    def load(self, out, in_, slot, W=None):
        self.P.dma(lambda e: e.dma_start(out=out, in_=in_), slot, writes=[slot] if W is None else W)

    def loadc(self, out, in_, slot):
        self.P.dma(lambda e: e.dma_start(out=out, in_=in_), slot, writes=[slot], eng="pool")

    def store(self, out, in_, slot):
        self.P.defer_dma(lambda e: e.dma_start(out=out, in_=in_), slot, reads=[slot])

    def mm(self, items, R, W):
        def f(e):
            r = None
            for (o, l, rh, st, sp) in items:
                r = e.matmul(o, lhsT=l, rhs=rh, start=st, stop=sp)
            return r
        self.P.pe(f, R, W)

    def tr(self, items, ident, R, W):
        def f(e):
            r = None
            for (o, i) in items:
                r = e.transpose(o, i, ident)
            return r
        self.P.pe(f, R, W)

    def scan(self, out, d0, d1, R, W):
        self.P.dve(lambda e: e.tensor_tensor_scan(out=out, data0=d0, data1=d1, initial=0.0, op0=ALU.mult, op1=ALU.add), R, W)

    def recip(self, out, in_, R, W):
        self.P.dve(lambda e: e.reciprocal(out=out, in_=in_), R, W)

    def reduce(self, out, in_, R, W):
        self.P.dve(lambda e: e.tensor_reduce(out=out, in_=in_, axis=AX.X, op=ALU.add), R, W)

    def sss(self, out, in_, scalar, op, R, W):
        self.P.dve(lambda e: e.tensor_single_scalar(out=out, in_=in_, scalar=scalar, op=op), R, W)


class _Stop(Exception):
    pass


def build(LAT, depth=DEPTH, dbg=False, stop=None):
    T = CTX + LAT
    NT = T // 128
    NCH = T // 64
    NG = T // 256
    nc = bass.Bass("TRN2", target_bir_lowering=False)

    def din(name, shape, dt=F32):
        return nc.dram_tensor(name, list(shape), dt, kind="ExternalInput").ap()

    def dscr(name, shape, dt):
        return nc.dram_tensor(name, list(shape), dt, kind="ExternalOutput" if dbg else "Internal").ap()

    XIN = din("xin", [T, D])
    CC_IN = din("cc", [128, 8, 2])
    W_ADA = din("w_ada", [DEPTH, D, 6 * D])
    BADA_COL = din("bada_col", [DEPTH, 128, 4, 8])
    BADA_G = din("bada_g", [DEPTH, 2, 128, D])
    W1 = din("w1", [DEPTH, D, NC1])
    GHR = din("ghr", [DEPTH, 128, D])
    HGL = din("hgl", [128, DEPTH, 2, 2])
    MLB = din("mlb", [DEPTH, 128, 16])
    RTL = din("rtl", [128, DEPTH, 2, 2])
    WA = din("wa", [DEPTH, 2, 16, 256])
    BA = din("ba", [128, DEPTH, 2, 2])
    W_OUT = din("w_out", [DEPTH, D, D])
    W_FF1 = din("w_ff1", [DEPTH, D, DFF])
    W_FF2 = din("w_ff2", [DEPTH, DFF, D])
    GFIN = din("gfin", [128, D])
    ROPEC = din("ropec", [128, T])
    ROPES = din("ropes", [128, T])
    CONSTS = din("consts", [128, 8, 128])
    OUT = nc.dram_tensor("out", [LAT, D], F32, kind="ExternalOutput").ap()

    XS = dscr("xs", [T, D], F32)
    QT = dscr("qt", [2, 8, 128, T], BF16)
    KT = dscr("kt", [2, 8, 128, T], BF16)
    KK = dscr("kk", [2, T, 1024], BF16)
    VV = dscr("vv", [2, T, 16, VD], BF16)
    GG = dscr("gg", [T, D], BF16)
    OO = dscr("oo", [2, T, D], F32)

    with ExitStack() as gs:
        P = Prog(nc, gs)
        A = Em(P)
        cnt = [0]

        def sbt(st, shape, dt=F32, name=None):
            cnt[0] += 1
            nm = "%s_%d" % (name or "t", cnt[0])
            return Tl(st.enter_context(nc.sbuf_tensor(nm, list(shape), dt)), nm)

        def pst(st, shape, dt=F32, name=None):
            cnt[0] += 1
            nm = "%s_%d" % (name or "p", cnt[0])
            return Tl(st.enter_context(nc.psum_tensor(nm, list(shape), dt)), nm)

        def rot(st, n, shape, dt=F32, name=None, psum=False):
            return Rot([(pst if psum else sbt)(st, shape, dt, name) for _ in range(n)])

        def sub(tl, ap, name):
            return Tl(ap, name)

        cst = sbt(gs, [128, 8, 128], F32, "cst")
        A.load(cst[:], CONSTS[:, :, :], cst)
        identb = sbt(gs, [128, 128], BF16, "identb")
        A.copy("dve", identb[:], cst[:, 0, :], [cst], [identb])
        maskf = sbt(gs, [64, 2, 64], F32, "maskf")
        A.copy("dve", maskf[:], cst[0:64, 5:7, 0:64], [cst], [maskf])
        masku = maskf[:].bitcast(mybir.dt.uint32)
        rmask = sbt(gs, [128, 512], F32, "rmask")
        A.memset(rmask[:], 1.0, [rmask])
        A.memset(rmask[:].rearrange("p (c t) -> p c t", t=64)[:, :, 0:1], 0.0, [rmask], [rmask])
        onescol = sbt(gs, [128, 1], F32, "onescol")
        A.memset(onescol[:], 1.0, [onescol])
        zcol = sbt(gs, [128, 1], F32, "zcol")
        A.memset(zcol[:], 0.0, [zcol])
        shcol = sbt(gs, [128, 2], F32, "shcol")
        A.memset(shcol[:, 0:1], -HG_SHIFT, [shcol])
        A.memset(shcol[:, 1:2], HG_SHIFT, [shcol], [shcol])
        epscol = sbt(gs, [128, 1], F32, "epscol")
        A.memset(epscol[:], EPS, [epscol])
        ccs = sbt(gs, [128, 8, 2], F32, "ccs")
        A.load(ccs[:], CC_IN[:, :, :], ccs)
        scs = sbt(gs, [128, 8, 2], F32, "scs")
        A.act(scs[:], ccs[:], AF.Silu, [ccs], [scs])
        P.flush()

        def norm_T(pools, src, r0, hT, col0, sc, sh):
            xp, jp, sp_, xnp, ptp = pools
            xt = xp.nxt()
            A.load(xt[:], src[r0:r0 + 128, :], xt)
            jk = jp.nxt()
            ss = sp_.nxt()
            A.act(jk[:], xt[:], AF.Square, [xt], [jk, ss], accum_out=ss[:, 0:1])
            A.act(ss[:, 1:2], ss[:, 0:1], AF.Ln, [ss, epscol], [ss], scale=1.0 / D, bias=epscol[:, 0:1])
            A.act(ss[:, 2:3], ss[:, 1:2], AF.Exp, [ss], [ss], scale=-0.5)
            xn = xnp.nxt()
            A.act(xn[:], xt[:], AF.Identity, [xt, ss], [xn], scale=ss[:, 2:3])
            pt = ptp.nxt()
            A.tr([(pt[:, kt, :], xn[:, kt * 128:(kt + 1) * 128]) for kt in range(8)], identb[:], [xn, identb], [pt])
            if not hasattr(hT, "kb"):
                hT.kb = [Buf("hTk") for _ in range(8)]
            ncount[0] += 1
            for kt in range(8):
                if ncount[0] % 2 == 0:
                    A.act(hT[:, kt, col0:col0 + 128], pt[:, kt, :], AF.Identity, [pt, modcol_ref[0]], [hT.kb[kt]], scale=sc[kt], bias=sh[kt])
                else:
                    A.ts("dve", hT[:, kt, col0:col0 + 128], pt[:, kt, :], sc[kt], sh[kt], ALU.mult, ALU.add, [pt, modcol_ref[0]], [hT.kb[kt]])
            return xt

        modcol_ref = [None]
        ncount = [0]

        def load_w_cast(dst, src2, nk):
            for k0 in range(0, nk, 8):
                A.loadc(dst[:, k0:k0 + 8, :], src2[k0 * 128:(k0 + 8) * 128, :].rearrange("(kt p) n -> p kt n", p=128), dst)

        def decay_core(lf, n, d, escale, bT, Dt, E1, E2, cc, cctl, tmpc, sh=0.0):
            if d == 0:
                A.scan(bT[:, 0:n], rmask[:, 0:n], lf[:, 0:n], [lf, rmask], [bT])
            else:
                A.scan(bT[:, 0:n][:, ::-1], rmask[:, 0:n], lf[:, 0:n][:, ::-1], [lf, rmask], [bT])
            nb = n // 64
            mid = 31 if d == 0 else 32
            last = 63 if d == 0 else 0
            b3 = bT[:, 0:n].rearrange("p (c t) -> p c t", t=64)
            D3 = Dt[:, 0:n].rearrange("p (c t) -> p c t", t=64)
            A.tt("pool", D3, b3, b3[:, :, mid:mid + 1].to_broadcast([128, nb, 64]), ALU.subtract, [bT], [Dt])
            bneg = shcol[:, 0:1] if sh else zcol[:, 0:1]
            bpos = shcol[:, 1:2] if sh else zcol[:, 0:1]
            A.act(E1[:, 0:n], Dt[:, 0:n], AF.Exp, [Dt], [E1], scale=escale, bias=bneg)
            A.act(E2[:, 0:n], Dt[:, 0:n], AF.Exp, [Dt], [E2], scale=-escale, bias=bneg)
            ref2 = bT[:, mid:n:64]
            bl2 = bT[:, last:n:64]
            A.act(cc[0], ref2, AF.Exp, [bT], [cctl], scale=escale, bias=bneg)
            A.act(cc[1], bl2, AF.Exp, [bT], [cctl], scale=escale)
            A.tt("pool", tmpc[:, 0:nb], bl2, ref2, ALU.subtract, [bT], [tmpc])
            A.act(cc[2], tmpc[:, 0:nb], AF.Exp, [tmpc], [cctl], scale=escale, bias=bpos)

        blocks = [(0, CTX, 0)] + [(CTX + 512 * i, 512, 1) for i in range(LAT // 512)]

        try:
          for l in range(depth):
            xsrc = XIN if l == 0 else XS
            with ExitStack() as ls:
                modcol = sbt(ls, [128, 4, 8, 2], F32, "modcol")
                modcol_ref[0] = modcol

                def mcol(which, cl):
                    return [modcol[:, which, kt, cl:cl + 1] for kt in range(8)]

                with ExitStack() as ls2:
                    CCt = sbt(ls2, [128, 4, 2, 2, 3, NCH], F32, "CCt")
                    ALt = sbt(ls2, [128, NCH, 8], F32, "ALt")
                    FLt = sbt(ls2, [64, NCH, 8], F32, "FLt")
                    ALP = sbt(ls2, [128, NCH, 2, 2], F32, "ALP")
                    Ec = [[[sbt(ls2, [128, 512], F32, "Ec") for _ in range(2)] for _ in range(2)] for _ in range(2)]
                    lbt = sbt(ls2, [128, 2, 2, 2], F32, "lbt")
                    lgam = sbt(ls2, [128, 2, 2], F32, "lgam")
                    bat = sbt(ls2, [128, 2, 2], F32, "bat")
                    wat = sbt(ls2, [16, 2, 256], F32, "wat")
                    mlbt = sbt(ls2, [128, 16], F32, "mlbt")
                    ghr = sbt(ls2, [128, D], F32, "ghr")

                    with ExitStack() as ph:
                        wad = rot(ph, 2, [128, 8, 512], F32, "wad")
                        pm = pst(ph, [128, 64], F32, "pm")
                        bcol = sbt(ph, [128, 4, 8], F32, "bcol")
                        A.load(bcol[:], BADA_COL[l, :, :, :], bcol)
                        for which, vec in enumerate((0, 1, 3, 4)):
                            for half in range(2):
                                blk = vec * 2 + half
                                w = wad.nxt()
                                A.load(w[:], W_ADA[l, :, blk * 512:(blk + 1) * 512].rearrange("(kt p) n -> p kt n", p=128), w)
                                items = []
                                for f4 in range(4):
                                    c0 = (which * 8 + half * 4 + f4) * 2
                                    for kt in range(8):
                                        items.append((pm[:, c0:c0 + 2], w[:, kt, f4 * 128:(f4 + 1) * 128], scs[:, kt, :], kt == 0, kt == 7))
                                A.mm(items, [w, scs], [pm])
                        A.tt("dve", modcol[:].rearrange("p a b c -> p (a b) c"), pm[:].rearrange("p (a c) -> p a c", c=2),
                             bcol[:].rearrange("p a b -> p (a b)").unsqueeze(2).to_broadcast([128, 32, 2]), ALU.add, [pm, bcol], [modcol])
                        for which in (1, 3):
                            A.ts("dve", modcol[:, which, :, :], modcol[:, which, :, :], 1.0, None, ALU.add, None, [modcol], [modcol])
                        hgl = sbt(ph, [128, DEPTH, 2, 2], F32, "hgl")
                        A.load(hgl[:], HGL[:, :, :, :], hgl)
                        if l == 0:
                            A.memset(lbt[:, 0, :, :], 0.0, [lbt])
                        else:
                            dl = sbt(ph, [128, 2, 2], F32, "dl")
                            A.tt("dve", dl[:], hgl[:, 1, :, :], hgl[:, 0, :, :], ALU.subtract, [hgl], [dl])
                            A.act(lbt[:, 0, :, :], dl[:], AF.Sigmoid, [dl], [lbt])
                        A.ts("dve", lbt[:, 1, :, :], lbt[:, 0, :, :], -1.0, 1.0, ALU.mult, ALU.add, [lbt], [lbt])
                        rtl = sbt(ph, [128, DEPTH, 2, 2], F32, "rtl")
                        A.load(rtl[:], RTL[:, :, :, :], rtl)
                        sgr = sbt(ph, [128, 2, 2], F32, "sgr")
                        A.act(sgr[:], rtl[:, l, :, :], AF.Sigmoid, [rtl], [sgr])
                        A.act(lgam[:], sgr[:], AF.Ln, [sgr], [lgam])
                        bap = sbt(ph, [128, DEPTH, 2, 2], F32, "bap")
                        A.load(bap[:], BA[:, :, :, :], bap)
                        A.copy("dve", bat[:], bap[:, l, :, :], [bap], [bat])
                        A.load(wat[:], WA[l, :, :, :].rearrange("d r c -> r d c"), wat)
                        A.load(mlbt[:], MLB[l, :, :], mlbt)
                        A.load(ghr[:], GHR[l, :, :], ghr)
                        lfc = sbt(ph, [128, 512], F32, "lfc")
                        bTc = sbt(ph, [128, 512], F32, "bTc")
                        Dtc = sbt(ph, [128, 512], F32, "Dtc")
                        ctmp = sbt(ph, [128, 3, 8], F32, "ctmp")
                        tmpc = sbt(ph, [128, 8], F32, "tmpc")
                        for d in range(2):
                            for j in range(2):
                                A.copy("dve", lfc[:], lgam[:, d, j:j + 1].to_broadcast([128, 512]), [lgam], [lfc])
                                decay_core(lfc, 512, d, 1.0, bTc, Dtc, Ec[d][j][0], Ec[d][j][1],
                                           [ctmp[:, k, :] for k in range(3)], ctmp, tmpc)
                                for kind in range(3):
                                    A.copy("dve", CCt[:, RT, d, j, kind, :], ctmp[:, kind, 0:1].to_broadcast([128, NCH]), [ctmp], [CCt])
                        P.flush()
                        if stop == 'PA':
                            raise _Stop()

                    with ExitStack() as ph:
                        w1 = sbt(ph, [128, 8, NFM], BF16, "w1fm")
                        load_w_cast(w1, W1[l, :, 0:NFM], 8)
                        hTp = rot(ph, 2, [128, 8, 512], BF16, "hT")
                        npools = (rot(ph, 2, [128, D], F32, "xt"), rot(ph, 1, [128, D], BF16, "jk"), rot(ph, 2, [128, 4], F32, "ss"),
                                  rot(ph, 2, [128, D], BF16, "xn"), rot(ph, 2, [128, 8, 128], BF16, "ptr", psum=True))
                        pf = rot(ph, 4, [128, 512], F32, "pf", psum=True)
                        ptk = rot(ph, 2, [128, 4, 128], BF16, "ptk", psum=True)
                        wk = rot(ph, 10, [128, 512], F32, "wk")
                        fq = rot(ph, 4, [128, 512], F32, "fq")
                        bfp = rot(ph, 6, [128, 512], BF16, "bfp")
                        kst = [sbt(ph, [128, 4, 1024], BF16, "kst") for _ in range(2)]
                        rope = [sbt(ph, [128, 512], F32, "rope") for _ in range(2)]
                        ga = [sbt(ph, [16, 512], F32, "ga") for _ in range(2)]
                        tmpcp = rot(ph, 2, [128, 8], F32, "tmpc")

                        def do_block(t0, n, cl):
                            ntile = n // 128
                            ch0 = t0 // 64
                            nb = n // 64
                            hT = hTp.nxt()
                            for i in range(ntile):
                                norm_T(npools, xsrc, t0 + 128 * i, hT, 128 * i, mcol(1, cl), mcol(0, cl))
                            A.load(rope[0][:, 0:n], ROPEC[:, t0:t0 + n], rope[0])
                            A.load(rope[1][:, 0:n], ROPES[:, t0:t0 + n], rope[1])
                            P.release()

                            def fm(col0, M=128):
                                ps = pf.nxt()
                                A.mm([(ps[0:M, 0:n], w1[:, kt, col0:col0 + M], hT[:, kt, 0:n], kt == 0, kt == 7) for kt in range(8)],
                                     [w1] + hT.kb, [ps])
                                return ps

                            def finish(m, d, j, q, k, lf, escale, kscale, E=None, dirs=None, sh=0.0):
                                P.release()
                                if E is None:
                                    bT, Dt, E1, E2 = wk.nxt(), wk.nxt(), wk.nxt(), wk.nxt()
                                    decay_core(lf, n, d, escale, bT, Dt, E1, E2,
                                               [CCt[:, m, d, j, kind, ch0:ch0 + nb] for kind in range(3)], CCt, tmpcp.nxt(), sh=sh)
                                elif E == "none":
                                    E1 = E2 = None
                                else:
                                    E1, E2 = E
                                qh = bfp.nxt()
                                kh = bfp.nxt()
                                if E1 is None:
                                    A.copy("act", qh[:, 0:n], q[:, 0:n], [q], [qh])
                                    P.act(lambda e: e.mul(out=kh[:, 0:n], in_=k[:, 0:n], mul=kscale), [k], [kh])
                                else:
                                    A.tt("dve", qh[:, 0:n], q[:, 0:n], E1[:, 0:n], ALU.mult, [q, E1], [qh])
                                    A.stt(kh[:, 0:n], k[:, 0:n], kscale, E2[:, 0:n], ALU.mult, ALU.mult, [k, E2], [kh])
                                for dd in (dirs or [d]):
                                    A.store(QT[dd, m * 2 + j, :, t0:t0 + n], qh[:, 0:n], qh)
                                    A.store(KT[dd, m * 2 + j, :, t0:t0 + n], kh[:, 0:n], kh)
                                pk = ptk.nxt()
                                A.tr([(pk[:, i, :], kh[:, 128 * i:128 * (i + 1)]) for i in range(ntile)], identb[:], [kh, identb], [pk])
                                for dd in (dirs or [d]):
                                    A.copy("act", kst[dd][:, 0:ntile, m * 256 + j * 128:m * 256 + (j + 1) * 128], pk[:, 0:ntile, :],
                                           [pk], [kst[dd]])

                            for j in range(2):
                                psq = fm((0 + j) * 128)
                                qs = fq.nxt()
                                sgq = wk.nxt()
                                A.act(sgq[:, 0:n], psq[:, 0:n], AF.Sigmoid, [psq], [sgq, psq])
                                A.tt("dve", qs[:, 0:n], psq[:, 0:n], sgq[:, 0:n], ALU.mult, [psq, sgq], [qs])
                                for d in range(2):
                                    psz = fm((2 + 2 * d + j) * 128)
                                    sg, f, lf, kk = wk.nxt(), wk.nxt(), wk.nxt(), wk.nxt()
                                    A.act(sg[:, 0:n], psz[:, 0:n], AF.Sigmoid, [psz], [sg])
                                    A.ts("dve", f[:, 0:n], sg[:, 0:n], lbt[:, 1, d, j:j + 1], lbt[:, 0, d, j:j + 1], ALU.mult, ALU.add,
                                         [sg, lbt], [f])
                                    A.act(lf[:, 0:n], f[:, 0:n], AF.Ln, [f], [lf])
                                    A.act(kk[:, 0:n], f[:, 0:n], AF.Identity, [f, onescol], [kk], scale=-1.0, bias=onescol[:, 0:1])
                                    finish(HG, d, j, qs, kk, lf, 1.0, 1.0, sh=HG_SHIFT)
                            for j in range(2):
                                psq = fm((6 + j) * 128)
                                psk = fm((8 + j) * 128)
                                finish(ML, 0, j, psq, psk, None, 1.0, 0.125, E="none", dirs=[0, 1])
                            for j in range(2):
                                rr = []
                                for base in (10, 14):
                                    ps0 = fm((base + j) * 128)
                                    ps1 = fm((base + 2 + j) * 128)
                                    t1, t2 = wk.nxt(), wk.nxt()
                                    r = fq.nxt()
                                    A.tt("dve", t1[:, 0:n], ps0[:, 0:n], rope[0][:, 0:n], ALU.mult, [ps0, rope[0]], [t1])
                                    A.tt("dve", t2[:, 0:n], ps1[:, 0:n], rope[1][:, 0:n], ALU.mult, [ps1, rope[1]], [t2])
                                    A.tt("pool", r[:, 0:n], t1[:, 0:n], t2[:, 0:n], ALU.add, [t1, t2], [r])
                                    rr.append(r)
                                for d in range(2):
                                    finish(RT, d, j, rr[0], rr[1], None, 1.0, 0.125, E=(Ec[d][j][0], Ec[d][j][1]))
                            for d in range(2):
                                psa = fm(22 * 128 + 16 * d, M=16)
                                A.copy("act", ga[d][:, 0:n], psa[0:16, 0:n], [psa], [ga[d]])
                            for j in range(2):
                                psq = fm((18 + j) * 128)
                                psk = fm((20 + j) * 128)
                                qr, kr = fq.nxt(), fq.nxt()
                                A.copy("act", qr[:, 0:n], psq[:, 0:n], [psq], [qr])
                                A.copy("dve", kr[:, 0:n], psk[:, 0:n], [psk], [kr])
                                for d in range(2):
                                    psz = pf.nxt()
                                    A.mm([(psz[:, 0:n], wat[0:16, d, j * 128:(j + 1) * 128], ga[d][0:16, 0:n], True, True)], [wat, ga[d]], [psz])
                                    sg, lf = wk.nxt(), wk.nxt()
                                    A.act(sg[:, 0:n], psz[:, 0:n], AF.Sigmoid, [psz, bat], [sg], bias=bat[:, d, j:j + 1])
                                    A.act(lf[:, 0:n], sg[:, 0:n], AF.Ln, [sg], [lf])
                                    finish(GL, d, j, qr, kr, lf, 1.0 / 16.0, 32.0 ** -0.5)
                            for dd in range(2):
                                A.store(KK[dd, t0:t0 + n, :].rearrange("(i p) x -> p i x", p=128), kst[dd][:, 0:ntile, :], kst[dd])

                        for (t0, n, cl) in blocks:
                            do_block(t0, n, cl)
                        P.flush()
                        if stop == 'P1a':
                            raise _Stop()

                    with ExitStack() as ph:
                        w1 = sbt(ph, [128, 8, NTM], BF16, "w1tm")
                        load_w_cast(w1, W1[l, :, NFM:NC1], 8)
                        hTp = rot(ph, 4, [128, 8, 128], BF16, "hT")
                        npools = (rot(ph, 4, [128, D], F32, "xt"), rot(ph, 2, [128, D], BF16, "jk"), rot(ph, 4, [128, 4], F32, "ss"),
                                  rot(ph, 4, [128, D], BF16, "xn"), rot(ph, 2, [128, 8, 128], BF16, "ptr", psum=True))
                        pt = rot(ph, 4, [128, 512], F32, "pt", psum=True)
                        psm = rot(ph, 2, [128, 64], F32, "psm", psum=True)
                        vstp = rot(ph, 4, [128, 16, VD], BF16, "vst")
                        vmlp = rot(ph, 4, [128, 4, VD], BF16, "vml")
                        vs1p = rot(ph, 4, [128, 4, VD], BF16, "vs1")
                        for tl_ in vstp.items + vmlp.items:
                            A.memset(tl_[:], 0.0, [tl_])
                            A.memset(tl_[:, :, 64:65], 1.0, [tl_], [tl_])
                        sgp = rot(ph, 6, [128, 512], F32, "sgp")
                        ggp = rot(ph, 4, [128, D], BF16, "ggp")
                        smp = rot(ph, 4, [128, 64], F32, "smp")

                        def do_tile(ti):
                            r0 = ti * 128
                            cl = 0 if r0 < CTX else 1
                            hT = hTp.nxt()
                            norm_T(npools, xsrc, r0, hT, 0, mcol(1, cl), mcol(0, cl))
                            P.release()

                            def tm(col0, N):
                                ps = pt.nxt()
                                import os
                                if os.environ.get("EXPA"):
                                    A.mm([(ps[:, 0:128], w1[:, kt, col0:col0 + 128], hT[:, kt, :], kt == 0, kt == 7) for kt in range(8)], [w1] + hT.kb, [ps])
                                elif os.environ.get("EXPB"):
                                    A.mm([(ps[:, 0:256], hT[:, kt, :], w1[:, kt, col0:col0 + 256], kt == 0, kt == 7) for kt in range(8)], [w1] + hT.kb, [ps])
                                else:
                                    items = []
                                    for n0 in range(0, N, 256):
                                        n1 = min(N, n0 + 256)
                                        items += [(ps[:, n0:n1], hT[:, kt, :], w1[:, kt, col0 + n0:col0 + n1], kt == 0, kt == 7) for kt in range(8)]
                                    A.mm(items, [w1] + hT.kb, [ps])
                                return ps
                            vst, vml, vs1 = vstp.nxt(), vmlp.nxt(), vs1p.nxt()
                            import os
                            SKIP = os.environ.get("SKIP", "")
                            if "v" in SKIP:
                                return
                            psA = tm(0, 512)
                            if "1" in SKIP:
                                return
                            if "a" not in SKIP:
                                A.copy("act", vst[:, 0:4, 0:64], psA[:, 0:256].rearrange("p (h v) -> p h v", v=64), [psA], [vst])
                            if "b" not in SKIP:
                                A.copy("act", vml[:, :, 0:64], psA[:, 256:512].rearrange("p (h v) -> p h v", v=64), [psA], [vml])
                            if "2" in SKIP:
                                return
                            psB = tm(512, 512)
                            A.copy("dve", vst[:, 8:16, 0:64], psB[:, 0:512].rearrange("p (h v) -> p h v", v=64), [psB], [vst])
                            if "g" in SKIP:
                                return
                            gg = ggp.nxt()
                            psC = tm(1024, 512)
                            s1 = sgp.nxt()
                            A.act(s1[:], psC[:], AF.Sigmoid, [psC], [s1])
                            A.tt("dve", gg[:, 0:512], s1[:], ghr[:, 0:512], ALU.mult, [s1, ghr], [gg])
                            psD = tm(1536, 512)
                            s2 = sgp.nxt()
                            A.act(s2[:], psD[:], AF.Sigmoid, [psD], [s2, psD])
                            A.tt("dve", s2[:], psD[:], s2[:], ALU.mult, [psD, s2], [s2])
                            A.tt("pool", gg[:, 512:1024], s2[:], ghr[:, 512:1024], ALU.mult, [s2, ghr], [gg])
                            A.store(GG[r0:r0 + 128, :], gg[:], gg)
                            if "m" in SKIP:
                                return
                            psG = tm(2048, 16)
                            sm = smp.nxt()
                            A.tt("dve", sm[:, 0:16], psG[:, 0:16], mlbt[:], ALU.add, [psG, mlbt], [sm])
                            gv = sm[:, 0:16].rearrange("p (d g h) -> p d g h", d=2, g=2)
                            A.act(sm[:, 16:24].rearrange("p (d h) -> p d h", d=2), gv[:, :, 1, :], AF.Sigmoid, [sm], [sm])
                            A.act(sm[:, 24:32], sm[:, 16:24], AF.Ln, [sm], [sm])
                            pb = psm.nxt()
                            A.mm([(pb[:, 0:4], cst[:, 1, :], sm[:, 24:28], True, True),
                                  (pb[:, 4:8], cst[:, 2, :], sm[:, 28:32], True, True),
                                  (pb[:, 8:16], cst[:, 3, :], sm[:, 24:32], True, True),
                                  (pb[:, 16:24], cst[:, 4, :], sm[:, 24:32], True, True)], [sm, cst], [pb])
                            A.act(ALt[:, 2 * ti:2 * ti + 2, :], pb[:, 8:24].rearrange("p (c g) -> p c g", g=8), AF.Exp, [pb], [ALt, pb])
                            for hh_ in range(2):
                                A.copy("pool", ALP[64 * hh_:64 * hh_ + 64, 2 * ti:2 * ti + 2, :, :],
                                       ALt[64 * hh_:64 * hh_ + 64, 2 * ti:2 * ti + 2, :].rearrange("p c (d a b) -> p c d a b", d=2, a=2)[:, :, :, :, hh_],
                                       [ALt], [ALP])
                            A.tt("dve", sm[:, 32:40].rearrange("p (d h) -> p d h", d=2), gv[:, :, 0, :],
                                 pb[:, 0:8].rearrange("p (d h) -> p d h", d=2), ALU.subtract, [pb, sm], [sm, pb])
                            A.act(sm[:, 40:48], sm[:, 32:40], AF.Exp, [sm], [sm])
                            if "f" in SKIP:
                                return
                            A.act(sm[:, 48:56], pb[:, 0:8], AF.Exp, [pb], [sm, pb], scale=-1.0)
                            A.copy("dve", FLt[0:64, 2 * ti, :], sm[0:64, 48:56], [sm], [FLt])
                            A.copy("dve", FLt[0:64, 2 * ti + 1, :], sm[64:128, 48:56], [sm], [FLt])
                            if "w" in SKIP:
                                return
                            A.tt("dve", vst[:, 4:8, :], vml[:], sm[:, 40:44].unsqueeze(2).to_broadcast([128, 4, VD]), ALU.mult, [sm, vml], [vst])
                            A.tt("dve", vs1[:], vml[:], sm[:, 44:48].unsqueeze(2).to_broadcast([128, 4, VD]), ALU.mult, [sm, vml], [vs1])
                            A.store(VV[0, r0:r0 + 128, :, :], vst[:], vst)
                            A.store(VV[1, r0:r0 + 128, 4:8, :], vs1[:], vs1)

                        for ti in range(NT):
                            do_tile(ti)
                        P.flush()
                        if stop == 'P1b':
                            raise _Stop()

                    with ExitStack() as ph:
                        qTp = [rot(ph, 2, [128, 8, 2, 256], BF16, "qbd") for _ in range(2)]
                        for d_ in range(2):
                            for t_ in qTp[d_].items:
                                A.memset(t_[:], 0.0, [t_])
                        kTp = [rot(ph, 2, [128, 8, 256], BF16, "kT") for _ in range(2)]
                        ktp = [rot(ph, 2, [64, 4, 1024], BF16, "ktok") for _ in range(2)]
                        vvp = [rot(ph, 2, [64, 4, 16 * VD], BF16, "vv") for _ in range(2)]
                        vmp = rot(ph, 2, [64, 4, 4 * VD], BF16, "vm")
                        S = [[[sbt(ph, [128, VD], F32, "S") for _ in range(2)] for _ in range(4)] for _ in range(2)]
                        Sb = [[[sbt(ph, [128, VD], BF16, "Sb") for _ in range(2)] for _ in range(4)] for _ in range(2)]
                        for d in range(2):
                            for m in range(4):
                                for hp in range(2):
                                    A.memset(S[d][m][hp][:], 0.0, [S[d][m][hp]])
                        psA = Rot([Tl(t_[:, 0:256].rearrange("p (h v) -> p h v", v=64), "psAv") for t_ in
                                   [pst(ph, [64, 512], F32, "psA") for _ in range(4)]])
                        pso = Rot([Tl(t_[:, 0:4 * VD].rearrange("p (h v) -> p h v", v=VD), "psov") for t_ in
                                   [pst(ph, [64, 512], F32, "pso") for _ in range(2)]])
                        psS = Rot([Tl(t_[:, 0:4 * VD].rearrange("p (a v) -> p a v", v=2 * VD), "psSv") for t_ in
                                   [pst(ph, [128, 512], F32, "psS") for _ in range(2)]])
                        mask4 = sbt(ph, [64, 2, 4, 64], F32, "mask4")
                        for d_ in range(2):
                            A.copy("dve", mask4[:, d_, :, :], maskf[:, d_, :].unsqueeze(1).to_broadcast([64, 4, 64]), [maskf], [mask4])
                        mask4u = mask4[:].bitcast(mybir.dt.uint32)
                        Asbd = [rot(ph, 5, [64, 4, 64], BF16, "Asb") for _ in range(2)]
                        for dd_ in range(2):
                            for t_ in Asbd[dd_].items:
                                A.memset(t_[:], 0.0, [t_])
                        tmpS = rot(ph, 6, [128, VD], F32, "tmpS")
                        osb = rot(ph, 2, [64, D], F32, "osb")
                        nrm = rot(ph, 2, [64, 16], F32, "nrm")

                        def col(m, d, hp, kind, c):
                            if m == ML:
                                if kind == 0:
                                    return onescol[:, 0:1], onescol
                                return ALP[:, c, d, hp:hp + 1], ALP
                            return CCt[:, m, d, hp, kind, c:c + 1], CCt

                        def do_chunk(d, c, cc, qT, kT, ktok, vv, vm):
                            ts = slice(64 * cc, 64 * cc + 64)
                            ob = osb.nxt()
                            vs = []
                            for m in range(4):
                                if m == ML and d == 1:
                                    vs.append((vm, 0))
                                else:
                                    vs.append((vv, 4 * m * VD))
                            a4s = []
                            for m in range(4):
                                for hp in range(2):
                                    St, Sbt = S[d][m][hp], Sb[d][m][hp]
                                    c1, c1t = col(m, d, hp, 0, c)
                                    A.act(Sbt[:, :], St[:, :], AF.Identity, [St, c1t], [Sbt], scale=c1)
                                pa = psA.nxt()
                                A.mm([(pa[:, h, :], kT[:, 2 * m + h // 2, ts], qT[:, 2 * m + h // 2, h % 2, ts], True, True) for h in range(4)],
                                     [kT, qT], [pa])
                                a4 = Asbd[d].nxt()
                                A.tt("dve", a4[:], pa[:], mask4[:, d, :, :], ALU.mult, [pa, mask4], [a4])
                                a4s.append(a4)
                            pos, pSs = [], []
                            for m in range(4):
                                vsrc, vbase = vs[m]
                                a4 = a4s[m]
                                po = pso.nxt()
                                items = []
                                for h in range(4):
                                    items.append((po[:, h, :], a4[:, h, :], vsrc[0:64, cc, vbase + h * VD:vbase + (h + 1) * VD], True, False))
                                    items.append((po[:, h, :], qT[:, 2 * m + h // 2, h % 2, ts], Sb[d][m][h // 2][:, :], False, True))
                                A.mm(items, [a4, vsrc, qT, Sb[d][m][0], Sb[d][m][1]], [po])
                                pS = psS.nxt()
                                A.mm([(pS[:, hp, :], ktok[0:64, cc, m * 256 + hp * 128:m * 256 + (hp + 1) * 128],
                                       vsrc[0:64, cc, vbase + 2 * hp * VD:vbase + (2 * hp + 2) * VD], True, True) for hp in range(2)],
                                     [ktok, vsrc], [pS])
                                for hp in range(2):
                                    St = S[d][m][hp]
                                    tS = tmpS.nxt()
                                    c2, c2t = col(m, d, hp, 1, c)
                                    c3, c3t = col(m, d, hp, 2, c)
                                    A.act(tS[:, :], St[:, :], AF.Identity, [St, c2t], [tS], scale=c2)
                                    for hh in range(2):
                                        p0 = 64 * hh
                                        A.stt(St[p0:p0 + 64, :], pS[p0:p0 + 64, hp, hh * VD:(hh + 1) * VD], c3[p0:p0 + 64, :], tS[p0:p0 + 64, :],
                                              ALU.mult, ALU.add, [pS, tS, c3t], [St])
                                ov = ob[:, m * 256:(m + 1) * 256].rearrange("p (h v) -> p h v", v=64)
                                if m == HG:
                                    P.act(lambda e, ov=ov, pin=po[:, :, 0:64]: e.mul(out=ov, in_=pin, mul=float(np.exp(2.0 * HG_SHIFT))), [po], [ob])
                                elif m != ML:
                                    A.copy("act", ov, po[:, :, 0:64], [po], [ob])
                                else:
                                    nr = nrm.nxt()
                                    A.act(nr[:, 0:4], po[:, :, 64], AF.Abs, [po], [nr, po])
                                    A.tt("dve", nr[:, 4:8], nr[:, 0:4], FLt[0:64, c, d * 4:d * 4 + 4], ALU.max, [nr, FLt], [nr])
                                    A.recip(nr[:, 8:12], nr[:, 4:8], [nr], [nr])
                                    A.tt("dve", ov, po[:, :, 0:64], nr[:, 8:12].unsqueeze(2).to_broadcast([64, 4, 64]), ALU.mult, [nr, po], [ob, po])
                            A.store(OO[d, 64 * c:64 * c + 64, :], ob[:], ob)

                        grp_order = [list(range(NG)), [0] + list(range(NG - 1, 0, -1))]
                        cc_order = [[0, 1, 2, 3], [3, 2, 1, 0]]
                        for gi in range(NG):
                            grp = []
                            for d in range(2):
                                g = grp_order[d][gi]
                                tg0 = 256 * g
                                qT, kT, ktok, vv = qTp[d].nxt(), kTp[d].nxt(), ktp[d].nxt(), vvp[d].nxt()
                                A.load(qT[0:64, :, 0, :], QT[d, :, 0:64, tg0:tg0 + 256].rearrange("m p t -> p m t"), qT)
                                A.load(qT[64:128, :, 1, :], QT[d, :, 64:128, tg0:tg0 + 256].rearrange("m p t -> p m t"), qT)
                                A.load(kT[:], KT[d, :, :, tg0:tg0 + 256].rearrange("m p t -> p m t"), kT)
                                A.load(ktok[:], KK[d, tg0:tg0 + 256, :].rearrange("(c p) x -> p c x", p=64), ktok)
                                A.load(vv[:], VV[0, tg0:tg0 + 256, :, :].rearrange("(c p) h v -> p c (h v)", p=64), vv)
                                vm = None
                                if d == 1:
                                    vm = vmp.nxt()
                                    A.load(vm[:], VV[1, tg0:tg0 + 256, 4:8, :].rearrange("(c p) h v -> p c (h v)", p=64), vm)
                                grp.append((g, qT, kT, ktok, vv, vm))
                            for k_ in range(4):
                                for d in range(2):
                                    g, qT, kT, ktok, vv, vm = grp[d]
                                    cc = cc_order[d][k_]
                                    P.release()
                                    do_chunk(d, 4 * g + cc, cc, qT, kT, ktok, vv, vm)
                        P.flush()
                        if stop == 'P2':
                            raise _Stop()

                with ExitStack() as ph:
                    g1 = [sbt(ph, [128, D], F32, "g1") for _ in range(2)]
                    g2 = [sbt(ph, [128, D], F32, "g2") for _ in range(2)]

                    def gtiles(gi, dst):
                        with ExitStack() as ph2:
                            wad = rot(ph2, 2, [128, 8, 512], F32, "wad")
                            bg = sbt(ph2, [128, D], F32, "bg")
                            pg = rot(ph2, 2, [128, 512], F32, "pg", psum=True)
                            crep = [sbt(ph2, [128, 8, 128], F32, "crep") for _ in range(2)]
                            for cl in range(2):
                                A.copy("dve", crep[cl][:], scs[:, :, cl:cl + 1].to_broadcast([128, 8, 128]), [scs], [crep[cl]])
                            A.load(bg[:], BADA_G[l, gi, :, :], bg)
                            vec = (2, 5)[gi]
                            for half in range(2):
                                blk = vec * 2 + half
                                w = wad.nxt()
                                A.load(w[:], W_ADA[l, :, blk * 512:(blk + 1) * 512].rearrange("(kt p) n -> p kt n", p=128), w)
                                for cl in range(2):
                                    ps = pg.nxt()
                                    A.mm([(ps[:], crep[cl][:, kt, :], w[:, kt, :], kt == 0, kt == 7) for kt in range(8)], [w, crep[cl]], [ps])
                                    A.tt("dve", dst[cl][:, half * 512:(half + 1) * 512], ps[:], bg[:, half * 512:(half + 1) * 512], ALU.add,
                                         [ps, bg], [dst[cl]])
                            P.flush()
                            if stop == 'P3a':
                                raise _Stop()
                    gtiles(0, g1)
                    wf1 = sbt(ph, [128, 8, DFF], BF16, "wf1")
                    load_w_cast(wf1, W_FF1[l, :, :], 8)
                    with ExitStack() as ph2:
                        wo = sbt(ph2, [128, 8, D], BF16, "wo")
                        load_w_cast(wo, W_OUT[l, :, :], 8)
                        o0p = rot(ph2, 3, [128, D], F32, "o0")
                        o1p = rot(ph2, 3, [128, D], F32, "o1")
                        gglp = rot(ph2, 3, [128, D], BF16, "ggl")
                        xp = rot(ph2, 4, [128, D], F32, "x3")
                        sqp = rot(ph2, 2, [128, D], F32, "sq")
                        ssp = rot(ph2, 3, [128, 48], F32, "ss3")
                        yp = rot(ph2, 3, [128, D], BF16, "y")
                        yTp = rot(ph2, 3, [128, 8, 128], BF16, "yT")
                        ptr = rot(ph2, 2, [128, 8, 128], BF16, "ptr3", psum=True)
                        pop = rot(ph2, 4, [128, 512], F32, "pop", psum=True)
                        tp = rot(ph2, 4, [128, 512], F32, "t3")

                        def do_tile3(ti):
                            r0 = ti * 128
                            cl = 0 if r0 < CTX else 1
                            o0, o1, ggl, xt = o0p.nxt(), o1p.nxt(), gglp.nxt(), xp.nxt()
                            A.load(o0[:], OO[0, r0:r0 + 128, :], o0)
                            A.load(o1[:], OO[1, r0:r0 + 128, :], o1)
                            A.load(ggl[:], GG[r0:r0 + 128, :], ggl)
                            A.load(xt[:], xsrc[r0:r0 + 128, :], xt)
                            P.release()
                            A.tt("pool", o0[:], o0[:], o1[:], ALU.add, [o0, o1], [o0])
                            sq, ss = sqp.nxt(), ssp.nxt()
                            A.act(sq[:], o0[:], AF.Square, [o0], [sq])
                            A.reduce(ss[:, 0:16], sq[:].rearrange("p (h v) -> p h v", v=64), [sq], [ss])
                            A.act(ss[:, 16:32], ss[:, 0:16], AF.Ln, [ss, epscol], [ss], scale=1.0 / 64, bias=epscol[:, 0:1])
                            A.act(ss[:, 32:48], ss[:, 16:32], AF.Exp, [ss], [ss], scale=-0.5)
                            o3 = o0[:].rearrange("p (h v) -> p h v", v=64)
                            A.tt("dve", o3, o3, ss[:, 32:48].unsqueeze(2).to_broadcast([128, 16, 64]), ALU.mult, [o0, ss], [o0])
                            y = yp.nxt()
                            A.tt("pool", y[:], o0[:], ggl[:], ALU.mult, [o0, ggl], [y])
                            pt = ptr.nxt()
                            A.tr([(pt[:, kt, :], y[:, kt * 128:(kt + 1) * 128]) for kt in range(8)], identb[:], [y, identb], [pt])
                            yT = yTp.nxt()
                            A.copy("act", yT[:, 0:4, :], pt[:, 0:4, :], [pt], [yT])
                            A.copy("dve", yT[:, 4:8, :], pt[:, 4:8, :], [pt], [yT])
                            for nb in range(2):
                                ps = pop.nxt()
                                A.mm([(ps[:], yT[:, kt, :], wo[:, kt, nb * 512:(nb + 1) * 512], kt == 0, kt == 7) for kt in range(8)], [yT, wo], [ps])
                                t = tp.nxt()
                                A.tt("dve", t[:], ps[:], g1[cl][:, nb * 512:(nb + 1) * 512], ALU.mult, [ps, g1[cl]], [t])
                                A.tt("pool", xt[:, nb * 512:(nb + 1) * 512], xt[:, nb * 512:(nb + 1) * 512], t[:], ALU.add, [t, xt], [xt])
                            A.store(XS[r0:r0 + 128, :], xt[:], xt)

                        for ti in range(NT):
                            do_tile3(ti)
                        P.flush()
                        if stop == 'P3a':
                            raise _Stop()

                    gtiles(1, g2)
                    with ExitStack() as ph2:
                        wf2 = sbt(ph2, [128, 32, D], BF16, "wf2")
                        load_w_cast(wf2, W_FF2[l, :, :], 32)
                        hTp = rot(ph2, 2, [128, 8, 256], BF16, "h2T")
                        npools = (rot(ph2, 3, [128, D], F32, "xt"), rot(ph2, 1, [128, D], BF16, "jk"), rot(ph2, 2, [128, 4], F32, "ss"),
                                  rot(ph2, 2, [128, D], BF16, "xn"), rot(ph2, 2, [128, 8, 128], BF16, "ptr", psum=True))
                        uTp = rot(ph2, 1, [128, 32, 256], BF16, "uT")
                        pu = Rot([Tl(t_[:, 0:256], "pus") for t_ in [pst(ph2, [128, 512], F32, "pu") for _ in range(3)]])
                        po2 = rot(ph2, 2, [128, 512], F32, "po2", psum=True)
                        sqp = rot(ph2, 3, [128, 256], F32, "sq2")
                        tp = rot(ph2, 1, [128, 512], F32, "t4")

                        def do_blk(bi):
                            cl = 0 if bi * 256 < CTX else 1
                            hT = hTp.nxt()
                            xts = []
                            for i in range(2):
                                xts.append(norm_T(npools, XS, bi * 256 + 128 * i, hT, 128 * i, mcol(3, cl), mcol(2, cl)))
                                if i == 0:
                                    P.release()
                            uT = uTp.nxt()
                            for fb in range(32):
                                ps = pu.nxt()
                                A.mm([(ps[:], wf1[:, kt, fb * 128:(fb + 1) * 128], hT[:, kt, :], kt == 0, kt == 7) for kt in range(8)], [wf1] + hT.kb, [ps])
                                sq = sqp.nxt()
                                A.act(sq[:], ps[:], AF.Square, [ps], [sq])
                                A.stt(uT[:, fb, :], ps[:], 0.0, sq[:], ALU.is_gt, ALU.mult, [ps, sq], [uT])
                            for i in range(2):
                                xt = xts[i]
                                for nb in range(2):
                                    ps = po2.nxt()
                                    A.mm([(ps[:], uT[:, fb, 128 * i:128 * (i + 1)], wf2[:, fb, nb * 512:(nb + 1) * 512], fb == 0, fb == 31)
                                          for fb in range(32)], [uT, wf2], [ps])
                                    t = tp.nxt()
                                    A.tt("dve", t[:], ps[:], g2[cl][:, nb * 512:(nb + 1) * 512], ALU.mult, [ps, g2[cl]], [t])
                                    A.tt("pool", xt[:, nb * 512:(nb + 1) * 512], xt[:, nb * 512:(nb + 1) * 512], t[:], ALU.add, [t, xt], [xt])
                                r0 = bi * 256 + 128 * i
                                A.store(XS[r0:r0 + 128, :], xt[:], xt)

                        for bi in range(T // 256):
                            do_blk(bi)
                        P.flush()
                        if stop == 'P3b':
                            raise _Stop()

        except _Stop:
            P.flush(final=True)
            build.ninst = P.ninst
            gs.pop_all()
            return nc
        with ExitStack() as ph:
            gf = sbt(ph, [128, D], F32, "gf")
            A.load(gf[:], GFIN[:, :], gf)
            xp = rot(ph, 3, [128, D], F32, "xf")
            jp = rot(ph, 1, [128, D], BF16, "jkf")
            sp_ = rot(ph, 2, [128, 4], F32, "ssf")
            for ti in range(LAT // 128):
                r0 = CTX + ti * 128
                xt, jk, ss = xp.nxt(), jp.nxt(), sp_.nxt()
                A.load(xt[:], XS[r0:r0 + 128, :], xt)
                P.release()
                A.act(jk[:], xt[:], AF.Square, [xt], [jk, ss], accum_out=ss[:, 0:1])
                A.act(ss[:, 1:2], ss[:, 0:1], AF.Ln, [ss, epscol], [ss], scale=1.0 / D, bias=epscol[:, 0:1])
                A.act(ss[:, 2:3], ss[:, 1:2], AF.Exp, [ss], [ss], scale=-0.5)
                A.stt(xt[:], xt[:], ss[:, 2:3], gf[:], ALU.mult, ALU.mult, [xt, ss, gf], [xt])
                A.store(OUT[ti * 128:(ti + 1) * 128, :], xt[:], xt)
            P.flush(final=True)
        build.ninst = P.ninst
    return nc


_IN_LAYOUT = (
    ('hg_q', 256), ('hg_f_fwd', 256), ('hg_f_bwd', 256), ('hg_i', 256), ('hg_g', 256),
    ('ml_q', 256), ('ml_k', 256), ('ml_v', 256), ('ml_if', 16), ('ml_o', 256),
    ('rt_q', 256), ('rt_k', 256), ('rt_v', 256), ('rt_g', 256),
    ('gl_q', 128), ('gl_k', 128), ('gl_v', 256),
    ('gl_a_fwd', 16), ('gl_a_bwd', 16), ('gl_g', 256),
)


def _col_ranges():
    off = {}
    o = 0
    for nme, s in _IN_LAYOUT:
        off[nme] = (o, s)
        o += s
    return off


def _w1_layout(w_in):
    off = _col_ranges()
    dep = w_in.shape[0]
    out = np.zeros((dep, D, NC1), np.float32)

    def cols(nme):
        o, s = off[nme]
        return w_in[:, :, o:o + s]
    perm = np.zeros(256, np.int64)
    for h in range(4):
        for dd in range(64):
            r = dd % 32
            partner = dd + 16 if r < 16 else dd - 16
            perm[h * 64 + dd] = h * 64 + partner

    def pad_gla(a):
        p = np.zeros((dep, D, 256), np.float32)
        for h in range(4):
            p[:, :, h * 64:h * 64 + 32] = a[:, :, h * 32:(h + 1) * 32]
        return p
    fmc = [cols('hg_q'), cols('hg_f_fwd'), cols('hg_f_bwd'), cols('ml_q'), cols('ml_k'),
           cols('rt_q'), cols('rt_q')[:, :, perm], cols('rt_k'), cols('rt_k')[:, :, perm],
           pad_gla(cols('gl_q')), pad_gla(cols('gl_k')), cols('gl_a_fwd'), cols('gl_a_bwd')]
    tmc = [cols('hg_i'), cols('ml_v'), cols('rt_v'), cols('gl_v'), cols('hg_g'), cols('ml_o'), cols('rt_g'), cols('gl_g'), cols('ml_if')]
    o = 0
    for a in fmc + tmc:
        out[:, :, o:o + a.shape[2]] = a
        o += a.shape[2]
    assert o == NC1
    return out


def _rope_tables(T):
    LATn = T - CTX
    tl = np.arange(LATn)
    row = (tl // 64).astype(np.float32)
    colp = (tl % 64).astype(np.float32)
    inv = (np.float32(10000.0) ** (-np.arange(16, dtype=np.float32) / np.float32(16))).astype(np.float32)
    cosT = np.ones((128, T), np.float32)
    sinT = np.zeros((128, T), np.float32)
    for p in range(128):
        dd = p % 64
        pos = row if dd < 32 else colp
        ang = (pos * inv[dd % 16]).astype(np.float32)
        sgn = -1.0 if (dd % 32) < 16 else 1.0
        cosT[p, CTX:] = np.cos(ang).astype(np.float32)
        sinT[p, CTX:] = (sgn * np.sin(ang)).astype(np.float32)
    return cosT, sinT


def _consts():
    c = np.zeros((128, 8, 128), np.float32)
    s = np.arange(128)[:, None]
    t = np.arange(128)[None, :]
    same = (s // 64) == (t // 64)
    c[:, 0, :] = (s == t)
    c[:, 1, :] = same & (s <= t)
    c[:, 2, :] = same & (s >= t)
    c[:, 3, :] = (s < 64) & (t >= 0)
    c[:, 4, :] = (s >= 64) & (t >= 0)
    c[:64, 5, :64] = (s[:64] <= t[:, :64])
    c[:64, 6, :64] = (s[:64] >= t[:, :64])
    return c


def make_shared(inp, T):
    dep = inp['w_ada'].shape[0]
    f32 = np.float32
    sh = {}
    sh['w_ada'] = np.ascontiguousarray(inp['w_ada'], f32)
    b_ada = np.asarray(inp['b_ada'], f32)
    bc = np.zeros((dep, 128, 4, 8), f32)
    for which, vec in enumerate((0, 1, 3, 4)):
        bc[:, :, which, :] = b_ada[:, vec * D:(vec + 1) * D].reshape(dep, 8, 128).transpose(0, 2, 1)
    sh['bada_col'] = bc
    bg = np.zeros((dep, 2, 128, D), f32)
    for gi, vec in enumerate((2, 5)):
        bg[:, gi, :, :] = b_ada[:, None, vec * D:(vec + 1) * D]
    sh['bada_g'] = bg
    sh['w1'] = _w1_layout(np.asarray(inp['w_in'], f32))
    sh['ghr'] = np.ascontiguousarray(np.broadcast_to(np.asarray(inp['g_heads'], f32)[:, None, :], (dep, 128, D)))
    hl = np.asarray(inp['hgrn_lb_logits'], f32)
    sh['hgl'] = np.ascontiguousarray(hl.reshape(dep, 2, 2, 128).transpose(3, 0, 1, 2))
    mb = np.asarray(inp['ml_gate_bias'], f32).reshape(dep, 16)
    sh['mlb'] = np.ascontiguousarray(np.broadcast_to(mb[:, None, :], (dep, 128, 16)))
    rl = np.asarray(inp['rt_decay_logit'], f32)
    rt = np.zeros((128, dep, 2, 2), f32)
    for j in range(2):
        for hh in range(2):
            rt[hh * 64:(hh + 1) * 64, :, :, j] = rl[None, :, :, 2 * j + hh]
    sh['rtl'] = rt
    wa = np.asarray(inp['gla_w_a'], f32)
    wap = np.zeros((dep, 2, 16, 256), f32)
    ba = np.asarray(inp['gla_b_a'], f32)
    bap = np.zeros((dep, 2, 256), f32)
    for h in range(4):
        wap[:, :, :, h * 64:h * 64 + 32] = wa[:, :, :, h * 32:(h + 1) * 32]
        bap[:, :, h * 64:h * 64 + 32] = ba[:, :, h * 32:(h + 1) * 32]
    sh['wa'] = wap
    sh['ba'] = np.ascontiguousarray(bap.reshape(dep, 2, 2, 128).transpose(3, 0, 1, 2))
    sh['w_out'] = np.ascontiguousarray(inp['w_out'], f32)
    sh['w_ff1'] = np.ascontiguousarray(inp['w_ff1'], f32)
    sh['w_ff2'] = np.ascontiguousarray(inp['w_ff2'], f32)
    sh['gfin'] = np.ascontiguousarray(np.broadcast_to(np.asarray(inp['g_final'], f32)[None, :], (128, D)))
    c, s = _rope_tables(T)
    sh['ropec'] = c
    sh['ropes'] = s
    sh['consts'] = _consts()
    return sh


def make_core(inp, b):
    f32 = np.float32
    m = {}
    m['xin'] = np.ascontiguousarray(np.concatenate([np.asarray(inp['ctx'][b], f32), np.asarray(inp['x'][b], f32)], axis=0))
    cc = np.zeros((128, 8, 2), f32)
    cc[:, :, 0] = np.asarray(inp['c_ctx'], f32).reshape(8, 128).T
    cc[:, :, 1] = np.asarray(inp['c'][b], f32).reshape(8, 128).T
    m['cc'] = cc
    return m


_CACHE = {}


def kernel(**inputs):
    x = inputs['x']
    B, LAT, _ = x.shape
    T = CTX + LAT
    if LAT not in _CACHE:
        _CACHE[LAT] = build(LAT)
    nc = _CACHE[LAT]
    sh = make_shared(inputs, T)
    in_maps = []
    for b in range(B):
        m = dict(sh)
        m.update(make_core(inputs, b))
        in_maps.append(m)
    res = run_bass_kernel_spmd(nc, in_maps, core_ids=list(range(B)))
    return np.stack([np.asarray(r["out"], np.float32) for r in res.results], axis=0)
```

```python
import numpy as np
from contextlib import ExitStack
import concourse.bass as bass
import concourse.mybir as mybir
from concourse.bass_utils import run_bass_kernel_spmd

F32 = mybir.dt.float32
BF16 = mybir.dt.bfloat16
AF = mybir.ActivationFunctionType
ALU = mybir.AluOpType
AX = mybir.AxisListType

D = 1024
CTX = 256
DEPTH = 2
DFF = 4096
VD = 66
NFM = 22 * 128 + 32
NTM = 2064
NC1 = NFM + NTM
EPS = 1e-6
ENGS = ("pe", "act", "dve", "pool", "sp")
HG, ML, RT, GL = 0, 1, 2, 3
HG_SHIFT = 20.0


class Buf:
    __slots__ = ("name", "last_write", "reads", "dsem", "dcount")

    def __init__(self, name):
        self.name = name
        self.last_write = None
        self.reads = []
        self.dsem = None
        self.dcount = 0


class Tl:
    def __init__(self, t, name, b=None):
        self.t = t
        self.b = b if b is not None else Buf(name)

    def __getitem__(self, i):
        return self.t[i]


def _b(x):
    return x.b if isinstance(x, Tl) else x


class Op:
    __slots__ = ("eng", "fn", "deps", "is_dma", "slot", "ticket", "signal", "retired", "seq")

    def __init__(self, eng, fn, is_dma, slot):
        self.eng = eng
        self.fn = fn
        self.deps = []
        self.is_dma = is_dma
        self.slot = slot
        self.ticket = None
        self.signal = False
        self.retired = False


class Prog:
    def __init__(self, nc, stack):
        self.nc = nc
        self.stack = stack
        self.ops = []
        self.esem = {e: stack.enter_context(nc.semaphore("sem_" + e)) for e in ENGS}
        self.ecount = {e: 0 for e in ENGS}
        self.waited = {e: {} for e in ENGS}
        self.pending_bar = {e: [] for e in ENGS}
        self.sempool = []
        self.nsem = 0
        self.engobj = {"pe": nc.tensor, "act": nc.scalar, "dve": nc.vector, "pool": nc.gpsimd, "sp": nc.sync}
        self.ninst = 0
        self.deferred = []
        self.seq = 0

    def defer_dma(self, fn, slot, reads=(), writes=(), eng="sp"):
        self.deferred.append((fn, slot, reads, writes, eng))

    def release(self, keep=0):
        n = len(self.deferred) - keep
        if n <= 0:
            return
        dd = self.deferred[:n]
        self.deferred = self.deferred[n:]
        for fn, slot, reads, writes, eng in dd:
            self.dma(fn, slot, reads, writes, eng)

    def add(self, eng, fn, reads=(), writes=(), is_dma=False, slot=None):
        op = Op(eng, fn, is_dma, slot)
        deps = set()
        for x in reads:
            b = _b(x)
            if b.last_write is not None:
                deps.add(b.last_write)
        for x in writes:
            b = _b(x)
            if b.last_write is not None:
                deps.add(b.last_write)
            for r in b.reads:
                deps.add(r)
        dl = [d for d in deps if (not d.retired) and not (eng == "pe" and d.eng == "pe" and not d.is_dma)]
        best = {}
        keep = []
        for d in dl:
            if d.is_dma:
                keep.append(d)
            elif d.eng not in best or best[d.eng].seq < d.seq:
                best[d.eng] = d
        op.deps = keep + list(best.values())
        self.seq += 1
        op.seq = self.seq
        for x in reads:
            _b(x).reads.append(op)
        for x in writes:
            b = _b(x)
            b.last_write = op
            b.reads = []
        self.ops.append(op)
        return op

    def pe(self, fn, reads=(), writes=()):
        return self.add("pe", fn, reads, writes)

    def act(self, fn, reads=(), writes=()):
        return self.add("act", fn, reads, writes)

    def dve(self, fn, reads=(), writes=()):
        return self.add("dve", fn, reads, writes)

    def pool(self, fn, reads=(), writes=()):
        return self.add("pool", fn, reads, writes)

    def dma(self, fn, slot, reads=(), writes=(), eng="sp"):
        return self.add(eng, fn, reads, writes, is_dma=True, slot=_b(slot))

    def flush(self, final=False):
        nc = self.nc
        self.release()
        ops = self.ops
        self.ops = []
        for op in ops:
            for d in op.deps:
                d.signal = True
        last = {}
        for op in ops:
            if not op.is_dma:
                last[op.eng] = op
        for op in last.values():
            op.signal = True
        slots = []
        swbar = []
        for op in ops:
            if op.is_dma and op.eng == "pool":
                sem = self.stack.enter_context(nc.semaphore("sw%d" % self.nsem))
                self.nsem += 1
                op.ticket = (sem, 16)
                swbar.append(op.ticket)
            elif op.is_dma:
                s = op.slot
                if s.dsem is None:
                    if self.sempool:
                        s.dsem, s.dcount = self.sempool.pop()
                    else:
                        s.dsem = self.stack.enter_context(nc.semaphore("ds%d" % self.nsem))
                        self.nsem += 1
                        s.dcount = 0
                    slots.append(s)
                s.dcount += 16
                op.ticket = (s.dsem, s.dcount)
            elif op.signal:
                self.ecount[op.eng] += 1
                op.ticket = (self.esem[op.eng], self.ecount[op.eng])
        per = {e: [] for e in ENGS}
        for op in ops:
            per[op.eng].append(op)
        bar = [last[e].ticket for e in last] + [(s.dsem, s.dcount) for s in slots] + swbar

        def run(ename, eng):
            waited = self.waited[ename]

            def w(sem, val):
                k = id(sem)
                if waited.get(k, 0) < val:
                    eng.wait_ge(sem, val)
                    waited[k] = val

            if per[ename] or final:
                for sem, val in self.pending_bar[ename]:
                    w(sem, val)
                self.pending_bar[ename] = []
            for op in per[ename]:
                need = {}
                for d in op.deps:
                    sem, val = d.ticket
                    k = id(sem)
                    if k not in need or need[k][1] < val:
                        need[k] = (sem, val)
                for sem, val in need.values():
                    w(sem, val)
                ins = op.fn(eng)
                self.ninst += 1
                if op.is_dma:
                    ins.then_inc(op.ticket[0], 16)
                elif op.signal:
                    ins.then_inc(op.ticket[0], 1)
            if final and ename == "sp":
                for sem, val in bar:
                    w(sem, val)

        with nc.Block() as block:
            @block.tensor
            def _(eng):
                run("pe", eng)

            @block.scalar
            def _(eng):
                run("act", eng)

            @block.vector
            def _(eng):
                run("dve", eng)

            @block.gpsimd
            def _(eng):
                run("pool", eng)

            @block.sync
            def _(eng):
                run("sp", eng)

        for e in ENGS:
            self.pending_bar[e].extend(bar)
        for op in ops:
            op.retired = True
        for s in slots:
            self.sempool.append((s.dsem, s.dcount))
            s.dsem = None


class Rot:
    def __init__(self, items):
        self.items = items
        self.i = 0

    def nxt(self):
        r = self.items[self.i % len(self.items)]
        self.i += 1
        return r


class Em:
    def __init__(self, P):
        self.P = P

    def act(self, out, in_, func, R, W, **kw):
        self.P.act(lambda e: e.activation(out=out, in_=in_, func=func, **kw), R, W)

    def tt(self, eng, out, in0, in1, op, R, W):
        self.P.add(eng, lambda e: e.tensor_tensor(out=out, in0=in0, in1=in1, op=op), R, W)

    def ts(self, eng, out, in0, s1, s2, op0, op1, R, W):
        if op1 is None:
            self.P.add(eng, lambda e: e.tensor_scalar(out=out, in0=in0, scalar1=s1, scalar2=None, op0=op0), R, W)
        else:
            self.P.add(eng, lambda e: e.tensor_scalar(out=out, in0=in0, scalar1=s1, scalar2=s2, op0=op0, op1=op1), R, W)

    def stt(self, out, in0, scalar, in1, op0, op1, R, W):
        self.P.dve(lambda e: e.scalar_tensor_tensor(out=out, in0=in0, scalar=scalar, in1=in1, op0=op0, op1=op1), R, W)

    def copy(self, eng, out, in_, R, W):
        if eng == "act":
            self.P.act(lambda e: e.copy(out=out, in_=in_), R, W)
        else:
            self.P.add(eng, lambda e: e.tensor_copy(out=out, in_=in_), R, W)

    def memset(self, ap, val, W, R=()):
        self.P.pool(lambda e: e.memset(ap, val), R, W)

    def load(self, out, in_, slot, W=None):
        self.P.dma(lambda e: e.dma_start(out=out, in_=in_), slot, writes=[slot] if W is None else W)

    def loadc(self, out, in_, slot):
        self.P.dma(lambda e: e.dma_start(out=out, in_=in_), slot, writes=[slot], eng="pool")

    def store(self, out, in_, slot):
        self.P.defer_dma(lambda e: e.dma_start(out=out, in_=in_), slot, reads=[slot])

    def mm(self, items, R, W):
        def f(e):
            r = None
            for (o, l, rh, st, sp) in items:
                r = e.matmul(o, lhsT=l, rhs=rh, start=st, stop=sp)
            return r
        self.P.pe(f, R, W)

    def tr(self, items, ident, R, W):
        def f(e):
            r = None
            for (o, i) in items:
                r = e.transpose(o, i, ident)
            return r
        self.P.pe(f, R, W)

    def scan(self, out, d0, d1, R, W):
        self.P.dve(lambda e: e.tensor_tensor_scan(out=out, data0=d0, data1=d1, initial=0.0, op0=ALU.mult, op1=ALU.add), R, W)

    def recip(self, out, in_, R, W):
        self.P.dve(lambda e: e.reciprocal(out=out, in_=in_), R, W)

    def reduce(self, out, in_, R, W):
        self.P.dve(lambda e: e.tensor_reduce(out=out, in_=in_, axis=AX.X, op=ALU.add), R, W)

    def sss(self, out, in_, scalar, op, R, W):
        self.P.dve(lambda e: e.tensor_single_scalar(out=out, in_=in_, scalar=scalar, op=op), R, W)


class _Stop(Exception):
    pass


def build(LAT, depth=DEPTH, dbg=False, stop=None):
    T = CTX + LAT
    NT = T // 128
    NCH = T // 64
    NG = T // 256
    nc = bass.Bass("TRN2", target_bir_lowering=False)

    def din(name, shape, dt=F32):
        return nc.dram_tensor(name, list(shape), dt, kind="ExternalInput").ap()

    def dscr(name, shape, dt):
        return nc.dram_tensor(name, list(shape), dt, kind="ExternalOutput" if dbg else "Internal").ap()

    XIN = din("xin", [T, D])
    CC_IN = din("cc", [128, 8, 2])
    W_ADA = din("w_ada", [DEPTH, D, 6 * D])
    BADA_COL = din("bada_col", [DEPTH, 128, 4, 8])
    BADA_G = din("bada_g", [DEPTH, 2, 128, D])
    W1 = din("w1", [DEPTH, D, NC1])
    GHR = din("ghr", [DEPTH, 128, D])
    HGL = din("hgl", [128, DEPTH, 2, 2])
    MLB = din("mlb", [DEPTH, 128, 16])
    RTL = din("rtl", [128, DEPTH, 2, 2])
    WA = din("wa", [DEPTH, 2, 16, 256])
    BA = din("ba", [128, DEPTH, 2, 2])
    W_OUT = din("w_out", [DEPTH, D, D])
    W_FF1 = din("w_ff1", [DEPTH, D, DFF])
    W_FF2 = din("w_ff2", [DEPTH, DFF, D])
    GFIN = din("gfin", [128, D])
    ROPEC = din("ropec", [128, T])
    ROPES = din("ropes", [128, T])
    CONSTS = din("consts", [128, 8, 128])
    OUT = nc.dram_tensor("out", [LAT, D], F32, kind="ExternalOutput").ap()

    XS = dscr("xs", [T, D], F32)
    QT = dscr("qt", [2, 8, 128, T], BF16)
    KT = dscr("kt", [2, 8, 128, T], BF16)
    KK = dscr("kk", [2, T, 1024], BF16)
    VV = dscr("vv", [2, T, 16, VD], BF16)
    GG = dscr("gg", [T, D], BF16)
    OO = dscr("oo", [2, T, D], F32)

    with ExitStack() as gs:
        P = Prog(nc, gs)
        A = Em(P)
        cnt = [0]

        def sbt(st, shape, dt=F32, name=None):
            cnt[0] += 1
            nm = "%s_%d" % (name or "t", cnt[0])
            return Tl(st.enter_context(nc.sbuf_tensor(nm, list(shape), dt)), nm)

        def pst(st, shape, dt=F32, name=None):
            cnt[0] += 1
            nm = "%s_%d" % (name or "p", cnt[0])
            return Tl(st.enter_context(nc.psum_tensor(nm, list(shape), dt)), nm)

        def rot(st, n, shape, dt=F32, name=None, psum=False):
            return Rot([(pst if psum else sbt)(st, shape, dt, name) for _ in range(n)])

        def sub(tl, ap, name):
            return Tl(ap, name)

        cst = sbt(gs, [128, 8, 128], F32, "cst")
        A.load(cst[:], CONSTS[:, :, :], cst)
        identb = sbt(gs, [128, 128], BF16, "identb")
        A.copy("dve", identb[:], cst[:, 0, :], [cst], [identb])
        maskf = sbt(gs, [64, 2, 64], F32, "maskf")
        A.copy("dve", maskf[:], cst[0:64, 5:7, 0:64], [cst], [maskf])
        masku = maskf[:].bitcast(mybir.dt.uint32)
        rmask = sbt(gs, [128, 512], F32, "rmask")
        A.memset(rmask[:], 1.0, [rmask])
        A.memset(rmask[:].rearrange("p (c t) -> p c t", t=64)[:, :, 0:1], 0.0, [rmask], [rmask])
        onescol = sbt(gs, [128, 1], F32, "onescol")
        A.memset(onescol[:], 1.0, [onescol])
        zcol = sbt(gs, [128, 1], F32, "zcol")
        A.memset(zcol[:], 0.0, [zcol])
        shcol = sbt(gs, [128, 2], F32, "shcol")
        A.memset(shcol[:, 0:1], -HG_SHIFT, [shcol])
        A.memset(shcol[:, 1:2], HG_SHIFT, [shcol], [shcol])
        epscol = sbt(gs, [128, 1], F32, "epscol")
        A.memset(epscol[:], EPS, [epscol])
        ccs = sbt(gs, [128, 8, 2], F32, "ccs")
        A.load(ccs[:], CC_IN[:, :, :], ccs)
        scs = sbt(gs, [128, 8, 2], F32, "scs")
        A.act(scs[:], ccs[:], AF.Silu, [ccs], [scs])
        P.flush()

        def norm_T(pools, src, r0, hT, col0, sc, sh):
            xp, jp, sp_, xnp, ptp = pools
            xt = xp.nxt()
            A.load(xt[:], src[r0:r0 + 128, :], xt)
            jk = jp.nxt()
            ss = sp_.nxt()
            A.act(jk[:], xt[:], AF.Square, [xt], [jk, ss], accum_out=ss[:, 0:1])
            A.act(ss[:, 1:2], ss[:, 0:1], AF.Ln, [ss, epscol], [ss], scale=1.0 / D, bias=epscol[:, 0:1])
            A.act(ss[:, 2:3], ss[:, 1:2], AF.Exp, [ss], [ss], scale=-0.5)
            xn = xnp.nxt()
            A.act(xn[:], xt[:], AF.Identity, [xt, ss], [xn], scale=ss[:, 2:3])
            pt = ptp.nxt()
            A.tr([(pt[:, kt, :], xn[:, kt * 128:(kt + 1) * 128]) for kt in range(8)], identb[:], [xn, identb], [pt])
            if not hasattr(hT, "kb"):
                hT.kb = [Buf("hTk") for _ in range(8)]
            ncount[0] += 1
            for kt in range(8):
                if ncount[0] % 2 == 0:
                    A.act(hT[:, kt, col0:col0 + 128], pt[:, kt, :], AF.Identity, [pt, modcol_ref[0]], [hT.kb[kt]], scale=sc[kt], bias=sh[kt])
                else:
                    A.ts("dve", hT[:, kt, col0:col0 + 128], pt[:, kt, :], sc[kt], sh[kt], ALU.mult, ALU.add, [pt, modcol_ref[0]], [hT.kb[kt]])
            return xt

        modcol_ref = [None]
        ncount = [0]

        def load_w_cast(dst, src2, nk):
            for k0 in range(0, nk, 8):
                A.loadc(dst[:, k0:k0 + 8, :], src2[k0 * 128:(k0 + 8) * 128, :].rearrange("(kt p) n -> p kt n", p=128), dst)

        def decay_core(lf, n, d, escale, bT, Dt, E1, E2, cc, cctl, tmpc, sh=0.0):
            if d == 0:
                A.scan(bT[:, 0:n], rmask[:, 0:n], lf[:, 0:n], [lf, rmask], [bT])
            else:
                A.scan(bT[:, 0:n][:, ::-1], rmask[:, 0:n], lf[:, 0:n][:, ::-1], [lf, rmask], [bT])
            nb = n // 64
            mid = 31 if d == 0 else 32
            last = 63 if d == 0 else 0
            b3 = bT[:, 0:n].rearrange("p (c t) -> p c t", t=64)
            D3 = Dt[:, 0:n].rearrange("p (c t) -> p c t", t=64)
            A.tt("pool", D3, b3, b3[:, :, mid:mid + 1].to_broadcast([128, nb, 64]), ALU.subtract, [bT], [Dt])
            bneg = shcol[:, 0:1] if sh else zcol[:, 0:1]
            bpos = shcol[:, 1:2] if sh else zcol[:, 0:1]
            A.act(E1[:, 0:n], Dt[:, 0:n], AF.Exp, [Dt], [E1], scale=escale, bias=bneg)
            A.act(E2[:, 0:n], Dt[:, 0:n], AF.Exp, [Dt], [E2], scale=-escale, bias=bneg)
            ref2 = bT[:, mid:n:64]
            bl2 = bT[:, last:n:64]
            A.act(cc[0], ref2, AF.Exp, [bT], [cctl], scale=escale, bias=bneg)
            A.act(cc[1], bl2, AF.Exp, [bT], [cctl], scale=escale)
            A.tt("pool", tmpc[:, 0:nb], bl2, ref2, ALU.subtract, [bT], [tmpc])
            A.act(cc[2], tmpc[:, 0:nb], AF.Exp, [tmpc], [cctl], scale=escale, bias=bpos)

        blocks = [(0, CTX, 0)] + [(CTX + 512 * i, 512, 1) for i in range(LAT // 512)]

        try:
          for l in range(depth):
            xsrc = XIN if l == 0 else XS
            with ExitStack() as ls:
                modcol = sbt(ls, [128, 4, 8, 2], F32, "modcol")
                modcol_ref[0] = modcol

                def mcol(which, cl):
                    return [modcol[:, which, kt, cl:cl + 1] for kt in range(8)]

                with ExitStack() as ls2:
                    CCt = sbt(ls2, [128, 4, 2, 2, 3, NCH], F32, "CCt")
                    ALt = sbt(ls2, [128, NCH, 8], F32, "ALt")
                    FLt = sbt(ls2, [64, NCH, 8], F32, "FLt")
                    ALP = sbt(ls2, [128, NCH, 2, 2], F32, "ALP")
                    Ec = [[[sbt(ls2, [128, 512], F32, "Ec") for _ in range(2)] for _ in range(2)] for _ in range(2)]
                    lbt = sbt(ls2, [128, 2, 2, 2], F32, "lbt")
                    lgam = sbt(ls2, [128, 2, 2], F32, "lgam")
                    bat = sbt(ls2, [128, 2, 2], F32, "bat")
                    wat = sbt(ls2, [16, 2, 256], F32, "wat")
                    mlbt = sbt(ls2, [128, 16], F32, "mlbt")
                    ghr = sbt(ls2, [128, D], F32, "ghr")

                    with ExitStack() as ph:
                        wad = rot(ph, 2, [128, 8, 512], F32, "wad")
                        pm = pst(ph, [128, 64], F32, "pm")
                        bcol = sbt(ph, [128, 4, 8], F32, "bcol")
                        A.load(bcol[:], BADA_COL[l, :, :, :], bcol)
                        for which, vec in enumerate((0, 1, 3, 4)):
                            for half in range(2):
                                blk = vec * 2 + half
                                w = wad.nxt()
                                A.load(w[:], W_ADA[l, :, blk * 512:(blk + 1) * 512].rearrange("(kt p) n -> p kt n", p=128), w)
                                items = []
                                for f4 in range(4):
                                    c0 = (which * 8 + half * 4 + f4) * 2
                                    for kt in range(8):
                                        items.append((pm[:, c0:c0 + 2], w[:, kt, f4 * 128:(f4 + 1) * 128], scs[:, kt, :], kt == 0, kt == 7))
                                A.mm(items, [w, scs], [pm])
                        A.tt("dve", modcol[:].rearrange("p a b c -> p (a b) c"), pm[:].rearrange("p (a c) -> p a c", c=2),
                             bcol[:].rearrange("p a b -> p (a b)").unsqueeze(2).to_broadcast([128, 32, 2]), ALU.add, [pm, bcol], [modcol])
                        for which in (1, 3):
                            A.ts("dve", modcol[:, which, :, :], modcol[:, which, :, :], 1.0, None, ALU.add, None, [modcol], [modcol])
                        hgl = sbt(ph, [128, DEPTH, 2, 2], F32, "hgl")
                        A.load(hgl[:], HGL[:, :, :, :], hgl)
                        if l == 0:
                            A.memset(lbt[:, 0, :, :], 0.0, [lbt])
                        else:
                            dl = sbt(ph, [128, 2, 2], F32, "dl")
                            A.tt("dve", dl[:], hgl[:, 1, :, :], hgl[:, 0, :, :], ALU.subtract, [hgl], [dl])
                            A.act(lbt[:, 0, :, :], dl[:], AF.Sigmoid, [dl], [lbt])
                        A.ts("dve", lbt[:, 1, :, :], lbt[:, 0, :, :], -1.0, 1.0, ALU.mult, ALU.add, [lbt], [lbt])
                        rtl = sbt(ph, [128, DEPTH, 2, 2], F32, "rtl")
                        A.load(rtl[:], RTL[:, :, :, :], rtl)
                        sgr = sbt(ph, [128, 2, 2], F32, "sgr")
                        A.act(sgr[:], rtl[:, l, :, :], AF.Sigmoid, [rtl], [sgr])
                        A.act(lgam[:], sgr[:], AF.Ln, [sgr], [lgam])
                        bap = sbt(ph, [128, DEPTH, 2, 2], F32, "bap")
                        A.load(bap[:], BA[:, :, :, :], bap)
                        A.copy("dve", bat[:], bap[:, l, :, :], [bap], [bat])
                        A.load(wat[:], WA[l, :, :, :].rearrange("d r c -> r d c"), wat)
                        A.load(mlbt[:], MLB[l, :, :], mlbt)
                        A.load(ghr[:], GHR[l, :, :], ghr)
                        lfc = sbt(ph, [128, 512], F32, "lfc")
                        bTc = sbt(ph, [128, 512], F32, "bTc")
                        Dtc = sbt(ph, [128, 512], F32, "Dtc")
                        ctmp = sbt(ph, [128, 3, 8], F32, "ctmp")
                        tmpc = sbt(ph, [128, 8], F32, "tmpc")
                        for d in range(2):
                            for j in range(2):
                                A.copy("dve", lfc[:], lgam[:, d, j:j + 1].to_broadcast([128, 512]), [lgam], [lfc])
                                decay_core(lfc, 512, d, 1.0, bTc, Dtc, Ec[d][j][0], Ec[d][j][1],
                                           [ctmp[:, k, :] for k in range(3)], ctmp, tmpc)
                                for kind in range(3):
                                    A.copy("dve", CCt[:, RT, d, j, kind, :], ctmp[:, kind, 0:1].to_broadcast([128, NCH]), [ctmp], [CCt])
                        P.flush()
                        if stop == 'PA':
                            raise _Stop()

                    with ExitStack() as ph:
                        w1 = sbt(ph, [128, 8, NFM], BF16, "w1fm")
                        load_w_cast(w1, W1[l, :, 0:NFM], 8)
                        hTp = rot(ph, 2, [128, 8, 512], BF16, "hT")
                        npools = (rot(ph, 2, [128, D], F32, "xt"), rot(ph, 1, [128, D], BF16, "jk"), rot(ph, 2, [128, 4], F32, "ss"),
                                  rot(ph, 2, [128, D], BF16, "xn"), rot(ph, 2, [128, 8, 128], BF16, "ptr", psum=True))
                        pf = rot(ph, 4, [128, 512], F32, "pf", psum=True)
                        ptk = rot(ph, 2, [128, 4, 128], BF16, "ptk", psum=True)
                        wk = rot(ph, 10, [128, 512], F32, "wk")
                        fq = rot(ph, 4, [128, 512], F32, "fq")
                        bfp = rot(ph, 6, [128, 512], BF16, "bfp")
                        kst = [sbt(ph, [128, 4, 1024], BF16, "kst") for _ in range(2)]
                        rope = [sbt(ph, [128, 512], F32, "rope") for _ in range(2)]
                        ga = [sbt(ph, [16, 512], F32, "ga") for _ in range(2)]
                        tmpcp = rot(ph, 2, [128, 8], F32, "tmpc")

                        def do_block(t0, n, cl):
                            ntile = n // 128
                            ch0 = t0 // 64
                            nb = n // 64
                            hT = hTp.nxt()
                            for i in range(ntile):
                                norm_T(npools, xsrc, t0 + 128 * i, hT, 128 * i, mcol(1, cl), mcol(0, cl))
                            A.load(rope[0][:, 0:n], ROPEC[:, t0:t0 + n], rope[0])
                            A.load(rope[1][:, 0:n], ROPES[:, t0:t0 + n], rope[1])
                            P.release()

                            def fm(col0, M=128):
                                ps = pf.nxt()
                                A.mm([(ps[0:M, 0:n], w1[:, kt, col0:col0 + M], hT[:, kt, 0:n], kt == 0, kt == 7) for kt in range(8)],
                                     [w1] + hT.kb, [ps])
                                return ps

                            def finish(m, d, j, q, k, lf, escale, kscale, E=None, dirs=None, sh=0.0):
                                P.release()
                                if E is None:
                                    bT, Dt, E1, E2 = wk.nxt(), wk.nxt(), wk.nxt(), wk.nxt()
                                    decay_core(lf, n, d, escale, bT, Dt, E1, E2,
                                               [CCt[:, m, d, j, kind, ch0:ch0 + nb] for kind in range(3)], CCt, tmpcp.nxt(), sh=sh)
                                elif E == "none":
                                    E1 = E2 = None
                                else:
                                    E1, E2 = E
                                qh = bfp.nxt()
                                kh = bfp.nxt()
                                if E1 is None:
                                    A.copy("act", qh[:, 0:n], q[:, 0:n], [q], [qh])
                                    P.act(lambda e: e.mul(out=kh[:, 0:n], in_=k[:, 0:n], mul=kscale), [k], [kh])
                                else:
                                    A.tt("dve", qh[:, 0:n], q[:, 0:n], E1[:, 0:n], ALU.mult, [q, E1], [qh])
                                    A.stt(kh[:, 0:n], k[:, 0:n], kscale, E2[:, 0:n], ALU.mult, ALU.mult, [k, E2], [kh])
                                for dd in (dirs or [d]):
                                    A.store(QT[dd, m * 2 + j, :, t0:t0 + n], qh[:, 0:n], qh)
                                    A.store(KT[dd, m * 2 + j, :, t0:t0 + n], kh[:, 0:n], kh)
                                pk = ptk.nxt()
                                A.tr([(pk[:, i, :], kh[:, 128 * i:128 * (i + 1)]) for i in range(ntile)], identb[:], [kh, identb], [pk])
                                for dd in (dirs or [d]):
                                    A.copy("act", kst[dd][:, 0:ntile, m * 256 + j * 128:m * 256 + (j + 1) * 128], pk[:, 0:ntile, :],
                                           [pk], [kst[dd]])

                            for j in range(2):
                                psq = fm((0 + j) * 128)
                                qs = fq.nxt()
                                sgq = wk.nxt()
                                A.act(sgq[:, 0:n], psq[:, 0:n], AF.Sigmoid, [psq], [sgq, psq])
                                A.tt("dve", qs[:, 0:n], psq[:, 0:n], sgq[:, 0:n], ALU.mult, [psq, sgq], [qs])
                                fs = []
                                for d in range(2):
                                    psz = fm((2 + 2 * d + j) * 128)
                                    sg, f = wk.nxt(), wk.nxt()
                                    A.act(sg[:, 0:n], psz[:, 0:n], AF.Sigmoid, [psz], [sg])
                                    A.ts("dve", f[:, 0:n], sg[:, 0:n], lbt[:, 1, d, j:j + 1], lbt[:, 0, d, j:j + 1], ALU.mult, ALU.add,
                                         [sg, lbt], [f])
                                    fs.append(f)
                                for d in range(2):
                                    f = fs[d]
                                    lf, kk = wk.nxt(), wk.nxt()
                                    A.act(lf[:, 0:n], f[:, 0:n], AF.Ln, [f], [lf])
                                    A.act(kk[:, 0:n], f[:, 0:n], AF.Identity, [f, onescol], [kk], scale=-1.0, bias=onescol[:, 0:1])
                                    finish(HG, d, j, qs, kk, lf, 1.0, 1.0, sh=HG_SHIFT)
                            for j in range(2):
                                psq = fm((6 + j) * 128)
                                psk = fm((8 + j) * 128)
                                finish(ML, 0, j, psq, psk, None, 1.0, 0.125, E="none", dirs=[0, 1])
                            for j in range(2):
                                rr = []
                                for base in (10, 14):
                                    ps0 = fm((base + j) * 128)
                                    ps1 = fm((base + 2 + j) * 128)
                                    t1, t2 = wk.nxt(), wk.nxt()
                                    r = fq.nxt()
                                    A.tt("dve", t1[:, 0:n], ps0[:, 0:n], rope[0][:, 0:n], ALU.mult, [ps0, rope[0]], [t1])
                                    A.tt("dve", t2[:, 0:n], ps1[:, 0:n], rope[1][:, 0:n], ALU.mult, [ps1, rope[1]], [t2])
                                    A.tt("pool", r[:, 0:n], t1[:, 0:n], t2[:, 0:n], ALU.add, [t1, t2], [r])
                                    rr.append(r)
                                for d in range(2):
                                    finish(RT, d, j, rr[0], rr[1], None, 1.0, 0.125, E=(Ec[d][j][0], Ec[d][j][1]))
                            for d in range(2):
                                psa = fm(22 * 128 + 16 * d, M=16)
                                A.copy("act", ga[d][:, 0:n], psa[0:16, 0:n], [psa], [ga[d]])
                            for j in range(2):
                                psq = fm((18 + j) * 128)
                                psk = fm((20 + j) * 128)
                                qr, kr = fq.nxt(), fq.nxt()
                                A.copy("act", qr[:, 0:n], psq[:, 0:n], [psq], [qr])
                                A.copy("dve", kr[:, 0:n], psk[:, 0:n], [psk], [kr])
                                sgs = []
                                for d in range(2):
                                    psz = pf.nxt()
                                    A.mm([(psz[:, 0:n], wat[0:16, d, j * 128:(j + 1) * 128], ga[d][0:16, 0:n], True, True)], [wat, ga[d]], [psz])
                                    sg = wk.nxt()
                                    A.act(sg[:, 0:n], psz[:, 0:n], AF.Sigmoid, [psz, bat], [sg], bias=bat[:, d, j:j + 1])
                                    sgs.append(sg)
                                for d in range(2):
                                    sg = sgs[d]
                                    lf = wk.nxt()
                                    A.act(lf[:, 0:n], sg[:, 0:n], AF.Ln, [sg], [lf])
                                    finish(GL, d, j, qr, kr, lf, 1.0 / 16.0, 32.0 ** -0.5)
                            for dd in range(2):
                                A.store(KK[dd, t0:t0 + n, :].rearrange("(i p) x -> p i x", p=128), kst[dd][:, 0:ntile, :], kst[dd])

                        for (t0, n, cl) in blocks:
                            do_block(t0, n, cl)
                        P.flush()
                        if stop == 'P1a':
                            raise _Stop()

                    with ExitStack() as ph:
                        w1 = sbt(ph, [128, 8, NTM], BF16, "w1tm")
                        load_w_cast(w1, W1[l, :, NFM:NC1], 8)
                        hTp = rot(ph, 4, [128, 8, 128], BF16, "hT")
                        npools = (rot(ph, 4, [128, D], F32, "xt"), rot(ph, 2, [128, D], BF16, "jk"), rot(ph, 4, [128, 4], F32, "ss"),
                                  rot(ph, 4, [128, D], BF16, "xn"), rot(ph, 2, [128, 8, 128], BF16, "ptr", psum=True))
                        pt = rot(ph, 4, [128, 512], F32, "pt", psum=True)
                        psm = rot(ph, 2, [128, 64], F32, "psm", psum=True)
                        vstp = rot(ph, 4, [128, 16, VD], BF16, "vst")
                        vmlp = rot(ph, 4, [128, 4, VD], BF16, "vml")
                        vs1p = rot(ph, 4, [128, 4, VD], BF16, "vs1")
                        for tl_ in vstp.items + vmlp.items:
                            A.memset(tl_[:], 0.0, [tl_])
                            A.memset(tl_[:, :, 64:65], 1.0, [tl_], [tl_])
                        sgp = rot(ph, 6, [128, 512], F32, "sgp")
                        ggp = rot(ph, 4, [128, D], BF16, "ggp")
                        smp = rot(ph, 4, [128, 64], F32, "smp")

                        def do_tile(ti):
                            r0 = ti * 128
                            cl = 0 if r0 < CTX else 1
                            hT = hTp.nxt()
                            norm_T(npools, xsrc, r0, hT, 0, mcol(1, cl), mcol(0, cl))
                            P.release(keep=3)

                            def tm(col0, N):
                                ps = pt.nxt()
                                import os
                                if os.environ.get("EXPA"):
                                    A.mm([(ps[:, 0:128], w1[:, kt, col0:col0 + 128], hT[:, kt, :], kt == 0, kt == 7) for kt in range(8)], [w1] + hT.kb, [ps])
                                elif os.environ.get("EXPB"):
                                    A.mm([(ps[:, 0:256], hT[:, kt, :], w1[:, kt, col0:col0 + 256], kt == 0, kt == 7) for kt in range(8)], [w1] + hT.kb, [ps])
                                else:
                                    items = []
                                    for n0 in range(0, N, 256):
                                        n1 = min(N, n0 + 256)
                                        items += [(ps[:, n0:n1], hT[:, kt, :], w1[:, kt, col0 + n0:col0 + n1], kt == 0, kt == 7) for kt in range(8)]
                                    A.mm(items, [w1] + hT.kb, [ps])
                                return ps
                            vst, vml, vs1 = vstp.nxt(), vmlp.nxt(), vs1p.nxt()
                            import os
                            SKIP = os.environ.get("SKIP", "")
                            if "v" in SKIP:
                                return
                            psA = tm(0, 512)
                            if "1" in SKIP:
                                return
                            if "a" not in SKIP:
                                A.copy("act", vst[:, 0:4, 0:64], psA[:, 0:256].rearrange("p (h v) -> p h v", v=64), [psA], [vst])
                            if "b" not in SKIP:
                                A.copy("act", vml[:, :, 0:64], psA[:, 256:512].rearrange("p (h v) -> p h v", v=64), [psA], [vml])
                            if "2" in SKIP:
                                return
                            psB = tm(512, 512)
                            A.copy("dve", vst[:, 8:16, 0:64], psB[:, 0:512].rearrange("p (h v) -> p h v", v=64), [psB], [vst])
                            if "g" in SKIP:
                                return
                            gg = ggp.nxt()
                            psC = tm(1024, 512)
                            s1 = sgp.nxt()
                            A.act(s1[:], psC[:], AF.Sigmoid, [psC], [s1])
                            A.tt("dve", gg[:, 0:512], s1[:], ghr[:, 0:512], ALU.mult, [s1, ghr], [gg])
                            psD = tm(1536, 512)
                            s2 = sgp.nxt()
                            A.act(s2[:], psD[:], AF.Sigmoid, [psD], [s2, psD])
                            A.tt("dve", s2[:], psD[:], s2[:], ALU.mult, [psD, s2], [s2])
                            A.tt("pool", gg[:, 512:1024], s2[:], ghr[:, 512:1024], ALU.mult, [s2, ghr], [gg])
                            A.store(GG[r0:r0 + 128, :], gg[:], gg)
                            if "m" in SKIP:
                                return
                            psG = tm(2048, 16)
                            sm = smp.nxt()
                            A.tt("dve", sm[:, 0:16], psG[:, 0:16], mlbt[:], ALU.add, [psG, mlbt], [sm])
                            gv = sm[:, 0:16].rearrange("p (d g h) -> p d g h", d=2, g=2)
                            A.act(sm[:, 16:24].rearrange("p (d h) -> p d h", d=2), gv[:, :, 1, :], AF.Sigmoid, [sm], [sm])
                            A.act(sm[:, 24:32], sm[:, 16:24], AF.Ln, [sm], [sm])
                            pb = psm.nxt()
                            A.mm([(pb[:, 0:4], cst[:, 1, :], sm[:, 24:28], True, True),
                                  (pb[:, 4:8], cst[:, 2, :], sm[:, 28:32], True, True),
                                  (pb[:, 8:16], cst[:, 3, :], sm[:, 24:32], True, True),
                                  (pb[:, 16:24], cst[:, 4, :], sm[:, 24:32], True, True)], [sm, cst], [pb])
                            A.act(ALt[:, 2 * ti:2 * ti + 2, :], pb[:, 8:24].rearrange("p (c g) -> p c g", g=8), AF.Exp, [pb], [ALt, pb])
                            for hh_ in range(2):
                                A.copy("pool", ALP[64 * hh_:64 * hh_ + 64, 2 * ti:2 * ti + 2, :, :],
                                       ALt[64 * hh_:64 * hh_ + 64, 2 * ti:2 * ti + 2, :].rearrange("p c (d a b) -> p c d a b", d=2, a=2)[:, :, :, :, hh_],
                                       [ALt], [ALP])
                            A.tt("dve", sm[:, 32:40].rearrange("p (d h) -> p d h", d=2), gv[:, :, 0, :],
                                 pb[:, 0:8].rearrange("p (d h) -> p d h", d=2), ALU.subtract, [pb, sm], [sm, pb])
                            A.act(sm[:, 40:48], sm[:, 32:40], AF.Exp, [sm], [sm])
                            if "f" in SKIP:
                                return
                            A.act(sm[:, 48:56], pb[:, 0:8], AF.Exp, [pb], [sm, pb], scale=-1.0)
                            A.copy("dve", FLt[0:64, 2 * ti, :], sm[0:64, 48:56], [sm], [FLt])
                            A.copy("dve", FLt[0:64, 2 * ti + 1, :], sm[64:128, 48:56], [sm], [FLt])
                            if "w" in SKIP:
                                return
                            A.tt("dve", vst[:, 4:8, :], vml[:], sm[:, 40:44].unsqueeze(2).to_broadcast([128, 4, VD]), ALU.mult, [sm, vml], [vst])
                            A.tt("dve", vs1[:], vml[:], sm[:, 44:48].unsqueeze(2).to_broadcast([128, 4, VD]), ALU.mult, [sm, vml], [vs1])
                            A.store(VV[0, r0:r0 + 128, :, :], vst[:], vst)
                            A.store(VV[1, r0:r0 + 128, 4:8, :], vs1[:], vs1)

                        for ti in range(NT):
                            do_tile(ti)
                        P.flush()
                        if stop == 'P1b':
                            raise _Stop()

                    with ExitStack() as ph:
                        qTp = [rot(ph, 2, [128, 8, 2, 256], BF16, "qbd") for _ in range(2)]
                        for d_ in range(2):
                            for t_ in qTp[d_].items:
                                A.memset(t_[:], 0.0, [t_])
                        kTp = [rot(ph, 2, [128, 8, 256], BF16, "kT") for _ in range(2)]
                        ktp = [rot(ph, 2, [64, 4, 1024], BF16, "ktok") for _ in range(2)]
                        vvp = [rot(ph, 2, [64, 4, 16 * VD], BF16, "vv") for _ in range(2)]
                        vmp = rot(ph, 2, [64, 4, 4 * VD], BF16, "vm")
                        S = [[[sbt(ph, [128, VD], F32, "S") for _ in range(2)] for _ in range(4)] for _ in range(2)]
                        Sb = [[[sbt(ph, [128, VD], BF16, "Sb") for _ in range(2)] for _ in range(4)] for _ in range(2)]
                        for d in range(2):
                            for m in range(4):
                                for hp in range(2):
                                    A.memset(S[d][m][hp][:], 0.0, [S[d][m][hp]])
                        psA = Rot([Tl(t_[:, 0:256].rearrange("p (h v) -> p h v", v=64), "psAv") for t_ in
                                   [pst(ph, [64, 512], F32, "psA") for _ in range(4)]])
                        pso = Rot([Tl(t_[:, 0:4 * VD].rearrange("p (h v) -> p h v", v=VD), "psov") for t_ in
                                   [pst(ph, [64, 512], F32, "pso") for _ in range(2)]])
                        psS = Rot([Tl(t_[:, 0:4 * VD].rearrange("p (a v) -> p a v", v=2 * VD), "psSv") for t_ in
                                   [pst(ph, [128, 512], F32, "psS") for _ in range(2)]])
                        mask4 = sbt(ph, [64, 2, 4, 64], F32, "mask4")
                        for d_ in range(2):
                            A.copy("dve", mask4[:, d_, :, :], maskf[:, d_, :].unsqueeze(1).to_broadcast([64, 4, 64]), [maskf], [mask4])
                        mask4u = mask4[:].bitcast(mybir.dt.uint32)
                        Asbd = [rot(ph, 5, [64, 4, 64], BF16, "Asb") for _ in range(2)]
                        for dd_ in range(2):
                            for t_ in Asbd[dd_].items:
                                A.memset(t_[:], 0.0, [t_])
                        tmpS = rot(ph, 6, [128, VD], F32, "tmpS")
                        osb = rot(ph, 2, [64, D], F32, "osb")
                        nrm = rot(ph, 2, [64, 16], F32, "nrm")

                        def col(m, d, hp, kind, c):
                            if m == ML:
                                if kind == 0:
                                    return onescol[:, 0:1], onescol
                                return ALP[:, c, d, hp:hp + 1], ALP
                            return CCt[:, m, d, hp, kind, c:c + 1], CCt

                        def do_chunk(d, c, cc, qT, kT, ktok, vv, vm):
                            ts = slice(64 * cc, 64 * cc + 64)
                            ob = osb.nxt()
                            vs = []
                            for m in range(4):
                                if m == ML and d == 1:
                                    vs.append((vm, 0))
                                else:
                                    vs.append((vv, 4 * m * VD))
                            a4s = []
                            for m in range(4):
                                for hp in range(2):
                                    St, Sbt = S[d][m][hp], Sb[d][m][hp]
                                    c1, c1t = col(m, d, hp, 0, c)
                                    A.act(Sbt[:, :], St[:, :], AF.Identity, [St, c1t], [Sbt], scale=c1)
                                pa = psA.nxt()
                                A.mm([(pa[:, h, :], kT[:, 2 * m + h // 2, ts], qT[:, 2 * m + h // 2, h % 2, ts], True, True) for h in range(4)],
                                     [kT, qT], [pa])
                                a4 = Asbd[d].nxt()
                                A.tt("dve", a4[:], pa[:], mask4[:, d, :, :], ALU.mult, [pa, mask4], [a4])
                                a4s.append(a4)
                            pos, pSs = [], []
                            for m in range(4):
                                vsrc, vbase = vs[m]
                                a4 = a4s[m]
                                po = pso.nxt()
                                items = []
                                for h in range(4):
                                    items.append((po[:, h, :], a4[:, h, :], vsrc[0:64, cc, vbase + h * VD:vbase + (h + 1) * VD], True, False))
                                    items.append((po[:, h, :], qT[:, 2 * m + h // 2, h % 2, ts], Sb[d][m][h // 2][:, :], False, True))
                                A.mm(items, [a4, vsrc, qT, Sb[d][m][0], Sb[d][m][1]], [po])
                                pS = psS.nxt()
                                A.mm([(pS[:, hp, :], ktok[0:64, cc, m * 256 + hp * 128:m * 256 + (hp + 1) * 128],
                                       vsrc[0:64, cc, vbase + 2 * hp * VD:vbase + (2 * hp + 2) * VD], True, True) for hp in range(2)],
                                     [ktok, vsrc], [pS])
                                for hp in range(2):
                                    St = S[d][m][hp]
                                    tS = tmpS.nxt()
                                    c2, c2t = col(m, d, hp, 1, c)
                                    c3, c3t = col(m, d, hp, 2, c)
                                    A.act(tS[:, :], St[:, :], AF.Identity, [St, c2t], [tS], scale=c2)
                                    for hh in range(2):
                                        p0 = 64 * hh
                                        A.stt(St[p0:p0 + 64, :], pS[p0:p0 + 64, hp, hh * VD:(hh + 1) * VD], c3[p0:p0 + 64, :], tS[p0:p0 + 64, :],
                                              ALU.mult, ALU.add, [pS, tS, c3t], [St])
                                ov = ob[:, m * 256:(m + 1) * 256].rearrange("p (h v) -> p h v", v=64)
                                if m == HG:
                                    P.act(lambda e, ov=ov, pin=po[:, :, 0:64]: e.mul(out=ov, in_=pin, mul=float(np.exp(2.0 * HG_SHIFT))), [po], [ob])
                                elif m != ML:
                                    A.copy("act", ov, po[:, :, 0:64], [po], [ob])
                                else:
                                    nr = nrm.nxt()
                                    A.act(nr[:, 0:4], po[:, :, 64], AF.Abs, [po], [nr, po])
                                    A.tt("dve", nr[:, 4:8], nr[:, 0:4], FLt[0:64, c, d * 4:d * 4 + 4], ALU.max, [nr, FLt], [nr])
                                    A.recip(nr[:, 8:12], nr[:, 4:8], [nr], [nr])
                                    A.tt("dve", ov, po[:, :, 0:64], nr[:, 8:12].unsqueeze(2).to_broadcast([64, 4, 64]), ALU.mult, [nr, po], [ob, po])
                            A.store(OO[d, 64 * c:64 * c + 64, :], ob[:], ob)

                        grp_order = [list(range(NG)), [0] + list(range(NG - 1, 0, -1))]
                        cc_order = [[0, 1, 2, 3], [3, 2, 1, 0]]
                        for gi in range(NG):
                            grp = []
                            for d in range(2):
                                g = grp_order[d][gi]
                                tg0 = 256 * g
                                qT, kT, ktok, vv = qTp[d].nxt(), kTp[d].nxt(), ktp[d].nxt(), vvp[d].nxt()
                                A.load(qT[0:64, :, 0, :], QT[d, :, 0:64, tg0:tg0 + 256].rearrange("m p t -> p m t"), qT)
                                A.load(qT[64:128, :, 1, :], QT[d, :, 64:128, tg0:tg0 + 256].rearrange("m p t -> p m t"), qT)
                                A.load(kT[:], KT[d, :, :, tg0:tg0 + 256].rearrange("m p t -> p m t"), kT)
                                A.load(ktok[:], KK[d, tg0:tg0 + 256, :].rearrange("(c p) x -> p c x", p=64), ktok)
                                A.load(vv[:], VV[0, tg0:tg0 + 256, :, :].rearrange("(c p) h v -> p c (h v)", p=64), vv)
                                vm = None
                                if d == 1:
                                    vm = vmp.nxt()
                                    A.load(vm[:], VV[1, tg0:tg0 + 256, 4:8, :].rearrange("(c p) h v -> p c (h v)", p=64), vm)
                                grp.append((g, qT, kT, ktok, vv, vm))
                            for k_ in range(4):
                                for d in range(2):
                                    g, qT, kT, ktok, vv, vm = grp[d]
                                    cc = cc_order[d][k_]
                                    P.release()
                                    do_chunk(d, 4 * g + cc, cc, qT, kT, ktok, vv, vm)
                        P.flush()
                        if stop == 'P2':
                            raise _Stop()

                with ExitStack() as ph:
                    g1 = [sbt(ph, [128, D], F32, "g1") for _ in range(2)]
                    g2 = [sbt(ph, [128, D], F32, "g2") for _ in range(2)]

                    def gtiles(gi, dst):
                        with ExitStack() as ph2:
                            wad = rot(ph2, 2, [128, 8, 512], F32, "wad")
                            bg = sbt(ph2, [128, D], F32, "bg")
                            pg = rot(ph2, 2, [128, 512], F32, "pg", psum=True)
                            crep = [sbt(ph2, [128, 8, 128], F32, "crep") for _ in range(2)]
                            for cl in range(2):
                                A.copy("dve", crep[cl][:], scs[:, :, cl:cl + 1].to_broadcast([128, 8, 128]), [scs], [crep[cl]])
                            A.load(bg[:], BADA_G[l, gi, :, :], bg)
                            vec = (2, 5)[gi]
                            for half in range(2):
                                blk = vec * 2 + half
                                w = wad.nxt()
                                A.load(w[:], W_ADA[l, :, blk * 512:(blk + 1) * 512].rearrange("(kt p) n -> p kt n", p=128), w)
                                for cl in range(2):
                                    ps = pg.nxt()
                                    A.mm([(ps[:], crep[cl][:, kt, :], w[:, kt, :], kt == 0, kt == 7) for kt in range(8)], [w, crep[cl]], [ps])
                                    A.tt("dve", dst[cl][:, half * 512:(half + 1) * 512], ps[:], bg[:, half * 512:(half + 1) * 512], ALU.add,
                                         [ps, bg], [dst[cl]])
                            P.flush()
                            if stop == 'P3a':
                                raise _Stop()
                    gtiles(0, g1)
                    wf1 = sbt(ph, [128, 8, DFF], BF16, "wf1")
                    load_w_cast(wf1, W_FF1[l, :, :], 8)
                    with ExitStack() as ph2:
                        wo = sbt(ph2, [128, 8, D], BF16, "wo")
                        load_w_cast(wo, W_OUT[l, :, :], 8)
                        o0p = rot(ph2, 3, [128, D], F32, "o0")
                        o1p = rot(ph2, 3, [128, D], F32, "o1")
                        gglp = rot(ph2, 3, [128, D], BF16, "ggl")
                        xp = rot(ph2, 4, [128, D], F32, "x3")
                        sqp = rot(ph2, 2, [128, D], F32, "sq")
                        ssp = rot(ph2, 3, [128, 48], F32, "ss3")
                        yp = rot(ph2, 3, [128, D], BF16, "y")
                        yTp = rot(ph2, 3, [128, 8, 128], BF16, "yT")
                        ptr = rot(ph2, 2, [128, 8, 128], BF16, "ptr3", psum=True)
                        pop = rot(ph2, 4, [128, 512], F32, "pop", psum=True)
                        tp = rot(ph2, 4, [128, 512], F32, "t3")

                        def do_tile3(ti):
                            r0 = ti * 128
                            cl = 0 if r0 < CTX else 1
                            o0, o1, ggl, xt = o0p.nxt(), o1p.nxt(), gglp.nxt(), xp.nxt()
                            A.load(o0[:], OO[0, r0:r0 + 128, :], o0)
                            A.load(o1[:], OO[1, r0:r0 + 128, :], o1)
                            A.load(ggl[:], GG[r0:r0 + 128, :], ggl)
                            A.load(xt[:], xsrc[r0:r0 + 128, :], xt)
                            P.release(keep=1)
                            A.tt("pool", o0[:], o0[:], o1[:], ALU.add, [o0, o1], [o0])
                            sq, ss = sqp.nxt(), ssp.nxt()
                            A.act(sq[:], o0[:], AF.Square, [o0], [sq])
                            A.reduce(ss[:, 0:16], sq[:].rearrange("p (h v) -> p h v", v=64), [sq], [ss])
                            A.act(ss[:, 16:32], ss[:, 0:16], AF.Ln, [ss, epscol], [ss], scale=1.0 / 64, bias=epscol[:, 0:1])
                            A.act(ss[:, 32:48], ss[:, 16:32], AF.Exp, [ss], [ss], scale=-0.5)
                            o3 = o0[:].rearrange("p (h v) -> p h v", v=64)
                            A.tt("dve", o3, o3, ss[:, 32:48].unsqueeze(2).to_broadcast([128, 16, 64]), ALU.mult, [o0, ss], [o0])
                            y = yp.nxt()
                            A.tt("pool", y[:], o0[:], ggl[:], ALU.mult, [o0, ggl], [y])
                            pt = ptr.nxt()
                            A.tr([(pt[:, kt, :], y[:, kt * 128:(kt + 1) * 128]) for kt in range(8)], identb[:], [y, identb], [pt])
                            yT = yTp.nxt()
                            A.copy("act", yT[:, 0:4, :], pt[:, 0:4, :], [pt], [yT])
                            A.copy("dve", yT[:, 4:8, :], pt[:, 4:8, :], [pt], [yT])
                            for nb in range(2):
                                ps = pop.nxt()
                                A.mm([(ps[:], yT[:, kt, :], wo[:, kt, nb * 512:(nb + 1) * 512], kt == 0, kt == 7) for kt in range(8)], [yT, wo], [ps])
                                t = tp.nxt()
                                A.tt("dve", t[:], ps[:], g1[cl][:, nb * 512:(nb + 1) * 512], ALU.mult, [ps, g1[cl]], [t])
                                A.tt("pool", xt[:, nb * 512:(nb + 1) * 512], xt[:, nb * 512:(nb + 1) * 512], t[:], ALU.add, [t, xt], [xt])
                            A.store(XS[r0:r0 + 128, :], xt[:], xt)

                        for ti in range(NT):
                            do_tile3(ti)
                        P.flush()
                        if stop == 'P3a':
                            raise _Stop()

                    gtiles(1, g2)
                    with ExitStack() as ph2:
                        wf2 = sbt(ph2, [128, 32, D], BF16, "wf2")
                        load_w_cast(wf2, W_FF2[l, :, :], 32)
                        hTp = rot(ph2, 2, [128, 8, 256], BF16, "h2T")
                        npools = (rot(ph2, 3, [128, D], F32, "xt"), rot(ph2, 1, [128, D], BF16, "jk"), rot(ph2, 2, [128, 4], F32, "ss"),
                                  rot(ph2, 2, [128, D], BF16, "xn"), rot(ph2, 2, [128, 8, 128], BF16, "ptr", psum=True))
                        uTp = rot(ph2, 1, [128, 32, 256], BF16, "uT")
                        pu = Rot([Tl(t_[:, 0:256], "pus") for t_ in [pst(ph2, [128, 512], F32, "pu") for _ in range(3)]])
                        po2 = rot(ph2, 2, [128, 512], F32, "po2", psum=True)
                        sqp = rot(ph2, 3, [128, 256], F32, "sq2")
                        tp = rot(ph2, 1, [128, 512], F32, "t4")

                        def do_blk(bi):
                            cl = 0 if bi * 256 < CTX else 1
                            hT = hTp.nxt()
                            xts = []
                            for i in range(2):
                                xts.append(norm_T(npools, XS, bi * 256 + 128 * i, hT, 128 * i, mcol(3, cl), mcol(2, cl)))
                                if i == 0:
                                    P.release()
                            uT = uTp.nxt()
                            for fb in range(32):
                                ps = pu.nxt()
                                A.mm([(ps[:], wf1[:, kt, fb * 128:(fb + 1) * 128], hT[:, kt, :], kt == 0, kt == 7) for kt in range(8)], [wf1] + hT.kb, [ps])
                                sq = sqp.nxt()
                                A.act(sq[:], ps[:], AF.Square, [ps], [sq])
                                A.stt(uT[:, fb, :], ps[:], 0.0, sq[:], ALU.is_gt, ALU.mult, [ps, sq], [uT])
                            for i in range(2):
                                xt = xts[i]
                                for nb in range(2):
                                    ps = po2.nxt()
                                    A.mm([(ps[:], uT[:, fb, 128 * i:128 * (i + 1)], wf2[:, fb, nb * 512:(nb + 1) * 512], fb == 0, fb == 31)
                                          for fb in range(32)], [uT, wf2], [ps])
                                    t = tp.nxt()
                                    A.tt("dve", t[:], ps[:], g2[cl][:, nb * 512:(nb + 1) * 512], ALU.mult, [ps, g2[cl]], [t])
                                    A.tt("pool", xt[:, nb * 512:(nb + 1) * 512], xt[:, nb * 512:(nb + 1) * 512], t[:], ALU.add, [t, xt], [xt])
                                r0 = bi * 256 + 128 * i
                                A.store(XS[r0:r0 + 128, :], xt[:], xt)

                        for bi in range(T // 256):
                            do_blk(bi)
                        P.flush()
                        if stop == 'P3b':
                            raise _Stop()

        except _Stop:
            P.flush(final=True)
            build.ninst = P.ninst
            gs.pop_all()
            return nc
        with ExitStack() as ph:
            gf = sbt(ph, [128, D], F32, "gf")
            A.load(gf[:], GFIN[:, :], gf)
            xp = rot(ph, 3, [128, D], F32, "xf")
            jp = rot(ph, 1, [128, D], BF16, "jkf")
            sp_ = rot(ph, 2, [128, 4], F32, "ssf")
            for ti in range(LAT // 128):
                r0 = CTX + ti * 128
                xt, jk, ss = xp.nxt(), jp.nxt(), sp_.nxt()
                A.load(xt[:], XS[r0:r0 + 128, :], xt)
                P.release()
                A.act(jk[:], xt[:], AF.Square, [xt], [jk, ss], accum_out=ss[:, 0:1])
                A.act(ss[:, 1:2], ss[:, 0:1], AF.Ln, [ss, epscol], [ss], scale=1.0 / D, bias=epscol[:, 0:1])
                A.act(ss[:, 2:3], ss[:, 1:2], AF.Exp, [ss], [ss], scale=-0.5)
                A.stt(xt[:], xt[:], ss[:, 2:3], gf[:], ALU.mult, ALU.mult, [xt, ss, gf], [xt])
                A.store(OUT[ti * 128:(ti + 1) * 128, :], xt[:], xt)
            P.flush(final=True)
        build.ninst = P.ninst
    return nc


_IN_LAYOUT = (
    ('hg_q', 256), ('hg_f_fwd', 256), ('hg_f_bwd', 256), ('hg_i', 256), ('hg_g', 256),
    ('ml_q', 256), ('ml_k', 256), ('ml_v', 256), ('ml_if', 16), ('ml_o', 256),
    ('rt_q', 256), ('rt_k', 256), ('rt_v', 256), ('rt_g', 256),
    ('gl_q', 128), ('gl_k', 128), ('gl_v', 256),
    ('gl_a_fwd', 16), ('gl_a_bwd', 16), ('gl_g', 256),
)


def _col_ranges():
    off = {}
    o = 0
    for nme, s in _IN_LAYOUT:
        off[nme] = (o, s)
        o += s
    return off


def _w1_layout(w_in):
    off = _col_ranges()
    dep = w_in.shape[0]
    out = np.zeros((dep, D, NC1), np.float32)

    def cols(nme):
        o, s = off[nme]
        return w_in[:, :, o:o + s]
    perm = np.zeros(256, np.int64)
    for h in range(4):
        for dd in range(64):
            r = dd % 32
            partner = dd + 16 if r < 16 else dd - 16
            perm[h * 64 + dd] = h * 64 + partner

    def pad_gla(a):
        p = np.zeros((dep, D, 256), np.float32)
        for h in range(4):
            p[:, :, h * 64:h * 64 + 32] = a[:, :, h * 32:(h + 1) * 32]
        return p
    fmc = [cols('hg_q'), cols('hg_f_fwd'), cols('hg_f_bwd'), cols('ml_q'), cols('ml_k'),
           cols('rt_q'), cols('rt_q')[:, :, perm], cols('rt_k'), cols('rt_k')[:, :, perm],
           pad_gla(cols('gl_q')), pad_gla(cols('gl_k')), cols('gl_a_fwd'), cols('gl_a_bwd')]
    tmc = [cols('hg_i'), cols('ml_v'), cols('rt_v'), cols('gl_v'), cols('hg_g'), cols('ml_o'), cols('rt_g'), cols('gl_g'), cols('ml_if')]
    o = 0
    for a in fmc + tmc:
        out[:, :, o:o + a.shape[2]] = a
        o += a.shape[2]
    assert o == NC1
    return out


def _rope_tables(T):
    LATn = T - CTX
    tl = np.arange(LATn)
    row = (tl // 64).astype(np.float32)
    colp = (tl % 64).astype(np.float32)
    inv = (np.float32(10000.0) ** (-np.arange(16, dtype=np.float32) / np.float32(16))).astype(np.float32)
    cosT = np.ones((128, T), np.float32)
    sinT = np.zeros((128, T), np.float32)
    for p in range(128):
        dd = p % 64
        pos = row if dd < 32 else colp
        ang = (pos * inv[dd % 16]).astype(np.float32)
        sgn = -1.0 if (dd % 32) < 16 else 1.0
        cosT[p, CTX:] = np.cos(ang).astype(np.float32)
        sinT[p, CTX:] = (sgn * np.sin(ang)).astype(np.float32)
    return cosT, sinT


def _consts():
    c = np.zeros((128, 8, 128), np.float32)
    s = np.arange(128)[:, None]
    t = np.arange(128)[None, :]
    same = (s // 64) == (t // 64)
    c[:, 0, :] = (s == t)
    c[:, 1, :] = same & (s <= t)
    c[:, 2, :] = same & (s >= t)
    c[:, 3, :] = (s < 64) & (t >= 0)
    c[:, 4, :] = (s >= 64) & (t >= 0)
    c[:64, 5, :64] = (s[:64] <= t[:, :64])
    c[:64, 6, :64] = (s[:64] >= t[:, :64])
    return c


def make_shared(inp, T):
    dep = inp['w_ada'].shape[0]
    f32 = np.float32
    sh = {}
    sh['w_ada'] = np.ascontiguousarray(inp['w_ada'], f32)
    b_ada = np.asarray(inp['b_ada'], f32)
    bc = np.zeros((dep, 128, 4, 8), f32)
    for which, vec in enumerate((0, 1, 3, 4)):
        bc[:, :, which, :] = b_ada[:, vec * D:(vec + 1) * D].reshape(dep, 8, 128).transpose(0, 2, 1)
    sh['bada_col'] = bc
    bg = np.zeros((dep, 2, 128, D), f32)
    for gi, vec in enumerate((2, 5)):
        bg[:, gi, :, :] = b_ada[:, None, vec * D:(vec + 1) * D]
    sh['bada_g'] = bg
    sh['w1'] = _w1_layout(np.asarray(inp['w_in'], f32))
    sh['ghr'] = np.ascontiguousarray(np.broadcast_to(np.asarray(inp['g_heads'], f32)[:, None, :], (dep, 128, D)))
    hl = np.asarray(inp['hgrn_lb_logits'], f32)
    sh['hgl'] = np.ascontiguousarray(hl.reshape(dep, 2, 2, 128).transpose(3, 0, 1, 2))
    mb = np.asarray(inp['ml_gate_bias'], f32).reshape(dep, 16)
    sh['mlb'] = np.ascontiguousarray(np.broadcast_to(mb[:, None, :], (dep, 128, 16)))
    rl = np.asarray(inp['rt_decay_logit'], f32)
    rt = np.zeros((128, dep, 2, 2), f32)
    for j in range(2):
        for hh in range(2):
            rt[hh * 64:(hh + 1) * 64, :, :, j] = rl[None, :, :, 2 * j + hh]
    sh['rtl'] = rt
    wa = np.asarray(inp['gla_w_a'], f32)
    wap = np.zeros((dep, 2, 16, 256), f32)
    ba = np.asarray(inp['gla_b_a'], f32)
    bap = np.zeros((dep, 2, 256), f32)
    for h in range(4):
        wap[:, :, :, h * 64:h * 64 + 32] = wa[:, :, :, h * 32:(h + 1) * 32]
        bap[:, :, h * 64:h * 64 + 32] = ba[:, :, h * 32:(h + 1) * 32]
    sh['wa'] = wap
    sh['ba'] = np.ascontiguousarray(bap.reshape(dep, 2, 2, 128).transpose(3, 0, 1, 2))
    sh['w_out'] = np.ascontiguousarray(inp['w_out'], f32)
    sh['w_ff1'] = np.ascontiguousarray(inp['w_ff1'], f32)
    sh['w_ff2'] = np.ascontiguousarray(inp['w_ff2'], f32)
    sh['gfin'] = np.ascontiguousarray(np.broadcast_to(np.asarray(inp['g_final'], f32)[None, :], (128, D)))
    c, s = _rope_tables(T)
    sh['ropec'] = c
    sh['ropes'] = s
    sh['consts'] = _consts()
    return sh


def make_core(inp, b):
    f32 = np.float32
    m = {}
    m['xin'] = np.ascontiguousarray(np.concatenate([np.asarray(inp['ctx'][b], f32), np.asarray(inp['x'][b], f32)], axis=0))
    cc = np.zeros((128, 8, 2), f32)
    cc[:, :, 0] = np.asarray(inp['c_ctx'], f32).reshape(8, 128).T
    cc[:, :, 1] = np.asarray(inp['c'][b], f32).reshape(8, 128).T
    m['cc'] = cc
    return m


_CACHE = {}


def kernel(**inputs):
    x = inputs['x']
    B, LAT, _ = x.shape
    T = CTX + LAT
    if LAT not in _CACHE:
        _CACHE[LAT] = build(LAT)
    nc = _CACHE[LAT]
    sh = make_shared(inputs, T)
    in_maps = []
    for b in range(B):
        m = dict(sh)
        m.update(make_core(inputs, b))
        in_maps.append(m)
    res = run_bass_kernel_spmd(nc, in_maps, core_ids=list(range(B)))
    return np.stack([np.asarray(r["out"], np.float32) for r in res.results], axis=0)
```

```python
import numpy as np
from contextlib import ExitStack
import concourse.bass as bass
import concourse.mybir as mybir
from concourse.bass_utils import run_bass_kernel_spmd

F32 = mybir.dt.float32
BF16 = mybir.dt.bfloat16
AF = mybir.ActivationFunctionType
ALU = mybir.AluOpType
AX = mybir.AxisListType

D = 1024
CTX = 256
DEPTH = 2
DFF = 4096
VD = 66
NFM = 22 * 128 + 32
NTM = 2064
NC1 = NFM + NTM
EPS = 1e-6
ENGS = ("pe", "act", "dve", "pool", "sp")
HG, ML, RT, GL = 0, 1, 2, 3
HG_SHIFT = 20.0


class Buf:
    __slots__ = ("name", "last_write", "reads", "dsem", "dcount")

    def __init__(self, name):
        self.name = name
        self.last_write = None
        self.reads = []
        self.dsem = None
        self.dcount = 0


class Tl:
    def __init__(self, t, name, b=None):
        self.t = t
        self.b = b if b is not None else Buf(name)

    def __getitem__(self, i):
        return self.t[i]


def _b(x):
    return x.b if isinstance(x, Tl) else x


class Op:
    __slots__ = ("eng", "fn", "deps", "is_dma", "slot", "ticket", "signal", "retired", "seq")

    def __init__(self, eng, fn, is_dma, slot):
        self.eng = eng
        self.fn = fn
        self.deps = []
        self.is_dma = is_dma
        self.slot = slot
        self.ticket = None
        self.signal = False
        self.retired = False


class Prog:
    def __init__(self, nc, stack):
        self.nc = nc
        self.stack = stack
        self.ops = []
        self.esem = {e: stack.enter_context(nc.semaphore("sem_" + e)) for e in ENGS}
        self.ecount = {e: 0 for e in ENGS}
        self.waited = {e: {} for e in ENGS}
        self.pending_bar = {e: [] for e in ENGS}
        self.sempool = []
        self.nsem = 0
        self.engobj = {"pe": nc.tensor, "act": nc.scalar, "dve": nc.vector, "pool": nc.gpsimd, "sp": nc.sync}
        self.ninst = 0
        self.deferred = []
        self.seq = 0

    def defer_dma(self, fn, slot, reads=(), writes=(), eng="sp"):
        self.deferred.append((fn, slot, reads, writes, eng))

    def release(self, keep=0):
        n = len(self.deferred) - keep
        if n <= 0:
            return
        dd = self.deferred[:n]
        self.deferred = self.deferred[n:]
        for fn, slot, reads, writes, eng in dd:
            self.dma(fn, slot, reads, writes, eng)

    def add(self, eng, fn, reads=(), writes=(), is_dma=False, slot=None):
        op = Op(eng, fn, is_dma, slot)
        deps = set()
        for x in reads:
            b = _b(x)
            if b.last_write is not None:
                deps.add(b.last_write)
        for x in writes:
            b = _b(x)
            if b.last_write is not None:
                deps.add(b.last_write)
            for r in b.reads:
                deps.add(r)
        dl = [d for d in deps if (not d.retired) and not (eng == "pe" and d.eng == "pe" and not d.is_dma)]
        best = {}
        keep = []
        for d in dl:
            if d.is_dma:
                keep.append(d)
            elif d.eng not in best or best[d.eng].seq < d.seq:
                best[d.eng] = d
        op.deps = keep + list(best.values())
        self.seq += 1
        op.seq = self.seq
        for x in reads:
            _b(x).reads.append(op)
        for x in writes:
            b = _b(x)
            b.last_write = op
            b.reads = []
        self.ops.append(op)
        return op

    def pe(self, fn, reads=(), writes=()):
        return self.add("pe", fn, reads, writes)

    def act(self, fn, reads=(), writes=()):
        return self.add("act", fn, reads, writes)

    def dve(self, fn, reads=(), writes=()):
        return self.add("dve", fn, reads, writes)

    def pool(self, fn, reads=(), writes=()):
        return self.add("pool", fn, reads, writes)

    def dma(self, fn, slot, reads=(), writes=(), eng="sp"):
        return self.add(eng, fn, reads, writes, is_dma=True, slot=_b(slot))

    def flush(self, final=False):
        nc = self.nc
        self.release()
        ops = self.ops
        self.ops = []
        for op in ops:
            for d in op.deps:
                d.signal = True
        last = {}
        for op in ops:
            if not op.is_dma:
                last[op.eng] = op
        for op in last.values():
            op.signal = True
        slots = []
        swbar = []
        for op in ops:
            if op.is_dma and op.eng == "pool":
                sem = self.stack.enter_context(nc.semaphore("sw%d" % self.nsem))
                self.nsem += 1
                op.ticket = (sem, 16)
                swbar.append(op.ticket)
            elif op.is_dma:
                s = op.slot
                if s.dsem is None:
                    if self.sempool:
                        s.dsem, s.dcount = self.sempool.pop()
                    else:
                        s.dsem = self.stack.enter_context(nc.semaphore("ds%d" % self.nsem))
                        self.nsem += 1
                        s.dcount = 0
                    slots.append(s)
                s.dcount += 16
                op.ticket = (s.dsem, s.dcount)
            elif op.signal:
                self.ecount[op.eng] += 1
                op.ticket = (self.esem[op.eng], self.ecount[op.eng])
        per = {e: [] for e in ENGS}
        for op in ops:
            per[op.eng].append(op)
        bar = [last[e].ticket for e in last] + [(s.dsem, s.dcount) for s in slots] + swbar

        def run(ename, eng):
            waited = self.waited[ename]

            def w(sem, val):
                k = id(sem)
                if waited.get(k, 0) < val:
                    eng.wait_ge(sem, val)
                    waited[k] = val

            if per[ename] or final:
                for sem, val in self.pending_bar[ename]:
                    w(sem, val)
                self.pending_bar[ename] = []
            for op in per[ename]:
                need = {}
                for d in op.deps:
                    sem, val = d.ticket
                    k = id(sem)
                    if k not in need or need[k][1] < val:
                        need[k] = (sem, val)
                for sem, val in need.values():
                    w(sem, val)
                ins = op.fn(eng)
                self.ninst += 1
                if op.is_dma:
                    ins.then_inc(op.ticket[0], 16)
                elif op.signal:
                    ins.then_inc(op.ticket[0], 1)
            if final and ename == "sp":
                for sem, val in bar:
                    w(sem, val)

        with nc.Block() as block:
            @block.tensor
            def _(eng):
                run("pe", eng)

            @block.scalar
            def _(eng):
                run("act", eng)

            @block.vector
            def _(eng):
                run("dve", eng)

            @block.gpsimd
            def _(eng):
                run("pool", eng)

            @block.sync
            def _(eng):
                run("sp", eng)

        for e in ENGS:
            self.pending_bar[e].extend(bar)
        for op in ops:
            op.retired = True
        for s in slots:
            self.sempool.append((s.dsem, s.dcount))
            s.dsem = None


class Rot:
    def __init__(self, items):
        self.items = items
        self.i = 0

    def nxt(self):
        r = self.items[self.i % len(self.items)]
        self.i += 1
        return r


class Em:
    def __init__(self, P):
        self.P = P

    def act(self, out, in_, func, R, W, **kw):
        self.P.act(lambda e: e.activation(out=out, in_=in_, func=func, **kw), R, W)

    def tt(self, eng, out, in0, in1, op, R, W):
        self.P.add(eng, lambda e: e.tensor_tensor(out=out, in0=in0, in1=in1, op=op), R, W)

    def ts(self, eng, out, in0, s1, s2, op0, op1, R, W):
        if op1 is None:
            self.P.add(eng, lambda e: e.tensor_scalar(out=out, in0=in0, scalar1=s1, scalar2=None, op0=op0), R, W)
        else:
            self.P.add(eng, lambda e: e.tensor_scalar(out=out, in0=in0, scalar1=s1, scalar2=s2, op0=op0, op1=op1), R, W)

    def stt(self, out, in0, scalar, in1, op0, op1, R, W):
        self.P.dve(lambda e: e.scalar_tensor_tensor(out=out, in0=in0, scalar=scalar, in1=in1, op0=op0, op1=op1), R, W)

    def copy(self, eng, out, in_, R, W):
        if eng == "act":
            self.P.act(lambda e: e.copy(out=out, in_=in_), R, W)
        else:
            self.P.add(eng, lambda e: e.tensor_copy(out=out, in_=in_), R, W)

    def memset(self, ap, val, W, R=()):
        self.P.pool(lambda e: e.memset(ap, val), R, W)

    def load(self, out, in_, slot, W=None):
        self.P.dma(lambda e: e.dma_start(out=out, in_=in_), slot, writes=[slot] if W is None else W)

    def loadc(self, out, in_, slot):
        self.P.dma(lambda e: e.dma_start(out=out, in_=in_), slot, writes=[slot], eng="pool")

    def store(self, out, in_, slot):
        self.P.defer_dma(lambda e: e.dma_start(out=out, in_=in_), slot, reads=[slot])

    def mm(self, items, R, W):
        def f(e):
            r = None
            for (o, l, rh, st, sp) in items:
                r = e.matmul(o, lhsT=l, rhs=rh, start=st, stop=sp)
            return r
        self.P.pe(f, R, W)

    def tr(self, items, ident, R, W):
        def f(e):
            r = None
            for (o, i) in items:
                r = e.transpose(o, i, ident)
            return r
        self.P.pe(f, R, W)

    def scan(self, out, d0, d1, R, W):
        self.P.dve(lambda e: e.tensor_tensor_scan(out=out, data0=d0, data1=d1, initial=0.0, op0=ALU.mult, op1=ALU.add), R, W)

    def recip(self, out, in_, R, W):
        self.P.dve(lambda e: e.reciprocal(out=out, in_=in_), R, W)

    def reduce(self, out, in_, R, W):
        self.P.dve(lambda e: e.tensor_reduce(out=out, in_=in_, axis=AX.X, op=ALU.add), R, W)

    def sss(self, out, in_, scalar, op, R, W):
        self.P.dve(lambda e: e.tensor_single_scalar(out=out, in_=in_, scalar=scalar, op=op), R, W)


class _Stop(Exception):
    pass


def build(LAT, depth=DEPTH, dbg=False, stop=None):
    T = CTX + LAT
    NT = T // 128
    NCH = T // 64
    NG = T // 256
    nc = bass.Bass("TRN2", target_bir_lowering=False)

    def din(name, shape, dt=F32):
        return nc.dram_tensor(name, list(shape), dt, kind="ExternalInput").ap()

    def dscr(name, shape, dt):
        return nc.dram_tensor(name, list(shape), dt, kind="ExternalOutput" if dbg else "Internal").ap()

    XIN = din("xin", [T, D])
    CC_IN = din("cc", [128, 8, 2])
    W_ADA = din("w_ada", [DEPTH, D, 6 * D])
    BADA_COL = din("bada_col", [DEPTH, 128, 4, 8])
    BADA_G = din("bada_g", [DEPTH, 2, 128, D])
    W1 = din("w1", [DEPTH, D, NC1])
    GHR = din("ghr", [DEPTH, 128, D])
    HGL = din("hgl", [128, DEPTH, 2, 2])
    MLB = din("mlb", [DEPTH, 128, 16])
    RTL = din("rtl", [128, DEPTH, 2, 2])
    WA = din("wa", [DEPTH, 2, 16, 256])
    BA = din("ba", [128, DEPTH, 2, 2])
    W_OUT = din("w_out", [DEPTH, D, D])
    W_FF1 = din("w_ff1", [DEPTH, D, DFF])
    W_FF2 = din("w_ff2", [DEPTH, DFF, D])
    GFIN = din("gfin", [128, D])
    ROPEC = din("ropec", [128, T])
    ROPES = din("ropes", [128, T])
    CONSTS = din("consts", [128, 8, 128])
    OUT = nc.dram_tensor("out", [LAT, D], F32, kind="ExternalOutput").ap()

    XS = dscr("xs", [T, D], F32)
    QT = dscr("qt", [2, 8, 128, T], BF16)
    KT = dscr("kt", [2, 8, 128, T], BF16)
    KK = dscr("kk", [2, T, 1024], BF16)
    VV = dscr("vv", [2, T, 16, VD], BF16)
    GG = dscr("gg", [T, D], BF16)
    OO = dscr("oo", [2, T, D], F32)
    HT = dscr("ht", [NT, 128, 8, 128], BF16)

    with ExitStack() as gs:
        P = Prog(nc, gs)
        A = Em(P)
        cnt = [0]

        def sbt(st, shape, dt=F32, name=None):
            cnt[0] += 1
            nm = "%s_%d" % (name or "t", cnt[0])
            return Tl(st.enter_context(nc.sbuf_tensor(nm, list(shape), dt)), nm)

        def pst(st, shape, dt=F32, name=None):
            cnt[0] += 1
            nm = "%s_%d" % (name or "p", cnt[0])
            return Tl(st.enter_context(nc.psum_tensor(nm, list(shape), dt)), nm)

        def rot(st, n, shape, dt=F32, name=None, psum=False):
            return Rot([(pst if psum else sbt)(st, shape, dt, name) for _ in range(n)])

        def sub(tl, ap, name):
            return Tl(ap, name)

        cst = sbt(gs, [128, 8, 128], F32, "cst")
        A.load(cst[:], CONSTS[:, :, :], cst)
        identb = sbt(gs, [128, 128], BF16, "identb")
        A.copy("dve", identb[:], cst[:, 0, :], [cst], [identb])
        maskf = sbt(gs, [64, 2, 64], F32, "maskf")
        A.copy("dve", maskf[:], cst[0:64, 5:7, 0:64], [cst], [maskf])
        masku = maskf[:].bitcast(mybir.dt.uint32)
        rmask = sbt(gs, [128, 512], F32, "rmask")
        A.memset(rmask[:], 1.0, [rmask])
        A.memset(rmask[:].rearrange("p (c t) -> p c t", t=64)[:, :, 0:1], 0.0, [rmask], [rmask])
        onescol = sbt(gs, [128, 1], F32, "onescol")
        A.memset(onescol[:], 1.0, [onescol])
        zcol = sbt(gs, [128, 1], F32, "zcol")
        A.memset(zcol[:], 0.0, [zcol])
        shcol = sbt(gs, [128, 2], F32, "shcol")
        A.memset(shcol[:, 0:1], -HG_SHIFT, [shcol])
        A.memset(shcol[:, 1:2], HG_SHIFT, [shcol], [shcol])
        epscol = sbt(gs, [128, 1], F32, "epscol")
        A.memset(epscol[:], EPS, [epscol])
        ccs = sbt(gs, [128, 8, 2], F32, "ccs")
        A.load(ccs[:], CC_IN[:, :, :], ccs)
        scs = sbt(gs, [128, 8, 2], F32, "scs")
        A.act(scs[:], ccs[:], AF.Silu, [ccs], [scs])
        P.flush()

        def norm_T(pools, src, r0, hT, col0, sc, sh):
            xp, jp, sp_, xnp, ptp = pools
            xt = xp.nxt()
            A.load(xt[:], src[r0:r0 + 128, :], xt)
            jk = jp.nxt()
            ss = sp_.nxt()
            A.act(jk[:], xt[:], AF.Square, [xt], [jk, ss], accum_out=ss[:, 0:1])
            A.act(ss[:, 1:2], ss[:, 0:1], AF.Ln, [ss, epscol], [ss], scale=1.0 / D, bias=epscol[:, 0:1])
            A.act(ss[:, 2:3], ss[:, 1:2], AF.Exp, [ss], [ss], scale=-0.5)
            xn = xnp.nxt()
            A.act(xn[:], xt[:], AF.Identity, [xt, ss], [xn], scale=ss[:, 2:3])
            pt = ptp.nxt()
            A.tr([(pt[:, kt, :], xn[:, kt * 128:(kt + 1) * 128]) for kt in range(8)], identb[:], [xn, identb], [pt])
            if not hasattr(hT, "kb"):
                hT.kb = [Buf("hTk") for _ in range(8)]
            ncount[0] += 1
            for kt in range(8):
                if ncount[0] % 2 == 0:
                    A.act(hT[:, kt, col0:col0 + 128], pt[:, kt, :], AF.Identity, [pt, modcol_ref[0]], [hT.kb[kt]], scale=sc[kt], bias=sh[kt])
                else:
                    A.ts("dve", hT[:, kt, col0:col0 + 128], pt[:, kt, :], sc[kt], sh[kt], ALU.mult, ALU.add, [pt, modcol_ref[0]], [hT.kb[kt]])
            return xt

        modcol_ref = [None]
        ncount = [0]

        def load_w_cast(dst, src2, nk):
            for k0 in range(0, nk, 8):
                A.loadc(dst[:, k0:k0 + 8, :], src2[k0 * 128:(k0 + 8) * 128, :].rearrange("(kt p) n -> p kt n", p=128), dst)

        def decay_core(lf, n, d, escale, bT, Dt, E1, E2, cc, cctl, tmpc, sh=0.0):
            if d == 0:
                A.scan(bT[:, 0:n], rmask[:, 0:n], lf[:, 0:n], [lf, rmask], [bT])
            else:
                A.scan(bT[:, 0:n][:, ::-1], rmask[:, 0:n], lf[:, 0:n][:, ::-1], [lf, rmask], [bT])
            nb = n // 64
            mid = 31 if d == 0 else 32
            last = 63 if d == 0 else 0
            b3 = bT[:, 0:n].rearrange("p (c t) -> p c t", t=64)
            D3 = Dt[:, 0:n].rearrange("p (c t) -> p c t", t=64)
            A.tt("pool", D3, b3, b3[:, :, mid:mid + 1].to_broadcast([128, nb, 64]), ALU.subtract, [bT], [Dt])
            bneg = shcol[:, 0:1] if sh else zcol[:, 0:1]
            bpos = shcol[:, 1:2] if sh else zcol[:, 0:1]
            A.act(E1[:, 0:n], Dt[:, 0:n], AF.Exp, [Dt], [E1], scale=escale, bias=bneg)
            A.act(E2[:, 0:n], Dt[:, 0:n], AF.Exp, [Dt], [E2], scale=-escale, bias=bneg)
            ref2 = bT[:, mid:n:64]
            bl2 = bT[:, last:n:64]
            A.act(cc[0], ref2, AF.Exp, [bT], [cctl], scale=escale, bias=bneg)
            A.act(cc[1], bl2, AF.Exp, [bT], [cctl], scale=escale)
            A.tt("pool", tmpc[:, 0:nb], bl2, ref2, ALU.subtract, [bT], [tmpc])
            A.act(cc[2], tmpc[:, 0:nb], AF.Exp, [tmpc], [cctl], scale=escale, bias=bpos)

        blocks = [(0, CTX, 0)] + [(CTX + 512 * i, 512, 1) for i in range(LAT // 512)]

        try:
          for l in range(depth):
            xsrc = XIN if l == 0 else XS
            with ExitStack() as ls:
                modcol = sbt(ls, [128, 4, 8, 2], F32, "modcol")
                modcol_ref[0] = modcol

                def mcol(which, cl):
                    return [modcol[:, which, kt, cl:cl + 1] for kt in range(8)]

                with ExitStack() as ls2:
                    CCt = sbt(ls2, [128, 4, 2, 2, 3, NCH], F32, "CCt")
                    ALt = sbt(ls2, [128, NCH, 8], F32, "ALt")
                    FLt = sbt(ls2, [64, NCH, 8], F32, "FLt")
                    ALP = sbt(ls2, [128, NCH, 2, 2], F32, "ALP")
                    Ec = [[[sbt(ls2, [128, 512], F32, "Ec") for _ in range(2)] for _ in range(2)] for _ in range(2)]
                    lbt = sbt(ls2, [128, 2, 2, 2], F32, "lbt")
                    lgam = sbt(ls2, [128, 2, 2], F32, "lgam")
                    bat = sbt(ls2, [128, 2, 2], F32, "bat")
                    wat = sbt(ls2, [16, 2, 256], F32, "wat")
                    mlbt = sbt(ls2, [128, 16], F32, "mlbt")
                    ghr = sbt(ls2, [128, D], F32, "ghr")

                    with ExitStack() as ph:
                        wad = rot(ph, 2, [128, 8, 512], F32, "wad")
                        pm = pst(ph, [128, 64], F32, "pm")
                        bcol = sbt(ph, [128, 4, 8], F32, "bcol")
                        A.load(bcol[:], BADA_COL[l, :, :, :], bcol)
                        for which, vec in enumerate((0, 1, 3, 4)):
                            for half in range(2):
                                blk = vec * 2 + half
                                w = wad.nxt()
                                A.load(w[:], W_ADA[l, :, blk * 512:(blk + 1) * 512].rearrange("(kt p) n -> p kt n", p=128), w)
                                items = []
                                for f4 in range(4):
                                    c0 = (which * 8 + half * 4 + f4) * 2
                                    for kt in range(8):
                                        items.append((pm[:, c0:c0 + 2], w[:, kt, f4 * 128:(f4 + 1) * 128], scs[:, kt, :], kt == 0, kt == 7))
                                A.mm(items, [w, scs], [pm])
                        A.tt("dve", modcol[:].rearrange("p a b c -> p (a b) c"), pm[:].rearrange("p (a c) -> p a c", c=2),
                             bcol[:].rearrange("p a b -> p (a b)").unsqueeze(2).to_broadcast([128, 32, 2]), ALU.add, [pm, bcol], [modcol])
                        for which in (1, 3):
                            A.ts("dve", modcol[:, which, :, :], modcol[:, which, :, :], 1.0, None, ALU.add, None, [modcol], [modcol])
                        hgl = sbt(ph, [128, DEPTH, 2, 2], F32, "hgl")
                        A.load(hgl[:], HGL[:, :, :, :], hgl)
                        if l == 0:
                            A.memset(lbt[:, 0, :, :], 0.0, [lbt])
                        else:
                            dl = sbt(ph, [128, 2, 2], F32, "dl")
                            A.tt("dve", dl[:], hgl[:, 1, :, :], hgl[:, 0, :, :], ALU.subtract, [hgl], [dl])
                            A.act(lbt[:, 0, :, :], dl[:], AF.Sigmoid, [dl], [lbt])
                        A.ts("dve", lbt[:, 1, :, :], lbt[:, 0, :, :], -1.0, 1.0, ALU.mult, ALU.add, [lbt], [lbt])
                        rtl = sbt(ph, [128, DEPTH, 2, 2], F32, "rtl")
                        A.load(rtl[:], RTL[:, :, :, :], rtl)
                        sgr = sbt(ph, [128, 2, 2], F32, "sgr")
                        A.act(sgr[:], rtl[:, l, :, :], AF.Sigmoid, [rtl], [sgr])
                        A.act(lgam[:], sgr[:], AF.Ln, [sgr], [lgam])
                        bap = sbt(ph, [128, DEPTH, 2, 2], F32, "bap")
                        A.load(bap[:], BA[:, :, :, :], bap)
                        A.copy("dve", bat[:], bap[:, l, :, :], [bap], [bat])
                        A.load(wat[:], WA[l, :, :, :].rearrange("d r c -> r d c"), wat)
                        A.load(mlbt[:], MLB[l, :, :], mlbt)
                        A.load(ghr[:], GHR[l, :, :], ghr)
                        lfc = sbt(ph, [128, 512], F32, "lfc")
                        bTc = sbt(ph, [128, 512], F32, "bTc")
                        Dtc = sbt(ph, [128, 512], F32, "Dtc")
                        ctmp = sbt(ph, [128, 3, 8], F32, "ctmp")
                        tmpc = sbt(ph, [128, 8], F32, "tmpc")
                        for d in range(2):
                            for j in range(2):
                                A.copy("dve", lfc[:], lgam[:, d, j:j + 1].to_broadcast([128, 512]), [lgam], [lfc])
                                decay_core(lfc, 512, d, 1.0, bTc, Dtc, Ec[d][j][0], Ec[d][j][1],
                                           [ctmp[:, k, :] for k in range(3)], ctmp, tmpc)
                                for kind in range(3):
                                    A.copy("dve", CCt[:, RT, d, j, kind, :], ctmp[:, kind, 0:1].to_broadcast([128, NCH]), [ctmp], [CCt])
                        P.flush()
                        if stop == 'PA':
                            raise _Stop()

                    with ExitStack() as ph:
                        w1 = sbt(ph, [128, 8, NFM], BF16, "w1fm")
                        load_w_cast(w1, W1[l, :, 0:NFM], 8)
                        hTp = rot(ph, 2, [128, 8, 512], BF16, "hT")
                        npools = (rot(ph, 2, [128, D], F32, "xt"), rot(ph, 1, [128, D], BF16, "jk"), rot(ph, 2, [128, 4], F32, "ss"),
                                  rot(ph, 2, [128, D], BF16, "xn"), rot(ph, 2, [128, 8, 128], BF16, "ptr", psum=True))
                        pf = rot(ph, 4, [128, 512], F32, "pf", psum=True)
                        ptk = rot(ph, 2, [128, 4, 128], BF16, "ptk", psum=True)
                        wk = rot(ph, 10, [128, 512], F32, "wk")
                        fq = rot(ph, 4, [128, 512], F32, "fq")
                        bfp = rot(ph, 6, [128, 512], BF16, "bfp")
                        kst = [sbt(ph, [128, 4, 1024], BF16, "kst") for _ in range(2)]
                        rope = [sbt(ph, [128, 512], F32, "rope") for _ in range(2)]
                        ga = [sbt(ph, [16, 512], F32, "ga") for _ in range(2)]
                        tmpcp = rot(ph, 2, [128, 8], F32, "tmpc")

                        def do_block(t0, n, cl):
                            ntile = n // 128
                            ch0 = t0 // 64
                            nb = n // 64
                            hT = hTp.nxt()
                            for i in range(ntile):
                                norm_T(npools, xsrc, t0 + 128 * i, hT, 128 * i, mcol(1, cl), mcol(0, cl))
                                P.defer_dma(lambda e, o_=HT[t0 // 128 + i, :, :, :], i_=hT[:, :, 128 * i:128 * (i + 1)]: e.dma_start(out=o_, in_=i_),
                                            hT.b, reads=list(hT.kb))
                            A.load(rope[0][:, 0:n], ROPEC[:, t0:t0 + n], rope[0])
                            A.load(rope[1][:, 0:n], ROPES[:, t0:t0 + n], rope[1])
                            P.release()

                            def fm(col0, M=128):
                                ps = pf.nxt()
                                A.mm([(ps[0:M, 0:n], w1[:, kt, col0:col0 + M], hT[:, kt, 0:n], kt == 0, kt == 7) for kt in range(8)],
                                     [w1] + hT.kb, [ps])
                                return ps

                            def finish(m, d, j, q, k, lf, escale, kscale, E=None, dirs=None, sh=0.0):
                                P.release()
                                if E is None:
                                    bT, Dt, E1, E2 = wk.nxt(), wk.nxt(), wk.nxt(), wk.nxt()
                                    decay_core(lf, n, d, escale, bT, Dt, E1, E2,
                                               [CCt[:, m, d, j, kind, ch0:ch0 + nb] for kind in range(3)], CCt, tmpcp.nxt(), sh=sh)
                                elif E == "none":
                                    E1 = E2 = None
                                else:
                                    E1, E2 = E
                                qh = bfp.nxt()
                                kh = bfp.nxt()
                                if E1 is None:
                                    A.copy("act", qh[:, 0:n], q[:, 0:n], [q], [qh])
                                    P.act(lambda e: e.mul(out=kh[:, 0:n], in_=k[:, 0:n], mul=kscale), [k], [kh])
                                else:
                                    A.tt("dve", qh[:, 0:n], q[:, 0:n], E1[:, 0:n], ALU.mult, [q, E1], [qh])
                                    A.stt(kh[:, 0:n], k[:, 0:n], kscale, E2[:, 0:n], ALU.mult, ALU.mult, [k, E2], [kh])
                                for dd in (dirs or [d]):
                                    A.store(QT[dd, m * 2 + j, :, t0:t0 + n], qh[:, 0:n], qh)
                                    A.store(KT[dd, m * 2 + j, :, t0:t0 + n], kh[:, 0:n], kh)
                                pk = ptk.nxt()
                                A.tr([(pk[:, i, :], kh[:, 128 * i:128 * (i + 1)]) for i in range(ntile)], identb[:], [kh, identb], [pk])
                                for dd in (dirs or [d]):
                                    A.copy("act", kst[dd][:, 0:ntile, m * 256 + j * 128:m * 256 + (j + 1) * 128], pk[:, 0:ntile, :],
                                           [pk], [kst[dd]])

                            for j in range(2):
                                psq = fm((0 + j) * 128)
                                qs = fq.nxt()
                                sgq = wk.nxt()
                                A.act(sgq[:, 0:n], psq[:, 0:n], AF.Sigmoid, [psq], [sgq, psq])
                                A.tt("dve", qs[:, 0:n], psq[:, 0:n], sgq[:, 0:n], ALU.mult, [psq, sgq], [qs])
                                fs = []
                                for d in range(2):
                                    psz = fm((2 + 2 * d + j) * 128)
                                    sg, f = wk.nxt(), wk.nxt()
                                    A.act(sg[:, 0:n], psz[:, 0:n], AF.Sigmoid, [psz], [sg])
                                    A.ts("dve", f[:, 0:n], sg[:, 0:n], lbt[:, 1, d, j:j + 1], lbt[:, 0, d, j:j + 1], ALU.mult, ALU.add,
                                         [sg, lbt], [f])
                                    fs.append(f)
                                for d in range(2):
                                    f = fs[d]
                                    lf, kk = wk.nxt(), wk.nxt()
                                    A.act(lf[:, 0:n], f[:, 0:n], AF.Ln, [f], [lf])
                                    A.act(kk[:, 0:n], f[:, 0:n], AF.Identity, [f, onescol], [kk], scale=-1.0, bias=onescol[:, 0:1])
                                    finish(HG, d, j, qs, kk, lf, 1.0, 1.0, sh=HG_SHIFT)
                            for j in range(2):
                                psq = fm((6 + j) * 128)
                                psk = fm((8 + j) * 128)
                                finish(ML, 0, j, psq, psk, None, 1.0, 0.125, E="none", dirs=[0, 1])
                            for j in range(2):
                                rr = []
                                for base in (10, 14):
                                    ps0 = fm((base + j) * 128)
                                    ps1 = fm((base + 2 + j) * 128)
                                    t1, t2 = wk.nxt(), wk.nxt()
                                    r = fq.nxt()
                                    A.tt("dve", t1[:, 0:n], ps0[:, 0:n], rope[0][:, 0:n], ALU.mult, [ps0, rope[0]], [t1])
                                    A.tt("dve", t2[:, 0:n], ps1[:, 0:n], rope[1][:, 0:n], ALU.mult, [ps1, rope[1]], [t2])
                                    A.tt("pool", r[:, 0:n], t1[:, 0:n], t2[:, 0:n], ALU.add, [t1, t2], [r])
                                    rr.append(r)
                                for d in range(2):
                                    finish(RT, d, j, rr[0], rr[1], None, 1.0, 0.125, E=(Ec[d][j][0], Ec[d][j][1]))
                            for d in range(2):
                                psa = fm(22 * 128 + 16 * d, M=16)
                                A.copy("act", ga[d][:, 0:n], psa[0:16, 0:n], [psa], [ga[d]])
                            for j in range(2):
                                psq = fm((18 + j) * 128)
                                psk = fm((20 + j) * 128)
                                qr, kr = fq.nxt(), fq.nxt()
                                A.copy("act", qr[:, 0:n], psq[:, 0:n], [psq], [qr])
                                A.copy("dve", kr[:, 0:n], psk[:, 0:n], [psk], [kr])
                                sgs = []
                                for d in range(2):
                                    psz = pf.nxt()
                                    A.mm([(psz[:, 0:n], wat[0:16, d, j * 128:(j + 1) * 128], ga[d][0:16, 0:n], True, True)], [wat, ga[d]], [psz])
                                    sg = wk.nxt()
                                    A.act(sg[:, 0:n], psz[:, 0:n], AF.Sigmoid, [psz, bat], [sg], bias=bat[:, d, j:j + 1])
                                    sgs.append(sg)
                                for d in range(2):
                                    sg = sgs[d]
                                    lf = wk.nxt()
                                    A.act(lf[:, 0:n], sg[:, 0:n], AF.Ln, [sg], [lf])
                                    finish(GL, d, j, qr, kr, lf, 1.0 / 16.0, 32.0 ** -0.5)
                            for dd in range(2):
                                A.store(KK[dd, t0:t0 + n, :].rearrange("(i p) x -> p i x", p=128), kst[dd][:, 0:ntile, :], kst[dd])

                        for (t0, n, cl) in blocks:
                            do_block(t0, n, cl)
                        P.flush()
                        if stop == 'P1a':
                            raise _Stop()

                    with ExitStack() as ph:
                        w1 = sbt(ph, [128, 8, NTM], BF16, "w1tm")
                        load_w_cast(w1, W1[l, :, NFM:NC1], 8)
                        hTp = rot(ph, 4, [128, 8, 128], BF16, "hT")
                        npools = (rot(ph, 4, [128, D], F32, "xt"), rot(ph, 2, [128, D], BF16, "jk"), rot(ph, 4, [128, 4], F32, "ss"),
                                  rot(ph, 4, [128, D], BF16, "xn"), rot(ph, 2, [128, 8, 128], BF16, "ptr", psum=True))
                        pt = rot(ph, 4, [128, 512], F32, "pt", psum=True)
                        psm = rot(ph, 2, [128, 64], F32, "psm", psum=True)
                        vstp = rot(ph, 4, [128, 16, VD], BF16, "vst")
                        vmlp = rot(ph, 4, [128, 4, VD], BF16, "vml")
                        vs1p = rot(ph, 4, [128, 4, VD], BF16, "vs1")
                        for tl_ in vstp.items + vmlp.items:
                            A.memset(tl_[:], 0.0, [tl_])
                            A.memset(tl_[:, :, 64:65], 1.0, [tl_], [tl_])
                        sgp = rot(ph, 6, [128, 512], F32, "sgp")
                        ggp = rot(ph, 4, [128, D], BF16, "ggp")
                        smp = rot(ph, 4, [128, 64], F32, "smp")

                        def do_tile(ti):
                            r0 = ti * 128
                            cl = 0 if r0 < CTX else 1
                            hT = hTp.nxt()
                            hT.kb = [hT.b]
                            A.load(hT[:], HT[ti, :, :, :], hT)
                            P.release(keep=3)

                            def tm(col0, N):
                                ps = pt.nxt()
                                import os
                                if os.environ.get("EXPA"):
                                    A.mm([(ps[:, 0:128], w1[:, kt, col0:col0 + 128], hT[:, kt, :], kt == 0, kt == 7) for kt in range(8)], [w1] + hT.kb, [ps])
                                elif os.environ.get("EXPB"):
                                    A.mm([(ps[:, 0:256], hT[:, kt, :], w1[:, kt, col0:col0 + 256], kt == 0, kt == 7) for kt in range(8)], [w1] + hT.kb, [ps])
                                else:
                                    items = []
                                    for n0 in range(0, N, 256):
                                        n1 = min(N, n0 + 256)
                                        items += [(ps[:, n0:n1], hT[:, kt, :], w1[:, kt, col0 + n0:col0 + n1], kt == 0, kt == 7) for kt in range(8)]
                                    A.mm(items, [w1] + hT.kb, [ps])
                                return ps
                            vst, vml, vs1 = vstp.nxt(), vmlp.nxt(), vs1p.nxt()
                            import os
                            SKIP = os.environ.get("SKIP", "")
                            if "v" in SKIP:
                                return
                            psA = tm(0, 512)
                            if "1" in SKIP:
                                return
                            if "a" not in SKIP:
                                A.copy("act", vst[:, 0:4, 0:64], psA[:, 0:256].rearrange("p (h v) -> p h v", v=64), [psA], [vst])
                            if "b" not in SKIP:
                                A.copy("act", vml[:, :, 0:64], psA[:, 256:512].rearrange("p (h v) -> p h v", v=64), [psA], [vml])
                            if "2" in SKIP:
                                return
                            psB = tm(512, 512)
                            A.copy("dve", vst[:, 8:16, 0:64], psB[:, 0:512].rearrange("p (h v) -> p h v", v=64), [psB], [vst])
                            if "g" in SKIP:
                                return
                            gg = ggp.nxt()
                            psC = tm(1024, 512)
                            s1 = sgp.nxt()
                            A.act(s1[:], psC[:], AF.Sigmoid, [psC], [s1])
                            A.tt("dve", gg[:, 0:512], s1[:], ghr[:, 0:512], ALU.mult, [s1, ghr], [gg])
                            psD = tm(1536, 512)
                            s2 = sgp.nxt()
                            A.act(s2[:], psD[:], AF.Sigmoid, [psD], [s2, psD])
                            A.tt("dve", s2[:], psD[:], s2[:], ALU.mult, [psD, s2], [s2])
                            A.tt("pool", gg[:, 512:1024], s2[:], ghr[:, 512:1024], ALU.mult, [s2, ghr], [gg])
                            A.store(GG[r0:r0 + 128, :], gg[:], gg)
                            if "m" in SKIP:
                                return
                            psG = tm(2048, 16)
                            sm = smp.nxt()
                            A.tt("dve", sm[:, 0:16], psG[:, 0:16], mlbt[:], ALU.add, [psG, mlbt], [sm])
                            gv = sm[:, 0:16].rearrange("p (d g h) -> p d g h", d=2, g=2)
                            A.act(sm[:, 16:24].rearrange("p (d h) -> p d h", d=2), gv[:, :, 1, :], AF.Sigmoid, [sm], [sm])
                            A.act(sm[:, 24:32], sm[:, 16:24], AF.Ln, [sm], [sm])
                            pb = psm.nxt()
                            A.mm([(pb[:, 0:4], cst[:, 1, :], sm[:, 24:28], True, True),
                                  (pb[:, 4:8], cst[:, 2, :], sm[:, 28:32], True, True),
                                  (pb[:, 8:16], cst[:, 3, :], sm[:, 24:32], True, True),
                                  (pb[:, 16:24], cst[:, 4, :], sm[:, 24:32], True, True)], [sm, cst], [pb])
                            A.act(ALt[:, 2 * ti:2 * ti + 2, :], pb[:, 8:24].rearrange("p (c g) -> p c g", g=8), AF.Exp, [pb], [ALt, pb])
                            for hh_ in range(2):
                                A.copy("pool", ALP[64 * hh_:64 * hh_ + 64, 2 * ti:2 * ti + 2, :, :],
                                       ALt[64 * hh_:64 * hh_ + 64, 2 * ti:2 * ti + 2, :].rearrange("p c (d a b) -> p c d a b", d=2, a=2)[:, :, :, :, hh_],
                                       [ALt], [ALP])
                            A.tt("dve", sm[:, 32:40].rearrange("p (d h) -> p d h", d=2), gv[:, :, 0, :],
                                 pb[:, 0:8].rearrange("p (d h) -> p d h", d=2), ALU.subtract, [pb, sm], [sm, pb])
                            A.act(sm[:, 40:48], sm[:, 32:40], AF.Exp, [sm], [sm])
                            if "f" in SKIP:
                                return
                            A.act(sm[:, 48:56], pb[:, 0:8], AF.Exp, [pb], [sm, pb], scale=-1.0)
                            A.copy("dve", FLt[0:64, 2 * ti, :], sm[0:64, 48:56], [sm], [FLt])
                            A.copy("dve", FLt[0:64, 2 * ti + 1, :], sm[64:128, 48:56], [sm], [FLt])
                            if "w" in SKIP:
                                return
                            A.tt("dve", vst[:, 4:8, :], vml[:], sm[:, 40:44].unsqueeze(2).to_broadcast([128, 4, VD]), ALU.mult, [sm, vml], [vst])
                            A.tt("dve", vs1[:], vml[:], sm[:, 44:48].unsqueeze(2).to_broadcast([128, 4, VD]), ALU.mult, [sm, vml], [vs1])
                            A.store(VV[0, r0:r0 + 128, :, :], vst[:], vst)
                            A.store(VV[1, r0:r0 + 128, 4:8, :], vs1[:], vs1)

                        for ti in range(NT):
                            do_tile(ti)
                        P.flush()
                        if stop == 'P1b':
                            raise _Stop()

                    with ExitStack() as ph:
                        qTp = [rot(ph, 2, [128, 8, 2, 256], BF16, "qbd") for _ in range(2)]
                        for d_ in range(2):
                            for t_ in qTp[d_].items:
                                A.memset(t_[:], 0.0, [t_])
                        kTp = [rot(ph, 2, [128, 8, 256], BF16, "kT") for _ in range(2)]
                        ktp = [rot(ph, 2, [64, 4, 1024], BF16, "ktok") for _ in range(2)]
                        vvp = [rot(ph, 2, [64, 4, 16 * VD], BF16, "vv") for _ in range(2)]
                        vmp = rot(ph, 2, [64, 4, 4 * VD], BF16, "vm")
                        S = [[[sbt(ph, [128, VD], F32, "S") for _ in range(2)] for _ in range(4)] for _ in range(2)]
                        Sb = [[[sbt(ph, [128, VD], BF16, "Sb") for _ in range(2)] for _ in range(4)] for _ in range(2)]
                        for d in range(2):
                            for m in range(4):
                                for hp in range(2):
                                    A.memset(S[d][m][hp][:], 0.0, [S[d][m][hp]])
                        psA = Rot([Tl(t_[:, 0:256].rearrange("p (h v) -> p h v", v=64), "psAv") for t_ in
                                   [pst(ph, [64, 512], F32, "psA") for _ in range(4)]])
                        pso = Rot([Tl(t_[:, 0:4 * VD].rearrange("p (h v) -> p h v", v=VD), "psov") for t_ in
                                   [pst(ph, [64, 512], F32, "pso") for _ in range(2)]])
                        psS = Rot([Tl(t_[:, 0:4 * VD].rearrange("p (a v) -> p a v", v=2 * VD), "psSv") for t_ in
                                   [pst(ph, [128, 512], F32, "psS") for _ in range(2)]])
                        mask4 = sbt(ph, [64, 2, 4, 64], F32, "mask4")
                        for d_ in range(2):
                            A.copy("dve", mask4[:, d_, :, :], maskf[:, d_, :].unsqueeze(1).to_broadcast([64, 4, 64]), [maskf], [mask4])
                        mask4u = mask4[:].bitcast(mybir.dt.uint32)
                        Asbd = [rot(ph, 5, [64, 4, 64], BF16, "Asb") for _ in range(2)]
                        for dd_ in range(2):
                            for t_ in Asbd[dd_].items:
                                A.memset(t_[:], 0.0, [t_])
                        tmpS = rot(ph, 6, [128, VD], F32, "tmpS")
                        osb = rot(ph, 2, [64, D], F32, "osb")
                        nrm = rot(ph, 2, [64, 16], F32, "nrm")

                        def col(m, d, hp, kind, c):
                            if m == ML:
                                if kind == 0:
                                    return onescol[:, 0:1], onescol
                                return ALP[:, c, d, hp:hp + 1], ALP
                            return CCt[:, m, d, hp, kind, c:c + 1], CCt

                        def do_chunk(d, c, cc, qT, kT, ktok, vv, vm):
                            ts = slice(64 * cc, 64 * cc + 64)
                            ob = osb.nxt()
                            vs = []
                            for m in range(4):
                                if m == ML and d == 1:
                                    vs.append((vm, 0))
                                else:
                                    vs.append((vv, 4 * m * VD))
                            a4s = []
                            for m in range(4):
                                for hp in range(2):
                                    St, Sbt = S[d][m][hp], Sb[d][m][hp]
                                    c1, c1t = col(m, d, hp, 0, c)
                                    A.act(Sbt[:, :], St[:, :], AF.Identity, [St, c1t], [Sbt], scale=c1)
                                pa = psA.nxt()
                                A.mm([(pa[:, h, :], kT[:, 2 * m + h // 2, ts], qT[:, 2 * m + h // 2, h % 2, ts], True, True) for h in range(4)],
                                     [kT, qT], [pa])
                                a4 = Asbd[d].nxt()
                                A.tt("dve", a4[:], pa[:], mask4[:, d, :, :], ALU.mult, [pa, mask4], [a4])
                                a4s.append(a4)
                            pos, pSs = [], []
                            for m in range(4):
                                vsrc, vbase = vs[m]
                                a4 = a4s[m]
                                po = pso.nxt()
                                items = []
                                for h in range(4):
                                    items.append((po[:, h, :], a4[:, h, :], vsrc[0:64, cc, vbase + h * VD:vbase + (h + 1) * VD], True, False))
                                    items.append((po[:, h, :], qT[:, 2 * m + h // 2, h % 2, ts], Sb[d][m][h // 2][:, :], False, True))
                                A.mm(items, [a4, vsrc, qT, Sb[d][m][0], Sb[d][m][1]], [po])
                                pS = psS.nxt()
                                A.mm([(pS[:, hp, :], ktok[0:64, cc, m * 256 + hp * 128:m * 256 + (hp + 1) * 128],
                                       vsrc[0:64, cc, vbase + 2 * hp * VD:vbase + (2 * hp + 2) * VD], True, True) for hp in range(2)],
                                     [ktok, vsrc], [pS])
                                for hp in range(2):
                                    St = S[d][m][hp]
                                    tS = tmpS.nxt()
                                    c2, c2t = col(m, d, hp, 1, c)
                                    c3, c3t = col(m, d, hp, 2, c)
                                    A.act(tS[:, :], St[:, :], AF.Identity, [St, c2t], [tS], scale=c2)
                                    for hh in range(2):
                                        p0 = 64 * hh
                                        A.stt(St[p0:p0 + 64, :], pS[p0:p0 + 64, hp, hh * VD:(hh + 1) * VD], c3[p0:p0 + 64, :], tS[p0:p0 + 64, :],
                                              ALU.mult, ALU.add, [pS, tS, c3t], [St])
                                ov = ob[:, m * 256:(m + 1) * 256].rearrange("p (h v) -> p h v", v=64)
                                if m == HG:
                                    P.act(lambda e, ov=ov, pin=po[:, :, 0:64]: e.mul(out=ov, in_=pin, mul=float(np.exp(2.0 * HG_SHIFT))), [po], [ob])
                                elif m != ML:
                                    A.copy("act", ov, po[:, :, 0:64], [po], [ob])
                                else:
                                    nr = nrm.nxt()
                                    A.act(nr[:, 0:4], po[:, :, 64], AF.Abs, [po], [nr, po])
                                    A.tt("dve", nr[:, 4:8], nr[:, 0:4], FLt[0:64, c, d * 4:d * 4 + 4], ALU.max, [nr, FLt], [nr])
                                    A.recip(nr[:, 8:12], nr[:, 4:8], [nr], [nr])
                                    A.tt("dve", ov, po[:, :, 0:64], nr[:, 8:12].unsqueeze(2).to_broadcast([64, 4, 64]), ALU.mult, [nr, po], [ob, po])
                            A.store(OO[d, 64 * c:64 * c + 64, :], ob[:], ob)

                        grp_order = [list(range(NG)), [0] + list(range(NG - 1, 0, -1))]
                        cc_order = [[0, 1, 2, 3], [3, 2, 1, 0]]
                        for gi in range(NG):
                            grp = []
                            for d in range(2):
                                g = grp_order[d][gi]
                                tg0 = 256 * g
                                qT, kT, ktok, vv = qTp[d].nxt(), kTp[d].nxt(), ktp[d].nxt(), vvp[d].nxt()
                                A.load(qT[0:64, :, 0, :], QT[d, :, 0:64, tg0:tg0 + 256].rearrange("m p t -> p m t"), qT)
                                A.load(qT[64:128, :, 1, :], QT[d, :, 64:128, tg0:tg0 + 256].rearrange("m p t -> p m t"), qT)
                                A.load(kT[:], KT[d, :, :, tg0:tg0 + 256].rearrange("m p t -> p m t"), kT)
                                A.load(ktok[:], KK[d, tg0:tg0 + 256, :].rearrange("(c p) x -> p c x", p=64), ktok)
                                A.load(vv[:], VV[0, tg0:tg0 + 256, :, :].rearrange("(c p) h v -> p c (h v)", p=64), vv)
                                vm = None
                                if d == 1:
                                    vm = vmp.nxt()
                                    A.load(vm[:], VV[1, tg0:tg0 + 256, 4:8, :].rearrange("(c p) h v -> p c (h v)", p=64), vm)
                                grp.append((g, qT, kT, ktok, vv, vm))
                            for k_ in range(4):
                                for d in range(2):
                                    g, qT, kT, ktok, vv, vm = grp[d]
                                    cc = cc_order[d][k_]
                                    P.release()
                                    do_chunk(d, 4 * g + cc, cc, qT, kT, ktok, vv, vm)
                        P.flush()
                        if stop == 'P2':
                            raise _Stop()

                with ExitStack() as ph:
                    g1 = [sbt(ph, [128, D], F32, "g1") for _ in range(2)]
                    g2 = [sbt(ph, [128, D], F32, "g2") for _ in range(2)]

                    def gtiles(gi, dst):
                        with ExitStack() as ph2:
                            wad = rot(ph2, 2, [128, 8, 512], F32, "wad")
                            bg = sbt(ph2, [128, D], F32, "bg")
                            pg = rot(ph2, 2, [128, 512], F32, "pg", psum=True)
                            crep = [sbt(ph2, [128, 8, 128], F32, "crep") for _ in range(2)]
                            for cl in range(2):
                                A.copy("dve", crep[cl][:], scs[:, :, cl:cl + 1].to_broadcast([128, 8, 128]), [scs], [crep[cl]])
                            A.load(bg[:], BADA_G[l, gi, :, :], bg)
                            vec = (2, 5)[gi]
                            for half in range(2):
                                blk = vec * 2 + half
                                w = wad.nxt()
                                A.load(w[:], W_ADA[l, :, blk * 512:(blk + 1) * 512].rearrange("(kt p) n -> p kt n", p=128), w)
                                for cl in range(2):
                                    ps = pg.nxt()
                                    A.mm([(ps[:], crep[cl][:, kt, :], w[:, kt, :], kt == 0, kt == 7) for kt in range(8)], [w, crep[cl]], [ps])
                                    A.tt("dve", dst[cl][:, half * 512:(half + 1) * 512], ps[:], bg[:, half * 512:(half + 1) * 512], ALU.add,
                                         [ps, bg], [dst[cl]])
                            P.flush()
                            if stop == 'P3a':
                                raise _Stop()
                    gtiles(0, g1)
                    wf1 = sbt(ph, [128, 8, DFF], BF16, "wf1")
                    load_w_cast(wf1, W_FF1[l, :, :], 8)
                    with ExitStack() as ph2:
                        wo = sbt(ph2, [128, 8, D], BF16, "wo")
                        load_w_cast(wo, W_OUT[l, :, :], 8)
                        o0p = rot(ph2, 3, [128, D], F32, "o0")
                        o1p = rot(ph2, 3, [128, D], F32, "o1")
                        gglp = rot(ph2, 3, [128, D], BF16, "ggl")
                        xp = rot(ph2, 4, [128, D], F32, "x3")
                        sqp = rot(ph2, 2, [128, D], F32, "sq")
                        ssp = rot(ph2, 3, [128, 48], F32, "ss3")
                        yp = rot(ph2, 3, [128, D], BF16, "y")
                        yTp = rot(ph2, 3, [128, 8, 128], BF16, "yT")
                        ptr = rot(ph2, 2, [128, 8, 128], BF16, "ptr3", psum=True)
                        pop = rot(ph2, 4, [128, 512], F32, "pop", psum=True)
                        tp = rot(ph2, 4, [128, 512], F32, "t3")

                        def do_tile3(ti):
                            r0 = ti * 128
                            cl = 0 if r0 < CTX else 1
                            o0, o1, ggl, xt = o0p.nxt(), o1p.nxt(), gglp.nxt(), xp.nxt()
                            A.load(o0[:], OO[0, r0:r0 + 128, :], o0)
                            A.load(o1[:], OO[1, r0:r0 + 128, :], o1)
                            A.load(ggl[:], GG[r0:r0 + 128, :], ggl)
                            A.load(xt[:], xsrc[r0:r0 + 128, :], xt)
                            P.release(keep=1)
                            A.tt("pool", o0[:], o0[:], o1[:], ALU.add, [o0, o1], [o0])
                            sq, ss = sqp.nxt(), ssp.nxt()
                            A.act(sq[:], o0[:], AF.Square, [o0], [sq])
                            A.reduce(ss[:, 0:16], sq[:].rearrange("p (h v) -> p h v", v=64), [sq], [ss])
                            A.act(ss[:, 16:32], ss[:, 0:16], AF.Ln, [ss, epscol], [ss], scale=1.0 / 64, bias=epscol[:, 0:1])
                            A.act(ss[:, 32:48], ss[:, 16:32], AF.Exp, [ss], [ss], scale=-0.5)
                            o3 = o0[:].rearrange("p (h v) -> p h v", v=64)
                            A.tt("dve", o3, o3, ss[:, 32:48].unsqueeze(2).to_broadcast([128, 16, 64]), ALU.mult, [o0, ss], [o0])
                            y = yp.nxt()
                            A.tt("pool", y[:], o0[:], ggl[:], ALU.mult, [o0, ggl], [y])
                            pt = ptr.nxt()
                            A.tr([(pt[:, kt, :], y[:, kt * 128:(kt + 1) * 128]) for kt in range(8)], identb[:], [y, identb], [pt])
                            yT = yTp.nxt()
                            A.copy("act", yT[:, 0:4, :], pt[:, 0:4, :], [pt], [yT])
                            A.copy("dve", yT[:, 4:8, :], pt[:, 4:8, :], [pt], [yT])
                            for nb in range(2):
                                ps = pop.nxt()
                                A.mm([(ps[:], yT[:, kt, :], wo[:, kt, nb * 512:(nb + 1) * 512], kt == 0, kt == 7) for kt in range(8)], [yT, wo], [ps])
                                t = tp.nxt()
                                A.tt("dve", t[:], ps[:], g1[cl][:, nb * 512:(nb + 1) * 512], ALU.mult, [ps, g1[cl]], [t])
                                A.tt("pool", xt[:, nb * 512:(nb + 1) * 512], xt[:, nb * 512:(nb + 1) * 512], t[:], ALU.add, [t, xt], [xt])
                            A.store(XS[r0:r0 + 128, :], xt[:], xt)

                        for ti in range(NT):
                            do_tile3(ti)
                        P.flush()
                        if stop == 'P3a':
                            raise _Stop()

                    gtiles(1, g2)
                    with ExitStack() as ph2:
                        wf2 = sbt(ph2, [128, 32, D], BF16, "wf2")
                        load_w_cast(wf2, W_FF2[l, :, :], 32)
                        hTp = rot(ph2, 2, [128, 8, 256], BF16, "h2T")
                        npools = (rot(ph2, 3, [128, D], F32, "xt"), rot(ph2, 1, [128, D], BF16, "jk"), rot(ph2, 2, [128, 4], F32, "ss"),
                                  rot(ph2, 2, [128, D], BF16, "xn"), rot(ph2, 2, [128, 8, 128], BF16, "ptr", psum=True))
                        uTp = rot(ph2, 1, [128, 32, 256], BF16, "uT")
                        pu = Rot([Tl(t_[:, 0:256], "pus") for t_ in [pst(ph2, [128, 512], F32, "pu") for _ in range(3)]])
                        po2 = rot(ph2, 2, [128, 512], F32, "po2", psum=True)
                        sqp = rot(ph2, 3, [128, 256], F32, "sq2")
                        tp = rot(ph2, 1, [128, 512], F32, "t4")

                        def do_blk(bi):
                            cl = 0 if bi * 256 < CTX else 1
                            hT = hTp.nxt()
                            xts = []
                            for i in range(2):
                                xts.append(norm_T(npools, XS, bi * 256 + 128 * i, hT, 128 * i, mcol(3, cl), mcol(2, cl)))
                                if i == 0:
                                    P.release()
                            uT = uTp.nxt()
                            for fb in range(32):
                                ps = pu.nxt()
                                A.mm([(ps[:], wf1[:, kt, fb * 128:(fb + 1) * 128], hT[:, kt, :], kt == 0, kt == 7) for kt in range(8)], [wf1] + hT.kb, [ps])
                                sq = sqp.nxt()
                                A.act(sq[:], ps[:], AF.Square, [ps], [sq])
                                A.stt(uT[:, fb, :], ps[:], 0.0, sq[:], ALU.is_gt, ALU.mult, [ps, sq], [uT])
                            for i in range(2):
                                xt = xts[i]
                                for nb in range(2):
                                    ps = po2.nxt()
                                    A.mm([(ps[:], uT[:, fb, 128 * i:128 * (i + 1)], wf2[:, fb, nb * 512:(nb + 1) * 512], fb == 0, fb == 31)
                                          for fb in range(32)], [uT, wf2], [ps])
                                    t = tp.nxt()
                                    A.tt("dve", t[:], ps[:], g2[cl][:, nb * 512:(nb + 1) * 512], ALU.mult, [ps, g2[cl]], [t])
                                    A.tt("pool", xt[:, nb * 512:(nb + 1) * 512], xt[:, nb * 512:(nb + 1) * 512], t[:], ALU.add, [t, xt], [xt])
                                r0 = bi * 256 + 128 * i
                                A.store(XS[r0:r0 + 128, :], xt[:], xt)

                        for bi in range(T // 256):
                            do_blk(bi)
                        P.flush()
                        if stop == 'P3b':
                            raise _Stop()

        except _Stop:
            P.flush(final=True)
            build.ninst = P.ninst
            gs.pop_all()
            return nc
        with ExitStack() as ph:
            gf = sbt(ph, [128, D], F32, "gf")
            A.load(gf[:], GFIN[:, :], gf)
            xp = rot(ph, 3, [128, D], F32, "xf")
            jp = rot(ph, 1, [128, D], BF16, "jkf")
            sp_ = rot(ph, 2, [128, 4], F32, "ssf")
            for ti in range(LAT // 128):
                r0 = CTX + ti * 128
                xt, jk, ss = xp.nxt(), jp.nxt(), sp_.nxt()
                A.load(xt[:], XS[r0:r0 + 128, :], xt)
                P.release()
                A.act(jk[:], xt[:], AF.Square, [xt], [jk, ss], accum_out=ss[:, 0:1])
                A.act(ss[:, 1:2], ss[:, 0:1], AF.Ln, [ss, epscol], [ss], scale=1.0 / D, bias=epscol[:, 0:1])
                A.act(ss[:, 2:3], ss[:, 1:2], AF.Exp, [ss], [ss], scale=-0.5)
                A.stt(xt[:], xt[:], ss[:, 2:3], gf[:], ALU.mult, ALU.mult, [xt, ss, gf], [xt])
                A.store(OUT[ti * 128:(ti + 1) * 128, :], xt[:], xt)
            P.flush(final=True)
        build.ninst = P.ninst
    return nc


_IN_LAYOUT = (
    ('hg_q', 256), ('hg_f_fwd', 256), ('hg_f_bwd', 256), ('hg_i', 256), ('hg_g', 256),
    ('ml_q', 256), ('ml_k', 256), ('ml_v', 256), ('ml_if', 16), ('ml_o', 256),
    ('rt_q', 256), ('rt_k', 256), ('rt_v', 256), ('rt_g', 256),
    ('gl_q', 128), ('gl_k', 128), ('gl_v', 256),
    ('gl_a_fwd', 16), ('gl_a_bwd', 16), ('gl_g', 256),
)


def _col_ranges():
    off = {}
    o = 0
    for nme, s in _IN_LAYOUT:
        off[nme] = (o, s)
        o += s
    return off


def _w1_layout(w_in):
    off = _col_ranges()
    dep = w_in.shape[0]
    out = np.zeros((dep, D, NC1), np.float32)

    def cols(nme):
        o, s = off[nme]
        return w_in[:, :, o:o + s]
    perm = np.zeros(256, np.int64)
    for h in range(4):
        for dd in range(64):
            r = dd % 32
            partner = dd + 16 if r < 16 else dd - 16
            perm[h * 64 + dd] = h * 64 + partner

    def pad_gla(a):
        p = np.zeros((dep, D, 256), np.float32)
        for h in range(4):
            p[:, :, h * 64:h * 64 + 32] = a[:, :, h * 32:(h + 1) * 32]
        return p
    fmc = [cols('hg_q'), cols('hg_f_fwd'), cols('hg_f_bwd'), cols('ml_q'), cols('ml_k'),
           cols('rt_q'), cols('rt_q')[:, :, perm], cols('rt_k'), cols('rt_k')[:, :, perm],
           pad_gla(cols('gl_q')), pad_gla(cols('gl_k')), cols('gl_a_fwd'), cols('gl_a_bwd')]
    tmc = [cols('hg_i'), cols('ml_v'), cols('rt_v'), cols('gl_v'), cols('hg_g'), cols('ml_o'), cols('rt_g'), cols('gl_g'), cols('ml_if')]
    o = 0
    for a in fmc + tmc:
        out[:, :, o:o + a.shape[2]] = a
        o += a.shape[2]
    assert o == NC1
    return out


def _rope_tables(T):
    LATn = T - CTX
    tl = np.arange(LATn)
    row = (tl // 64).astype(np.float32)
    colp = (tl % 64).astype(np.float32)
    inv = (np.float32(10000.0) ** (-np.arange(16, dtype=np.float32) / np.float32(16))).astype(np.float32)
    cosT = np.ones((128, T), np.float32)
    sinT = np.zeros((128, T), np.float32)
    for p in range(128):
        dd = p % 64
        pos = row if dd < 32 else colp
        ang = (pos * inv[dd % 16]).astype(np.float32)
        sgn = -1.0 if (dd % 32) < 16 else 1.0
        cosT[p, CTX:] = np.cos(ang).astype(np.float32)
        sinT[p, CTX:] = (sgn * np.sin(ang)).astype(np.float32)
    return cosT, sinT


def _consts():
    c = np.zeros((128, 8, 128), np.float32)
    s = np.arange(128)[:, None]
    t = np.arange(128)[None, :]
    same = (s // 64) == (t // 64)
    c[:, 0, :] = (s == t)
    c[:, 1, :] = same & (s <= t)
    c[:, 2, :] = same & (s >= t)
    c[:, 3, :] = (s < 64) & (t >= 0)
    c[:, 4, :] = (s >= 64) & (t >= 0)
    c[:64, 5, :64] = (s[:64] <= t[:, :64])
    c[:64, 6, :64] = (s[:64] >= t[:, :64])
    return c


def make_shared(inp, T):
    dep = inp['w_ada'].shape[0]
    f32 = np.float32
    sh = {}
    sh['w_ada'] = np.ascontiguousarray(inp['w_ada'], f32)
    b_ada = np.asarray(inp['b_ada'], f32)
    bc = np.zeros((dep, 128, 4, 8), f32)
    for which, vec in enumerate((0, 1, 3, 4)):
        bc[:, :, which, :] = b_ada[:, vec * D:(vec + 1) * D].reshape(dep, 8, 128).transpose(0, 2, 1)
    sh['bada_col'] = bc
    bg = np.zeros((dep, 2, 128, D), f32)
    for gi, vec in enumerate((2, 5)):
        bg[:, gi, :, :] = b_ada[:, None, vec * D:(vec + 1) * D]
    sh['bada_g'] = bg
    sh['w1'] = _w1_layout(np.asarray(inp['w_in'], f32))
    sh['ghr'] = np.ascontiguousarray(np.broadcast_to(np.asarray(inp['g_heads'], f32)[:, None, :], (dep, 128, D)))
    hl = np.asarray(inp['hgrn_lb_logits'], f32)
    sh['hgl'] = np.ascontiguousarray(hl.reshape(dep, 2, 2, 128).transpose(3, 0, 1, 2))
    mb = np.asarray(inp['ml_gate_bias'], f32).reshape(dep, 16)
    sh['mlb'] = np.ascontiguousarray(np.broadcast_to(mb[:, None, :], (dep, 128, 16)))
    rl = np.asarray(inp['rt_decay_logit'], f32)
    rt = np.zeros((128, dep, 2, 2), f32)
    for j in range(2):
        for hh in range(2):
            rt[hh * 64:(hh + 1) * 64, :, :, j] = rl[None, :, :, 2 * j + hh]
    sh['rtl'] = rt
    wa = np.asarray(inp['gla_w_a'], f32)
    wap = np.zeros((dep, 2, 16, 256), f32)
    ba = np.asarray(inp['gla_b_a'], f32)
    bap = np.zeros((dep, 2, 256), f32)
    for h in range(4):
        wap[:, :, :, h * 64:h * 64 + 32] = wa[:, :, :, h * 32:(h + 1) * 32]
        bap[:, :, h * 64:h * 64 + 32] = ba[:, :, h * 32:(h + 1) * 32]
    sh['wa'] = wap
    sh['ba'] = np.ascontiguousarray(bap.reshape(dep, 2, 2, 128).transpose(3, 0, 1, 2))
    sh['w_out'] = np.ascontiguousarray(inp['w_out'], f32)
    sh['w_ff1'] = np.ascontiguousarray(inp['w_ff1'], f32)
    sh['w_ff2'] = np.ascontiguousarray(inp['w_ff2'], f32)
    sh['gfin'] = np.ascontiguousarray(np.broadcast_to(np.asarray(inp['g_final'], f32)[None, :], (128, D)))
    c, s = _rope_tables(T)
    sh['ropec'] = c
    sh['ropes'] = s
    sh['consts'] = _consts()
    return sh


def make_core(inp, b):
    f32 = np.float32
    m = {}
    m['xin'] = np.ascontiguousarray(np.concatenate([np.asarray(inp['ctx'][b], f32), np.asarray(inp['x'][b], f32)], axis=0))
    cc = np.zeros((128, 8, 2), f32)
    cc[:, :, 0] = np.asarray(inp['c_ctx'], f32).reshape(8, 128).T
    cc[:, :, 1] = np.asarray(inp['c'][b], f32).reshape(8, 128).T
    m['cc'] = cc
    return m


_CACHE = {}


def kernel(**inputs):
    x = inputs['x']
    B, LAT, _ = x.shape
    T = CTX + LAT
    if LAT not in _CACHE:
        _CACHE[LAT] = build(LAT)
    nc = _CACHE[LAT]
    sh = make_shared(inputs, T)
    in_maps = []
    for b in range(B):
        m = dict(sh)
        m.update(make_core(inputs, b))
        in_maps.append(m)
    res = run_bass_kernel_spmd(nc, in_maps, core_ids=list(range(B)))
    return np.stack([np.asarray(r["out"], np.float32) for r in res.results], axis=0)
```

```python
import numpy as np
from contextlib import ExitStack
import concourse.bass as bass
import concourse.mybir as mybir
from concourse.bass_utils import run_bass_kernel_spmd

F32 = mybir.dt.float32
BF16 = mybir.dt.bfloat16
AF = mybir.ActivationFunctionType
ALU = mybir.AluOpType
AX = mybir.AxisListType

D = 1024
CTX = 256
DEPTH = 2
DFF = 4096
VD = 66
NFM = 22 * 128 + 32
NTM = 2064
NC1 = NFM + NTM
EPS = 1e-6
ENGS = ("pe", "act", "dve", "pool", "sp")
HG, ML, RT, GL = 0, 1, 2, 3
HG_SHIFT = 20.0


class Buf:
    __slots__ = ("name", "last_write", "reads", "dsem", "dcount")

    def __init__(self, name):
        self.name = name
        self.last_write = None
        self.reads = []
        self.dsem = None
        self.dcount = 0


class Tl:
    def __init__(self, t, name, b=None):
        self.t = t
        self.b = b if b is not None else Buf(name)

    def __getitem__(self, i):
        return self.t[i]


def _b(x):
    return x.b if isinstance(x, Tl) else x


class Op:
    __slots__ = ("eng", "fn", "deps", "is_dma", "slot", "ticket", "signal", "retired", "seq")

    def __init__(self, eng, fn, is_dma, slot):
        self.eng = eng
        self.fn = fn
        self.deps = []
        self.is_dma = is_dma
        self.slot = slot
        self.ticket = None
        self.signal = False
        self.retired = False


class Prog:
    def __init__(self, nc, stack):
        self.nc = nc
        self.stack = stack
        self.ops = []
        self.esem = {e: stack.enter_context(nc.semaphore("sem_" + e)) for e in ENGS}
        self.ecount = {e: 0 for e in ENGS}
        self.waited = {e: {} for e in ENGS}
        self.pending_bar = {e: [] for e in ENGS}
        self.sempool = []
        self.nsem = 0
        self.engobj = {"pe": nc.tensor, "act": nc.scalar, "dve": nc.vector, "pool": nc.gpsimd, "sp": nc.sync}
        self.ninst = 0
        self.deferred = []
        self.seq = 0

    def defer_dma(self, fn, slot, reads=(), writes=(), eng="sp"):
        self.deferred.append((fn, slot, reads, writes, eng))

    def release(self, keep=0):
        n = len(self.deferred) - keep
        if n <= 0:
            return
        dd = self.deferred[:n]
        self.deferred = self.deferred[n:]
        for fn, slot, reads, writes, eng in dd:
            self.dma(fn, slot, reads, writes, eng)

    def add(self, eng, fn, reads=(), writes=(), is_dma=False, slot=None):
        op = Op(eng, fn, is_dma, slot)
        deps = set()
        for x in reads:
            b = _b(x)
            if b.last_write is not None:
                deps.add(b.last_write)
        for x in writes:
            b = _b(x)
            if b.last_write is not None:
                deps.add(b.last_write)
            for r in b.reads:
                deps.add(r)
        dl = [d for d in deps if (not d.retired) and not (eng == "pe" and d.eng == "pe" and not d.is_dma)]
        best = {}
        keep = []
        for d in dl:
            if d.is_dma:
                keep.append(d)
            elif d.eng not in best or best[d.eng].seq < d.seq:
                best[d.eng] = d
        op.deps = keep + list(best.values())
        self.seq += 1
        op.seq = self.seq
        for x in reads:
            _b(x).reads.append(op)
        for x in writes:
            b = _b(x)
            b.last_write = op
            b.reads = []
        self.ops.append(op)
        return op

    def pe(self, fn, reads=(), writes=()):
        return self.add("pe", fn, reads, writes)

    def act(self, fn, reads=(), writes=()):
        return self.add("act", fn, reads, writes)

    def dve(self, fn, reads=(), writes=()):
        return self.add("dve", fn, reads, writes)

    def pool(self, fn, reads=(), writes=()):
        return self.add("pool", fn, reads, writes)

    def dma(self, fn, slot, reads=(), writes=(), eng="sp"):
        return self.add(eng, fn, reads, writes, is_dma=True, slot=_b(slot))

    def flush(self, final=False):
        nc = self.nc
        self.release()
        ops = self.ops
        self.ops = []
        for op in ops:
            for d in op.deps:
                d.signal = True
        last = {}
        for op in ops:
            if not op.is_dma:
                last[op.eng] = op
        for op in last.values():
            op.signal = True
        slots = []
        swbar = []
        for op in ops:
            if op.is_dma and op.eng == "pool":
                sem = self.stack.enter_context(nc.semaphore("sw%d" % self.nsem))
                self.nsem += 1
                op.ticket = (sem, 16)
                swbar.append(op.ticket)
            elif op.is_dma:
                s = op.slot
                if s.dsem is None:
                    if self.sempool:
                        s.dsem, s.dcount = self.sempool.pop()
                    else:
                        s.dsem = self.stack.enter_context(nc.semaphore("ds%d" % self.nsem))
                        self.nsem += 1
                        s.dcount = 0
                    slots.append(s)
                s.dcount += 16
                op.ticket = (s.dsem, s.dcount)
            elif op.signal:
                self.ecount[op.eng] += 1
                op.ticket = (self.esem[op.eng], self.ecount[op.eng])
        per = {e: [] for e in ENGS}
        for op in ops:
            per[op.eng].append(op)
        bar = [last[e].ticket for e in last] + [(s.dsem, s.dcount) for s in slots] + swbar

        def run(ename, eng):
            waited = self.waited[ename]

            def w(sem, val):
                k = id(sem)
                if waited.get(k, 0) < val:
                    eng.wait_ge(sem, val)
                    waited[k] = val

            if per[ename] or final:
                for sem, val in self.pending_bar[ename]:
                    w(sem, val)
                self.pending_bar[ename] = []
            for op in per[ename]:
                need = {}
                for d in op.deps:
                    sem, val = d.ticket
                    k = id(sem)
                    if k not in need or need[k][1] < val:
                        need[k] = (sem, val)
                for sem, val in need.values():
                    w(sem, val)
                ins = op.fn(eng)
                self.ninst += 1
                if op.is_dma:
                    ins.then_inc(op.ticket[0], 16)
                elif op.signal:
                    ins.then_inc(op.ticket[0], 1)
            if final and ename == "sp":
                for sem, val in bar:
                    w(sem, val)

        with nc.Block() as block:
            @block.tensor
            def _(eng):
                run("pe", eng)

            @block.scalar
            def _(eng):
                run("act", eng)

            @block.vector
            def _(eng):
                run("dve", eng)

            @block.gpsimd
            def _(eng):
                run("pool", eng)

            @block.sync
            def _(eng):
                run("sp", eng)

        for e in ENGS:
            self.pending_bar[e].extend(bar)
        for op in ops:
            op.retired = True
        for s in slots:
            self.sempool.append((s.dsem, s.dcount))
            s.dsem = None


class Rot:
    def __init__(self, items):
        self.items = items
        self.i = 0

    def nxt(self):
        r = self.items[self.i % len(self.items)]
        self.i += 1
        return r


class Em:
    def __init__(self, P):
        self.P = P

    def act(self, out, in_, func, R, W, **kw):
        self.P.act(lambda e: e.activation(out=out, in_=in_, func=func, **kw), R, W)

    def tt(self, eng, out, in0, in1, op, R, W):
        self.P.add(eng, lambda e: e.tensor_tensor(out=out, in0=in0, in1=in1, op=op), R, W)

    def ts(self, eng, out, in0, s1, s2, op0, op1, R, W):
        if op1 is None:
            self.P.add(eng, lambda e: e.tensor_scalar(out=out, in0=in0, scalar1=s1, scalar2=None, op0=op0), R, W)
        else:
            self.P.add(eng, lambda e: e.tensor_scalar(out=out, in0=in0, scalar1=s1, scalar2=s2, op0=op0, op1=op1), R, W)

    def stt(self, out, in0, scalar, in1, op0, op1, R, W):
        self.P.dve(lambda e: e.scalar_tensor_tensor(out=out, in0=in0, scalar=scalar, in1=in1, op0=op0, op1=op1), R, W)

    def copy(self, eng, out, in_, R, W):
        if eng == "act":
            self.P.act(lambda e: e.copy(out=out, in_=in_), R, W)
        else:
            self.P.add(eng, lambda e: e.tensor_copy(out=out, in_=in_), R, W)

    def memset(self, ap, val, W, R=()):
        self.P.pool(lambda e: e.memset(ap, val), R, W)

    def load(self, out, in_, slot, W=None):
        self.P.dma(lambda e: e.dma_start(out=out, in_=in_), slot, writes=[slot] if W is None else W)

    def loadc(self, out, in_, slot):
        self.P.dma(lambda e: e.dma_start(out=out, in_=in_), slot, writes=[slot], eng="pool")

    def store(self, out, in_, slot):
        self.P.defer_dma(lambda e: e.dma_start(out=out, in_=in_), slot, reads=[slot])

    def mm(self, items, R, W):
        def f(e):
            r = None
            for (o, l, rh, st, sp) in items:
                r = e.matmul(o, lhsT=l, rhs=rh, start=st, stop=sp)
            return r
        self.P.pe(f, R, W)

    def tr(self, items, ident, R, W):
        def f(e):
            r = None
            for (o, i) in items:
                r = e.transpose(o, i, ident)
            return r
        self.P.pe(f, R, W)

    def scan(self, out, d0, d1, R, W):
        self.P.dve(lambda e: e.tensor_tensor_scan(out=out, data0=d0, data1=d1, initial=0.0, op0=ALU.mult, op1=ALU.add), R, W)

    def recip(self, out, in_, R, W):
        self.P.dve(lambda e: e.reciprocal(out=out, in_=in_), R, W)

    def reduce(self, out, in_, R, W):
        self.P.dve(lambda e: e.tensor_reduce(out=out, in_=in_, axis=AX.X, op=ALU.add), R, W)

    def sss(self, out, in_, scalar, op, R, W):
        self.P.dve(lambda e: e.tensor_single_scalar(out=out, in_=in_, scalar=scalar, op=op), R, W)


class _Stop(Exception):
    pass


def build(LAT, depth=DEPTH, dbg=False, stop=None):
    T = CTX + LAT
    NT = T // 128
    NCH = T // 64
    NG = T // 256
    nc = bass.Bass("TRN2", target_bir_lowering=False)

    def din(name, shape, dt=F32):
        return nc.dram_tensor(name, list(shape), dt, kind="ExternalInput").ap()

    def dscr(name, shape, dt):
        return nc.dram_tensor(name, list(shape), dt, kind="ExternalOutput" if dbg else "Internal").ap()

    XIN = din("xin", [T, D])
    CC_IN = din("cc", [128, 8, 2])
    W_ADA = din("w_ada", [DEPTH, D, 6 * D])
    BADA_COL = din("bada_col", [DEPTH, 128, 4, 8])
    BADA_G = din("bada_g", [DEPTH, 2, 128, D])
    W1 = din("w1", [DEPTH, D, NC1])
    GHR = din("ghr", [DEPTH, 128, D])
    HGL = din("hgl", [128, DEPTH, 2, 2])
    MLB = din("mlb", [DEPTH, 128, 16])
    RTL = din("rtl", [128, DEPTH, 2, 2])
    WA = din("wa", [DEPTH, 2, 16, 256])
    BA = din("ba", [128, DEPTH, 2, 2])
    W_OUT = din("w_out", [DEPTH, D, D])
    W_FF1 = din("w_ff1", [DEPTH, D, DFF])
    W_FF2 = din("w_ff2", [DEPTH, DFF, D])
    GFIN = din("gfin", [128, D])
    ROPEC = din("ropec", [128, T])
    ROPES = din("ropes", [128, T])
    CONSTS = din("consts", [128, 8, 128])
    OUT = nc.dram_tensor("out", [LAT, D], F32, kind="ExternalOutput").ap()

    XS = dscr("xs", [T, D], F32)
    QT = dscr("qt", [2, 8, 128, T], BF16)
    KT = dscr("kt", [2, 8, 128, T], BF16)
    KK = dscr("kk", [2, T, 1024], BF16)
    VV = dscr("vv", [2, T, 16, VD], BF16)
    GG = dscr("gg", [T, D], BF16)
    OO = dscr("oo", [2, T, D], F32)
    HT = dscr("ht", [NT, 128, 8, 128], BF16)

    with ExitStack() as gs:
        P = Prog(nc, gs)
        A = Em(P)
        cnt = [0]

        def sbt(st, shape, dt=F32, name=None):
            cnt[0] += 1
            nm = "%s_%d" % (name or "t", cnt[0])
            return Tl(st.enter_context(nc.sbuf_tensor(nm, list(shape), dt)), nm)

        def pst(st, shape, dt=F32, name=None):
            cnt[0] += 1
            nm = "%s_%d" % (name or "p", cnt[0])
            return Tl(st.enter_context(nc.psum_tensor(nm, list(shape), dt)), nm)

        def rot(st, n, shape, dt=F32, name=None, psum=False):
            return Rot([(pst if psum else sbt)(st, shape, dt, name) for _ in range(n)])

        def sub(tl, ap, name):
            return Tl(ap, name)

        cst = sbt(gs, [128, 8, 128], F32, "cst")
        A.load(cst[:], CONSTS[:, :, :], cst)
        identb = sbt(gs, [128, 128], BF16, "identb")
        A.copy("dve", identb[:], cst[:, 0, :], [cst], [identb])
        maskf = sbt(gs, [64, 2, 64], F32, "maskf")
        A.copy("dve", maskf[:], cst[0:64, 5:7, 0:64], [cst], [maskf])
        masku = maskf[:].bitcast(mybir.dt.uint32)
        rmask = sbt(gs, [128, 512], F32, "rmask")
        A.memset(rmask[:], 1.0, [rmask])
        A.memset(rmask[:].rearrange("p (c t) -> p c t", t=64)[:, :, 0:1], 0.0, [rmask], [rmask])
        onescol = sbt(gs, [128, 1], F32, "onescol")
        A.memset(onescol[:], 1.0, [onescol])
        zcol = sbt(gs, [128, 1], F32, "zcol")
        A.memset(zcol[:], 0.0, [zcol])
        shcol = sbt(gs, [128, 2], F32, "shcol")
        A.memset(shcol[:, 0:1], -HG_SHIFT, [shcol])
        A.memset(shcol[:, 1:2], HG_SHIFT, [shcol], [shcol])
        epscol = sbt(gs, [128, 1], F32, "epscol")
        A.memset(epscol[:], EPS, [epscol])
        ccs = sbt(gs, [128, 8, 2], F32, "ccs")
        A.load(ccs[:], CC_IN[:, :, :], ccs)
        scs = sbt(gs, [128, 8, 2], F32, "scs")
        A.act(scs[:], ccs[:], AF.Silu, [ccs], [scs])
        P.flush()

        def norm_T(pools, src, r0, hT, col0, sc, sh):
            xp, jp, sp_, xnp, ptp = pools
            xt = xp.nxt()
            A.load(xt[:], src[r0:r0 + 128, :], xt)
            jk = jp.nxt()
            ss = sp_.nxt()
            A.act(jk[:], xt[:], AF.Square, [xt], [jk, ss], accum_out=ss[:, 0:1])
            A.act(ss[:, 1:2], ss[:, 0:1], AF.Ln, [ss, epscol], [ss], scale=1.0 / D, bias=epscol[:, 0:1])
            A.act(ss[:, 2:3], ss[:, 1:2], AF.Exp, [ss], [ss], scale=-0.5)
            xn = xnp.nxt()
            A.act(xn[:], xt[:], AF.Identity, [xt, ss], [xn], scale=ss[:, 2:3])
            pt = ptp.nxt()
            A.tr([(pt[:, kt, :], xn[:, kt * 128:(kt + 1) * 128]) for kt in range(8)], identb[:], [xn, identb], [pt])
            if not hasattr(hT, "kb"):
                hT.kb = [Buf("hTk") for _ in range(8)]
            ncount[0] += 1
            for kt in range(8):
                if ncount[0] % 2 == 0:
                    A.act(hT[:, kt, col0:col0 + 128], pt[:, kt, :], AF.Identity, [pt, modcol_ref[0]], [hT.kb[kt]], scale=sc[kt], bias=sh[kt])
                else:
                    A.ts("dve", hT[:, kt, col0:col0 + 128], pt[:, kt, :], sc[kt], sh[kt], ALU.mult, ALU.add, [pt, modcol_ref[0]], [hT.kb[kt]])
            return xt

        modcol_ref = [None]
        ncount = [0]

        def load_w_cast(dst, src2, nk):
            for k0 in range(0, nk, 8):
                A.loadc(dst[:, k0:k0 + 8, :], src2[k0 * 128:(k0 + 8) * 128, :].rearrange("(kt p) n -> p kt n", p=128), dst)

        def decay_core(lf, n, d, escale, bT, Dt, E1, E2, cc, cctl, tmpc, sh=0.0):
            if d == 0:
                A.scan(bT[:, 0:n], rmask[:, 0:n], lf[:, 0:n], [lf, rmask], [bT])
            else:
                A.scan(bT[:, 0:n][:, ::-1], rmask[:, 0:n], lf[:, 0:n][:, ::-1], [lf, rmask], [bT])
            nb = n // 64
            mid = 31 if d == 0 else 32
            last = 63 if d == 0 else 0
            b3 = bT[:, 0:n].rearrange("p (c t) -> p c t", t=64)
            D3 = Dt[:, 0:n].rearrange("p (c t) -> p c t", t=64)
            A.tt("pool", D3, b3, b3[:, :, mid:mid + 1].to_broadcast([128, nb, 64]), ALU.subtract, [bT], [Dt])
            bneg = shcol[:, 0:1] if sh else zcol[:, 0:1]
            bpos = shcol[:, 1:2] if sh else zcol[:, 0:1]
            A.act(E1[:, 0:n], Dt[:, 0:n], AF.Exp, [Dt], [E1], scale=escale, bias=bneg)
            A.act(E2[:, 0:n], Dt[:, 0:n], AF.Exp, [Dt], [E2], scale=-escale, bias=bneg)
            ref2 = bT[:, mid:n:64]
            bl2 = bT[:, last:n:64]
            A.act(cc[0], ref2, AF.Exp, [bT], [cctl], scale=escale, bias=bneg)
            A.act(cc[1], bl2, AF.Exp, [bT], [cctl], scale=escale)
            A.tt("pool", tmpc[:, 0:nb], bl2, ref2, ALU.subtract, [bT], [tmpc])
            A.act(cc[2], tmpc[:, 0:nb], AF.Exp, [tmpc], [cctl], scale=escale, bias=bpos)

        blocks = [(0, CTX, 0)] + [(CTX + 512 * i, 512, 1) for i in range(LAT // 512)]

        try:
          for l in range(depth):
            xsrc = XIN if l == 0 else XS
            with ExitStack() as ls:
                modcol = sbt(ls, [128, 4, 8, 2], F32, "modcol")
                modcol_ref[0] = modcol

                def mcol(which, cl):
                    return [modcol[:, which, kt, cl:cl + 1] for kt in range(8)]

                with ExitStack() as ls2:
                    CCt = sbt(ls2, [128, 4, 2, 2, 3, NCH], F32, "CCt")
                    ALt = sbt(ls2, [128, NCH, 8], F32, "ALt")
                    FLt = sbt(ls2, [64, NCH, 8], F32, "FLt")
                    ALP = sbt(ls2, [128, NCH, 2, 2], F32, "ALP")
                    Ec = [[[sbt(ls2, [128, 512], F32, "Ec") for _ in range(2)] for _ in range(2)] for _ in range(2)]
                    lbt = sbt(ls2, [128, 2, 2, 2], F32, "lbt")
                    lgam = sbt(ls2, [128, 2, 2], F32, "lgam")
                    bat = sbt(ls2, [128, 2, 2], F32, "bat")
                    wat = sbt(ls2, [16, 2, 256], F32, "wat")
                    mlbt = sbt(ls2, [128, 16], F32, "mlbt")
                    ghr = sbt(ls2, [128, D], F32, "ghr")

                    with ExitStack() as ph:
                        wad = rot(ph, 2, [128, 8, 512], F32, "wad")
                        pm = pst(ph, [128, 64], F32, "pm")
                        bcol = sbt(ph, [128, 4, 8], F32, "bcol")
                        A.load(bcol[:], BADA_COL[l, :, :, :], bcol)
                        for which, vec in enumerate((0, 1, 3, 4)):
                            for half in range(2):
                                blk = vec * 2 + half
                                w = wad.nxt()
                                A.load(w[:], W_ADA[l, :, blk * 512:(blk + 1) * 512].rearrange("(kt p) n -> p kt n", p=128), w)
                                items = []
                                for f4 in range(4):
                                    c0 = (which * 8 + half * 4 + f4) * 2
                                    for kt in range(8):
                                        items.append((pm[:, c0:c0 + 2], w[:, kt, f4 * 128:(f4 + 1) * 128], scs[:, kt, :], kt == 0, kt == 7))
                                A.mm(items, [w, scs], [pm])
                        A.tt("dve", modcol[:].rearrange("p a b c -> p (a b) c"), pm[:].rearrange("p (a c) -> p a c", c=2),
                             bcol[:].rearrange("p a b -> p (a b)").unsqueeze(2).to_broadcast([128, 32, 2]), ALU.add, [pm, bcol], [modcol])
                        for which in (1, 3):
                            A.ts("dve", modcol[:, which, :, :], modcol[:, which, :, :], 1.0, None, ALU.add, None, [modcol], [modcol])
                        hgl = sbt(ph, [128, DEPTH, 2, 2], F32, "hgl")
                        A.load(hgl[:], HGL[:, :, :, :], hgl)
                        if l == 0:
                            A.memset(lbt[:, 0, :, :], 0.0, [lbt])
                        else:
                            dl = sbt(ph, [128, 2, 2], F32, "dl")
                            A.tt("dve", dl[:], hgl[:, 1, :, :], hgl[:, 0, :, :], ALU.subtract, [hgl], [dl])
                            A.act(lbt[:, 0, :, :], dl[:], AF.Sigmoid, [dl], [lbt])
                        A.ts("dve", lbt[:, 1, :, :], lbt[:, 0, :, :], -1.0, 1.0, ALU.mult, ALU.add, [lbt], [lbt])
                        rtl = sbt(ph, [128, DEPTH, 2, 2], F32, "rtl")
                        A.load(rtl[:], RTL[:, :, :, :], rtl)
                        sgr = sbt(ph, [128, 2, 2], F32, "sgr")
                        A.act(sgr[:], rtl[:, l, :, :], AF.Sigmoid, [rtl], [sgr])
                        A.act(lgam[:], sgr[:], AF.Ln, [sgr], [lgam])
                        bap = sbt(ph, [128, DEPTH, 2, 2], F32, "bap")
                        A.load(bap[:], BA[:, :, :, :], bap)
                        A.copy("dve", bat[:], bap[:, l, :, :], [bap], [bat])
                        A.load(wat[:], WA[l, :, :, :].rearrange("d r c -> r d c"), wat)
                        A.load(mlbt[:], MLB[l, :, :], mlbt)
                        A.load(ghr[:], GHR[l, :, :], ghr)
                        lfc = sbt(ph, [128, 512], F32, "lfc")
                        bTc = sbt(ph, [128, 512], F32, "bTc")
                        Dtc = sbt(ph, [128, 512], F32, "Dtc")
                        ctmp = sbt(ph, [128, 3, 8], F32, "ctmp")
                        tmpc = sbt(ph, [128, 8], F32, "tmpc")
                        for d in range(2):
                            for j in range(2):
                                A.copy("dve", lfc[:], lgam[:, d, j:j + 1].to_broadcast([128, 512]), [lgam], [lfc])
                                decay_core(lfc, 512, d, 1.0, bTc, Dtc, Ec[d][j][0], Ec[d][j][1],
                                           [ctmp[:, k, :] for k in range(3)], ctmp, tmpc)
                                for kind in range(3):
                                    A.copy("dve", CCt[:, RT, d, j, kind, :], ctmp[:, kind, 0:1].to_broadcast([128, NCH]), [ctmp], [CCt])
                        P.flush()
                        if stop == 'PA':
                            raise _Stop()

                    with ExitStack() as ph:
                        w1 = sbt(ph, [128, 8, NFM], BF16, "w1fm")
                        load_w_cast(w1, W1[l, :, 0:NFM], 8)
                        hTp = rot(ph, 2, [128, 8, 512], BF16, "hT")
                        npools = (rot(ph, 2, [128, D], F32, "xt"), rot(ph, 1, [128, D], BF16, "jk"), rot(ph, 2, [128, 4], F32, "ss"),
                                  rot(ph, 2, [128, D], BF16, "xn"), rot(ph, 2, [128, 8, 128], BF16, "ptr", psum=True))
                        pf = rot(ph, 4, [128, 512], F32, "pf", psum=True)
                        ptk = rot(ph, 2, [128, 4, 128], BF16, "ptk", psum=True)
                        wk = rot(ph, 10, [128, 512], F32, "wk")
                        fq = rot(ph, 4, [128, 512], F32, "fq")
                        bfp = rot(ph, 6, [128, 512], BF16, "bfp")
                        kst = [sbt(ph, [128, 4, 1024], BF16, "kst") for _ in range(2)]
                        rope = [sbt(ph, [128, 512], F32, "rope") for _ in range(2)]
                        ga = [sbt(ph, [16, 512], F32, "ga") for _ in range(2)]
                        tmpcp = rot(ph, 2, [128, 8], F32, "tmpc")

                        def do_block(t0, n, cl):
                            ntile = n // 128
                            ch0 = t0 // 64
                            nb = n // 64
                            hT = hTp.nxt()
                            for i in range(ntile):
                                norm_T(npools, xsrc, t0 + 128 * i, hT, 128 * i, mcol(1, cl), mcol(0, cl))
                                P.defer_dma(lambda e, o_=HT[t0 // 128 + i, :, :, :], i_=hT[:, :, 128 * i:128 * (i + 1)]: e.dma_start(out=o_, in_=i_),
                                            hT.b, reads=list(hT.kb))
                            A.load(rope[0][:, 0:n], ROPEC[:, t0:t0 + n], rope[0])
                            A.load(rope[1][:, 0:n], ROPES[:, t0:t0 + n], rope[1])
                            P.release()

                            def fm(col0, M=128):
                                ps = pf.nxt()
                                A.mm([(ps[0:M, 0:n], w1[:, kt, col0:col0 + M], hT[:, kt, 0:n], kt == 0, kt == 7) for kt in range(8)],
                                     [w1] + hT.kb, [ps])
                                return ps

                            def finish(m, d, j, q, k, lf, escale, kscale, E=None, dirs=None, sh=0.0):
                                P.release()
                                if E is None:
                                    bT, Dt, E1, E2 = wk.nxt(), wk.nxt(), wk.nxt(), wk.nxt()
                                    decay_core(lf, n, d, escale, bT, Dt, E1, E2,
                                               [CCt[:, m, d, j, kind, ch0:ch0 + nb] for kind in range(3)], CCt, tmpcp.nxt(), sh=sh)
                                elif E == "none":
                                    E1 = E2 = None
                                else:
                                    E1, E2 = E
                                qh = bfp.nxt()
                                kh = bfp.nxt()
                                if E1 is None:
                                    A.copy("act", qh[:, 0:n], q[:, 0:n], [q], [qh])
                                    P.act(lambda e: e.mul(out=kh[:, 0:n], in_=k[:, 0:n], mul=kscale), [k], [kh])
                                else:
                                    A.tt("dve", qh[:, 0:n], q[:, 0:n], E1[:, 0:n], ALU.mult, [q, E1], [qh])
                                    A.stt(kh[:, 0:n], k[:, 0:n], kscale, E2[:, 0:n], ALU.mult, ALU.mult, [k, E2], [kh])
                                for dd in (dirs or [d]):
                                    A.store(QT[dd, m * 2 + j, :, t0:t0 + n], qh[:, 0:n], qh)
                                    A.store(KT[dd, m * 2 + j, :, t0:t0 + n], kh[:, 0:n], kh)
                                pk = ptk.nxt()
                                A.tr([(pk[:, i, :], kh[:, 128 * i:128 * (i + 1)]) for i in range(ntile)], identb[:], [kh, identb], [pk])
                                for dd in (dirs or [d]):
                                    A.copy("dve", kst[dd][:, 0:ntile, m * 256 + j * 128:m * 256 + (j + 1) * 128], pk[:, 0:ntile, :],
                                           [pk], [kst[dd]])

                            for j in range(2):
                                psq = fm((0 + j) * 128)
                                qs = fq.nxt()
                                sgq = wk.nxt()
                                A.act(sgq[:, 0:n], psq[:, 0:n], AF.Sigmoid, [psq], [sgq, psq])
                                A.tt("dve", qs[:, 0:n], psq[:, 0:n], sgq[:, 0:n], ALU.mult, [psq, sgq], [qs])
                                fs = []
                                for d in range(2):
                                    psz = fm((2 + 2 * d + j) * 128)
                                    sg, f = wk.nxt(), wk.nxt()
                                    A.act(sg[:, 0:n], psz[:, 0:n], AF.Sigmoid, [psz], [sg])
                                    A.ts("dve", f[:, 0:n], sg[:, 0:n], lbt[:, 1, d, j:j + 1], lbt[:, 0, d, j:j + 1], ALU.mult, ALU.add,
                                         [sg, lbt], [f])
                                    fs.append(f)
                                for d in range(2):
                                    f = fs[d]
                                    lf, kk = wk.nxt(), wk.nxt()
                                    A.act(lf[:, 0:n], f[:, 0:n], AF.Ln, [f], [lf])
                                    A.act(kk[:, 0:n], f[:, 0:n], AF.Identity, [f, onescol], [kk], scale=-1.0, bias=onescol[:, 0:1])
                                    finish(HG, d, j, qs, kk, lf, 1.0, 1.0, sh=HG_SHIFT)
                            for j in range(2):
                                psq = fm((6 + j) * 128)
                                psk = fm((8 + j) * 128)
                                finish(ML, 0, j, psq, psk, None, 1.0, 0.125, E="none", dirs=[0, 1])
                            for j in range(2):
                                rr = []
                                for base in (10, 14):
                                    ps0 = fm((base + j) * 128)
                                    ps1 = fm((base + 2 + j) * 128)
                                    t1, t2 = wk.nxt(), wk.nxt()
                                    r = fq.nxt()
                                    A.tt("dve", t1[:, 0:n], ps0[:, 0:n], rope[0][:, 0:n], ALU.mult, [ps0, rope[0]], [t1])
                                    A.tt("dve", t2[:, 0:n], ps1[:, 0:n], rope[1][:, 0:n], ALU.mult, [ps1, rope[1]], [t2])
                                    A.tt("pool", r[:, 0:n], t1[:, 0:n], t2[:, 0:n], ALU.add, [t1, t2], [r])
                                    rr.append(r)
                                for d in range(2):
                                    finish(RT, d, j, rr[0], rr[1], None, 1.0, 0.125, E=(Ec[d][j][0], Ec[d][j][1]))
                            for d in range(2):
                                psa = fm(22 * 128 + 16 * d, M=16)
                                A.copy("act", ga[d][:, 0:n], psa[0:16, 0:n], [psa], [ga[d]])
                            for j in range(2):
                                psq = fm((18 + j) * 128)
                                psk = fm((20 + j) * 128)
                                qr, kr = fq.nxt(), fq.nxt()
                                A.copy("act", qr[:, 0:n], psq[:, 0:n], [psq], [qr])
                                A.copy("dve", kr[:, 0:n], psk[:, 0:n], [psk], [kr])
                                sgs = []
                                for d in range(2):
                                    psz = pf.nxt()
                                    A.mm([(psz[:, 0:n], wat[0:16, d, j * 128:(j + 1) * 128], ga[d][0:16, 0:n], True, True)], [wat, ga[d]], [psz])
                                    sg = wk.nxt()
                                    A.act(sg[:, 0:n], psz[:, 0:n], AF.Sigmoid, [psz, bat], [sg], bias=bat[:, d, j:j + 1])
                                    sgs.append(sg)
                                for d in range(2):
                                    sg = sgs[d]
                                    lf = wk.nxt()
                                    A.act(lf[:, 0:n], sg[:, 0:n], AF.Ln, [sg], [lf])
                                    finish(GL, d, j, qr, kr, lf, 1.0 / 16.0, 32.0 ** -0.5)
                            for dd in range(2):
                                A.store(KK[dd, t0:t0 + n, :].rearrange("(i p) x -> p i x", p=128), kst[dd][:, 0:ntile, :], kst[dd])

                        for (t0, n, cl) in blocks:
                            do_block(t0, n, cl)
                        P.flush()
                        if stop == 'P1a':
                            raise _Stop()

                    with ExitStack() as ph:
                        w1 = sbt(ph, [128, 8, NTM], BF16, "w1tm")
                        load_w_cast(w1, W1[l, :, NFM:NC1], 8)
                        hTp = rot(ph, 4, [128, 8, 128], BF16, "hT")
                        npools = (rot(ph, 4, [128, D], F32, "xt"), rot(ph, 2, [128, D], BF16, "jk"), rot(ph, 4, [128, 4], F32, "ss"),
                                  rot(ph, 4, [128, D], BF16, "xn"), rot(ph, 2, [128, 8, 128], BF16, "ptr", psum=True))
                        pt = rot(ph, 4, [128, 512], F32, "pt", psum=True)
                        psm = rot(ph, 2, [128, 64], F32, "psm", psum=True)
                        vstp = rot(ph, 4, [128, 16, VD], BF16, "vst")
                        vmlp = rot(ph, 4, [128, 4, VD], BF16, "vml")
                        vs1p = rot(ph, 4, [128, 4, VD], BF16, "vs1")
                        for tl_ in vstp.items + vmlp.items:
                            A.memset(tl_[:], 0.0, [tl_])
                            A.memset(tl_[:, :, 64:65], 1.0, [tl_], [tl_])
                        sgp = rot(ph, 6, [128, 512], F32, "sgp")
                        ggp = rot(ph, 4, [128, D], BF16, "ggp")
                        smp = rot(ph, 4, [128, 64], F32, "smp")

                        def do_tile(ti):
                            r0 = ti * 128
                            cl = 0 if r0 < CTX else 1
                            hT = hTp.nxt()
                            hT.kb = [hT.b]
                            A.load(hT[:], HT[ti, :, :, :], hT)
                            P.release(keep=3)

                            def tm(col0, N):
                                ps = pt.nxt()
                                import os
                                if os.environ.get("EXPA"):
                                    A.mm([(ps[:, 0:128], w1[:, kt, col0:col0 + 128], hT[:, kt, :], kt == 0, kt == 7) for kt in range(8)], [w1] + hT.kb, [ps])
                                elif os.environ.get("EXPB"):
                                    A.mm([(ps[:, 0:256], hT[:, kt, :], w1[:, kt, col0:col0 + 256], kt == 0, kt == 7) for kt in range(8)], [w1] + hT.kb, [ps])
                                else:
                                    items = []
                                    for n0 in range(0, N, 256):
                                        n1 = min(N, n0 + 256)
                                        items += [(ps[:, n0:n1], hT[:, kt, :], w1[:, kt, col0 + n0:col0 + n1], kt == 0, kt == 7) for kt in range(8)]
                                    A.mm(items, [w1] + hT.kb, [ps])
                                return ps
                            vst, vml, vs1 = vstp.nxt(), vmlp.nxt(), vs1p.nxt()
                            import os
                            SKIP = os.environ.get("SKIP", "")
                            if "v" in SKIP:
                                return
                            psA = tm(0, 512)
                            if "1" in SKIP:
                                return
                            if "a" not in SKIP:
                                A.copy("act", vst[:, 0:4, 0:64], psA[:, 0:256].rearrange("p (h v) -> p h v", v=64), [psA], [vst])
                            if "b" not in SKIP:
                                A.copy("act", vml[:, :, 0:64], psA[:, 256:512].rearrange("p (h v) -> p h v", v=64), [psA], [vml])
                            if "2" in SKIP:
                                return
                            psB = tm(512, 512)
                            A.copy("dve", vst[:, 8:16, 0:64], psB[:, 0:512].rearrange("p (h v) -> p h v", v=64), [psB], [vst])
                            if "g" in SKIP:
                                return
                            gg = ggp.nxt()
                            psC = tm(1024, 512)
                            s1 = sgp.nxt()
                            A.act(s1[:], psC[:], AF.Sigmoid, [psC], [s1])
                            A.tt("dve", gg[:, 0:512], s1[:], ghr[:, 0:512], ALU.mult, [s1, ghr], [gg])
                            psD = tm(1536, 512)
                            s2 = sgp.nxt()
                            A.act(s2[:], psD[:], AF.Sigmoid, [psD], [s2, psD])
                            A.tt("dve", s2[:], psD[:], s2[:], ALU.mult, [psD, s2], [s2])
                            A.tt("pool", gg[:, 512:1024], s2[:], ghr[:, 512:1024], ALU.mult, [s2, ghr], [gg])
                            A.store(GG[r0:r0 + 128, :], gg[:], gg)
                            if "m" in SKIP:
                                return
                            psG = tm(2048, 16)
                            sm = smp.nxt()
                            A.tt("dve", sm[:, 0:16], psG[:, 0:16], mlbt[:], ALU.add, [psG, mlbt], [sm])
                            gv = sm[:, 0:16].rearrange("p (d g h) -> p d g h", d=2, g=2)
                            A.act(sm[:, 16:24].rearrange("p (d h) -> p d h", d=2), gv[:, :, 1, :], AF.Sigmoid, [sm], [sm])
                            A.act(sm[:, 24:32], sm[:, 16:24], AF.Ln, [sm], [sm])
                            pb = psm.nxt()
                            A.mm([(pb[:, 0:4], cst[:, 1, :], sm[:, 24:28], True, True),
                                  (pb[:, 4:8], cst[:, 2, :], sm[:, 28:32], True, True),
                                  (pb[:, 8:16], cst[:, 3, :], sm[:, 24:32], True, True),
                                  (pb[:, 16:24], cst[:, 4, :], sm[:, 24:32], True, True)], [sm, cst], [pb])
                            A.act(ALt[:, 2 * ti:2 * ti + 2, :], pb[:, 8:24].rearrange("p (c g) -> p c g", g=8), AF.Exp, [pb], [ALt, pb])
                            for hh_ in range(2):
                                A.copy("pool", ALP[64 * hh_:64 * hh_ + 64, 2 * ti:2 * ti + 2, :, :],
                                       ALt[64 * hh_:64 * hh_ + 64, 2 * ti:2 * ti + 2, :].rearrange("p c (d a b) -> p c d a b", d=2, a=2)[:, :, :, :, hh_],
                                       [ALt], [ALP])
                            A.tt("dve", sm[:, 32:40].rearrange("p (d h) -> p d h", d=2), gv[:, :, 0, :],
                                 pb[:, 0:8].rearrange("p (d h) -> p d h", d=2), ALU.subtract, [pb, sm], [sm, pb])
                            A.act(sm[:, 40:48], sm[:, 32:40], AF.Exp, [sm], [sm])
                            if "f" in SKIP:
                                return
                            A.act(sm[:, 48:56], pb[:, 0:8], AF.Exp, [pb], [sm, pb], scale=-1.0)
                            A.copy("dve", FLt[0:64, 2 * ti, :], sm[0:64, 48:56], [sm], [FLt])
                            A.copy("dve", FLt[0:64, 2 * ti + 1, :], sm[64:128, 48:56], [sm], [FLt])
                            if "w" in SKIP:
                                return
                            A.tt("dve", vst[:, 4:8, :], vml[:], sm[:, 40:44].unsqueeze(2).to_broadcast([128, 4, VD]), ALU.mult, [sm, vml], [vst])
                            A.tt("dve", vs1[:], vml[:], sm[:, 44:48].unsqueeze(2).to_broadcast([128, 4, VD]), ALU.mult, [sm, vml], [vs1])
                            A.store(VV[0, r0:r0 + 128, :, :], vst[:], vst)
                            A.store(VV[1, r0:r0 + 128, 4:8, :], vs1[:], vs1)

                        for ti in range(NT):
                            do_tile(ti)
                        P.flush()
                        if stop == 'P1b':
                            raise _Stop()

                    with ExitStack() as ph:
                        qTp = [rot(ph, 2, [128, 8, 2, 256], BF16, "qbd") for _ in range(2)]
                        for d_ in range(2):
                            for t_ in qTp[d_].items:
                                A.memset(t_[:], 0.0, [t_])
                        kTp = [rot(ph, 2, [128, 8, 256], BF16, "kT") for _ in range(2)]
                        ktp = [rot(ph, 2, [64, 4, 1024], BF16, "ktok") for _ in range(2)]
                        vvp = [rot(ph, 2, [64, 4, 16 * VD], BF16, "vv") for _ in range(2)]
                        vmp = rot(ph, 2, [64, 4, 4 * VD], BF16, "vm")
                        S = [[[sbt(ph, [128, VD], F32, "S") for _ in range(2)] for _ in range(4)] for _ in range(2)]
                        Sb = [[[sbt(ph, [128, VD], BF16, "Sb") for _ in range(2)] for _ in range(4)] for _ in range(2)]
                        for d in range(2):
                            for m in range(4):
                                for hp in range(2):
                                    A.memset(S[d][m][hp][:], 0.0, [S[d][m][hp]])
                        psA = Rot([Tl(t_[:, 0:256].rearrange("p (h v) -> p h v", v=64), "psAv") for t_ in
                                   [pst(ph, [64, 512], F32, "psA") for _ in range(4)]])
                        pso = Rot([Tl(t_[:, 0:4 * VD].rearrange("p (h v) -> p h v", v=VD), "psov") for t_ in
                                   [pst(ph, [64, 512], F32, "pso") for _ in range(2)]])
                        psS = Rot([Tl(t_[:, 0:4 * VD].rearrange("p (a v) -> p a v", v=2 * VD), "psSv") for t_ in
                                   [pst(ph, [128, 512], F32, "psS") for _ in range(2)]])
                        mask4 = sbt(ph, [64, 2, 4, 64], F32, "mask4")
                        for d_ in range(2):
                            A.copy("dve", mask4[:, d_, :, :], maskf[:, d_, :].unsqueeze(1).to_broadcast([64, 4, 64]), [maskf], [mask4])
                        mask4u = mask4[:].bitcast(mybir.dt.uint32)
                        Asbd = [rot(ph, 5, [64, 4, 64], BF16, "Asb") for _ in range(2)]
                        for dd_ in range(2):
                            for t_ in Asbd[dd_].items:
                                A.memset(t_[:], 0.0, [t_])
                        tmpS = rot(ph, 6, [128, VD], F32, "tmpS")
                        osb = rot(ph, 2, [64, D], F32, "osb")
                        nrm = rot(ph, 2, [64, 16], F32, "nrm")

                        def col(m, d, hp, kind, c):
                            if m == ML:
                                if kind == 0:
                                    return onescol[:, 0:1], onescol
                                return ALP[:, c, d, hp:hp + 1], ALP
                            return CCt[:, m, d, hp, kind, c:c + 1], CCt

                        def do_chunk(d, c, cc, qT, kT, ktok, vv, vm):
                            ts = slice(64 * cc, 64 * cc + 64)
                            ob = osb.nxt()
                            vs = []
                            for m in range(4):
                                if m == ML and d == 1:
                                    vs.append((vm, 0))
                                else:
                                    vs.append((vv, 4 * m * VD))
                            a4s = []
                            for m in range(4):
                                for hp in range(2):
                                    St, Sbt = S[d][m][hp], Sb[d][m][hp]
                                    c1, c1t = col(m, d, hp, 0, c)
                                    A.act(Sbt[:, :], St[:, :], AF.Identity, [St, c1t], [Sbt], scale=c1)
                                pa = psA.nxt()
                                A.mm([(pa[:, h, :], kT[:, 2 * m + h // 2, ts], qT[:, 2 * m + h // 2, h % 2, ts], True, True) for h in range(4)],
                                     [kT, qT], [pa])
                                a4 = Asbd[d].nxt()
                                A.tt("dve", a4[:], pa[:], mask4[:, d, :, :], ALU.mult, [pa, mask4], [a4])
                                a4s.append(a4)
                            pos, pSs = [], []
                            for m in range(4):
                                vsrc, vbase = vs[m]
                                a4 = a4s[m]
                                po = pso.nxt()
                                items = []
                                for h in range(4):
                                    items.append((po[:, h, :], a4[:, h, :], vsrc[0:64, cc, vbase + h * VD:vbase + (h + 1) * VD], True, False))
                                    items.append((po[:, h, :], qT[:, 2 * m + h // 2, h % 2, ts], Sb[d][m][h // 2][:, :], False, True))
                                A.mm(items, [a4, vsrc, qT, Sb[d][m][0], Sb[d][m][1]], [po])
                                pS = psS.nxt()
                                A.mm([(pS[:, hp, :], ktok[0:64, cc, m * 256 + hp * 128:m * 256 + (hp + 1) * 128],
                                       vsrc[0:64, cc, vbase + 2 * hp * VD:vbase + (2 * hp + 2) * VD], True, True) for hp in range(2)],
                                     [ktok, vsrc], [pS])
                                for hp in range(2):
                                    St = S[d][m][hp]
                                    tS = tmpS.nxt()
                                    c2, c2t = col(m, d, hp, 1, c)
                                    c3, c3t = col(m, d, hp, 2, c)
                                    A.act(tS[:, :], St[:, :], AF.Identity, [St, c2t], [tS], scale=c2)
                                    for hh in range(2):
                                        p0 = 64 * hh
                                        A.stt(St[p0:p0 + 64, :], pS[p0:p0 + 64, hp, hh * VD:(hh + 1) * VD], c3[p0:p0 + 64, :], tS[p0:p0 + 64, :],
                                              ALU.mult, ALU.add, [pS, tS, c3t], [St])
                                ov = ob[:, m * 256:(m + 1) * 256].rearrange("p (h v) -> p h v", v=64)
                                if m == HG:
                                    P.act(lambda e, ov=ov, pin=po[:, :, 0:64]: e.mul(out=ov, in_=pin, mul=float(np.exp(2.0 * HG_SHIFT))), [po], [ob])
                                elif m != ML:
                                    A.copy("act", ov, po[:, :, 0:64], [po], [ob])
                                else:
                                    nr = nrm.nxt()
                                    A.act(nr[:, 0:4], po[:, :, 64], AF.Abs, [po], [nr, po])
                                    A.tt("dve", nr[:, 4:8], nr[:, 0:4], FLt[0:64, c, d * 4:d * 4 + 4], ALU.max, [nr, FLt], [nr])
                                    A.recip(nr[:, 8:12], nr[:, 4:8], [nr], [nr])
                                    A.tt("dve", ov, po[:, :, 0:64], nr[:, 8:12].unsqueeze(2).to_broadcast([64, 4, 64]), ALU.mult, [nr, po], [ob, po])
                            A.store(OO[d, 64 * c:64 * c + 64, :], ob[:], ob)

                        grp_order = [list(range(NG)), [0] + list(range(NG - 1, 0, -1))]
                        cc_order = [[0, 1, 2, 3], [3, 2, 1, 0]]
                        for gi in range(NG):
                            grp = []
                            for d in range(2):
                                g = grp_order[d][gi]
                                tg0 = 256 * g
                                qT, kT, ktok, vv = qTp[d].nxt(), kTp[d].nxt(), ktp[d].nxt(), vvp[d].nxt()
                                A.load(qT[0:64, :, 0, :], QT[d, :, 0:64, tg0:tg0 + 256].rearrange("m p t -> p m t"), qT)
                                A.load(qT[64:128, :, 1, :], QT[d, :, 64:128, tg0:tg0 + 256].rearrange("m p t -> p m t"), qT)
                                A.load(kT[:], KT[d, :, :, tg0:tg0 + 256].rearrange("m p t -> p m t"), kT)
                                A.load(ktok[:], KK[d, tg0:tg0 + 256, :].rearrange("(c p) x -> p c x", p=64), ktok)
                                A.load(vv[:], VV[0, tg0:tg0 + 256, :, :].rearrange("(c p) h v -> p c (h v)", p=64), vv)
                                vm = None
                                if d == 1:
                                    vm = vmp.nxt()
                                    A.load(vm[:], VV[1, tg0:tg0 + 256, 4:8, :].rearrange("(c p) h v -> p c (h v)", p=64), vm)
                                grp.append((g, qT, kT, ktok, vv, vm))
                            for k_ in range(4):
                                for d in range(2):
                                    g, qT, kT, ktok, vv, vm = grp[d]
                                    cc = cc_order[d][k_]
                                    P.release()
                                    do_chunk(d, 4 * g + cc, cc, qT, kT, ktok, vv, vm)
                        P.flush()
                        if stop == 'P2':
                            raise _Stop()

                with ExitStack() as ph:
                    g1 = [sbt(ph, [128, D], F32, "g1") for _ in range(2)]
                    g2 = [sbt(ph, [128, D], F32, "g2") for _ in range(2)]

                    def gtiles(gi, dst):
                        with ExitStack() as ph2:
                            wad = rot(ph2, 2, [128, 8, 512], F32, "wad")
                            bg = sbt(ph2, [128, D], F32, "bg")
                            pg = rot(ph2, 2, [128, 512], F32, "pg", psum=True)
                            crep = [sbt(ph2, [128, 8, 128], F32, "crep") for _ in range(2)]
                            for cl in range(2):
                                A.copy("dve", crep[cl][:], scs[:, :, cl:cl + 1].to_broadcast([128, 8, 128]), [scs], [crep[cl]])
                            A.load(bg[:], BADA_G[l, gi, :, :], bg)
                            vec = (2, 5)[gi]
                            for half in range(2):
                                blk = vec * 2 + half
                                w = wad.nxt()
                                A.load(w[:], W_ADA[l, :, blk * 512:(blk + 1) * 512].rearrange("(kt p) n -> p kt n", p=128), w)
                                for cl in range(2):
                                    ps = pg.nxt()
                                    A.mm([(ps[:], crep[cl][:, kt, :], w[:, kt, :], kt == 0, kt == 7) for kt in range(8)], [w, crep[cl]], [ps])
                                    A.tt("dve", dst[cl][:, half * 512:(half + 1) * 512], ps[:], bg[:, half * 512:(half + 1) * 512], ALU.add,
                                         [ps, bg], [dst[cl]])
                            P.flush()
                            if stop == 'P3a':
                                raise _Stop()
                    gtiles(0, g1)
                    wf1 = sbt(ph, [128, 8, DFF], BF16, "wf1")
                    load_w_cast(wf1, W_FF1[l, :, :], 8)
                    with ExitStack() as ph2:
                        wo = sbt(ph2, [128, 8, D], BF16, "wo")
                        load_w_cast(wo, W_OUT[l, :, :], 8)
                        o0p = rot(ph2, 3, [128, D], F32, "o0")
                        o1p = rot(ph2, 3, [128, D], F32, "o1")
                        gglp = rot(ph2, 3, [128, D], BF16, "ggl")
                        xp = rot(ph2, 4, [128, D], F32, "x3")
                        sqp = rot(ph2, 2, [128, D], F32, "sq")
                        ssp = rot(ph2, 3, [128, 48], F32, "ss3")
                        yp = rot(ph2, 3, [128, D], BF16, "y")
                        yTp = rot(ph2, 3, [128, 8, 128], BF16, "yT")
                        ptr = rot(ph2, 2, [128, 8, 128], BF16, "ptr3", psum=True)
                        pop = rot(ph2, 4, [128, 512], F32, "pop", psum=True)
                        tp = rot(ph2, 4, [128, 512], F32, "t3")

                        def do_tile3(ti):
                            r0 = ti * 128
                            cl = 0 if r0 < CTX else 1
                            o0, o1, ggl, xt = o0p.nxt(), o1p.nxt(), gglp.nxt(), xp.nxt()
                            A.load(o0[:], OO[0, r0:r0 + 128, :], o0)
                            A.load(o1[:], OO[1, r0:r0 + 128, :], o1)
                            A.load(ggl[:], GG[r0:r0 + 128, :], ggl)
                            A.load(xt[:], xsrc[r0:r0 + 128, :], xt)
                            P.release(keep=1)
                            A.tt("pool", o0[:], o0[:], o1[:], ALU.add, [o0, o1], [o0])
                            sq, ss = sqp.nxt(), ssp.nxt()
                            A.act(sq[:], o0[:], AF.Square, [o0], [sq])
                            A.reduce(ss[:, 0:16], sq[:].rearrange("p (h v) -> p h v", v=64), [sq], [ss])
                            A.act(ss[:, 16:32], ss[:, 0:16], AF.Ln, [ss, epscol], [ss], scale=1.0 / 64, bias=epscol[:, 0:1])
                            A.act(ss[:, 32:48], ss[:, 16:32], AF.Exp, [ss], [ss], scale=-0.5)
                            o3 = o0[:].rearrange("p (h v) -> p h v", v=64)
                            A.tt("dve", o3, o3, ss[:, 32:48].unsqueeze(2).to_broadcast([128, 16, 64]), ALU.mult, [o0, ss], [o0])
                            y = yp.nxt()
                            A.tt("pool", y[:], o0[:], ggl[:], ALU.mult, [o0, ggl], [y])
                            pt = ptr.nxt()
                            A.tr([(pt[:, kt, :], y[:, kt * 128:(kt + 1) * 128]) for kt in range(8)], identb[:], [y, identb], [pt])
                            yT = yTp.nxt()
                            A.copy("act", yT[:, 0:4, :], pt[:, 0:4, :], [pt], [yT])
                            A.copy("dve", yT[:, 4:8, :], pt[:, 4:8, :], [pt], [yT])
                            for nb in range(2):
                                ps = pop.nxt()
                                A.mm([(ps[:], yT[:, kt, :], wo[:, kt, nb * 512:(nb + 1) * 512], kt == 0, kt == 7) for kt in range(8)], [yT, wo], [ps])
                                t = tp.nxt()
                                A.tt("dve", t[:], ps[:], g1[cl][:, nb * 512:(nb + 1) * 512], ALU.mult, [ps, g1[cl]], [t])
                                A.tt("pool", xt[:, nb * 512:(nb + 1) * 512], xt[:, nb * 512:(nb + 1) * 512], t[:], ALU.add, [t, xt], [xt])
                            A.store(XS[r0:r0 + 128, :], xt[:], xt)

                        for ti in range(NT):
                            do_tile3(ti)
                        P.flush()
                        if stop == 'P3a':
                            raise _Stop()

                    gtiles(1, g2)
                    with ExitStack() as ph2:
                        wf2 = sbt(ph2, [128, 32, D], BF16, "wf2")
                        load_w_cast(wf2, W_FF2[l, :, :], 32)
                        hTp = rot(ph2, 2, [128, 8, 256], BF16, "h2T")
                        npools = (rot(ph2, 3, [128, D], F32, "xt"), rot(ph2, 1, [128, D], BF16, "jk"), rot(ph2, 2, [128, 4], F32, "ss"),
                                  rot(ph2, 2, [128, D], BF16, "xn"), rot(ph2, 2, [128, 8, 128], BF16, "ptr", psum=True))
                        uTp = rot(ph2, 1, [128, 32, 256], BF16, "uT")
                        pu = Rot([Tl(t_[:, 0:256], "pus") for t_ in [pst(ph2, [128, 512], F32, "pu") for _ in range(3)]])
                        po2 = rot(ph2, 2, [128, 512], F32, "po2", psum=True)
                        sqp = rot(ph2, 3, [128, 256], F32, "sq2")
                        tp = rot(ph2, 1, [128, 512], F32, "t4")

                        def do_blk(bi):
                            cl = 0 if bi * 256 < CTX else 1
                            hT = hTp.nxt()
                            xts = []
                            for i in range(2):
                                xts.append(norm_T(npools, XS, bi * 256 + 128 * i, hT, 128 * i, mcol(3, cl), mcol(2, cl)))
                                if i == 0:
                                    P.release()
                            uT = uTp.nxt()
                            for fb in range(32):
                                ps = pu.nxt()
                                A.mm([(ps[:], wf1[:, kt, fb * 128:(fb + 1) * 128], hT[:, kt, :], kt == 0, kt == 7) for kt in range(8)], [wf1] + hT.kb, [ps])
                                sq = sqp.nxt()
                                A.act(sq[:], ps[:], AF.Square, [ps], [sq])
                                A.stt(uT[:, fb, :], ps[:], 0.0, sq[:], ALU.is_gt, ALU.mult, [ps, sq], [uT])
                            for i in range(2):
                                xt = xts[i]
                                for nb in range(2):
                                    ps = po2.nxt()
                                    A.mm([(ps[:], uT[:, fb, 128 * i:128 * (i + 1)], wf2[:, fb, nb * 512:(nb + 1) * 512], fb == 0, fb == 31)
                                          for fb in range(32)], [uT, wf2], [ps])
                                    t = tp.nxt()
                                    A.tt("dve", t[:], ps[:], g2[cl][:, nb * 512:(nb + 1) * 512], ALU.mult, [ps, g2[cl]], [t])
                                    A.tt("pool", xt[:, nb * 512:(nb + 1) * 512], xt[:, nb * 512:(nb + 1) * 512], t[:], ALU.add, [t, xt], [xt])
                                r0 = bi * 256 + 128 * i
                                A.store(XS[r0:r0 + 128, :], xt[:], xt)

                        for bi in range(T // 256):
                            do_blk(bi)
                        P.flush()
                        if stop == 'P3b':
                            raise _Stop()

        except _Stop:
            P.flush(final=True)
            build.ninst = P.ninst
            gs.pop_all()
            return nc
        with ExitStack() as ph:
            gf = sbt(ph, [128, D], F32, "gf")
            A.load(gf[:], GFIN[:, :], gf)
            xp = rot(ph, 3, [128, D], F32, "xf")
            jp = rot(ph, 1, [128, D], BF16, "jkf")
            sp_ = rot(ph, 2, [128, 4], F32, "ssf")
            for ti in range(LAT // 128):
                r0 = CTX + ti * 128
                xt, jk, ss = xp.nxt(), jp.nxt(), sp_.nxt()
                A.load(xt[:], XS[r0:r0 + 128, :], xt)
                P.release()
                A.act(jk[:], xt[:], AF.Square, [xt], [jk, ss], accum_out=ss[:, 0:1])
                A.act(ss[:, 1:2], ss[:, 0:1], AF.Ln, [ss, epscol], [ss], scale=1.0 / D, bias=epscol[:, 0:1])
                A.act(ss[:, 2:3], ss[:, 1:2], AF.Exp, [ss], [ss], scale=-0.5)
                A.stt(xt[:], xt[:], ss[:, 2:3], gf[:], ALU.mult, ALU.mult, [xt, ss, gf], [xt])
                A.store(OUT[ti * 128:(ti + 1) * 128, :], xt[:], xt)
            P.flush(final=True)
        build.ninst = P.ninst
    return nc


_IN_LAYOUT = (
    ('hg_q', 256), ('hg_f_fwd', 256), ('hg_f_bwd', 256), ('hg_i', 256), ('hg_g', 256),
    ('ml_q', 256), ('ml_k', 256), ('ml_v', 256), ('ml_if', 16), ('ml_o', 256),
    ('rt_q', 256), ('rt_k', 256), ('rt_v', 256), ('rt_g', 256),
    ('gl_q', 128), ('gl_k', 128), ('gl_v', 256),
    ('gl_a_fwd', 16), ('gl_a_bwd', 16), ('gl_g', 256),
)


def _col_ranges():
    off = {}
    o = 0
    for nme, s in _IN_LAYOUT:
        off[nme] = (o, s)
        o += s
    return off


def _w1_layout(w_in):
    off = _col_ranges()
    dep = w_in.shape[0]
    out = np.zeros((dep, D, NC1), np.float32)

    def cols(nme):
        o, s = off[nme]
        return w_in[:, :, o:o + s]
    perm = np.zeros(256, np.int64)
    for h in range(4):
        for dd in range(64):
            r = dd % 32
            partner = dd + 16 if r < 16 else dd - 16
            perm[h * 64 + dd] = h * 64 + partner

    def pad_gla(a):
        p = np.zeros((dep, D, 256), np.float32)
        for h in range(4):
            p[:, :, h * 64:h * 64 + 32] = a[:, :, h * 32:(h + 1) * 32]
        return p
    fmc = [cols('hg_q'), cols('hg_f_fwd'), cols('hg_f_bwd'), cols('ml_q'), cols('ml_k'),
           cols('rt_q'), cols('rt_q')[:, :, perm], cols('rt_k'), cols('rt_k')[:, :, perm],
           pad_gla(cols('gl_q')), pad_gla(cols('gl_k')), cols('gl_a_fwd'), cols('gl_a_bwd')]
    tmc = [cols('hg_i'), cols('ml_v'), cols('rt_v'), cols('gl_v'), cols('hg_g'), cols('ml_o'), cols('rt_g'), cols('gl_g'), cols('ml_if')]
    o = 0
    for a in fmc + tmc:
        out[:, :, o:o + a.shape[2]] = a
        o += a.shape[2]
    assert o == NC1
    return out


def _rope_tables(T):
    LATn = T - CTX
    tl = np.arange(LATn)
    row = (tl // 64).astype(np.float32)
    colp = (tl % 64).astype(np.float32)
    inv = (np.float32(10000.0) ** (-np.arange(16, dtype=np.float32) / np.float32(16))).astype(np.float32)
    cosT = np.ones((128, T), np.float32)
    sinT = np.zeros((128, T), np.float32)
    for p in range(128):
        dd = p % 64
        pos = row if dd < 32 else colp
        ang = (pos * inv[dd % 16]).astype(np.float32)
        sgn = -1.0 if (dd % 32) < 16 else 1.0
        cosT[p, CTX:] = np.cos(ang).astype(np.float32)
        sinT[p, CTX:] = (sgn * np.sin(ang)).astype(np.float32)
    return cosT, sinT


def _consts():
    c = np.zeros((128, 8, 128), np.float32)
    s = np.arange(128)[:, None]
    t = np.arange(128)[None, :]
    same = (s // 64) == (t // 64)
    c[:, 0, :] = (s == t)
    c[:, 1, :] = same & (s <= t)
    c[:, 2, :] = same & (s >= t)
    c[:, 3, :] = (s < 64) & (t >= 0)
    c[:, 4, :] = (s >= 64) & (t >= 0)
    c[:64, 5, :64] = (s[:64] <= t[:, :64])
    c[:64, 6, :64] = (s[:64] >= t[:, :64])
    return c


def make_shared(inp, T):
    dep = inp['w_ada'].shape[0]
    f32 = np.float32
    sh = {}
    sh['w_ada'] = np.ascontiguousarray(inp['w_ada'], f32)
    b_ada = np.asarray(inp['b_ada'], f32)
    bc = np.zeros((dep, 128, 4, 8), f32)
    for which, vec in enumerate((0, 1, 3, 4)):
        bc[:, :, which, :] = b_ada[:, vec * D:(vec + 1) * D].reshape(dep, 8, 128).transpose(0, 2, 1)
    sh['bada_col'] = bc
    bg = np.zeros((dep, 2, 128, D), f32)
    for gi, vec in enumerate((2, 5)):
        bg[:, gi, :, :] = b_ada[:, None, vec * D:(vec + 1) * D]
    sh['bada_g'] = bg
    sh['w1'] = _w1_layout(np.asarray(inp['w_in'], f32))
    sh['ghr'] = np.ascontiguousarray(np.broadcast_to(np.asarray(inp['g_heads'], f32)[:, None, :], (dep, 128, D)))
    hl = np.asarray(inp['hgrn_lb_logits'], f32)
    sh['hgl'] = np.ascontiguousarray(hl.reshape(dep, 2, 2, 128).transpose(3, 0, 1, 2))
    mb = np.asarray(inp['ml_gate_bias'], f32).reshape(dep, 16)
    sh['mlb'] = np.ascontiguousarray(np.broadcast_to(mb[:, None, :], (dep, 128, 16)))
    rl = np.asarray(inp['rt_decay_logit'], f32)
    rt = np.zeros((128, dep, 2, 2), f32)
    for j in range(2):
        for hh in range(2):
            rt[hh * 64:(hh + 1) * 64, :, :, j] = rl[None, :, :, 2 * j + hh]
    sh['rtl'] = rt
    wa = np.asarray(inp['gla_w_a'], f32)
    wap = np.zeros((dep, 2, 16, 256), f32)
    ba = np.asarray(inp['gla_b_a'], f32)
    bap = np.zeros((dep, 2, 256), f32)
    for h in range(4):
        wap[:, :, :, h * 64:h * 64 + 32] = wa[:, :, :, h * 32:(h + 1) * 32]
        bap[:, :, h * 64:h * 64 + 32] = ba[:, :, h * 32:(h + 1) * 32]
    sh['wa'] = wap
    sh['ba'] = np.ascontiguousarray(bap.reshape(dep, 2, 2, 128).transpose(3, 0, 1, 2))
    sh['w_out'] = np.ascontiguousarray(inp['w_out'], f32)
    sh['w_ff1'] = np.ascontiguousarray(inp['w_ff1'], f32)
    sh['w_ff2'] = np.ascontiguousarray(inp['w_ff2'], f32)
    sh['gfin'] = np.ascontiguousarray(np.broadcast_to(np.asarray(inp['g_final'], f32)[None, :], (128, D)))
    c, s = _rope_tables(T)
    sh['ropec'] = c
    sh['ropes'] = s
    sh['consts'] = _consts()
    return sh


def make_core(inp, b):
    f32 = np.float32
    m = {}
    m['xin'] = np.ascontiguousarray(np.concatenate([np.asarray(inp['ctx'][b], f32), np.asarray(inp['x'][b], f32)], axis=0))
    cc = np.zeros((128, 8, 2), f32)
    cc[:, :, 0] = np.asarray(inp['c_ctx'], f32).reshape(8, 128).T
    cc[:, :, 1] = np.asarray(inp['c'][b], f32).reshape(8, 128).T
    m['cc'] = cc
    return m


_CACHE = {}


def kernel(**inputs):
    x = inputs['x']
    B, LAT, _ = x.shape
    T = CTX + LAT
    if LAT not in _CACHE:
        _CACHE[LAT] = build(LAT)
    nc = _CACHE[LAT]
    sh = make_shared(inputs, T)
    in_maps = []
    for b in range(B):
        m = dict(sh)
        m.update(make_core(inputs, b))
        in_maps.append(m)
    res = run_bass_kernel_spmd(nc, in_maps, core_ids=list(range(B)))
    return np.stack([np.asarray(r["out"], np.float32) for r in res.results], axis=0)
```

```python
import numpy as np
from contextlib import ExitStack
import concourse.bass as bass
import concourse.mybir as mybir
from concourse.bass_utils import run_bass_kernel_spmd

F32 = mybir.dt.float32
BF16 = mybir.dt.bfloat16
AF = mybir.ActivationFunctionType
ALU = mybir.AluOpType
AX = mybir.AxisListType

D = 1024
CTX = 256
DEPTH = 2
DFF = 4096
VD = 66
NFM = 22 * 128 + 32
NTM = 2064
NC1 = NFM + NTM
EPS = 1e-6
ENGS = ("pe", "act", "dve", "pool", "sp")
HG, ML, RT, GL = 0, 1, 2, 3
HG_SHIFT = 20.0


class Buf:
    __slots__ = ("name", "last_write", "reads", "dsem", "dcount")

    def __init__(self, name):
        self.name = name
        self.last_write = None
        self.reads = []
        self.dsem = None
        self.dcount = 0


class Tl:
    def __init__(self, t, name, b=None):
        self.t = t
        self.b = b if b is not None else Buf(name)

    def __getitem__(self, i):
        return self.t[i]


def _b(x):
    return x.b if isinstance(x, Tl) else x


class Op:
    __slots__ = ("eng", "fn", "deps", "is_dma", "slot", "ticket", "signal", "retired", "seq")

    def __init__(self, eng, fn, is_dma, slot):
        self.eng = eng
        self.fn = fn
        self.deps = []
        self.is_dma = is_dma
        self.slot = slot
        self.ticket = None
        self.signal = False
        self.retired = False


class Prog:
    def __init__(self, nc, stack):
        self.nc = nc
        self.stack = stack
        self.ops = []
        self.esem = {e: stack.enter_context(nc.semaphore("sem_" + e)) for e in ENGS}
        self.ecount = {e: 0 for e in ENGS}
        self.waited = {e: {} for e in ENGS}
        self.pending_bar = {e: [] for e in ENGS}
        self.sempool = []
        self.nsem = 0
        self.engobj = {"pe": nc.tensor, "act": nc.scalar, "dve": nc.vector, "pool": nc.gpsimd, "sp": nc.sync}
        self.ninst = 0
        self.deferred = []
        self.seq = 0

    def defer_dma(self, fn, slot, reads=(), writes=(), eng="sp"):
        self.deferred.append((fn, slot, reads, writes, eng))

    def release(self, keep=0):
        n = len(self.deferred) - keep
        if n <= 0:
            return
        dd = self.deferred[:n]
        self.deferred = self.deferred[n:]
        for fn, slot, reads, writes, eng in dd:
            self.dma(fn, slot, reads, writes, eng)

    def add(self, eng, fn, reads=(), writes=(), is_dma=False, slot=None):
        op = Op(eng, fn, is_dma, slot)
        deps = set()
        for x in reads:
            b = _b(x)
            if b.last_write is not None:
                deps.add(b.last_write)
        for x in writes:
            b = _b(x)
            if b.last_write is not None:
                deps.add(b.last_write)
            for r in b.reads:
                deps.add(r)
        dl = [d for d in deps if (not d.retired) and not (eng == "pe" and d.eng == "pe" and not d.is_dma)]
        best = {}
        keep = []
        for d in dl:
            if d.is_dma:
                keep.append(d)
            elif d.eng not in best or best[d.eng].seq < d.seq:
                best[d.eng] = d
        op.deps = keep + list(best.values())
        self.seq += 1
        op.seq = self.seq
        for x in reads:
            _b(x).reads.append(op)
        for x in writes:
            b = _b(x)
            b.last_write = op
            b.reads = []
        self.ops.append(op)
        return op

    def pe(self, fn, reads=(), writes=()):
        return self.add("pe", fn, reads, writes)

    def act(self, fn, reads=(), writes=()):
        return self.add("act", fn, reads, writes)

    def dve(self, fn, reads=(), writes=()):
        return self.add("dve", fn, reads, writes)

    def pool(self, fn, reads=(), writes=()):
        return self.add("pool", fn, reads, writes)

    def dma(self, fn, slot, reads=(), writes=(), eng="sp"):
        return self.add(eng, fn, reads, writes, is_dma=True, slot=_b(slot))

    def flush(self, final=False):
        nc = self.nc
        self.release()
        ops = self.ops
        self.ops = []
        for op in ops:
            for d in op.deps:
                d.signal = True
        last = {}
        for op in ops:
            if not op.is_dma:
                last[op.eng] = op
        for op in last.values():
            op.signal = True
        slots = []
        swbar = []
        for op in ops:
            if op.is_dma and op.eng == "pool":
                sem = self.stack.enter_context(nc.semaphore("sw%d" % self.nsem))
                self.nsem += 1
                op.ticket = (sem, 16)
                swbar.append(op.ticket)
            elif op.is_dma:
                s = op.slot
                if s.dsem is None:
                    if self.sempool:
                        s.dsem, s.dcount = self.sempool.pop()
                    else:
                        s.dsem = self.stack.enter_context(nc.semaphore("ds%d" % self.nsem))
                        self.nsem += 1
                        s.dcount = 0
                    slots.append(s)
                s.dcount += 16
                op.ticket = (s.dsem, s.dcount)
            elif op.signal:
                self.ecount[op.eng] += 1
                op.ticket = (self.esem[op.eng], self.ecount[op.eng])
        per = {e: [] for e in ENGS}
        for op in ops:
            per[op.eng].append(op)
        bar = [last[e].ticket for e in last] + [(s.dsem, s.dcount) for s in slots] + swbar

        def run(ename, eng):
            waited = self.waited[ename]

            def w(sem, val):
                k = id(sem)
                if waited.get(k, 0) < val:
                    eng.wait_ge(sem, val)
                    waited[k] = val

            if per[ename] or final:
                for sem, val in self.pending_bar[ename]:
                    w(sem, val)
                self.pending_bar[ename] = []
            for op in per[ename]:
                need = {}
                for d in op.deps:
                    sem, val = d.ticket
                    k = id(sem)
                    if k not in need or need[k][1] < val:
                        need[k] = (sem, val)
                for sem, val in need.values():
                    w(sem, val)
                ins = op.fn(eng)
                self.ninst += 1
                if op.is_dma:
                    ins.then_inc(op.ticket[0], 16)
                elif op.signal:
                    ins.then_inc(op.ticket[0], 1)
            if final and ename == "sp":
                for sem, val in bar:
                    w(sem, val)

        with nc.Block() as block:
            @block.tensor
            def _(eng):
                run("pe", eng)

            @block.scalar
            def _(eng):
                run("act", eng)

            @block.vector
            def _(eng):
                run("dve", eng)

            @block.gpsimd
            def _(eng):
                run("pool", eng)

            @block.sync
            def _(eng):
                run("sp", eng)

        for e in ENGS:
            self.pending_bar[e].extend(bar)
        for op in ops:
            op.retired = True
        for s in slots:
            self.sempool.append((s.dsem, s.dcount))
            s.dsem = None


class Rot:
    def __init__(self, items):
        self.items = items
        self.i = 0

    def nxt(self):
        r = self.items[self.i % len(self.items)]
        self.i += 1
        return r


class Em:
    def __init__(self, P):
        self.P = P

    def act(self, out, in_, func, R, W, **kw):
        self.P.act(lambda e: e.activation(out=out, in_=in_, func=func, **kw), R, W)

    def tt(self, eng, out, in0, in1, op, R, W):
        self.P.add(eng, lambda e: e.tensor_tensor(out=out, in0=in0, in1=in1, op=op), R, W)

    def ts(self, eng, out, in0, s1, s2, op0, op1, R, W):
        if op1 is None:
            self.P.add(eng, lambda e: e.tensor_scalar(out=out, in0=in0, scalar1=s1, scalar2=None, op0=op0), R, W)
        else:
            self.P.add(eng, lambda e: e.tensor_scalar(out=out, in0=in0, scalar1=s1, scalar2=s2, op0=op0, op1=op1), R, W)

    def stt(self, out, in0, scalar, in1, op0, op1, R, W):
        self.P.dve(lambda e: e.scalar_tensor_tensor(out=out, in0=in0, scalar=scalar, in1=in1, op0=op0, op1=op1), R, W)

    def copy(self, eng, out, in_, R, W):
        if eng == "act":
            self.P.act(lambda e: e.copy(out=out, in_=in_), R, W)
        else:
            self.P.add(eng, lambda e: e.tensor_copy(out=out, in_=in_), R, W)

    def memset(self, ap, val, W, R=()):
        self.P.pool(lambda e: e.memset(ap, val), R, W)

    def load(self, out, in_, slot, W=None):
        self.P.dma(lambda e: e.dma_start(out=out, in_=in_), slot, writes=[slot] if W is None else W)

    def loadc(self, out, in_, slot):
        self.P.dma(lambda e: e.dma_start(out=out, in_=in_), slot, writes=[slot], eng="pool")

    def store(self, out, in_, slot):
        self.P.defer_dma(lambda e: e.dma_start(out=out, in_=in_), slot, reads=[slot])

    def mm(self, items, R, W):
        def f(e):
            r = None
            for (o, l, rh, st, sp) in items:
                r = e.matmul(o, lhsT=l, rhs=rh, start=st, stop=sp)
            return r
        self.P.pe(f, R, W)

    def tr(self, items, ident, R, W):
        def f(e):
            r = None
            for (o, i) in items:
                r = e.transpose(o, i, ident)
            return r
        self.P.pe(f, R, W)

    def scan(self, out, d0, d1, R, W):
        self.P.dve(lambda e: e.tensor_tensor_scan(out=out, data0=d0, data1=d1, initial=0.0, op0=ALU.mult, op1=ALU.add), R, W)

    def recip(self, out, in_, R, W):
        self.P.dve(lambda e: e.reciprocal(out=out, in_=in_), R, W)

    def reduce(self, out, in_, R, W):
        self.P.dve(lambda e: e.tensor_reduce(out=out, in_=in_, axis=AX.X, op=ALU.add), R, W)

    def sss(self, out, in_, scalar, op, R, W):
        self.P.dve(lambda e: e.tensor_single_scalar(out=out, in_=in_, scalar=scalar, op=op), R, W)


class _Stop(Exception):
    pass


def build(LAT, depth=DEPTH, dbg=False, stop=None):
    T = CTX + LAT
    NT = T // 128
    NCH = T // 64
    NG = T // 256
    nc = bass.Bass("TRN2", target_bir_lowering=False)

    def din(name, shape, dt=F32):
        return nc.dram_tensor(name, list(shape), dt, kind="ExternalInput").ap()

    def dscr(name, shape, dt):
        return nc.dram_tensor(name, list(shape), dt, kind="ExternalOutput" if dbg else "Internal").ap()

    XIN = din("xin", [T, D])
    CC_IN = din("cc", [128, 8, 2])
    W_ADA = din("w_ada", [DEPTH, D, 6 * D])
    BADA_COL = din("bada_col", [DEPTH, 128, 4, 8])
    BADA_G = din("bada_g", [DEPTH, 2, 128, D])
    W1 = din("w1", [DEPTH, D, NC1])
    GHR = din("ghr", [DEPTH, 128, D])
    HGL = din("hgl", [128, DEPTH, 2, 2])
    MLB = din("mlb", [DEPTH, 128, 16])
    RTL = din("rtl", [128, DEPTH, 2, 2])
    WA = din("wa", [DEPTH, 2, 16, 256])
    BA = din("ba", [128, DEPTH, 2, 2])
    W_OUT = din("w_out", [DEPTH, D, D])
    W_FF1 = din("w_ff1", [DEPTH, D, DFF])
    W_FF2 = din("w_ff2", [DEPTH, DFF, D])
    GFIN = din("gfin", [128, D])
    ROPEC = din("ropec", [128, T])
    ROPES = din("ropes", [128, T])
    CONSTS = din("consts", [128, 8, 128])
    OUT = nc.dram_tensor("out", [LAT, D], F32, kind="ExternalOutput").ap()

    XS = dscr("xs", [T, D], F32)
    QT = dscr("qt", [2, 8, 128, T], BF16)
    KT = dscr("kt", [2, 8, 128, T], BF16)
    KK = dscr("kk", [2, T, 1024], BF16)
    VV = dscr("vv", [2, T, 16, VD], BF16)
    GG = dscr("gg", [T, D], BF16)
    OO = dscr("oo", [2, T, D], F32)
    HT = dscr("ht", [NT, 128, 8, 128], BF16)

    with ExitStack() as gs:
        P = Prog(nc, gs)
        A = Em(P)
        cnt = [0]

        def sbt(st, shape, dt=F32, name=None):
            cnt[0] += 1
            nm = "%s_%d" % (name or "t", cnt[0])
            return Tl(st.enter_context(nc.sbuf_tensor(nm, list(shape), dt)), nm)

        def pst(st, shape, dt=F32, name=None):
            cnt[0] += 1
            nm = "%s_%d" % (name or "p", cnt[0])
            return Tl(st.enter_context(nc.psum_tensor(nm, list(shape), dt)), nm)

        def rot(st, n, shape, dt=F32, name=None, psum=False):
            return Rot([(pst if psum else sbt)(st, shape, dt, name) for _ in range(n)])

        def sub(tl, ap, name):
            return Tl(ap, name)

        cst = sbt(gs, [128, 8, 128], F32, "cst")
        A.load(cst[:], CONSTS[:, :, :], cst)
        identb = sbt(gs, [128, 128], BF16, "identb")
        A.copy("dve", identb[:], cst[:, 0, :], [cst], [identb])
        maskf = sbt(gs, [64, 2, 64], F32, "maskf")
        A.copy("dve", maskf[:], cst[0:64, 5:7, 0:64], [cst], [maskf])
        masku = maskf[:].bitcast(mybir.dt.uint32)
        rmask = sbt(gs, [128, 512], F32, "rmask")
        A.memset(rmask[:], 1.0, [rmask])
        A.memset(rmask[:].rearrange("p (c t) -> p c t", t=64)[:, :, 0:1], 0.0, [rmask], [rmask])
        onescol = sbt(gs, [128, 1], F32, "onescol")
        A.memset(onescol[:], 1.0, [onescol])
        zcol = sbt(gs, [128, 1], F32, "zcol")
        A.memset(zcol[:], 0.0, [zcol])
        shcol = sbt(gs, [128, 2], F32, "shcol")
        A.memset(shcol[:, 0:1], -HG_SHIFT, [shcol])
        A.memset(shcol[:, 1:2], HG_SHIFT, [shcol], [shcol])
        epscol = sbt(gs, [128, 1], F32, "epscol")
        A.memset(epscol[:], EPS, [epscol])
        ccs = sbt(gs, [128, 8, 2], F32, "ccs")
        A.load(ccs[:], CC_IN[:, :, :], ccs)
        scs = sbt(gs, [128, 8, 2], F32, "scs")
        A.act(scs[:], ccs[:], AF.Silu, [ccs], [scs])
        P.flush()

        def norm_T(pools, src, r0, hT, col0, sc, sh):
            xp, jp, sp_, xnp, ptp = pools
            xt = xp.nxt()
            A.load(xt[:], src[r0:r0 + 128, :], xt)
            jk = jp.nxt()
            ss = sp_.nxt()
            A.act(jk[:], xt[:], AF.Square, [xt], [jk, ss], accum_out=ss[:, 0:1])
            A.act(ss[:, 1:2], ss[:, 0:1], AF.Ln, [ss, epscol], [ss], scale=1.0 / D, bias=epscol[:, 0:1])
            A.act(ss[:, 2:3], ss[:, 1:2], AF.Exp, [ss], [ss], scale=-0.5)
            xn = xnp.nxt()
            A.act(xn[:], xt[:], AF.Identity, [xt, ss], [xn], scale=ss[:, 2:3])
            pt = ptp.nxt()
            A.tr([(pt[:, kt, :], xn[:, kt * 128:(kt + 1) * 128]) for kt in range(8)], identb[:], [xn, identb], [pt])
            if not hasattr(hT, "kb"):
                hT.kb = [Buf("hTk") for _ in range(8)]
            ncount[0] += 1
            for kt in range(8):
                if ncount[0] % 2 == 0:
                    A.act(hT[:, kt, col0:col0 + 128], pt[:, kt, :], AF.Identity, [pt, modcol_ref[0]], [hT.kb[kt]], scale=sc[kt], bias=sh[kt])
                else:
                    A.ts("dve", hT[:, kt, col0:col0 + 128], pt[:, kt, :], sc[kt], sh[kt], ALU.mult, ALU.add, [pt, modcol_ref[0]], [hT.kb[kt]])
            return xt

        modcol_ref = [None]
        ncount = [0]

        def load_w_cast(dst, src2, nk):
            for k0 in range(0, nk, 8):
                A.loadc(dst[:, k0:k0 + 8, :], src2[k0 * 128:(k0 + 8) * 128, :].rearrange("(kt p) n -> p kt n", p=128), dst)

        def decay_core(lf, n, d, escale, bT, Dt, E1, E2, cc, cctl, tmpc, sh=0.0):
            if d == 0:
                A.scan(bT[:, 0:n], rmask[:, 0:n], lf[:, 0:n], [lf, rmask], [bT])
            else:
                A.scan(bT[:, 0:n][:, ::-1], rmask[:, 0:n], lf[:, 0:n][:, ::-1], [lf, rmask], [bT])
            nb = n // 64
            mid = 31 if d == 0 else 32
            last = 63 if d == 0 else 0
            b3 = bT[:, 0:n].rearrange("p (c t) -> p c t", t=64)
            D3 = Dt[:, 0:n].rearrange("p (c t) -> p c t", t=64)
            A.tt("pool", D3, b3, b3[:, :, mid:mid + 1].to_broadcast([128, nb, 64]), ALU.subtract, [bT], [Dt])
            bneg = shcol[:, 0:1] if sh else zcol[:, 0:1]
            bpos = shcol[:, 1:2] if sh else zcol[:, 0:1]
            A.act(E1[:, 0:n], Dt[:, 0:n], AF.Exp, [Dt], [E1], scale=escale, bias=bneg)
            A.act(E2[:, 0:n], Dt[:, 0:n], AF.Exp, [Dt], [E2], scale=-escale, bias=bneg)
            ref2 = bT[:, mid:n:64]
            bl2 = bT[:, last:n:64]
            A.act(cc[0], ref2, AF.Exp, [bT], [cctl], scale=escale, bias=bneg)
            A.act(cc[1], bl2, AF.Exp, [bT], [cctl], scale=escale)
            A.tt("pool", tmpc[:, 0:nb], bl2, ref2, ALU.subtract, [bT], [tmpc])
            A.act(cc[2], tmpc[:, 0:nb], AF.Exp, [tmpc], [cctl], scale=escale, bias=bpos)

        blocks = [(0, CTX, 0)] + [(CTX + 512 * i, 512, 1) for i in range(LAT // 512)]

        try:
          for l in range(depth):
            xsrc = XIN if l == 0 else XS
            with ExitStack() as ls:
                modcol = sbt(ls, [128, 4, 8, 2], F32, "modcol")
                modcol_ref[0] = modcol

                def mcol(which, cl):
                    return [modcol[:, which, kt, cl:cl + 1] for kt in range(8)]

                with ExitStack() as ls2:
                    CCt = sbt(ls2, [128, 4, 2, 2, 3, NCH], F32, "CCt")
                    ALt = sbt(ls2, [128, NCH, 8], F32, "ALt")
                    FLt = sbt(ls2, [64, NCH, 8], F32, "FLt")
                    ALP = sbt(ls2, [128, NCH, 2, 2], F32, "ALP")
                    Ec = [[[sbt(ls2, [128, 512], F32, "Ec") for _ in range(2)] for _ in range(2)] for _ in range(2)]
                    lbt = sbt(ls2, [128, 2, 2, 2], F32, "lbt")
                    lgam = sbt(ls2, [128, 2, 2], F32, "lgam")
                    bat = sbt(ls2, [128, 2, 2], F32, "bat")
                    wat = sbt(ls2, [16, 2, 256], F32, "wat")
                    mlbt = sbt(ls2, [128, 16], F32, "mlbt")
                    ghr = sbt(ls2, [128, D], F32, "ghr")

                    with ExitStack() as ph:
                        wad = rot(ph, 2, [128, 8, 512], F32, "wad")
                        pm = pst(ph, [128, 64], F32, "pm")
                        bcol = sbt(ph, [128, 4, 8], F32, "bcol")
                        A.load(bcol[:], BADA_COL[l, :, :, :], bcol)
                        for which, vec in enumerate((0, 1, 3, 4)):
                            for half in range(2):
                                blk = vec * 2 + half
                                w = wad.nxt()
                                A.load(w[:], W_ADA[l, :, blk * 512:(blk + 1) * 512].rearrange("(kt p) n -> p kt n", p=128), w)
                                items = []
                                for f4 in range(4):
                                    c0 = (which * 8 + half * 4 + f4) * 2
                                    for kt in range(8):
                                        items.append((pm[:, c0:c0 + 2], w[:, kt, f4 * 128:(f4 + 1) * 128], scs[:, kt, :], kt == 0, kt == 7))
                                A.mm(items, [w, scs], [pm])
                        A.tt("dve", modcol[:].rearrange("p a b c -> p (a b) c"), pm[:].rearrange("p (a c) -> p a c", c=2),
                             bcol[:].rearrange("p a b -> p (a b)").unsqueeze(2).to_broadcast([128, 32, 2]), ALU.add, [pm, bcol], [modcol])
                        for which in (1, 3):
                            A.ts("dve", modcol[:, which, :, :], modcol[:, which, :, :], 1.0, None, ALU.add, None, [modcol], [modcol])
                        hgl = sbt(ph, [128, DEPTH, 2, 2], F32, "hgl")
                        A.load(hgl[:], HGL[:, :, :, :], hgl)
                        if l == 0:
                            A.memset(lbt[:, 0, :, :], 0.0, [lbt])
                        else:
                            dl = sbt(ph, [128, 2, 2], F32, "dl")
                            A.tt("dve", dl[:], hgl[:, 1, :, :], hgl[:, 0, :, :], ALU.subtract, [hgl], [dl])
                            A.act(lbt[:, 0, :, :], dl[:], AF.Sigmoid, [dl], [lbt])
                        A.ts("dve", lbt[:, 1, :, :], lbt[:, 0, :, :], -1.0, 1.0, ALU.mult, ALU.add, [lbt], [lbt])
                        rtl = sbt(ph, [128, DEPTH, 2, 2], F32, "rtl")
                        A.load(rtl[:], RTL[:, :, :, :], rtl)
                        sgr = sbt(ph, [128, 2, 2], F32, "sgr")
                        A.act(sgr[:], rtl[:, l, :, :], AF.Sigmoid, [rtl], [sgr])
                        A.act(lgam[:], sgr[:], AF.Ln, [sgr], [lgam])
                        bap = sbt(ph, [128, DEPTH, 2, 2], F32, "bap")
                        A.load(bap[:], BA[:, :, :, :], bap)
                        A.copy("dve", bat[:], bap[:, l, :, :], [bap], [bat])
                        A.load(wat[:], WA[l, :, :, :].rearrange("d r c -> r d c"), wat)
                        A.load(mlbt[:], MLB[l, :, :], mlbt)
                        A.load(ghr[:], GHR[l, :, :], ghr)
                        lfc = sbt(ph, [128, 512], F32, "lfc")
                        bTc = sbt(ph, [128, 512], F32, "bTc")
                        Dtc = sbt(ph, [128, 512], F32, "Dtc")
                        ctmp = sbt(ph, [128, 3, 8], F32, "ctmp")
                        tmpc = sbt(ph, [128, 8], F32, "tmpc")
                        for d in range(2):
                            for j in range(2):
                                A.copy("dve", lfc[:], lgam[:, d, j:j + 1].to_broadcast([128, 512]), [lgam], [lfc])
                                decay_core(lfc, 512, d, 1.0, bTc, Dtc, Ec[d][j][0], Ec[d][j][1],
                                           [ctmp[:, k, :] for k in range(3)], ctmp, tmpc)
                                for kind in range(3):
                                    A.copy("dve", CCt[:, RT, d, j, kind, :], ctmp[:, kind, 0:1].to_broadcast([128, NCH]), [ctmp], [CCt])
                        P.flush()
                        if stop == 'PA':
                            raise _Stop()

                    with ExitStack() as ph:
                        w1 = sbt(ph, [128, 8, NFM], BF16, "w1fm")
                        load_w_cast(w1, W1[l, :, 0:NFM], 8)
                        hTp = rot(ph, 2, [128, 8, 512], BF16, "hT")
                        npools = (rot(ph, 2, [128, D], F32, "xt"), rot(ph, 1, [128, D], BF16, "jk"), rot(ph, 2, [128, 4], F32, "ss"),
                                  rot(ph, 2, [128, D], BF16, "xn"), rot(ph, 2, [128, 8, 128], BF16, "ptr", psum=True))
                        pf = rot(ph, 4, [128, 512], F32, "pf", psum=True)
                        ptk = rot(ph, 2, [128, 4, 128], BF16, "ptk", psum=True)
                        wk = rot(ph, 10, [128, 512], F32, "wk")
                        fq = rot(ph, 4, [128, 512], F32, "fq")
                        bfp = rot(ph, 6, [128, 512], BF16, "bfp")
                        kst = [sbt(ph, [128, 4, 1024], BF16, "kst") for _ in range(2)]
                        rope = [sbt(ph, [128, 512], F32, "rope") for _ in range(2)]
                        ga = [sbt(ph, [16, 512], F32, "ga") for _ in range(2)]
                        tmpcp = rot(ph, 2, [128, 8], F32, "tmpc")

                        def do_block(t0, n, cl):
                            ntile = n // 128
                            ch0 = t0 // 64
                            nb = n // 64
                            hT = hTp.nxt()
                            for i in range(ntile):
                                norm_T(npools, xsrc, t0 + 128 * i, hT, 128 * i, mcol(1, cl), mcol(0, cl))
                                P.defer_dma(lambda e, o_=HT[t0 // 128 + i, :, :, :], i_=hT[:, :, 128 * i:128 * (i + 1)]: e.dma_start(out=o_, in_=i_),
                                            hT.b, reads=list(hT.kb))
                            A.load(rope[0][:, 0:n], ROPEC[:, t0:t0 + n], rope[0])
                            A.load(rope[1][:, 0:n], ROPES[:, t0:t0 + n], rope[1])
                            P.release()

                            def fm(col0, M=128):
                                ps = pf.nxt()
                                A.mm([(ps[0:M, 0:n], w1[:, kt, col0:col0 + M], hT[:, kt, 0:n], kt == 0, kt == 7) for kt in range(8)],
                                     [w1] + hT.kb, [ps])
                                return ps

                            def finish(m, d, j, q, k, lf, escale, kscale, E=None, dirs=None, sh=0.0):
                                P.release()
                                if E is None:
                                    bT, Dt, E1, E2 = wk.nxt(), wk.nxt(), wk.nxt(), wk.nxt()
                                    decay_core(lf, n, d, escale, bT, Dt, E1, E2,
                                               [CCt[:, m, d, j, kind, ch0:ch0 + nb] for kind in range(3)], CCt, tmpcp.nxt(), sh=sh)
                                elif E == "none":
                                    E1 = E2 = None
                                else:
                                    E1, E2 = E
                                qh = bfp.nxt()
                                kh = bfp.nxt()
                                if E1 is None:
                                    A.copy("dve", qh[:, 0:n], q[:, 0:n], [q], [qh])
                                    A.ts("dve", kh[:, 0:n], k[:, 0:n], kscale, None, ALU.mult, None, [k], [kh])
                                else:
                                    A.tt("dve", qh[:, 0:n], q[:, 0:n], E1[:, 0:n], ALU.mult, [q, E1], [qh])
                                    A.stt(kh[:, 0:n], k[:, 0:n], kscale, E2[:, 0:n], ALU.mult, ALU.mult, [k, E2], [kh])
                                for dd in (dirs or [d]):
                                    A.store(QT[dd, m * 2 + j, :, t0:t0 + n], qh[:, 0:n], qh)
                                    A.store(KT[dd, m * 2 + j, :, t0:t0 + n], kh[:, 0:n], kh)
                                pk = ptk.nxt()
                                A.tr([(pk[:, i, :], kh[:, 128 * i:128 * (i + 1)]) for i in range(ntile)], identb[:], [kh, identb], [pk])
                                for dd in (dirs or [d]):
                                    A.copy("dve", kst[dd][:, 0:ntile, m * 256 + j * 128:m * 256 + (j + 1) * 128], pk[:, 0:ntile, :],
                                           [pk], [kst[dd]])

                            for j in range(2):
                                psq = fm((0 + j) * 128)
                                qs = fq.nxt()
                                sgq = wk.nxt()
                                A.act(sgq[:, 0:n], psq[:, 0:n], AF.Sigmoid, [psq], [sgq, psq])
                                A.tt("dve", qs[:, 0:n], psq[:, 0:n], sgq[:, 0:n], ALU.mult, [psq, sgq], [qs])
                                fs = []
                                for d in range(2):
                                    psz = fm((2 + 2 * d + j) * 128)
                                    sg, f = wk.nxt(), wk.nxt()
                                    A.act(sg[:, 0:n], psz[:, 0:n], AF.Sigmoid, [psz], [sg])
                                    A.ts("dve", f[:, 0:n], sg[:, 0:n], lbt[:, 1, d, j:j + 1], lbt[:, 0, d, j:j + 1], ALU.mult, ALU.add,
                                         [sg, lbt], [f])
                                    fs.append(f)
                                for d in range(2):
                                    f = fs[d]
                                    lf, kk = wk.nxt(), wk.nxt()
                                    A.act(lf[:, 0:n], f[:, 0:n], AF.Ln, [f], [lf])
                                    A.act(kk[:, 0:n], f[:, 0:n], AF.Identity, [f, onescol], [kk], scale=-1.0, bias=onescol[:, 0:1])
                                    finish(HG, d, j, qs, kk, lf, 1.0, 1.0, sh=HG_SHIFT)
                            for j in range(2):
                                psq = fm((6 + j) * 128)
                                psk = fm((8 + j) * 128)
                                finish(ML, 0, j, psq, psk, None, 1.0, 0.125, E="none", dirs=[0, 1])
                            for j in range(2):
                                rr = []
                                for base in (10, 14):
                                    ps0 = fm((base + j) * 128)
                                    ps1 = fm((base + 2 + j) * 128)
                                    t1, t2 = wk.nxt(), wk.nxt()
                                    r = fq.nxt()
                                    A.tt("dve", t1[:, 0:n], ps0[:, 0:n], rope[0][:, 0:n], ALU.mult, [ps0, rope[0]], [t1])
                                    A.tt("dve", t2[:, 0:n], ps1[:, 0:n], rope[1][:, 0:n], ALU.mult, [ps1, rope[1]], [t2])
                                    A.tt("pool", r[:, 0:n], t1[:, 0:n], t2[:, 0:n], ALU.add, [t1, t2], [r])
                                    rr.append(r)
                                for d in range(2):
                                    finish(RT, d, j, rr[0], rr[1], None, 1.0, 0.125, E=(Ec[d][j][0], Ec[d][j][1]))
                            for d in range(2):
                                psa = fm(22 * 128 + 16 * d, M=16)
                                A.copy("act", ga[d][:, 0:n], psa[0:16, 0:n], [psa], [ga[d]])
                            for j in range(2):
                                psq = fm((18 + j) * 128)
                                psk = fm((20 + j) * 128)
                                qr, kr = fq.nxt(), fq.nxt()
                                A.copy("act", qr[:, 0:n], psq[:, 0:n], [psq], [qr])
                                A.copy("dve", kr[:, 0:n], psk[:, 0:n], [psk], [kr])
                                sgs = []
                                for d in range(2):
                                    psz = pf.nxt()
                                    A.mm([(psz[:, 0:n], wat[0:16, d, j * 128:(j + 1) * 128], ga[d][0:16, 0:n], True, True)], [wat, ga[d]], [psz])
                                    sg = wk.nxt()
                                    A.act(sg[:, 0:n], psz[:, 0:n], AF.Sigmoid, [psz, bat], [sg], bias=bat[:, d, j:j + 1])
                                    sgs.append(sg)
                                for d in range(2):
                                    sg = sgs[d]
                                    lf = wk.nxt()
                                    A.act(lf[:, 0:n], sg[:, 0:n], AF.Ln, [sg], [lf])
                                    finish(GL, d, j, qr, kr, lf, 1.0 / 16.0, 32.0 ** -0.5)
                            for dd in range(2):
                                A.store(KK[dd, t0:t0 + n, :].rearrange("(i p) x -> p i x", p=128), kst[dd][:, 0:ntile, :], kst[dd])

                        for (t0, n, cl) in blocks:
                            do_block(t0, n, cl)
                        P.flush()
                        if stop == 'P1a':
                            raise _Stop()

                    with ExitStack() as ph:
                        w1 = sbt(ph, [128, 8, NTM], BF16, "w1tm")
                        load_w_cast(w1, W1[l, :, NFM:NC1], 8)
                        hTp = rot(ph, 4, [128, 8, 128], BF16, "hT")
                        npools = (rot(ph, 4, [128, D], F32, "xt"), rot(ph, 2, [128, D], BF16, "jk"), rot(ph, 4, [128, 4], F32, "ss"),
                                  rot(ph, 4, [128, D], BF16, "xn"), rot(ph, 2, [128, 8, 128], BF16, "ptr", psum=True))
                        pt = rot(ph, 4, [128, 512], F32, "pt", psum=True)
                        psm = rot(ph, 2, [128, 64], F32, "psm", psum=True)
                        vstp = rot(ph, 4, [128, 16, VD], BF16, "vst")
                        vmlp = rot(ph, 4, [128, 4, VD], BF16, "vml")
                        vs1p = rot(ph, 4, [128, 4, VD], BF16, "vs1")
                        for tl_ in vstp.items + vmlp.items:
                            A.memset(tl_[:], 0.0, [tl_])
                            A.memset(tl_[:, :, 64:65], 1.0, [tl_], [tl_])
                        sgp = rot(ph, 6, [128, 512], F32, "sgp")
                        ggp = rot(ph, 4, [128, D], BF16, "ggp")
                        smp = rot(ph, 4, [128, 64], F32, "smp")

                        def do_tile(ti):
                            r0 = ti * 128
                            cl = 0 if r0 < CTX else 1
                            hT = hTp.nxt()
                            hT.kb = [hT.b]
                            A.load(hT[:], HT[ti, :, :, :], hT)
                            P.release(keep=3)

                            def tm(col0, N):
                                ps = pt.nxt()
                                import os
                                if os.environ.get("EXPA"):
                                    A.mm([(ps[:, 0:128], w1[:, kt, col0:col0 + 128], hT[:, kt, :], kt == 0, kt == 7) for kt in range(8)], [w1] + hT.kb, [ps])
                                elif os.environ.get("EXPB"):
                                    A.mm([(ps[:, 0:256], hT[:, kt, :], w1[:, kt, col0:col0 + 256], kt == 0, kt == 7) for kt in range(8)], [w1] + hT.kb, [ps])
                                else:
                                    items = []
                                    for n0 in range(0, N, 256):
                                        n1 = min(N, n0 + 256)
                                        items += [(ps[:, n0:n1], hT[:, kt, :], w1[:, kt, col0 + n0:col0 + n1], kt == 0, kt == 7) for kt in range(8)]
                                    A.mm(items, [w1] + hT.kb, [ps])
                                return ps
                            vst, vml, vs1 = vstp.nxt(), vmlp.nxt(), vs1p.nxt()
                            import os
                            SKIP = os.environ.get("SKIP", "")
                            if "v" in SKIP:
                                return
                            psA = tm(0, 512)
                            if "1" in SKIP:
                                return
                            if "a" not in SKIP:
                                A.copy("act", vst[:, 0:4, 0:64], psA[:, 0:256].rearrange("p (h v) -> p h v", v=64), [psA], [vst])
                            if "b" not in SKIP:
                                A.copy("act", vml[:, :, 0:64], psA[:, 256:512].rearrange("p (h v) -> p h v", v=64), [psA], [vml])
                            if "2" in SKIP:
                                return
                            psB = tm(512, 512)
                            A.copy("dve", vst[:, 8:16, 0:64], psB[:, 0:512].rearrange("p (h v) -> p h v", v=64), [psB], [vst])
                            if "g" in SKIP:
                                return
                            gg = ggp.nxt()
                            psC = tm(1024, 512)
                            s1 = sgp.nxt()
                            A.act(s1[:], psC[:], AF.Sigmoid, [psC], [s1])
                            A.tt("dve", gg[:, 0:512], s1[:], ghr[:, 0:512], ALU.mult, [s1, ghr], [gg])
                            psD = tm(1536, 512)
                            s2 = sgp.nxt()
                            A.act(s2[:], psD[:], AF.Sigmoid, [psD], [s2, psD])
                            A.tt("dve", s2[:], psD[:], s2[:], ALU.mult, [psD, s2], [s2])
                            A.tt("pool", gg[:, 512:1024], s2[:], ghr[:, 512:1024], ALU.mult, [s2, ghr], [gg])
                            A.store(GG[r0:r0 + 128, :], gg[:], gg)
                            if "m" in SKIP:
                                return
                            psG = tm(2048, 16)
                            sm = smp.nxt()
                            A.tt("dve", sm[:, 0:16], psG[:, 0:16], mlbt[:], ALU.add, [psG, mlbt], [sm])
                            gv = sm[:, 0:16].rearrange("p (d g h) -> p d g h", d=2, g=2)
                            A.act(sm[:, 16:24].rearrange("p (d h) -> p d h", d=2), gv[:, :, 1, :], AF.Sigmoid, [sm], [sm])
                            A.act(sm[:, 24:32], sm[:, 16:24], AF.Ln, [sm], [sm])
                            pb = psm.nxt()
                            A.mm([(pb[:, 0:4], cst[:, 1, :], sm[:, 24:28], True, True),
                                  (pb[:, 4:8], cst[:, 2, :], sm[:, 28:32], True, True),
                                  (pb[:, 8:16], cst[:, 3, :], sm[:, 24:32], True, True),
                                  (pb[:, 16:24], cst[:, 4, :], sm[:, 24:32], True, True)], [sm, cst], [pb])
                            A.act(ALt[:, 2 * ti:2 * ti + 2, :], pb[:, 8:24].rearrange("p (c g) -> p c g", g=8), AF.Exp, [pb], [ALt, pb])
                            for hh_ in range(2):
                                A.copy("pool", ALP[64 * hh_:64 * hh_ + 64, 2 * ti:2 * ti + 2, :, :],
                                       ALt[64 * hh_:64 * hh_ + 64, 2 * ti:2 * ti + 2, :].rearrange("p c (d a b) -> p c d a b", d=2, a=2)[:, :, :, :, hh_],
                                       [ALt], [ALP])
                            A.tt("dve", sm[:, 32:40].rearrange("p (d h) -> p d h", d=2), gv[:, :, 0, :],
                                 pb[:, 0:8].rearrange("p (d h) -> p d h", d=2), ALU.subtract, [pb, sm], [sm, pb])
                            A.act(sm[:, 40:48], sm[:, 32:40], AF.Exp, [sm], [sm])
                            if "f" in SKIP:
                                return
                            A.act(sm[:, 48:56], pb[:, 0:8], AF.Exp, [pb], [sm, pb], scale=-1.0)
                            A.copy("dve", FLt[0:64, 2 * ti, :], sm[0:64, 48:56], [sm], [FLt])
                            A.copy("dve", FLt[0:64, 2 * ti + 1, :], sm[64:128, 48:56], [sm], [FLt])
                            if "w" in SKIP:
                                return
                            A.tt("dve", vst[:, 4:8, :], vml[:], sm[:, 40:44].unsqueeze(2).to_broadcast([128, 4, VD]), ALU.mult, [sm, vml], [vst])
                            A.tt("dve", vs1[:], vml[:], sm[:, 44:48].unsqueeze(2).to_broadcast([128, 4, VD]), ALU.mult, [sm, vml], [vs1])
                            A.store(VV[0, r0:r0 + 128, :, :], vst[:], vst)
                            A.store(VV[1, r0:r0 + 128, 4:8, :], vs1[:], vs1)

                        for ti in range(NT):
                            do_tile(ti)
                        P.flush()
                        if stop == 'P1b':
                            raise _Stop()

                    with ExitStack() as ph:
                        qTp = [rot(ph, 2, [128, 8, 2, 256], BF16, "qbd") for _ in range(2)]
                        for d_ in range(2):
                            for t_ in qTp[d_].items:
                                A.memset(t_[:], 0.0, [t_])
                        kTp = [rot(ph, 2, [128, 8, 256], BF16, "kT") for _ in range(2)]
                        ktp = [rot(ph, 2, [64, 4, 1024], BF16, "ktok") for _ in range(2)]
                        vvp = [rot(ph, 2, [64, 4, 16 * VD], BF16, "vv") for _ in range(2)]
                        vmp = rot(ph, 2, [64, 4, 4 * VD], BF16, "vm")
                        S = [[[sbt(ph, [128, VD], F32, "S") for _ in range(2)] for _ in range(4)] for _ in range(2)]
                        Sb = [[[sbt(ph, [128, VD], BF16, "Sb") for _ in range(2)] for _ in range(4)] for _ in range(2)]
                        for d in range(2):
                            for m in range(4):
                                for hp in range(2):
                                    A.memset(S[d][m][hp][:], 0.0, [S[d][m][hp]])
                        psA = Rot([Tl(t_[:, 0:256].rearrange("p (h v) -> p h v", v=64), "psAv") for t_ in
                                   [pst(ph, [64, 512], F32, "psA") for _ in range(4)]])
                        pso = Rot([Tl(t_[:, 0:4 * VD].rearrange("p (h v) -> p h v", v=VD), "psov") for t_ in
                                   [pst(ph, [64, 512], F32, "pso") for _ in range(2)]])
                        psS = Rot([Tl(t_[:, 0:4 * VD].rearrange("p (a v) -> p a v", v=2 * VD), "psSv") for t_ in
                                   [pst(ph, [128, 512], F32, "psS") for _ in range(2)]])
                        mask4 = sbt(ph, [64, 2, 4, 64], F32, "mask4")
                        for d_ in range(2):
                            A.copy("dve", mask4[:, d_, :, :], maskf[:, d_, :].unsqueeze(1).to_broadcast([64, 4, 64]), [maskf], [mask4])
                        mask4u = mask4[:].bitcast(mybir.dt.uint32)
                        Asbd = [rot(ph, 5, [64, 4, 64], BF16, "Asb") for _ in range(2)]
                        for dd_ in range(2):
                            for t_ in Asbd[dd_].items:
                                A.memset(t_[:], 0.0, [t_])
                        tmpS = rot(ph, 6, [128, VD], F32, "tmpS")
                        osb = rot(ph, 2, [64, D], F32, "osb")
                        nrm = rot(ph, 2, [64, 16], F32, "nrm")

                        def col(m, d, hp, kind, c):
                            if m == ML:
                                if kind == 0:
                                    return onescol[:, 0:1], onescol
                                return ALP[:, c, d, hp:hp + 1], ALP
                            return CCt[:, m, d, hp, kind, c:c + 1], CCt

                        def do_chunk(d, c, cc, qT, kT, ktok, vv, vm):
                            ts = slice(64 * cc, 64 * cc + 64)
                            ob = osb.nxt()
                            vs = []
                            for m in range(4):
                                if m == ML and d == 1:
                                    vs.append((vm, 0))
                                else:
                                    vs.append((vv, 4 * m * VD))
                            a4s = []
                            for m in range(4):
                                for hp in range(2):
                                    St, Sbt = S[d][m][hp], Sb[d][m][hp]
                                    c1, c1t = col(m, d, hp, 0, c)
                                    A.act(Sbt[:, :], St[:, :], AF.Identity, [St, c1t], [Sbt], scale=c1)
                                pa = psA.nxt()
                                A.mm([(pa[:, h, :], kT[:, 2 * m + h // 2, ts], qT[:, 2 * m + h // 2, h % 2, ts], True, True) for h in range(4)],
                                     [kT, qT], [pa])
                                a4 = Asbd[d].nxt()
                                A.tt("dve", a4[:], pa[:], mask4[:, d, :, :], ALU.mult, [pa, mask4], [a4])
                                a4s.append(a4)
                            pos, pSs = [], []
                            for m in range(4):
                                vsrc, vbase = vs[m]
                                a4 = a4s[m]
                                po = pso.nxt()
                                items = []
                                for h in range(4):
                                    items.append((po[:, h, :], a4[:, h, :], vsrc[0:64, cc, vbase + h * VD:vbase + (h + 1) * VD], True, False))
                                    items.append((po[:, h, :], qT[:, 2 * m + h // 2, h % 2, ts], Sb[d][m][h // 2][:, :], False, True))
                                A.mm(items, [a4, vsrc, qT, Sb[d][m][0], Sb[d][m][1]], [po])
                                pS = psS.nxt()
                                A.mm([(pS[:, hp, :], ktok[0:64, cc, m * 256 + hp * 128:m * 256 + (hp + 1) * 128],
                                       vsrc[0:64, cc, vbase + 2 * hp * VD:vbase + (2 * hp + 2) * VD], True, True) for hp in range(2)],
                                     [ktok, vsrc], [pS])
                                for hp in range(2):
                                    St = S[d][m][hp]
                                    tS = tmpS.nxt()
                                    c2, c2t = col(m, d, hp, 1, c)
                                    c3, c3t = col(m, d, hp, 2, c)
                                    A.act(tS[:, :], St[:, :], AF.Identity, [St, c2t], [tS], scale=c2)
                                    for hh in range(2):
                                        p0 = 64 * hh
                                        A.stt(St[p0:p0 + 64, :], pS[p0:p0 + 64, hp, hh * VD:(hh + 1) * VD], c3[p0:p0 + 64, :], tS[p0:p0 + 64, :],
                                              ALU.mult, ALU.add, [pS, tS, c3t], [St])
                                ov = ob[:, m * 256:(m + 1) * 256].rearrange("p (h v) -> p h v", v=64)
                                if m == HG:
                                    P.act(lambda e, ov=ov, pin=po[:, :, 0:64]: e.mul(out=ov, in_=pin, mul=float(np.exp(2.0 * HG_SHIFT))), [po], [ob])
                                elif m != ML:
                                    A.copy("act", ov, po[:, :, 0:64], [po], [ob])
                                else:
                                    nr = nrm.nxt()
                                    A.act(nr[:, 0:4], po[:, :, 64], AF.Abs, [po], [nr, po])
                                    A.tt("dve", nr[:, 4:8], nr[:, 0:4], FLt[0:64, c, d * 4:d * 4 + 4], ALU.max, [nr, FLt], [nr])
                                    A.recip(nr[:, 8:12], nr[:, 4:8], [nr], [nr])
                                    A.tt("dve", ov, po[:, :, 0:64], nr[:, 8:12].unsqueeze(2).to_broadcast([64, 4, 64]), ALU.mult, [nr, po], [ob, po])
                            A.store(OO[d, 64 * c:64 * c + 64, :], ob[:], ob)

                        grp_order = [list(range(NG)), [0] + list(range(NG - 1, 0, -1))]
                        cc_order = [[0, 1, 2, 3], [3, 2, 1, 0]]
                        for gi in range(NG):
                            grp = []
                            for d in range(2):
                                g = grp_order[d][gi]
                                tg0 = 256 * g
                                qT, kT, ktok, vv = qTp[d].nxt(), kTp[d].nxt(), ktp[d].nxt(), vvp[d].nxt()
                                A.load(qT[0:64, :, 0, :], QT[d, :, 0:64, tg0:tg0 + 256].rearrange("m p t -> p m t"), qT)
                                A.load(qT[64:128, :, 1, :], QT[d, :, 64:128, tg0:tg0 + 256].rearrange("m p t -> p m t"), qT)
                                A.load(kT[:], KT[d, :, :, tg0:tg0 + 256].rearrange("m p t -> p m t"), kT)
                                A.load(ktok[:], KK[d, tg0:tg0 + 256, :].rearrange("(c p) x -> p c x", p=64), ktok)
                                A.load(vv[:], VV[0, tg0:tg0 + 256, :, :].rearrange("(c p) h v -> p c (h v)", p=64), vv)
                                vm = None
                                if d == 1:
                                    vm = vmp.nxt()
                                    A.load(vm[:], VV[1, tg0:tg0 + 256, 4:8, :].rearrange("(c p) h v -> p c (h v)", p=64), vm)
                                grp.append((g, qT, kT, ktok, vv, vm))
                            for k_ in range(4):
                                for d in range(2):
                                    g, qT, kT, ktok, vv, vm = grp[d]
                                    cc = cc_order[d][k_]
                                    P.release()
                                    do_chunk(d, 4 * g + cc, cc, qT, kT, ktok, vv, vm)
                        P.flush()
                        if stop == 'P2':
                            raise _Stop()

                with ExitStack() as ph:
                    g1 = [sbt(ph, [128, D], F32, "g1") for _ in range(2)]
                    g2 = [sbt(ph, [128, D], F32, "g2") for _ in range(2)]

                    def gtiles(gi, dst):
                        with ExitStack() as ph2:
                            wad = rot(ph2, 2, [128, 8, 512], F32, "wad")
                            bg = sbt(ph2, [128, D], F32, "bg")
                            pg = rot(ph2, 2, [128, 512], F32, "pg", psum=True)
                            crep = [sbt(ph2, [128, 8, 128], F32, "crep") for _ in range(2)]
                            for cl in range(2):
                                A.copy("dve", crep[cl][:], scs[:, :, cl:cl + 1].to_broadcast([128, 8, 128]), [scs], [crep[cl]])
                            A.load(bg[:], BADA_G[l, gi, :, :], bg)
                            vec = (2, 5)[gi]
                            for half in range(2):
                                blk = vec * 2 + half
                                w = wad.nxt()
                                A.load(w[:], W_ADA[l, :, blk * 512:(blk + 1) * 512].rearrange("(kt p) n -> p kt n", p=128), w)
                                for cl in range(2):
                                    ps = pg.nxt()
                                    A.mm([(ps[:], crep[cl][:, kt, :], w[:, kt, :], kt == 0, kt == 7) for kt in range(8)], [w, crep[cl]], [ps])
                                    A.tt("dve", dst[cl][:, half * 512:(half + 1) * 512], ps[:], bg[:, half * 512:(half + 1) * 512], ALU.add,
                                         [ps, bg], [dst[cl]])
                            P.flush()
                            if stop == 'P3a':
                                raise _Stop()
                    gtiles(0, g1)
                    wf1 = sbt(ph, [128, 8, DFF], BF16, "wf1")
                    load_w_cast(wf1, W_FF1[l, :, :], 8)
                    with ExitStack() as ph2:
                        wo = sbt(ph2, [128, 8, D], BF16, "wo")
                        load_w_cast(wo, W_OUT[l, :, :], 8)
                        o0p = rot(ph2, 3, [128, D], F32, "o0")
                        o1p = rot(ph2, 3, [128, D], F32, "o1")
                        gglp = rot(ph2, 3, [128, D], BF16, "ggl")
                        xp = rot(ph2, 4, [128, D], F32, "x3")
                        sqp = rot(ph2, 2, [128, D], F32, "sq")
                        ssp = rot(ph2, 3, [128, 48], F32, "ss3")
                        yp = rot(ph2, 3, [128, D], BF16, "y")
                        yTp = rot(ph2, 3, [128, 8, 128], BF16, "yT")
                        ptr = rot(ph2, 2, [128, 8, 128], BF16, "ptr3", psum=True)
                        pop = rot(ph2, 4, [128, 512], F32, "pop", psum=True)
                        tp = rot(ph2, 4, [128, 512], F32, "t3")

                        def do_tile3(ti):
                            r0 = ti * 128
                            cl = 0 if r0 < CTX else 1
                            o0, o1, ggl, xt = o0p.nxt(), o1p.nxt(), gglp.nxt(), xp.nxt()
                            A.load(o0[:], OO[0, r0:r0 + 128, :], o0)
                            A.load(o1[:], OO[1, r0:r0 + 128, :], o1)
                            A.load(ggl[:], GG[r0:r0 + 128, :], ggl)
                            A.load(xt[:], xsrc[r0:r0 + 128, :], xt)
                            P.release(keep=1)
                            A.tt("pool", o0[:], o0[:], o1[:], ALU.add, [o0, o1], [o0])
                            sq, ss = sqp.nxt(), ssp.nxt()
                            A.act(sq[:], o0[:], AF.Square, [o0], [sq])
                            A.reduce(ss[:, 0:16], sq[:].rearrange("p (h v) -> p h v", v=64), [sq], [ss])
                            A.act(ss[:, 16:32], ss[:, 0:16], AF.Ln, [ss, epscol], [ss], scale=1.0 / 64, bias=epscol[:, 0:1])
                            A.act(ss[:, 32:48], ss[:, 16:32], AF.Exp, [ss], [ss], scale=-0.5)
                            o3 = o0[:].rearrange("p (h v) -> p h v", v=64)
                            A.tt("dve", o3, o3, ss[:, 32:48].unsqueeze(2).to_broadcast([128, 16, 64]), ALU.mult, [o0, ss], [o0])
                            y = yp.nxt()
                            A.tt("dve", y[:], o0[:], ggl[:], ALU.mult, [o0, ggl], [y])
                            pt = ptr.nxt()
                            A.tr([(pt[:, kt, :], y[:, kt * 128:(kt + 1) * 128]) for kt in range(8)], identb[:], [y, identb], [pt])
                            yT = yTp.nxt()
                            A.copy("act", yT[:, 0:4, :], pt[:, 0:4, :], [pt], [yT])
                            A.copy("dve", yT[:, 4:8, :], pt[:, 4:8, :], [pt], [yT])
                            for nb in range(2):
                                ps = pop.nxt()
                                A.mm([(ps[:], yT[:, kt, :], wo[:, kt, nb * 512:(nb + 1) * 512], kt == 0, kt == 7) for kt in range(8)], [yT, wo], [ps])
                                t = tp.nxt()
                                A.tt("dve", t[:], ps[:], g1[cl][:, nb * 512:(nb + 1) * 512], ALU.mult, [ps, g1[cl]], [t])
                                A.tt("pool", xt[:, nb * 512:(nb + 1) * 512], xt[:, nb * 512:(nb + 1) * 512], t[:], ALU.add, [t, xt], [xt])
                            A.store(XS[r0:r0 + 128, :], xt[:], xt)

                        for ti in range(NT):
                            do_tile3(ti)
                        P.flush()
                        if stop == 'P3a':
                            raise _Stop()

                    gtiles(1, g2)
                    with ExitStack() as ph2:
                        wf2 = sbt(ph2, [128, 32, D], BF16, "wf2")
                        load_w_cast(wf2, W_FF2[l, :, :], 32)
                        hTp = rot(ph2, 2, [128, 8, 256], BF16, "h2T")
                        npools = (rot(ph2, 3, [128, D], F32, "xt"), rot(ph2, 1, [128, D], BF16, "jk"), rot(ph2, 2, [128, 4], F32, "ss"),
                                  rot(ph2, 2, [128, D], BF16, "xn"), rot(ph2, 2, [128, 8, 128], BF16, "ptr", psum=True))
                        uTp = rot(ph2, 1, [128, 32, 256], BF16, "uT")
                        pu = Rot([Tl(t_[:, 0:256], "pus") for t_ in [pst(ph2, [128, 512], F32, "pu") for _ in range(3)]])
                        po2 = rot(ph2, 2, [128, 512], F32, "po2", psum=True)
                        sqp = rot(ph2, 3, [128, 256], F32, "sq2")
                        tp = rot(ph2, 1, [128, 512], F32, "t4")

                        def do_blk(bi):
                            cl = 0 if bi * 256 < CTX else 1
                            hT = hTp.nxt()
                            xts = []
                            for i in range(2):
                                xts.append(norm_T(npools, XS, bi * 256 + 128 * i, hT, 128 * i, mcol(3, cl), mcol(2, cl)))
                                if i == 0:
                                    P.release()
                            uT = uTp.nxt()
                            for fb in range(32):
                                ps = pu.nxt()
                                A.mm([(ps[:], wf1[:, kt, fb * 128:(fb + 1) * 128], hT[:, kt, :], kt == 0, kt == 7) for kt in range(8)], [wf1] + hT.kb, [ps])
                                sq = sqp.nxt()
                                A.act(sq[:], ps[:], AF.Square, [ps], [sq])
                                A.stt(uT[:, fb, :], ps[:], 0.0, sq[:], ALU.is_gt, ALU.mult, [ps, sq], [uT])
                            for i in range(2):
                                xt = xts[i]
                                for nb in range(2):
                                    ps = po2.nxt()
                                    A.mm([(ps[:], uT[:, fb, 128 * i:128 * (i + 1)], wf2[:, fb, nb * 512:(nb + 1) * 512], fb == 0, fb == 31)
                                          for fb in range(32)], [uT, wf2], [ps])
                                    t = tp.nxt()
                                    A.tt("dve", t[:], ps[:], g2[cl][:, nb * 512:(nb + 1) * 512], ALU.mult, [ps, g2[cl]], [t])
                                    A.tt("pool", xt[:, nb * 512:(nb + 1) * 512], xt[:, nb * 512:(nb + 1) * 512], t[:], ALU.add, [t, xt], [xt])
                                r0 = bi * 256 + 128 * i
                                A.store(XS[r0:r0 + 128, :], xt[:], xt)

                        for bi in range(T // 256):
                            do_blk(bi)
                        P.flush()
                        if stop == 'P3b':
                            raise _Stop()

        except _Stop:
            P.flush(final=True)
            build.ninst = P.ninst
            gs.pop_all()
            return nc
        with ExitStack() as ph:
            gf = sbt(ph, [128, D], F32, "gf")
            A.load(gf[:], GFIN[:, :], gf)
            xp = rot(ph, 3, [128, D], F32, "xf")
            jp = rot(ph, 1, [128, D], BF16, "jkf")
            sp_ = rot(ph, 2, [128, 4], F32, "ssf")
            for ti in range(LAT // 128):
                r0 = CTX + ti * 128
                xt, jk, ss = xp.nxt(), jp.nxt(), sp_.nxt()
                A.load(xt[:], XS[r0:r0 + 128, :], xt)
                P.release()
                A.act(jk[:], xt[:], AF.Square, [xt], [jk, ss], accum_out=ss[:, 0:1])
                A.act(ss[:, 1:2], ss[:, 0:1], AF.Ln, [ss, epscol], [ss], scale=1.0 / D, bias=epscol[:, 0:1])
                A.act(ss[:, 2:3], ss[:, 1:2], AF.Exp, [ss], [ss], scale=-0.5)
                A.stt(xt[:], xt[:], ss[:, 2:3], gf[:], ALU.mult, ALU.mult, [xt, ss, gf], [xt])
                A.store(OUT[ti * 128:(ti + 1) * 128, :], xt[:], xt)
            P.flush(final=True)
        build.ninst = P.ninst
    return nc


_IN_LAYOUT = (
    ('hg_q', 256), ('hg_f_fwd', 256), ('hg_f_bwd', 256), ('hg_i', 256), ('hg_g', 256),
    ('ml_q', 256), ('ml_k', 256), ('ml_v', 256), ('ml_if', 16), ('ml_o', 256),
    ('rt_q', 256), ('rt_k', 256), ('rt_v', 256), ('rt_g', 256),
    ('gl_q', 128), ('gl_k', 128), ('gl_v', 256),
    ('gl_a_fwd', 16), ('gl_a_bwd', 16), ('gl_g', 256),
)


def _col_ranges():
    off = {}
    o = 0
    for nme, s in _IN_LAYOUT:
        off[nme] = (o, s)
        o += s
    return off


def _w1_layout(w_in):
    off = _col_ranges()
    dep = w_in.shape[0]
    out = np.zeros((dep, D, NC1), np.float32)

    def cols(nme):
        o, s = off[nme]
        return w_in[:, :, o:o + s]
    perm = np.zeros(256, np.int64)
    for h in range(4):
        for dd in range(64):
            r = dd % 32
            partner = dd + 16 if r < 16 else dd - 16
            perm[h * 64 + dd] = h * 64 + partner

    def pad_gla(a):
        p = np.zeros((dep, D, 256), np.float32)
        for h in range(4):
            p[:, :, h * 64:h * 64 + 32] = a[:, :, h * 32:(h + 1) * 32]
        return p
    fmc = [cols('hg_q'), cols('hg_f_fwd'), cols('hg_f_bwd'), cols('ml_q'), cols('ml_k'),
           cols('rt_q'), cols('rt_q')[:, :, perm], cols('rt_k'), cols('rt_k')[:, :, perm],
           pad_gla(cols('gl_q')), pad_gla(cols('gl_k')), cols('gl_a_fwd'), cols('gl_a_bwd')]
    tmc = [cols('hg_i'), cols('ml_v'), cols('rt_v'), cols('gl_v'), cols('hg_g'), cols('ml_o'), cols('rt_g'), cols('gl_g'), cols('ml_if')]
    o = 0
    for a in fmc + tmc:
        out[:, :, o:o + a.shape[2]] = a
        o += a.shape[2]
    assert o == NC1
    return out


def _rope_tables(T):
    LATn = T - CTX
    tl = np.arange(LATn)
    row = (tl // 64).astype(np.float32)
    colp = (tl % 64).astype(np.float32)
    inv = (np.float32(10000.0) ** (-np.arange(16, dtype=np.float32) / np.float32(16))).astype(np.float32)
    cosT = np.ones((128, T), np.float32)
    sinT = np.zeros((128, T), np.float32)
    for p in range(128):
        dd = p % 64
        pos = row if dd < 32 else colp
        ang = (pos * inv[dd % 16]).astype(np.float32)
        sgn = -1.0 if (dd % 32) < 16 else 1.0
        cosT[p, CTX:] = np.cos(ang).astype(np.float32)
        sinT[p, CTX:] = (sgn * np.sin(ang)).astype(np.float32)
    return cosT, sinT


def _consts():
    c = np.zeros((128, 8, 128), np.float32)
    s = np.arange(128)[:, None]
    t = np.arange(128)[None, :]
    same = (s // 64) == (t // 64)
    c[:, 0, :] = (s == t)
    c[:, 1, :] = same & (s <= t)
    c[:, 2, :] = same & (s >= t)
    c[:, 3, :] = (s < 64) & (t >= 0)
    c[:, 4, :] = (s >= 64) & (t >= 0)
    c[:64, 5, :64] = (s[:64] <= t[:, :64])
    c[:64, 6, :64] = (s[:64] >= t[:, :64])
    return c


def make_shared(inp, T):
    dep = inp['w_ada'].shape[0]
    f32 = np.float32
    sh = {}
    sh['w_ada'] = np.ascontiguousarray(inp['w_ada'], f32)
    b_ada = np.asarray(inp['b_ada'], f32)
    bc = np.zeros((dep, 128, 4, 8), f32)
    for which, vec in enumerate((0, 1, 3, 4)):
        bc[:, :, which, :] = b_ada[:, vec * D:(vec + 1) * D].reshape(dep, 8, 128).transpose(0, 2, 1)
    sh['bada_col'] = bc
    bg = np.zeros((dep, 2, 128, D), f32)
    for gi, vec in enumerate((2, 5)):
        bg[:, gi, :, :] = b_ada[:, None, vec * D:(vec + 1) * D]
    sh['bada_g'] = bg
    sh['w1'] = _w1_layout(np.asarray(inp['w_in'], f32))
    sh['ghr'] = np.ascontiguousarray(np.broadcast_to(np.asarray(inp['g_heads'], f32)[:, None, :], (dep, 128, D)))
    hl = np.asarray(inp['hgrn_lb_logits'], f32)
    sh['hgl'] = np.ascontiguousarray(hl.reshape(dep, 2, 2, 128).transpose(3, 0, 1, 2))
    mb = np.asarray(inp['ml_gate_bias'], f32).reshape(dep, 16)
    sh['mlb'] = np.ascontiguousarray(np.broadcast_to(mb[:, None, :], (dep, 128, 16)))
    rl = np.asarray(inp['rt_decay_logit'], f32)
    rt = np.zeros((128, dep, 2, 2), f32)
    for j in range(2):
        for hh in range(2):
            rt[hh * 64:(hh + 1) * 64, :, :, j] = rl[None, :, :, 2 * j + hh]
    sh['rtl'] = rt
    wa = np.asarray(inp['gla_w_a'], f32)
    wap = np.zeros((dep, 2, 16, 256), f32)
    ba = np.asarray(inp['gla_b_a'], f32)
    bap = np.zeros((dep, 2, 256), f32)
    for h in range(4):
        wap[:, :, :, h * 64:h * 64 + 32] = wa[:, :, :, h * 32:(h + 1) * 32]
        bap[:, :, h * 64:h * 64 + 32] = ba[:, :, h * 32:(h + 1) * 32]
    sh['wa'] = wap
    sh['ba'] = np.ascontiguousarray(bap.reshape(dep, 2, 2, 128).transpose(3, 0, 1, 2))
    sh['w_out'] = np.ascontiguousarray(inp['w_out'], f32)
    sh['w_ff1'] = np.ascontiguousarray(inp['w_ff1'], f32)
    sh['w_ff2'] = np.ascontiguousarray(inp['w_ff2'], f32)
    sh['gfin'] = np.ascontiguousarray(np.broadcast_to(np.asarray(inp['g_final'], f32)[None, :], (128, D)))
    c, s = _rope_tables(T)
    sh['ropec'] = c
    sh['ropes'] = s
    sh['consts'] = _consts()
    return sh


def make_core(inp, b):
    f32 = np.float32
    m = {}
    m['xin'] = np.ascontiguousarray(np.concatenate([np.asarray(inp['ctx'][b], f32), np.asarray(inp['x'][b], f32)], axis=0))
    cc = np.zeros((128, 8, 2), f32)
    cc[:, :, 0] = np.asarray(inp['c_ctx'], f32).reshape(8, 128).T
    cc[:, :, 1] = np.asarray(inp['c'][b], f32).reshape(8, 128).T
    m['cc'] = cc
    return m


_CACHE = {}


def kernel(**inputs):
    x = inputs['x']
    B, LAT, _ = x.shape
    T = CTX + LAT
    if LAT not in _CACHE:
        _CACHE[LAT] = build(LAT)
    nc = _CACHE[LAT]
    sh = make_shared(inputs, T)
    in_maps = []
    for b in range(B):
        m = dict(sh)
        m.update(make_core(inputs, b))
        in_maps.append(m)
    res = run_bass_kernel_spmd(nc, in_maps, core_ids=list(range(B)))
    return np.stack([np.asarray(r["out"], np.float32) for r in res.results], axis=0)
```
